# Optimizing a Trainium2 kernel written in Bass

```python
import math
import jax
import jax.numpy as jnp
from jax import lax
import numpy as np

D_MODEL = 1024
BATCH = 8
SEQ = 8192
DEPTH = 2

MEM_LEN = 256
NORM_EPS = 1e-6
MASK_VALUE = -1e30
TINY = 1e-30

HG_HEADS = 4
HG_DK = 128
HG_DV = 128
HG_WIDTH = HG_HEADS * HG_DK
HG_VWIDTH = HG_HEADS * HG_DV
HG_CHUNK = 64

DA_HEADS = 4
DA_DK = 64
DA_DV = 2 * DA_DK
DA_WIDTH = DA_HEADS * 2 * DA_DK
DA_VWIDTH = DA_HEADS * DA_DV
Q_BLOCK = 128
ALIBI_MAX_BIAS = 8.0

RW_HEADS = 8
RW_HEAD = 64
RW_WIDTH = RW_HEADS * RW_HEAD
RW_DECAY_RANK = 64
RW_A_RANK = 64
RW_GATE_RANK = 128
RW_COLS = 3 * RW_WIDTH + RW_DECAY_RANK + RW_A_RANK + RW_GATE_RANK
RW_GN_EPS = 64e-5
RW_SPLITS = (RW_WIDTH, 2 * RW_WIDTH, 3 * RW_WIDTH, 3 * RW_WIDTH + RW_DECAY_RANK,
             3 * RW_WIDTH + RW_DECAY_RANK + RW_A_RANK)

N_BRANCH = 3
BRANCH_WIDTH = 512

OFF_HG_F = HG_WIDTH
OFF_HG_I = 2 * HG_WIDTH
OFF_HG_G = OFF_HG_I + HG_VWIDTH
OFF_DA_Q = OFF_HG_G + HG_VWIDTH
OFF_DA_K = OFF_DA_Q + DA_WIDTH
OFF_DA_V = OFF_DA_K + DA_WIDTH
OFF_RW = OFF_DA_V + DA_VWIDTH
OFF_GATE = OFF_RW + RW_COLS
N_IN = OFF_GATE + N_BRANCH * D_MODEL
IN_SPLITS = (OFF_HG_F, OFF_HG_I, OFF_HG_G, OFF_DA_Q, OFF_DA_K, OFF_DA_V, OFF_RW, OFF_GATE)

XA_HEADS = 4
XA_HEAD = D_MODEL // XA_HEADS

D_FF = 2816
CONV_W = 3

kernel_name = "hybrid_hgrn2_diffattn_rwkv7_gated_block"


def rms_norm(x, g, eps=NORM_EPS):
    xf = x.astype(jnp.float32)
    y = xf * lax.rsqrt(jnp.mean(xf * xf, axis=-1, keepdims=True) + eps)
    return (y * g.astype(jnp.float32)).astype(x.dtype)


def group_norm(y, g, b, eps):
    mu = jnp.mean(y, axis=-1, keepdims=True)
    var = jnp.mean(jnp.square(y - mu), axis=-1, keepdims=True)
    return (y - mu) * lax.rsqrt(var + eps) * g.astype(jnp.float32) + b.astype(jnp.float32)


def alibi_slopes(n_heads):
    return 2.0 ** (-ALIBI_MAX_BIAS * jnp.arange(1, n_heads + 1, dtype=jnp.float32) / n_heads)


def hgrn2_branch(q_p, f_p, i_p, g_p, lb, norm_g):
    B, S, _ = q_p.shape
    dt = q_p.dtype
    q = jax.nn.silu(q_p.astype(jnp.float32))
    fp = f_p.astype(jnp.float32)
    lbf = lb.astype(jnp.float32)
    f = lbf + (1.0 - lbf) * jax.nn.sigmoid(fp)
    log_f = jnp.log(jnp.maximum(f, TINY))
    k = (1.0 - lbf) * jax.nn.sigmoid(-fp)
    v = i_p.astype(jnp.float32)
    n_chunks = S // HG_CHUNK

    def to_chunks(t, d):
        return t.reshape(B, n_chunks, HG_CHUNK, HG_HEADS, d).transpose(1, 0, 3, 2, 4)

    xs = (to_chunks(q, HG_DK), to_chunks(k, HG_DK), to_chunks(v, HG_DV), to_chunks(log_f, HG_DK))
    causal = jnp.tril(jnp.ones((HG_CHUNK, HG_CHUNK), dtype=bool))[:, :, None]

    def step(state, chunk):
        qb, kb, vb, gb = chunk
        b = jnp.cumsum(gb, axis=2)
        diff = b[:, :, :, None, :] - b[:, :, None, :, :]
        decay = jnp.exp(jnp.where(causal, diff, MASK_VALUE))
        scores = jnp.einsum('bhtk,bhtsk,bhsk->bhts', qb, decay, kb)
        o = (jnp.einsum('bhts,bhsv->bhtv', scores, vb)
             + jnp.einsum('bhtk,bhkv->bhtv', qb * jnp.exp(b), state))
        b_last = b[:, :, -1:, :]
        state = (jnp.exp(b_last[:, :, 0, :])[..., None] * state
                 + jnp.einsum('bhsk,bhsv->bhkv', kb * jnp.exp(b_last - b), vb))
        return state, o

    s0 = jnp.zeros((B, HG_HEADS, HG_DK, HG_DV), jnp.float32)
    _, o = lax.scan(step, s0, xs)
    o = o.transpose(1, 0, 3, 2, 4).reshape(B, S, HG_HEADS, HG_DV)
    o = rms_norm(o, norm_g.reshape(HG_HEADS, HG_DV)).reshape(B, S, HG_VWIDTH)
    return (o * jax.nn.silu(g_p.astype(jnp.float32))).astype(dt)


def diff_attention(q_p, k_p, v_p, lam, subln_g, lambda_init):
    B, S, _ = q_p.shape
    dt = q_p.dtype
    q = q_p.reshape(B, S, DA_HEADS, 2, DA_DK)
    k = k_p.reshape(B, S, DA_HEADS, 2, DA_DK)
    v = v_p.reshape(B, S, DA_HEADS, DA_DV)
    lf = lam.astype(jnp.float32)
    lam_full = jnp.exp(jnp.sum(lf[0] * lf[1])) - jnp.exp(jnp.sum(lf[2] * lf[3])) + lambda_init
    slopes = alibi_slopes(DA_HEADS)
    scale = DA_DK ** -0.5
    n_blk = S // Q_BLOCK
    q_blocks = q.reshape(B, n_blk, Q_BLOCK, DA_HEADS, 2, DA_DK).swapaxes(0, 1)
    k_pos = jnp.arange(S)

    def block(args):
        qb, start = args
        dist = (start + jnp.arange(Q_BLOCK))[:, None] - k_pos[None, :]
        bias = -slopes[:, None, None] * dist.astype(jnp.float32)
        s = jnp.einsum('bqhcd,bshcd->bhcqs', qb, k,
                       preferred_element_type=jnp.float32) * scale + bias[None, :, None]
        s = jnp.where((dist >= 0)[None, None, None], s, MASK_VALUE)
        p = jax.nn.softmax(s, axis=-1)
        a = p[:, :, 0] - lam_full * p[:, :, 1]
        return jnp.einsum('bhqs,bshe->bqhe', a.astype(dt), v)

    starts = jnp.arange(n_blk) * Q_BLOCK
    o = lax.map(block, (q_blocks, starts))
    o = o.swapaxes(0, 1).reshape(B, S, DA_HEADS, DA_DV)
    o = rms_norm(o, subln_g) * (1.0 - lambda_init)
    return o.reshape(B, S, DA_VWIDTH)


def rwkv7_branch(z, mu, w0, w_up, a0, a_up, g_up, k_k, k_a, r_k, ln_g, ln_b):
    B, S, _ = z.shape
    dt = z.dtype
    z_prev = jnp.pad(z[:, :-1], ((0, 0), (1, 0), (0, 0)))
    z = z + mu * (z_prev - z)
    r, k, v, wd, ad, gd = jnp.split(z, RW_SPLITS, axis=-1)
    w = (w0 + jnp.tanh(wd) @ w_up).astype(jnp.float32)
    decay = jnp.exp(-jnp.exp(-jax.nn.softplus(-w) - 0.5))
    a = jax.nn.sigmoid(a0 + ad @ a_up).astype(jnp.float32)
    g = jax.nn.sigmoid(gd) @ g_up
    heads = lambda t: t.astype(jnp.float32).reshape(B, S, RW_HEADS, RW_HEAD)
    kk = heads(k * k_k)
    kk = kk / jnp.maximum(jnp.sqrt(jnp.sum(kk * kk, axis=-1, keepdims=True)), 1e-12)
    k = heads(k) * (1.0 + (heads(a) - 1.0) * k_a.astype(jnp.float32).reshape(RW_HEADS, RW_HEAD))
    r, v, a, decay = heads(r), heads(v), heads(a), heads(decay)

    def step(state, inp):
        r_t, w_t, k_t, v_t, kk_t, a_t = inp
        sa = jnp.einsum('bhvk,bhk->bhv', state, -kk_t)
        state = (state * w_t[:, :, None, :]
                 + sa[..., None] * (kk_t * a_t)[:, :, None, :]
                 + v_t[..., None] * k_t[:, :, None, :])
        return state, jnp.einsum('bhvk,bhk->bhv', state, r_t)

    xs = tuple(t.swapaxes(0, 1) for t in (r, decay, k, v, kk, a))
    s0 = jnp.zeros((B, RW_HEADS, RW_HEAD, RW_HEAD), jnp.float32)
    _, y = lax.scan(step, s0, xs)
    y = y.swapaxes(0, 1)
    y = group_norm(y, ln_g.reshape(RW_HEADS, RW_HEAD), ln_b.reshape(RW_HEADS, RW_HEAD), RW_GN_EPS)
    bonus = jnp.sum(r * k * r_k.astype(jnp.float32), axis=-1, keepdims=True) * v
    y = (y + bonus).reshape(B, S, RW_WIDTH)
    return (y * g.astype(jnp.float32)).astype(dt)


def memory_cross_attention(h, mem_n, wq, wkv, wo):
    B, S, _ = h.shape
    M = mem_n.shape[1]
    q = (h @ wq).reshape(B, S, XA_HEADS, XA_HEAD)
    kv = (mem_n @ wkv).reshape(B, M, 2, XA_HEADS, XA_HEAD)
    s = jnp.einsum('bshd,bmhd->bhsm', q, kv[:, :, 0],
                   preferred_element_type=jnp.float32) * (XA_HEAD ** -0.5)
    p = jax.nn.softmax(s, axis=-1).astype(h.dtype)
    o = jnp.einsum('bhsm,bmhd->bshd', p, kv[:, :, 1]).reshape(B, S, D_MODEL)
    return o @ wo


def conv_gated_ffn(h, w_up, conv_w, conv_b, w_down):
    u, v = jnp.split(h @ w_up, 2, axis=-1)
    u = lax.conv_general_dilated(u, conv_w[:, None, :], window_strides=(1,),
                                 padding=[(CONV_W - 1, 0)],
                                 dimension_numbers=('NWC', 'WIO', 'NWC'),
                                 feature_group_count=D_FF) + conv_b
    return (jax.nn.silu(u) * v) @ w_down


def setup_inputs(seed: int = 0) -> dict:
    key = jax.random.key(seed)
    ks = iter(jax.random.split(key, 40))
    L = DEPTH

    def nrm(shape, scale):
        return scale * jax.random.normal(next(ks), shape, jnp.float32)

    def gain(shape):
        return 1.0 + 0.02 * jax.random.normal(next(ks), shape, jnp.float32)

    return {
        "x": nrm((BATCH, SEQ, D_MODEL), 1.0),
        "mem": nrm((BATCH, MEM_LEN, D_MODEL), 1.0),
        "norm_mix_g": gain((L, D_MODEL)),
        "w_in": nrm((L, D_MODEL, N_IN), D_MODEL ** -0.5),
        "b_gate": nrm((L, N_BRANCH * D_MODEL), 0.02),
        "hgrn_lb_param": nrm((L, HG_WIDTH), 1.0),
        "hgrn_norm_g": gain((L, HG_VWIDTH)),
        "diff_lambda": nrm((L, 4, DA_DK), 0.1),
        "diff_subln_g": gain((L, DA_DV)),
        "rwkv_mu": jax.random.uniform(next(ks), (L, RW_COLS), jnp.float32),
        "rwkv_w0": nrm((L, RW_WIDTH), 0.5),
        "rwkv_w_up": nrm((L, RW_DECAY_RANK, RW_WIDTH), RW_DECAY_RANK ** -0.5),
        "rwkv_a0": nrm((L, RW_WIDTH), 0.1),
        "rwkv_a_up": nrm((L, RW_A_RANK, RW_WIDTH), RW_A_RANK ** -0.5),
        "rwkv_g_up": nrm((L, RW_GATE_RANK, RW_WIDTH), RW_GATE_RANK ** -0.5),
        "rwkv_k_k": 0.85 + nrm((L, RW_WIDTH), 0.02),
        "rwkv_k_a": gain((L, RW_WIDTH)),
        "rwkv_r_k": nrm((L, RW_HEADS, RW_HEAD), 0.1),
        "rwkv_ln_g": gain((L, RW_WIDTH)),
        "rwkv_ln_b": nrm((L, RW_WIDTH), 0.02),
        "w_branch": nrm((L, N_BRANCH, BRANCH_WIDTH, D_MODEL), BRANCH_WIDTH ** -0.5),
        "w_out": nrm((L, D_MODEL, D_MODEL), D_MODEL ** -0.5),
        "norm_xa_g": gain((L, D_MODEL)),
        "norm_mem_g": gain((L, D_MODEL)),
        "xa_wq": nrm((L, D_MODEL, D_MODEL), D_MODEL ** -0.5),
        "xa_wkv": nrm((L, D_MODEL, 2 * D_MODEL), D_MODEL ** -0.5),
        "xa_wo": nrm((L, D_MODEL, D_MODEL), D_MODEL ** -0.5),
        "norm_ffn_g": gain((L, D_MODEL)),
        "ffn_w_up": nrm((L, D_MODEL, 2 * D_FF), D_MODEL ** -0.5),
        "ffn_conv_w": nrm((L, CONV_W, D_FF), CONV_W ** -0.5),
        "ffn_conv_b": nrm((L, D_FF), 0.02),
        "ffn_w_down": nrm((L, D_FF, D_MODEL), D_FF ** -0.5),
        "final_norm_g": gain((D_MODEL,)),
    }


def reference(x, mem, norm_mix_g, w_in, b_gate, hgrn_lb_param, hgrn_norm_g, diff_lambda,
              diff_subln_g, rwkv_mu, rwkv_w0, rwkv_w_up, rwkv_a0, rwkv_a_up, rwkv_g_up,
              rwkv_k_k, rwkv_k_a, rwkv_r_k, rwkv_ln_g, rwkv_ln_b, w_branch, w_out,
              norm_xa_g, norm_mem_g, xa_wq, xa_wkv, xa_wo, norm_ffn_g, ffn_w_up,
              ffn_conv_w, ffn_conv_b, ffn_w_down, final_norm_g):
    B, S, _ = x.shape
    lb_p = jax.nn.softmax(hgrn_lb_param.astype(jnp.float32), axis=0)
    lower_bounds = jnp.cumsum(lb_p, axis=0) - lb_p[0]
    for l in range(DEPTH):
        h = rms_norm(x, norm_mix_g[l])
        proj = h @ w_in[l]
        hq, hf, hi, hg, dq, dk, dv, rw, g_pre = jnp.split(proj, IN_SPLITS, axis=-1)
        o_hg = hgrn2_branch(hq, hf, hi, hg, lower_bounds[l], hgrn_norm_g[l])
        lambda_init = 0.8 - 0.6 * math.exp(-0.3 * l)
        o_da = diff_attention(dq, dk, dv, diff_lambda[l], diff_subln_g[l], lambda_init)
        o_rw = rwkv7_branch(rw, rwkv_mu[l], rwkv_w0[l], rwkv_w_up[l], rwkv_a0[l], rwkv_a_up[l],
                            rwkv_g_up[l], rwkv_k_k[l], rwkv_k_a[l], rwkv_r_k[l],
                            rwkv_ln_g[l], rwkv_ln_b[l])
        branches = jnp.stack([o_hg, o_da, o_rw], axis=2)
        gate = jax.nn.sigmoid((g_pre + b_gate[l]).astype(jnp.float32)).astype(x.dtype)
        gate = gate.reshape(B, S, N_BRANCH, D_MODEL)
        merged = jnp.sum(gate * jnp.einsum('bsnc,ncd->bsnd', branches, w_branch[l]), axis=2)
        x = x + merged @ w_out[l]
        h = rms_norm(x, norm_xa_g[l])
        x = x + memory_cross_attention(h, rms_norm(mem, norm_mem_g[l]), xa_wq[l], xa_wkv[l], xa_wo[l])
        h = rms_norm(x, norm_ffn_g[l])
        x = x + conv_gated_ffn(h, ffn_w_up[l], ffn_conv_w[l], ffn_conv_b[l], ffn_w_down[l])
    return rms_norm(x, final_norm_g)
```

```python
import contextlib
import math
import numpy as np
import ml_dtypes
import concourse.bass as bass
import concourse.mybir as mybir
from concourse.bass_utils import run_bass_kernel_spmd

F32 = mybir.dt.float32
BF16 = mybir.dt.bfloat16
AF = mybir.ActivationFunctionType
ALU = mybir.AluOpType
AX = mybir.AxisListType

D = 1024
L = 2
NIN = 8448
DFF = 2816
MEM = 256
ENGS = ("pe", "act", "dve", "pool", "sp")
NSLOT = 6
LAM = math.exp(-0.5)


class Buf:
    __slots__ = ("name", "last_w", "readers", "excl")

    def __init__(self, name=""):
        self.name = name
        self.last_w = None
        self.readers = []
        self.excl = False


class T:
    def __init__(self, t, name=""):
        self.t = t
        self.b = Buf(name)

    def __getitem__(self, idx):
        return self.t[idx]


def _b(x):
    return x.b if isinstance(x, T) else x


class Sched:
    def __init__(self, nc):
        self.nc = nc
        self.q = {e: [] for e in ENGS}
        self.cnt = {}
        self.seen = {e: {} for e in ENGS}
        self.semkeys = []
        for e in ENGS:
            self._mk(e)
        self.dma_n = {e: 0 for e in ENGS}
        for e in ("sp", "act", "pool"):
            for i in range(NSLOT):
                self._mk(("dma", e, i))
        self.n_instr = 0
        self.nblk = 0

    def _mk(self, k):
        self.cnt[k] = 0
        self.semkeys.append(k)

    def _need(self, e, deps):
        best = {}
        for d in deps:
            if d is None:
                continue
            k, v = d
            if k == e and e == "pe":
                continue
            if self.seen[e].get(k, 0) >= v:
                continue
            if best.get(k, 0) < v:
                best[k] = v
        for k, v in best.items():
            self.seen[e][k] = v
            self.q[e].append(("wait", k, v))

    def _deps(self, reads, writes):
        deps = []
        for r in reads:
            r = _b(r)
            deps.append(r.last_w)
            if r.excl:
                deps.extend(r.readers)
        for w in writes:
            w = _b(w)
            deps.append(w.last_w)
            deps.extend(w.readers)
        return deps

    MAXOPS = None

    def op(self, e, fn, reads=(), writes=()):
        if Sched.MAXOPS is not None and self.n_instr >= Sched.MAXOPS:
            return
        self._need(e, self._deps(reads, writes))
        self.cnt[e] += 1
        v = self.cnt[e]
        self.q[e].append(("op", fn, e))
        for w in writes:
            w = _b(w)
            w.last_w = (e, v)
            w.readers = []
        for r in reads:
            _b(r).readers.append((e, v))
        self.n_instr += 1

    def dma(self, e, out, in_, reads=(), writes=()):
        if Sched.MAXOPS is not None and self.n_instr >= Sched.MAXOPS:
            return
        n = self.dma_n[e]
        self.dma_n[e] += 1
        slot = ("dma", e, n % NSLOT)
        deps = self._deps(reads, writes)
        if self.cnt[slot] > 0:
            deps.append((slot, self.cnt[slot]))
        self._need(e, deps)
        self.cnt[slot] += 16
        v = self.cnt[slot]
        self.q[e].append(("dma", out, in_, slot))
        for w in writes:
            w = _b(w)
            w.last_w = (slot, v)
            w.readers = []
        for r in reads:
            _b(r).readers.append((slot, v))
        self.n_instr += 1

    def drain(self):
        deps = [(k, v) for k, v in self.cnt.items() if isinstance(k, tuple) and v > 0]
        self._need("sp", deps)

    def emit(self):
        nc = self.nc
        self.drain()
        sems = {}
        for k in self.semkeys:
            nm = "s_" + "_".join(str(x) for x in (k if isinstance(k, tuple) else (k,))) + f"_{self.nblk}"
            sems[k] = nc.alloc_semaphore(name=nm)
        self.nblk += 1
        if self.nblk == 1:
            nc.clear_and_free_semaphores(list(sems.values()))
            nc.all_engine_barrier()
            for k in self.semkeys:
                nm = "s0_" + "_".join(str(x) for x in (k if isinstance(k, tuple) else (k,)))
                sems[k] = nc.alloc_semaphore(name=nm)
        with contextlib.ExitStack() as st:
            st.enter_context(nc.allow_non_contiguous_dma(reason="small strided parameter loads"))
            block = st.enter_context(nc.Block())

            def run(eh, items):
                for it in items:
                    if it[0] == "wait":
                        eh.wait_ge(sems[it[1]], it[2])
                    elif it[0] == "raw":
                        it[1](eh)
                    elif it[0] == "op":
                        it[1](eh).then_inc(sems[it[2]], 1)
                    else:
                        eh.dma_start(out=it[1], in_=it[2]).then_inc(sems[it[3]], 16)

            @block.tensor
            def _(e):
                run(e, self.q["pe"])

            @block.scalar
            def _(e):
                run(e, self.q["act"])

            @block.vector
            def _(e):
                run(e, self.q["dve"])

            @block.gpsimd
            def _(e):
                run(e, self.q["pool"])

            @block.sync
            def _(e):
                run(e, self.q["sp"])
        nc.clear_and_free_semaphores(list(sems.values()))
        nc.all_engine_barrier()
        for k in self.cnt:
            self.cnt[k] = 0
        self.seen = {e: {} for e in ENGS}
        self.q = {e: [] for e in ENGS}


class Ctx:
    def __init__(self, nc, S_, st):
        self.nc = nc
        self.S = S_
        self.st = st
        self.rr = 0

    _uid = [0]

    def sb(self, name, shape, dt=F32):
        Ctx._uid[0] += 1
        name = f"t{Ctx._uid[0]}_{name}"
        return T(self.st.enter_context(self.nc.sbuf_tensor(name, list(shape), dt)), name)

    def ps(self, name, shape, dt=F32):
        Ctx._uid[0] += 1
        name = f"p{Ctx._uid[0]}_{name}"
        t = T(self.st.enter_context(self.nc.psum_tensor(name, list(shape), dt)), name)
        t.b.excl = True
        return t

    def sbn(self, name, shape, dt=F32, n=2):
        return [self.sb(f"{name}{i}", shape, dt) for i in range(n)]

    def psn(self, name, shape, dt=F32, n=2):
        return [self.ps(f"{name}{i}", shape, dt) for i in range(n)]

    _mode = [None]

    def _pe_mode(self, lhsT):
        def rnd(n):
            return 32 if n <= 32 else (64 if n <= 64 else 128)
        shp = lhsT.shape
        k = shp[0]
        mfree = 1
        for d in shp[1:]:
            mfree *= d
        mode = (rnd(k), rnd(mfree))
        if Ctx._mode[0] is not None and Ctx._mode[0] != mode:
            if not (Sched.MAXOPS is not None and self.S.n_instr >= Sched.MAXOPS):
                self.S.q["pe"].append(("raw", lambda e: e.drain()))
        Ctx._mode[0] = mode

    def mm(self, out, lhsT, rhs, start, stop, reads, writes):
        self._pe_mode(lhsT)
        self.S.op("pe", lambda e: e.matmul(out, lhsT=lhsT, rhs=rhs, start=start, stop=stop), reads, writes)

    def tr(self, out, in_, ident, reads, writes):
        self._pe_mode(in_)
        self.S.op("pe", lambda e: e.transpose(out=out, in_=in_, identity=ident), reads, writes)

    def act(self, out, in_, func, reads, writes, bias=None, scale=None, accum=None, eng="act"):
        kw = {}
        if bias is not None:
            kw["bias"] = bias
        if scale is not None:
            kw["scale"] = scale
        if accum is not None:
            kw["accum_out"] = accum
        self.S.op("act", lambda e: e.activation(out=out, in_=in_, func=func, **kw), reads, writes)

    def tt(self, eng, out, in0, in1, op, reads, writes):
        self.S.op(eng, lambda e: e.tensor_tensor(out=out, in0=in0, in1=in1, op=op), reads, writes)

    def ts(self, eng, out, in0, s1, op0, reads, writes, s2=None, op1=None):
        if op1 is None:
            self.S.op(eng, lambda e: e.tensor_scalar(out=out, in0=in0, scalar1=s1, scalar2=None, op0=op0), reads, writes)
        else:
            self.S.op(eng, lambda e: e.tensor_scalar(out=out, in0=in0, scalar1=s1, scalar2=s2, op0=op0, op1=op1), reads, writes)

    def stt(self, out, in0, scalar, in1, op0, op1, reads, writes):
        self.S.op("dve", lambda e: e.scalar_tensor_tensor(out=out, in0=in0, scalar=scalar, in1=in1, op0=op0, op1=op1), reads, writes)

    def cp(self, eng, out, in_, reads, writes):
        if eng == "act":
            self.S.op("act", lambda e: e.activation(out=out, in_=in_, func=AF.Copy), reads, writes)
        else:
            self.S.op(eng, lambda e: e.tensor_copy(out=out, in_=in_), reads, writes)

    def red(self, out, in_, op, reads, writes):
        self.S.op("dve", lambda e: e.tensor_reduce(out=out, in_=in_, axis=AX.X, op=op), reads, writes)

    def recip(self, out, in_, reads, writes):
        self.S.op("dve", lambda e: e.reciprocal(out=out, in_=in_), reads, writes)

    def memset(self, eng, ap, val, writes):
        self.S.op(eng, lambda e: e.memset(ap, val), (), writes)

    def dma(self, out, in_, reads=(), writes=(), q=None):
        if q is None:
            q = ("sp", "act", "pool")[self.rr % 3]
            self.rr += 1
        self.S.dma(q, out, in_, reads, writes)


def host_consts():
    c = {}
    idx = np.arange(128)
    s = idx[:, None]
    t = idx[None, :]
    same = (s // 64) == (t // 64)
    c["ident"] = np.eye(128, dtype=np.float32)
    c["ones"] = np.ones((128, 128), np.float32)
    tri64 = ((s <= t) & same).astype(np.float32)
    mid64 = ((s <= (t // 64) * 64 + 31) & same).astype(np.float32)
    blk64 = same.astype(np.float32)
    c["hgm"] = np.concatenate([tri64, tri64 - mid64, blk64 - tri64], axis=1)
    incl = (s <= t).astype(np.float32)
    strict = (s < t).astype(np.float32)
    rev = (s > t).astype(np.float32)
    c["rwm"] = np.concatenate([incl, strict, rev], axis=1)
    c["maskg"] = np.concatenate([strict, incl, strict, incl], axis=1)
    scale = 64 ** -0.5
    slopes = 2.0 ** (-8.0 * np.arange(1, 5) / 4)
    ki = np.arange(128)[:, None].astype(np.float64)
    qi = np.arange(512)[None, :].astype(np.float64)
    al = np.zeros((4, 5, 128, 512), np.float32)
    for h in range(4):
        al[h, 0] = (-slopes[h] * (qi - ki) / scale)
        for d in range(4):
            dist = qi - ki - 128 * d
            al[h, 1 + d] = np.where(dist >= 0, -slopes[h] * dist / scale, -1e30)
    c["alibi"] = al.transpose(2, 0, 1, 3).reshape(128, 4 * 5 * 512).copy()
    ct = np.zeros((128, 4 * 65), np.float32)
    for h in range(4):
        ct[:, h * 65:(h + 1) * 65] = (-slopes[h] * 128.0 * np.arange(65))[None, :]
    c["ctab"] = ct
    return c


CONST_SHAPES = {"ident": (128, 128), "ones": (128, 128), "hgm": (128, 384), "rwm": (128, 384),
                "maskg": (128, 512), "alibi": (128, 4 * 5 * 512), "ctab": (128, 4 * 65)}

PARAMS = [
    ("norm_mix_g", (L, D)), ("w_in", (L, D, NIN)), ("b_gate", (L, 3072)), ("hgrn_lb_param", (L, 512)),
    ("hgrn_norm_g", (L, 512)), ("diff_lambda", (L, 4, 64)), ("diff_subln_g", (L, 128)),
    ("rwkv_mu", (L, 1792)), ("rwkv_w0", (L, 512)), ("rwkv_w_up", (L, 64, 512)), ("rwkv_a0", (L, 512)),
    ("rwkv_a_up", (L, 64, 512)), ("rwkv_g_up", (L, 128, 512)), ("rwkv_k_k", (L, 512)), ("rwkv_k_a", (L, 512)),
    ("rwkv_r_k", (L, 8, 64)), ("rwkv_ln_g", (L, 512)), ("rwkv_ln_b", (L, 512)),
    ("w_branch", (L, 3, 512, D)), ("w_out", (L, D, D)), ("norm_xa_g", (L, D)), ("norm_mem_g", (L, D)),
    ("xa_wq", (L, D, D)), ("xa_wkv", (L, D, 2 * D)), ("xa_wo", (L, D, D)), ("norm_ffn_g", (L, D)),
    ("ffn_w_up", (L, D, 2 * DFF)), ("ffn_conv_w", (L, 3, DFF)), ("ffn_conv_b", (L, DFF)),
    ("ffn_w_down", (L, DFF, D)), ("final_norm_g", (D,)),
]


class Model:
    def __init__(self, S, debug=False, phases=None):
        self.S = S
        self.debug = debug
        self.phases = phases
        nc = bass.Bass("TRN2", target_bir_lowering=False)
        self.nc = nc
        self.din = {}
        self.din["x"] = nc.dram_tensor("x", [S, D], F32, kind="ExternalInput").ap()
        self.din["mem"] = nc.dram_tensor("mem", [MEM, D], F32, kind="ExternalInput").ap()
        for n, shp in PARAMS:
            self.din[n] = nc.dram_tensor(n, list(shp), F32, kind="ExternalInput").ap()
        for n, shp in CONST_SHAPES.items():
            self.din["c_" + n] = nc.dram_tensor("c_" + n, list(shp), F32, kind="ExternalInput").ap()
        self.out = nc.dram_tensor("out", [S, D], F32, kind="ExternalOutput").ap()
        self.scr = {}
        self.sched = Sched(nc)

    def scratch(self, name, shape, dt):
        kind = "ExternalOutput" if self.debug else "Internal"
        self.scr[name] = self.nc.dram_tensor(name, list(shape), dt, kind=kind).ap()
        return self.scr[name]

    def ctx(self, st):
        return Ctx(self.nc, self.sched, st)


def phase_prep(m):
    nc, S_ = m.nc, m.sched
    specs = [("w_in", "norm_mix_g", D, NIN), ("w_out", None, D, D), ("xa_wq", "norm_xa_g", D, D),
             ("xa_wkv", "norm_mem_g", D, 2 * D), ("xa_wo", None, D, D), ("ffn_w_up", "norm_ffn_g", D, 2 * DFF),
             ("ffn_w_down", None, DFF, D), ("w_branch", None, 1536, D)]
    for name, g, K, N in specs:
        m.scratch("b_" + name, [L, K, N], BF16)
    with contextlib.ExitStack() as st:
        c = m.ctx(st)
        gt = c.sb("gt", [128, 4, L, 8])
        gi = 0
        gmap = {}
        for name, g, K, N in specs:
            if g is not None:
                gmap[g] = gi
                for l in range(L):
                    c.dma(gt[:, gi, l, :], m.din[g][l].rearrange("(c p) -> p c", p=128), writes=[gt])
                gi += 1
        W = 2048
        ins = c.sbn("pin", [128, W], F32, 3)
        outs = c.sbn("pout", [128, W], BF16, 3)
        k = 0
        for name, g, K, N in specs:
            for l in range(L):
                src = m.din[name][l]
                if name == "w_branch":
                    src = src.rearrange("a k n -> (a k) n")
                dst = m.scr["b_" + name][l]
                for kc in range(K // 128):
                    for n0 in range(0, N, W):
                        w = min(W, N - n0)
                        ti, to = ins[k % 3], outs[k % 3]
                        c.dma(ti[:, 0:w], src[kc * 128:(kc + 1) * 128, n0:n0 + w], writes=[ti])
                        eng = ("dve", "pool")[k % 2]
                        if g is not None:
                            c.ts(eng, to[:, 0:w], ti[:, 0:w], gt[:, gmap[g], l, kc:kc + 1], ALU.mult, [ti, gt], [to])
                        else:
                            c.cp(eng, to[:, 0:w], ti[:, 0:w], [ti], [to])
                        c.dma(dst[kc * 128:(kc + 1) * 128, n0:n0 + w], to[:, 0:w], reads=[to])
                        k += 1
        S_.emit()


def load_consts(m, c, names):
    r = {}
    for n in names:
        shp = CONST_SHAPES[n]
        t = c.sb("c_" + n, shp, F32)
        c.dma(t[:], m.din["c_" + n][:, :], writes=[t])
        r[n] = t
    return r


def make_bf(c, src, shape, name):
    t = c.sb(name, shape, BF16)
    c.cp("dve", t[:], src[:], [src], [t])
    return t


def norm_T(c, xsrc, hT, col0, xt, xb, junk, ss, rstd, eps, pT, identb):
    c.dma(xt[:], xsrc, writes=[xt])
    c.act(junk[:], xt[:], AF.Square, [xt], [junk, ss], accum=ss[:, 0:1])
    c.act(rstd[:, 0:1], ss[:, 0:1], AF.Sqrt, [ss, eps], [rstd], bias=eps[:, 0:1], scale=1.0 / D)
    c.recip(rstd[:, 0:1], rstd[:, 0:1], [rstd], [rstd])
    c.ts("dve", xb[:], xt[:], rstd[:, 0:1], ALU.mult, [xt, rstd], [xb])
    for kc in range(8):
        c.tr(pT[:, kc, :], xb[:, kc * 128:(kc + 1) * 128], identb[:], [xb, identb], [pT])
    c.cp("pool" if False else "act", hT[:, :, col0:col0 + 128], pT[:], [pT], [hT])


def phase_A(m, l, xsrc):
    S = m.S
    TG = min(S, 2048)
    TB = min(512, TG)
    sc = m.scr
    if "hg" not in sc:
        m.scratch("hg", [S, 2048], F32)
        m.scratch("dqT", [512, S], BF16)
        m.scratch("dkT", [512, S], BF16)
        m.scratch("dvv", [S, 512], BF16)
        m.scratch("rwz", [S, 1536], F32)
        m.scratch("rwcT", [256, S], F32)
        m.scratch("gateT", [3072, S], BF16)
    blocks = [
        (0, 512, "tok", "hg", 0, AF.Silu, F32), (512, 512, "tok", "hg", 512, AF.Sigmoid, F32),
        (1024, 512, "tok", "hg", 1024, AF.Copy, F32), (1536, 512, "tok", "hg", 1536, AF.Silu, F32),
        (2048, 512, "feat", "dqT", 0, AF.Copy, BF16), (2560, 512, "feat", "dkT", 0, AF.Copy, BF16),
        (3072, 512, "tok", "dvv", 0, AF.Copy, BF16),
        (3584, 512, "tok", "rwz", 0, AF.Copy, F32), (4096, 512, "tok", "rwz", 512, AF.Copy, F32),
        (4608, 512, "tok", "rwz", 1024, AF.Copy, F32), (5120, 256, "feat", "rwcT", 0, AF.Copy, F32),
    ] + [(5376 + i * 512, 512, "gate", "gateT", i * 512, AF.Sigmoid, BF16) for i in range(6)]
    with contextlib.ExitStack() as st:
        c = m.ctx(st)
        cs = load_consts(m, c, ["ident"])
        identb = make_bf(c, cs["ident"], [128, 128], "identb")
        eps = c.sb("eps", [128, 1])
        c.memset("dve", eps[:], 1e-6, [eps])
        bg = c.sb("bg", [128, 24])
        c.dma(bg[:], m.din["b_gate"][l].rearrange("(c p) -> p c", p=128), writes=[bg])
        hT = c.sb("hT", [128, 8, TG], BF16)
        xts = c.sbn("xt", [128, D], F32, 2)
        xbs = c.sbn("xb", [128, D], BF16, 2)
        junk = c.sb("junk", [128, D], F32)
        sss = c.sbn("ss", [128, 1], F32, 2)
        rstds = c.sbn("rstd", [128, 1], F32, 2)
        pTs = c.psn("pT", [128, 8, 128], BF16, 2)
        wbs = c.sbn("wb", [128, 8, 512], BF16, 2)
        pos = c.psn("po", [128, 512], F32, 4)
        o32 = c.sbn("o32", [128, 512], F32, 3)
        o16 = c.sbn("o16", [128, 512], BF16, 3)
        wsrc = sc["b_w_in"][l].rearrange("(c p) n -> p c n", p=128)
        it = 0
        for g0 in range(0, S, TG):
            for tt in range(TG // 128):
                i = tt % 2
                norm_T(c, xsrc[g0 + tt * 128:g0 + (tt + 1) * 128, :], hT, tt * 128, xts[i], xbs[i], junk, sss[i],
                       rstds[i], eps, pTs[i], identb)
            for bi, (c0, ncol, kind, dname, doff, func, dt) in enumerate(blocks):
                wb = wbs[bi % 2]
                c.dma(wb[:, :, 0:ncol], wsrc[:, :, c0:c0 + ncol], writes=[wb], q="sp")
                dest = sc[dname]
                if kind == "tok":
                    for tt in range(TG // 128):
                        po = pos[it % 4]
                        ot = (o32 if dt == F32 else o16)[it % 3]
                        it += 1
                        for kc in range(8):
                            c.mm(po[:, 0:ncol], hT[:, kc, tt * 128:(tt + 1) * 128], wb[:, kc, 0:ncol], kc == 0, kc == 7,
                                 [hT, wb], [po])
                        c.act(ot[:, 0:ncol], po[:, 0:ncol], func, [po], [ot])
                        c.dma(dest[g0 + tt * 128:g0 + (tt + 1) * 128, doff:doff + ncol], ot[:, 0:ncol], reads=[ot])
                else:
                    for fc in range(ncol // 128):
                        for tb in range(TG // TB):
                            po = pos[it % 4]
                            ot = (o32 if dt == F32 else o16)[it % 3]
                            it += 1
                            for kc in range(8):
                                c.mm(po[:, 0:TB], wb[:, kc, fc * 128:(fc + 1) * 128], hT[:, kc, tb * TB:(tb + 1) * TB],
                                     kc == 0, kc == 7, [hT, wb], [po])
                            if kind == "gate":
                                gc = (doff + fc * 128) // 128
                                c.act(ot[:, 0:TB], po[:, 0:TB], func, [po, bg], [ot], bias=bg[:, gc:gc + 1])
                            else:
                                c.act(ot[:, 0:TB], po[:, 0:TB], func, [po], [ot])
                            r0 = doff + fc * 128
                            c.dma(dest[r0:r0 + 128, g0 + tb * TB:g0 + (tb + 1) * TB], ot[:, 0:TB], reads=[ot])
        m.sched.emit()


def rowb(m, c, name, src1d, F):
    t = c.sb(name, [128, F], F32)
    c.dma(t[:], src1d.partition_broadcast(128), writes=[t])
    return t


def sub(parent):
    v = T(parent.t, parent.b.name + "_v")
    return v


def bc3(ap2, n=64):
    H = ap2.shape[1]
    return ap2.unsqueeze(2).to_broadcast([128, H, n])


def v3(ap2, n=64):
    return ap2.rearrange("p (h n) -> p h n", n=n)


def phase_B(m, l):
    S = m.S
    sc = m.scr
    if "brT" not in sc:
        m.scratch("brT", [3, 512, S], BF16)
    with contextlib.ExitStack() as st:
        c = m.ctx(st)
        cs = load_consts(m, c, ["ident", "hgm", "ones"])
        ident, hgm, ones = cs["ident"], cs["hgm"], cs["ones"]
        eps = c.sb("eps", [128, 1])
        c.memset("dve", eps[:], 1e-6, [eps])
        lbrow = c.sb("lbrow", [128, 512])
        omlb = c.sb("omlb", [128, 512])
        if l == 0:
            c.memset("dve", lbrow[:], 0.0, [lbrow])
        else:
            a0 = rowb(m, c, "lba0", m.din["hgrn_lb_param"][0], 512)
            a1 = rowb(m, c, "lba1", m.din["hgrn_lb_param"][1], 512)
            c.tt("dve", a1[:], a1[:], a0[:], ALU.subtract, [a0, a1], [a1])
            c.act(lbrow[:], a1[:], AF.Sigmoid, [a1], [lbrow])
        c.ts("dve", omlb[:], lbrow[:], -1.0, ALU.mult, [lbrow], [omlb], s2=1.0, op1=ALU.add)
        ngrow = rowb(m, c, "ngrow", m.din["hgrn_norm_g"][l], 512)
        Sst = [c.sbn(f"Sst{h}_", [128, 128], F32, 2) for h in range(4)]
        for h in range(4):
            c.memset("pool", Sst[h][0][:], 0.0, [Sst[h][0]])
        hgt = c.sbn("hgt", [128, 2048], F32, 2)
        fv = c.sbn("fv", [128, 512], F32, 2)
        kf = c.sbn("kf", [128, 512], F32, 2)
        lf = c.sbn("lf", [128, 512], F32, 2)
        ee = c.sbn("ee", [128, 4, 512], F32, 2)
        qk = c.sbn("qk", [128, 4, 512], F32, 2)
        FT = c.sbn("FT", [128, 3, 128], F32, 2)
        AT = c.sbn("AT", [128, 128], F32, 2)
        dcol = c.sbn("dcol", [128, 2], F32, 4)
        ssq = c.sbn("ssq", [128, 4], F32, 2)
        rstd = c.sbn("rstdh", [128, 4], F32, 2)
        gn = c.sbn("gn", [128, 512], F32, 2)
        ob = c.sbn("ob", [128, 512], BF16, 2)
        obf = c.sbn("obf", [128, 512], F32, 2)
        obT = c.sbn("obT", [128, 4, 128], BF16, 2)
        junk = c.sb("junkb", [128, 128], F32)
        pcs = c.psn("pcs", [128, 512], F32, 3)
        pTr1 = c.ps("pTr", [128, 512], F32)
        pTr = [pTr1, pTr1]
        psc = c.ps("psc", [128, 512], F32)
        pkv = c.ps("pkv", [128, 512], F32)
        pcol = c.ps("pcol", [128, 512], F32)
        po = c.ps("pob", [128, 512], F32)
        pTb = pTr1
        cn = 0
        for ti in range(S // 128):
            t0 = ti * 128
            i = ti % 2
            hg = hgt[i]
            c.dma(hg[:], sc["hg"][t0:t0 + 128, :], writes=[hg])
            q, sg, vv, gate = hg[:, 0:512], hg[:, 512:1024], hg[:, 1024:1536], hg[:, 1536:2048]
            f = fv[i]
            c.tt("pool", f[:], sg, omlb[:], ALU.mult, [hg, omlb], [f])
            c.tt("pool", f[:], f[:], lbrow[:], ALU.add, [f, lbrow], [f])
            c.ts("pool", f[:], f[:], 1e-30, ALU.max, [f], [f])
            c.ts("dve", kf[i][:], f[:], -1.0, ALU.mult, [f], [kf[i]], s2=1.0, op1=ALU.add)
            c.act(lf[i][:], f[:], AF.Ln, [f], [lf[i]])
            for k in range(3):
                c.mm(pcs[k][:], hgm[:, k * 128:(k + 1) * 128], lf[i][:], True, True, [hgm, lf[i]], [pcs[k]])
            e = ee[i]
            c.act(e[:, 0, :], pcs[1][:], AF.Exp, [pcs[1]], [e])
            c.act(e[:, 1, :], pcs[1][:], AF.Exp, [pcs[1]], [e], scale=-1.0)
            c.act(e[:, 2, :], pcs[0][:], AF.Exp, [pcs[0]], [e])
            c.act(e[:, 3, :], pcs[2][:], AF.Exp, [pcs[2]], [e])
            w = qk[i]
            c.tt("dve", w[:, 0, :], q, e[:, 0, :], ALU.mult, [hg, e], [w])
            c.tt("pool", w[:, 1, :], kf[i][:], e[:, 1, :], ALU.mult, [kf[i], e], [w])
            c.tt("dve", w[:, 2, :], q, e[:, 2, :], ALU.mult, [hg, e], [w])
            c.tt("pool", w[:, 3, :], kf[i][:], e[:, 3, :], ALU.mult, [kf[i], e], [w])
            c.tt("pool", gn[i][:], gate, ngrow[:], ALU.mult, [hg, ngrow], [gn[i]])
            for h in range(4):
                hc = slice(h * 128, (h + 1) * 128)
                ft = FT[h % 2]
                ptr = pTr[h % 2]
                for k in range(3):
                    c.tr(ptr[:, k * 128:(k + 1) * 128], w[:, k, hc], ident[:], [w, ident], [ptr])
                c.cp("act", ft[:].rearrange("p a b -> p (a b)"), ptr[:, 0:384], [ptr], [ft])
                c.mm(psc[:, 0:128], ft[:, 1, :], ft[:, 0, :], True, True, [ft], [psc])
                at = AT[h % 2]
                c.tt("dve", at[:], psc[:, 0:128], hgm[:, 0:128], ALU.mult, [psc, hgm], [at])
                c.mm(po[:, hc], at[:], vv[:, hc] if False else hg[:, 1024 + h * 128:1024 + (h + 1) * 128], True, False, [at, hg], [po])
                for ch in range(2):
                    P = slice(ch * 64, (ch + 1) * 64)
                    Sc = Sst[h][(2 * ti + ch) % 2]
                    Sn = Sst[h][(2 * ti + ch + 1) % 2]
                    c.mm(po[P, hc], ft[:, 2, P], Sc[:], False, ch == 1, [ft, Sc], [po])
                    dc = dcol[cn % 4]
                    cn += 1
                    c.mm(pcol[:, 256:258], lf[i][P, hc], ones[P, 0:2], True, True, [lf[i], ones], [pcol])
                    c.act(dc[:], pcol[:, 256:258], AF.Exp, [pcol], [dc])
                    c.mm(pkv[:, 128:256], w[P, 3, hc], hg[P, 1024 + h * 128:1024 + (h + 1) * 128], True, True, [w, hg], [pkv])
                    c.stt(Sn[:], Sc[:], dc[:, 0:1], pkv[:, 128:256], ALU.mult, ALU.add, [Sc, dc, pkv], [Sn])
            for h in range(4):
                hc = slice(h * 128, (h + 1) * 128)
                c.act(junk[:], po[:, hc], AF.Square, [po], [junk, ssq[i]], accum=ssq[i][:, h:h + 1])
            c.act(rstd[i][:], ssq[i][:], AF.Sqrt, [ssq[i], eps], [rstd[i]], bias=eps[:, 0:1], scale=1.0 / 128)
            c.recip(rstd[i][:], rstd[i][:], [rstd[i]], [rstd[i]])
            for h in range(4):
                hc = slice(h * 128, (h + 1) * 128)
                c.stt(obf[i][:, hc], po[:, hc], rstd[i][:, h:h + 1], gn[i][:, hc], ALU.mult, ALU.mult, [po, rstd[i], gn[i]], [obf[i]])
            for h in range(4):
                hc = slice(h * 128, (h + 1) * 128)
                c.tr(pTb[:, hc], obf[i][:, hc], ident[:], [obf[i], ident], [pTb])
            c.cp("act", obT[i][:].rearrange("p a b -> p (a b)"), pTb[:], [pTb], [obT[i]])
            c.dma(sc["brT"][0].rearrange("(h v) t -> v h t", v=128)[:, :, t0:t0 + 128], obT[i][:], reads=[obT[i]])
        m.sched.emit()


def phase_C(m, l):
    S = m.S
    sc = m.scr
    QB = min(512, S)
    nq = QB // 128
    NQ = S // QB
    NJ = S // 128
    lambda_init = 0.8 - 0.6 * math.exp(-0.3 * l)
    with contextlib.ExitStack() as st:
        c = m.ctx(st)
        cs = load_consts(m, c, ["ones", "ctab"])
        ones, ctab = cs["ones"], cs["ctab"]
        onesb = make_bf(c, ones, [128, 128], "onesb")
        eps = c.sb("eps", [128, 1])
        c.memset("dve", eps[:], 1e-6, [eps])
        lamr = rowb(m, c, "lamr", m.din["diff_lambda"][l].rearrange("a b -> (a b)"), 256)
        ltmp = c.sb("ltmp", [128, 128])
        lsum = c.sb("lsum", [128, 2])
        c.tt("dve", ltmp[:, 0:64], lamr[:, 0:64], lamr[:, 64:128], ALU.mult, [lamr], [ltmp])
        c.tt("dve", ltmp[:, 64:128], lamr[:, 128:192], lamr[:, 192:256], ALU.mult, [lamr, ltmp], [ltmp])
        c.red(lsum[:], ltmp[:].rearrange("p (a b) -> p a b", b=64), ALU.add, [ltmp], [lsum])
        c.act(lsum[:], lsum[:], AF.Exp, [lsum], [lsum])
        nlam = c.sb("nlam", [128, 1])
        c.tt("dve", nlam[:], lsum[:, 1:2], lsum[:, 0:1], ALU.subtract, [lsum], [nlam])
        c.ts("dve", nlam[:], nlam[:], -lambda_init, ALU.add, [nlam], [nlam])
        gcol = c.sb("gcol", [128, 1])
        c.dma(gcol[:], m.din["diff_subln_g"][l].rearrange("(p o) -> p o", o=1), writes=[gcol])
        c.ts("dve", gcol[:], gcol[:], 1.0 - lambda_init, ALU.mult, [gcol], [gcol])
        qT = c.sb("qT", [128, S], BF16)
        kT = c.sb("kT", [128, S], BF16)
        V = c.sb("V", [128, NJ, 128], BF16)
        AL = c.sb("AL", [128, 5, 512], F32)
        TMP = c.sbn("tmpc", [128, 512], F32, 3)
        PT = c.sbn("ptc", [128, 512], BF16, 3)
        PST = c.psn("pst", [128, 512], F32, 4)
        PO = c.psn("poc", [128, 512], F32, 2)
        PL = c.psn("plc", [128, 512], F32, 2)
        rl = c.sbn("rlc", [128, 512], F32, 2)
        oc = c.sbn("occ", [128, 512], F32, 2)
        od = c.sb("odc", [128, 512], F32)
        sq = c.sb("sqc", [128, 512], F32)
        rs = c.sb("rsc", [128, 512], F32)
        obo = c.sbn("oboc", [128, 512], BF16, 2)
        cnt = 0
        for h in range(4):
            c.dma(qT[:], sc["dqT"][h * 128:(h + 1) * 128, :], writes=[qT], q="sp")
            c.dma(kT[:], sc["dkT"][h * 128:(h + 1) * 128, :], writes=[kT], q="act")
            vsrc = sc["dvv"].rearrange("(j p) c -> p j c", p=128)
            for j0 in range(0, NJ, 8):
                j1 = min(NJ, j0 + 8)
                c.dma(V[:, j0:j1, :], vsrc[:, j0:j1, h * 128:(h + 1) * 128], writes=[V], q=("pool", "sp", "act")[(j0 // 8) % 3])
            c.dma(AL[:].rearrange("p a b -> p (a b)"), m.din["c_alibi"][:, h * 2560:(h + 1) * 2560], writes=[AL], q="sp")
            for I in range(NQ):
                qs = slice(I * QB, (I + 1) * QB)
                jmax = nq * (I + 1) - 1
                for j in range(jmax + 1):
                    d = j - nq * I
                    var = 0 if d < 0 else 1 + d
                    mc = (nq * I - j) if d < 0 else 0
                    for c2 in range(2):
                        P = slice(c2 * 64, (c2 + 1) * 64)
                        pst = PST[cnt % 4]
                        tmp = TMP[cnt % 3]
                        pt = PT[cnt % 3]
                        cnt += 1
                        c.mm(pst[:, 0:QB], kT[P, j * 128:(j + 1) * 128], qT[P, qs], True, True, [kT, qT], [pst])
                        c.tt("dve", tmp[:, 0:QB], pst[:, 0:QB], AL[:, var, 0:QB], ALU.add, [pst, AL], [tmp])
                        c.act(pt[:, 0:QB], tmp[:, 0:QB], AF.Exp, [tmp, ctab], [pt], bias=ctab[:, h * 65 + mc:h * 65 + mc + 1], scale=0.125)
                        c.mm(PO[c2][:, 0:QB], V[:, j, :], pt[:, 0:QB], j == 0, j == jmax, [V, pt], [PO[c2]])
                        c.mm(PL[c2][:, 0:QB], onesb[:], pt[:, 0:QB], j == 0, j == jmax, [onesb, pt], [PL[c2]])
                for c2 in range(2):
                    c.recip(rl[c2][:, 0:QB], PL[c2][:, 0:QB], [PL[c2]], [rl[c2]])
                    c.tt("dve", oc[c2][:, 0:QB], PO[c2][:, 0:QB], rl[c2][:, 0:QB], ALU.mult, [PO[c2], rl[c2]], [oc[c2]])
                c.stt(od[:, 0:QB], oc[1][:, 0:QB], nlam[:, 0:1], oc[0][:, 0:QB], ALU.mult, ALU.add, [oc[0], oc[1], nlam], [od])
                c.tt("pool", sq[:, 0:QB], od[:, 0:QB], od[:, 0:QB], ALU.mult, [od], [sq])
                pss = PST[cnt % 4]
                cnt += 1
                c.mm(pss[:, 0:QB], ones[:], sq[:, 0:QB], True, True, [ones, sq], [pss])
                c.act(rs[:, 0:QB], pss[:, 0:QB], AF.Sqrt, [pss, eps], [rs], bias=eps[:, 0:1], scale=1.0 / 128)
                c.recip(rs[:, 0:QB], rs[:, 0:QB], [rs], [rs])
                o_ = obo[I % 2]
                c.stt(o_[:, 0:QB], od[:, 0:QB], gcol[:, 0:1], rs[:, 0:QB], ALU.mult, ALU.mult, [od, gcol, rs], [o_])
                c.dma(sc["brT"][1][h * 128:(h + 1) * 128, qs], o_[:, 0:QB], reads=[o_])
        m.sched.emit()


def phase_E(m, l, xsrc, xdst):
    S = m.S
    sc = m.scr
    TB = min(512, S)
    with contextlib.ExitStack() as st:
        c = m.ctx(st)
        wbr = c.sb("wbr", [128, 12, D], BF16)
        wout = c.sb("wout", [128, 8, D], BF16)
        c.dma(wbr[:], sc["b_w_branch"][l].rearrange("(c p) n -> p c n", p=128), writes=[wbr], q="sp")
        c.dma(wout[:], sc["b_w_out"][l].rearrange("(c p) n -> p c n", p=128), writes=[wout], q="act")
        br = c.sbn("br", [128, 12, TB], BF16, 2)
        gt = c.sbn("gte", [128, 24, TB], BF16, 2)
        acc = c.sbn("acce", [128, TB], F32, 2)
        tmp = c.sbn("tmpe", [128, TB], F32, 3)
        mT = c.sbn("mT", [128, 8, TB], BF16, 2)
        xt = c.sbn("xte", [128, D], F32, 2)
        xo = c.sbn("xoe", [128, D], F32, 2)
        PM = c.psn("pme", [128, 512], F32, 4)
        PO = c.psn("poe", [128, 512], F32, 4)
        k = 0
        k2 = 0
        for tb in range(S // TB):
            ts_ = slice(tb * TB, (tb + 1) * TB)
            b_, g_, m_ = br[tb % 2], gt[tb % 2], mT[tb % 2]
            c.dma(b_[:], sc["brT"].rearrange("n (c p) t -> p (n c) t", p=128)[:, :, ts_], writes=[b_], q="sp")
            c.dma(g_[:], sc["gateT"].rearrange("(c p) t -> p c t", p=128)[:, :, ts_], writes=[g_], q="act")
            for dmc in range(8):
                a_ = acc[dmc % 2]
                for n in range(3):
                    pm = PM[k % 4]
                    for kc in range(4):
                        c.mm(pm[:, 0:TB], wbr[:, n * 4 + kc, dmc * 128:(dmc + 1) * 128], b_[:, n * 4 + kc, :], kc == 0, kc == 3, [wbr, b_], [pm])
                    if n == 0:
                        c.tt("dve", a_[:], pm[:, 0:TB], g_[:, n * 8 + dmc, :], ALU.mult, [pm, g_], [a_])
                    else:
                        t_ = tmp[k % 3]
                        c.tt("dve", t_[:], pm[:, 0:TB], g_[:, n * 8 + dmc, :], ALU.mult, [pm, g_], [t_])
                        if n == 1:
                            c.tt("pool", a_[:], a_[:], t_[:], ALU.add, [a_, t_], [a_])
                        else:
                            c.tt("pool", m_[:, dmc, :], a_[:], t_[:], ALU.add, [a_, t_], [m_])
                    k += 1
            for tt in range(TB // 128):
                x_, o_ = xt[k2 % 2], xo[k2 % 2]
                r0 = tb * TB + tt * 128
                c.dma(x_[:], xsrc[r0:r0 + 128, :], writes=[x_], q="pool")
                for cb in range(2):
                    po = PO[(2 * k2 + cb) % 4]
                    for kc in range(8):
                        c.mm(po[:], m_[:, kc, tt * 128:(tt + 1) * 128], wout[:, kc, cb * 512:(cb + 1) * 512], kc == 0, kc == 7, [m_, wout], [po])
                    c.tt("dve", o_[:, cb * 512:(cb + 1) * 512], po[:], x_[:, cb * 512:(cb + 1) * 512], ALU.add, [po, x_], [o_])
                c.dma(xdst[r0:r0 + 128, :], o_[:], reads=[o_], q="sp")
                k2 += 1
        m.sched.emit()


class NormBufs:
    def __init__(self, c, tag):
        self.xts = c.sbn("xt" + tag, [128, D], F32, 2)
        self.xbs = c.sbn("xb" + tag, [128, D], BF16, 2)
        self.junk = c.sb("junk" + tag, [128, D], F32)
        self.sss = c.sbn("ss" + tag, [128, 1], F32, 2)
        self.rstds = c.sbn("rstd" + tag, [128, 1], F32, 2)
        self.pTs = c.psn("pT" + tag, [128, 8, 128], BF16, 2)
        self.eps = c.sb("eps" + tag, [128, 1])
        c.memset("dve", self.eps[:], 1e-6, [self.eps])
        self.n = 0

    def run(self, c, xsrc, hT, col0, identb):
        i = self.n % 2
        self.n += 1
        norm_T(c, xsrc, hT, col0, self.xts[i], self.xbs[i], self.junk, self.sss[i], self.rstds[i], self.eps,
               self.pTs[i], identb)


def phase_F(m, l, xsrc, xdst):
    S = m.S
    sc = m.scr
    TB = min(512, S)
    with contextlib.ExitStack() as st:
        c = m.ctx(st)
        cs = load_consts(m, c, ["ident", "ones"])
        identb = make_bf(c, cs["ident"], [128, 128], "identb")
        onesb = make_bf(c, cs["ones"], [128, 128], "onesb")
        nb = NormBufs(c, "f")
        wq = c.sb("wq", [128, 8, D], BF16)
        wkv = c.sb("wkv", [128, 8, 2 * D], BF16)
        wo = c.sb("wo", [128, 8, D], BF16)
        c.dma(wq[:], sc["b_xa_wq"][l].rearrange("(c p) n -> p c n", p=128), writes=[wq], q="sp")
        c.dma(wkv[:], sc["b_xa_wkv"][l].rearrange("(c p) n -> p c n", p=128), writes=[wkv], q="act")
        c.dma(wo[:], sc["b_xa_wo"][l].rearrange("(c p) n -> p c n", p=128), writes=[wo], q="pool")
        memT = c.sb("memT", [128, 8, MEM], BF16)
        KT = c.sb("KT", [128, 8, MEM], BF16)
        Vm = c.sb("Vm", [128, 2, D], BF16)
        PA = c.psn("paf", [128, 512], F32, 3)
        for mt in range(2):
            nb.run(c, m.din["mem"][mt * 128:(mt + 1) * 128, :], memT, mt * 128, identb)
        k = 0
        for fc in range(8):
            pa = PA[k % 3]
            k += 1
            for kc in range(8):
                c.mm(pa[:, 0:MEM], wkv[:, kc, fc * 128:(fc + 1) * 128], memT[:, kc, :], kc == 0, kc == 7, [wkv, memT], [pa])
            c.cp("act", KT[:, fc, :], pa[:, 0:MEM], [pa], [KT])
        for mt in range(2):
            for cb in range(2):
                pa = PA[k % 3]
                k += 1
                for kc in range(8):
                    c.mm(pa[:], memT[:, kc, mt * 128:(mt + 1) * 128], wkv[:, kc, D + cb * 512:D + (cb + 1) * 512], kc == 0, kc == 7, [wkv, memT], [pa])
                c.cp("act", Vm[:, mt, cb * 512:(cb + 1) * 512], pa[:], [pa], [Vm])
        hT = c.sbn("hTf", [128, 8, TB], BF16, 2)
        qT = c.sbn("qTf", [128, 8, TB], BF16, 2)
        oT = c.sbn("oTf", [128, 8, TB], BF16, 2)
        pt = c.sbn("ptf", [128, 2, TB], BF16, 3)
        rl = c.sbn("rlf", [128, TB], F32, 2)
        xt = c.sbn("xtf", [128, D], F32, 2)
        xo = c.sbn("xof", [128, D], F32, 2)
        PB = c.psn("pbf", [128, 512], F32, 3)
        k2 = 0
        kp = 0
        for tb in range(S // TB):
            h_, q_, o_ = hT[tb % 2], qT[tb % 2], oT[tb % 2]
            for tt in range(TB // 128):
                r0 = tb * TB + tt * 128
                nb.run(c, xsrc[r0:r0 + 128, :], h_, tt * 128, identb)
            for fc in range(8):
                pa = PA[k % 3]
                k += 1
                for kc in range(8):
                    c.mm(pa[:, 0:TB], wq[:, kc, fc * 128:(fc + 1) * 128], h_[:, kc, :], kc == 0, kc == 7, [wq, h_], [pa])
                c.cp("act", q_[:, fc, :], pa[:, 0:TB], [pa], [q_])
            for h in range(4):
                p_ = pt[kp % 3]
                kp += 1
                for mc in range(2):
                    pa = PA[k % 3]
                    k += 1
                    for dc in range(2):
                        c.mm(pa[:, 0:TB], KT[:, h * 2 + dc, mc * 128:(mc + 1) * 128], q_[:, h * 2 + dc, :], dc == 0, dc == 1, [KT, q_], [pa])
                    c.act(p_[:, mc, :], pa[:, 0:TB], AF.Exp, [pa], [p_], scale=1.0 / 16)
                pl = PB[2]
                for mc in range(2):
                    c.mm(pl[:, 0:TB], onesb[:], p_[:, mc, :], mc == 0, mc == 1, [onesb, p_], [pl])
                r_ = rl[h % 2]
                c.recip(r_[:], pl[:, 0:TB], [pl], [r_])
                for dc in range(2):
                    pb = PB[dc]
                    for mc in range(2):
                        c.mm(pb[:, 0:TB], Vm[:, mc, h * 256 + dc * 128:h * 256 + (dc + 1) * 128], p_[:, mc, :], mc == 0, mc == 1, [Vm, p_], [pb])
                    c.tt("dve", o_[:, h * 2 + dc, :], pb[:, 0:TB], r_[:], ALU.mult, [pb, r_], [o_])
            for tt in range(TB // 128):
                x_, xo_ = xt[k2 % 2], xo[k2 % 2]
                r0 = tb * TB + tt * 128
                c.dma(x_[:], xsrc[r0:r0 + 128, :], writes=[x_], q="pool")
                for cb in range(2):
                    pa = PA[k % 3]
                    k += 1
                    for kc in range(8):
                        c.mm(pa[:], o_[:, kc, tt * 128:(tt + 1) * 128], wo[:, kc, cb * 512:(cb + 1) * 512], kc == 0, kc == 7, [o_, wo], [pa])
                    c.tt("dve", xo_[:, cb * 512:(cb + 1) * 512], pa[:], x_[:, cb * 512:(cb + 1) * 512], ALU.add, [pa, x_], [xo_])
                c.dma(xdst[r0:r0 + 128, :], xo_[:], reads=[xo_], q="sp")
                k2 += 1
        m.sched.emit()


def phase_G(m, l, xsrc, xdst):
    S = m.S
    sc = m.scr
    TBK = min(1024, S)
    SB = min(512, TBK)
    NFC = DFF // 128
    with contextlib.ExitStack() as st:
        c = m.ctx(st)
        cs = load_consts(m, c, ["ident"])
        identb = make_bf(c, cs["ident"], [128, 128], "identb")
        nb = NormBufs(c, "g")
        wdn = c.sb("wdn", [128, NFC, D], BF16)
        c.dma(wdn[:], sc["b_ffn_w_down"][l].rearrange("(c p) n -> p c n", p=128), writes=[wdn], q="pool")
        cw = c.sb("cw", [128, NFC, 3])
        for j in range(3):
            c.dma(cw[:, :, j], m.din["ffn_conv_w"][l, j].rearrange("(c p) -> p c", p=128), writes=[cw])
        cb_ = c.sb("cbias", [128, NFC])
        c.dma(cb_[:], m.din["ffn_conv_b"][l].rearrange("(c p) -> p c", p=128), writes=[cb_])
        halo = c.sb("halo", [128, NFC, 2])
        c.memset("dve", halo[:], 0.0, [halo])
        hT = c.sb("hTg", [128, 8, TBK], BF16)
        hid = c.sb("hid", [128, NFC, TBK], BF16)
        wu = c.sbn("wu", [128, 8, 512], BF16, 2)
        wv = c.sbn("wv", [128, 8, 512], BF16, 2)
        uext = c.sbn("uext", [128, SB + 2], F32, 2)
        t1 = c.sbn("t1g", [128, SB], F32, 2)
        t2 = c.sbn("t2g", [128, SB], F32, 2)
        sg = c.sbn("sgg", [128, SB], F32, 2)
        xt = c.sbn("xtg", [128, D], F32, 2)
        xo = c.sbn("xog", [128, D], F32, 2)
        PU = c.psn("pug", [128, 512], F32, 2)
        PV = c.psn("pvg", [128, 512], F32, 2)
        PO = c.psn("pog", [128, 512], F32, 2)
        wsrc = sc["b_ffn_w_up"][l].rearrange("(c p) n -> p c n", p=128)
        k = 0
        k2 = 0
        for tbk in range(S // TBK):
            for tt in range(TBK // 128):
                r0 = tbk * TBK + tt * 128
                nb.run(c, xsrc[r0:r0 + 128, :], hT, tt * 128, identb)
            for grp in range(6):
                ncol = 512 if grp < 5 else 256
                wu_, wv_ = wu[grp % 2], wv[grp % 2]
                c.dma(wu_[:, :, 0:ncol], wsrc[:, :, grp * 512:grp * 512 + ncol], writes=[wu_], q="sp")
                c.dma(wv_[:, :, 0:ncol], wsrc[:, :, DFF + grp * 512:DFF + grp * 512 + ncol], writes=[wv_], q="act")
                for fcl in range(ncol // 128):
                    fc = grp * 4 + fcl
                    for sbi in range(TBK // SB):
                        ss_ = slice(sbi * SB, (sbi + 1) * SB)
                        pu, pv = PU[k % 2], PV[k % 2]
                        ue, a1, a2, s_ = uext[k % 2], t1[k % 2], t2[k % 2], sg[k % 2]
                        k += 1
                        for kc in range(8):
                            c.mm(pu[:, 0:SB], wu_[:, kc, fcl * 128:(fcl + 1) * 128], hT[:, kc, ss_], kc == 0, kc == 7, [wu_, hT], [pu])
                        for kc in range(8):
                            c.mm(pv[:, 0:SB], wv_[:, kc, fcl * 128:(fcl + 1) * 128], hT[:, kc, ss_], kc == 0, kc == 7, [wv_, hT], [pv])
                        c.cp("pool", ue[:, 0:2], halo[:, fc, :], [halo], [ue])
                        c.cp("act", ue[:, 2:SB + 2], pu[:, 0:SB], [pu], [ue])
                        c.cp("pool", halo[:, fc, :], ue[:, SB:SB + 2], [ue], [halo])
                        c.act(a1[:], pu[:, 0:SB], AF.Identity, [pu, cw, cb_], [a1], bias=cb_[:, fc:fc + 1], scale=cw[:, fc, 2:3])
                        c.stt(a2[:], ue[:, 1:SB + 1], cw[:, fc, 1:2], a1[:], ALU.mult, ALU.add, [ue, cw, a1], [a2])
                        c.stt(a1[:], ue[:, 0:SB], cw[:, fc, 0:1], a2[:], ALU.mult, ALU.add, [ue, cw, a2], [a1])
                        c.act(s_[:], a1[:], AF.Silu, [a1], [s_])
                        c.tt("dve", hid[:, fc, ss_], s_[:], pv[:, 0:SB], ALU.mult, [s_, pv], [hid])
            for tt in range(TBK // 128):
                x_, xo_ = xt[k2 % 2], xo[k2 % 2]
                r0 = tbk * TBK + tt * 128
                c.dma(x_[:], xsrc[r0:r0 + 128, :], writes=[x_], q="pool")
                for cb in range(2):
                    po = PO[cb]
                    for fc in range(NFC):
                        c.mm(po[:], hid[:, fc, tt * 128:(tt + 1) * 128], wdn[:, fc, cb * 512:(cb + 1) * 512], fc == 0, fc == NFC - 1, [hid, wdn], [po])
                    c.tt("dve", xo_[:, cb * 512:(cb + 1) * 512], po[:], x_[:, cb * 512:(cb + 1) * 512], ALU.add, [po, x_], [xo_])
                c.dma(xdst[r0:r0 + 128, :], xo_[:], reads=[xo_], q="sp")
                k2 += 1
        m.sched.emit()


def phase_H(m, xsrc):
    S = m.S
    with contextlib.ExitStack() as st:
        c = m.ctx(st)
        grow = rowb(m, c, "fgrow", m.din["final_norm_g"], D)
        eps = c.sb("eps", [128, 1])
        c.memset("dve", eps[:], 1e-6, [eps])
        xt = c.sbn("xth", [128, D], F32, 3)
        xo = c.sbn("xoh", [128, D], F32, 3)
        junk = c.sb("junkh", [128, D], F32)
        ss = c.sbn("ssh", [128, 1], F32, 3)
        for ti in range(S // 128):
            i = ti % 3
            c.dma(xt[i][:], xsrc[ti * 128:(ti + 1) * 128, :], writes=[xt[i]])
            c.act(junk[:], xt[i][:], AF.Square, [xt[i]], [junk, ss[i]], accum=ss[i][:, 0:1])
            c.act(ss[i][:], ss[i][:], AF.Sqrt, [ss[i], eps], [ss[i]], bias=eps[:, 0:1], scale=1.0 / D)
            c.recip(ss[i][:], ss[i][:], [ss[i]], [ss[i]])
            c.stt(xo[i][:], xt[i][:], ss[i][:, 0:1], grow[:], ALU.mult, ALU.mult, [xt[i], ss[i], grow], [xo[i]])
            c.dma(m.out[ti * 128:(ti + 1) * 128, :], xo[i][:], reads=[xo[i]])
        m.sched.emit()


def phase_D(m, l):
    S = m.S
    sc = m.scr
    with contextlib.ExitStack() as st:
        c = m.ctx(st)
        cs = load_consts(m, c, ["ident", "rwm", "maskg", "ones"])
        ident, rwm, maskg, ones = cs["ident"], cs["rwm"], cs["maskg"], cs["ones"]
        din = m.din
        w0r = rowb(m, c, "w0r", din["rwkv_w0"][l], 512)
        a0r = rowb(m, c, "a0r", din["rwkv_a0"][l], 512)
        kkr = rowb(m, c, "kkr", din["rwkv_k_k"][l], 512)
        kar = rowb(m, c, "kar", din["rwkv_k_a"][l], 512)
        rkr = rowb(m, c, "rkr", din["rwkv_r_k"][l].rearrange("a b -> (a b)"), 512)
        lngr = rowb(m, c, "lngr", din["rwkv_ln_g"][l], 512)
        lnbr = rowb(m, c, "lnbr", din["rwkv_ln_b"][l], 512)
        mur = rowb(m, c, "mur", din["rwkv_mu"][l][0:1536], 1536)
        mucol = c.sb("mucol", [128, 2])
        c.dma(mucol[:], din["rwkv_mu"][l][1536:1792].rearrange("(c p) -> p c", p=128), writes=[mucol])
        WUP = c.sb("WUP", [128, 512])
        c.dma(WUP[0:64, :], din["rwkv_w_up"][l], writes=[WUP])
        c.dma(WUP[64:128, :], din["rwkv_a_up"][l], writes=[WUP])
        GUP = c.sb("GUP", [128, 512])
        c.dma(GUP[:], din["rwkv_g_up"][l], writes=[GUP])
        epsg = c.sb("epsg", [128, 1])
        c.memset("dve", epsg[:], 64e-5, [epsg])
        STs = c.sbn("STs", [128, 4, 64], F32, 2)
        c.memset("dve", STs[0][:], 0.0, [STs[0]])
        Gbd = c.sbn("Gbd", [128, 128], F32, 4)
        for p in range(4):
            c.memset("pool", Gbd[p][:], 0.0, [Gbd[p]])
        Hs = c.sb("Hs", [128, 4, 64])
        z = c.sbn("z", [128, 1536], F32, 2)
        zp = c.sbn("zp", [128, 1536], F32, 2)
        zm = c.sbn("zm", [128, 1536], F32, 2)
        cod = c.sbn("cod", [128, 2, 129], F32, 2)
        dcd = c.sb("dcd", [128, 2, 128])
        cm = c.sb("cm", [128, 2, 128])
        lw = c.sb("lw", [128, 128])
        sgd = c.sb("sgd", [128, 128])
        tmpw = c.sb("tmpw", [128, 512])
        sigw = c.sb("sigw", [128, 512])
        av = c.sb("av", [128, 512])
        gv = c.sb("gv", [128, 512])
        kkraw = c.sb("kkraw", [128, 512])
        sqk = c.sb("sqk", [128, 512])
        s8 = c.sbn("s8_", [128, 8], F32, 6)
        kk = c.sb("kk", [128, 512])
        kmod = c.sb("kmod", [128, 512])
        bv = c.sb("bv", [128, 512])
        ee = c.sb("eed", [128, 4, 512])
        TM = c.sb("TM", [128, 4, 512])
        BH = c.sb("BH", [128, 512])
        KH = c.sb("KH", [128, 512])
        PCc = c.sb("PCc", [128, 4])
        FT = c.sbn("FTd", [128, 4, 128], F32, 4)
        GM = c.sbn("GM", [128, 512], F32, 2)
        Lc = c.sbn("Lc", [128, 2, 128], F32, 4)
        Wc = c.sbn("Wc", [128, 128], F32, 4)
        Rp = c.sb("Rp", [128, 512])
        Yl = c.sb("Yl", [128, 512])
        RT = c.sbn("RTd", [128, 128], F32, 2)
        yv = c.sb("yv", [128, 512])
        yc = c.sb("yc", [128, 512])
        sq2 = c.sb("sq2", [128, 512])
        bon = c.sb("bon", [128, 512])
        yo = c.sb("yo", [128, 512])
        obT = c.sbn("obTd", [128, 4, 128], BF16, 2)
        B0 = c.ps("B0", [128, 512]); B1 = c.ps("B1", [128, 512]); B2 = c.ps("B2", [128, 512])
        B3 = c.ps("B3", [128, 512]); B4 = c.ps("B4", [128, 512]); B5 = c.ps("B5", [128, 512])
        B6 = c.ps("B6", [128, 512]); B7 = c.ps("B7", [128, 512])
        pL = B5
        pY = pGs = pH = pS = pRT = B6
        rwcsrc = sc["rwcT"].rearrange("(c p) t -> p c t", p=128)
        for ti in range(S // 128):
            t0 = ti * 128
            i = ti % 2
            z_, zp_, zm_, cod_ = z[i], zp[i], zm[i], cod[i]
            c.dma(z_[:], sc["rwz"][t0:t0 + 128, :], writes=[z_], q="sp")
            if ti == 0:
                c.memset("pool", zp_[0:1, :], 0.0, [zp_])
                c.dma(zp_[1:128, :], sc["rwz"][0:127, :], writes=[zp_], q="act")
                c.memset("pool", cod_[:, :, 0:1], 0.0, [cod_])
                c.dma(cod_[:, :, 1:129], rwcsrc[:, :, 0:128], writes=[cod_], q="pool")
            else:
                c.dma(zp_[:], sc["rwz"][t0 - 1:t0 + 127, :], writes=[zp_], q="act")
                c.dma(cod_[:], rwcsrc[:, :, t0 - 1:t0 + 128], writes=[cod_], q="pool")
            c.tt("dve", zm_[:], zp_[:], z_[:], ALU.subtract, [zp_, z_], [zm_])
            c.tt("pool", zm_[:], zm_[:], mur[:], ALU.mult, [zm_, mur], [zm_])
            c.tt("dve", zm_[:], zm_[:], z_[:], ALU.add, [zm_, z_], [zm_])
            r_, k_, v_ = zm_[:, 0:512], zm_[:, 512:1024], zm_[:, 1024:1536]
            c.tt("pool", dcd[:], cod_[:, :, 0:128], cod_[:, :, 1:129], ALU.subtract, [cod_], [dcd])
            for ch in range(2):
                c.stt(cm[:, ch, :], dcd[:, ch, :], mucol[:, ch:ch + 1], cod_[:, ch, 1:129], ALU.mult, ALU.add, [dcd, mucol, cod_], [cm])
            c.act(lw[0:64, :], cm[0:64, 0, :], AF.Tanh, [cm], [lw])
            c.cp("pool", lw[64:128, :], cm[64:128, 0, :], [cm], [lw])
            c.act(sgd[:], cm[:, 1, :], AF.Sigmoid, [cm], [sgd])
            c.mm(B0[:], lw[0:64, :], WUP[0:64, :], True, True, [lw, WUP], [B0])
            c.mm(B1[:], lw[64:128, :], WUP[64:128, :], True, True, [lw, WUP], [B1])
            c.mm(B2[:], sgd[:], GUP[:], True, True, [sgd, GUP], [B2])
            c.tt("dve", tmpw[:], B0[:], w0r[:], ALU.add, [B0, w0r], [tmpw])
            c.act(sigw[:], tmpw[:], AF.Sigmoid, [tmpw], [sigw])
            c.tt("dve", tmpw[:], B1[:], a0r[:], ALU.add, [B1, a0r, sigw], [tmpw])
            c.act(av[:], tmpw[:], AF.Sigmoid, [tmpw], [av])
            c.cp("act", gv[:], B2[:], [B2], [gv])
            c.tt("pool", kkraw[:], k_, kkr[:], ALU.mult, [zm_, kkr], [kkraw])
            c.tt("pool", sqk[:], kkraw[:], kkraw[:], ALU.mult, [kkraw], [sqk])
            c.red(s8[0][:], v3(sqk[:]), ALU.add, [sqk], [s8[0]])
            c.act(s8[0][:], s8[0][:], AF.Sqrt, [s8[0]], [s8[0]])
            c.ts("dve", s8[0][:], s8[0][:], 1e-12, ALU.max, [s8[0]], [s8[0]])
            c.recip(s8[0][:], s8[0][:], [s8[0]], [s8[0]])
            c.tt("dve", v3(kk[:]), v3(kkraw[:]), bc3(s8[0][:]), ALU.mult, [kkraw, s8[0]], [kk])
            c.stt(kmod[:], av[:], -1.0, kar[:], ALU.add, ALU.mult, [av, kar], [kmod])
            c.stt(kmod[:], kmod[:], 1.0, k_, ALU.add, ALU.mult, [kmod, zm_], [kmod])
            c.tt("pool", bv[:], kk[:], av[:], ALU.mult, [kk, av], [bv])
            c.mm(B0[:], rwm[:, 0:128], sigw[:], True, True, [rwm, sigw], [B0])
            c.mm(B1[:], rwm[:, 128:256], sigw[:], True, True, [rwm, sigw], [B1])
            c.mm(B2[:], rwm[:, 256:384], sigw[:], True, True, [rwm, sigw], [B2])
            c.act(ee[:, 0, :], B1[:], AF.Exp, [B1], [ee], scale=-LAM)
            c.act(ee[:, 1, :], B0[:], AF.Exp, [B0], [ee], scale=-LAM)
            c.act(ee[:, 2, :], B0[:], AF.Exp, [B0], [ee], scale=LAM)
            c.act(ee[:, 3, :], B2[:], AF.Exp, [B2], [ee], scale=-LAM)
            c.stt(TM[:, 0, :], kk[:], -1.0, ee[:, 0, :], ALU.mult, ALU.mult, [kk, ee], [TM])
            c.tt("pool", TM[:, 1, :], r_, ee[:, 1, :], ALU.mult, [zm_, ee], [TM])
            c.tt("dve", TM[:, 2, :], bv[:], ee[:, 2, :], ALU.mult, [bv, ee], [TM])
            c.tt("pool", TM[:, 3, :], kmod[:], ee[:, 2, :], ALU.mult, [kmod, ee], [TM])
            c.tt("dve", BH[:], bv[:], ee[:, 3, :], ALU.mult, [bv, ee], [BH])
            c.tt("pool", KH[:], kmod[:], ee[:, 3, :], ALU.mult, [kmod, ee], [KH])
            for p in range(4):
                c.mm(B3[:, 2 * p:2 * p + 2], sigw[:, p * 128:(p + 1) * 128], ones[:, 0:2], True, True, [sigw, ones], [B3])
            c.act(PCc[:], B3[:, 0:8].rearrange("p (a b) -> p a b", b=2)[:, :, 0], AF.Exp, [B3], [PCc], scale=-LAM)
            STc, STn = STs[ti % 2], STs[(ti + 1) % 2]
            for p in range(4):
                pc = slice(p * 128, (p + 1) * 128)
                ft = FT[p]
                for q in range(4):
                    c.tr(B3[:, q * 128:(q + 1) * 128], TM[:, q, pc], ident[:], [TM, ident], [B3])
                c.cp("act", ft[:].rearrange("p a b -> p (a b)"), B3[:], [B3], [ft])
                for hl in range(2):
                    h = 2 * p + hl
                    P = slice(hl * 64, (hl + 1) * 64)
                    hc = slice(h * 64, (h + 1) * 64)
                    vh = zm_[:, 1024 + h * 64:1024 + (h + 1) * 64]
                    gm = GM[h % 2]
                    c.mm(B4[:, 0:256], ft[P, 2, :], ft[P, 0:2, :], True, True, [ft], [B4])
                    c.mm(B4[:, 256:512], ft[P, 3, :], ft[P, 0:2, :], True, True, [ft], [B4])
                    c.tt("dve", gm[:], B4[:], maskg[:], ALU.mult, [B4, maskg], [gm])
                    LabT, MrbT, LakT, MrkT = gm[:, 0:128], gm[:, 128:256], gm[:, 256:384], gm[:, 384:512]
                    lc = Lc[0]
                    c.tr(pL[:, 384:512], LabT, ident[:], [gm, ident], [pL])
                    c.cp("act", lc[:, 0, :], pL[:, 384:512], [pL], [lc])
                    c.cp("pool", lc[:, 1, :], LabT, [gm], [lc])
                    wc = Wc[0]
                    c.mm(B5[:, 0:64], LakT, vh, True, True, [gm, zm_], [B5])
                    c.cp("act", wc[:, 64:128], B5[:, 0:64], [B5], [wc])
                    c.cp("pool", wc[:, 0:64], TM[:, 0, hc], [TM], [wc])
                    for j in range(7):
                        lcn = Lc[(j + 1) % 4]
                        wcn = Wc[(j + 1) % 4]
                        c.mm(B5[:, 0:128], lc[:, 1, :], wc[:], True, True, [lc, wc], [B5])
                        if j < 6:
                            c.mm(B5[:, 128:256], lc[:, 1, :], lc[:, 0, :], True, True, [lc], [B5])
                            c.mm(B5[:, 256:384], lc[:, 0, :], lc[:, 1, :], True, True, [lc], [B5])
                            c.cp("dve", lcn[:].rearrange("p a b -> p (a b)"), B5[:, 128:384], [B5], [lcn])
                        c.tt("dve", wcn[:], B5[:, 0:128], wc[:], ALU.add, [B5, wc], [wcn])
                        lc, wc = lcn, wcn
                    c.mm(pY[:, 0:128], MrbT, wc[:], True, False, [gm, wc], [pY])
                    c.mm(pY[:, 64:128], MrkT, vh, False, True, [gm, zm_], [pY])
                    c.tt("dve", Rp[:, hc], TM[:, 1, hc], pY[:, 0:64], ALU.add, [TM, pY], [Rp])
                    c.cp("act", Yl[:, hc], pY[:, 64:128], [pY], [Yl])
                    gcol = slice(128 + hl * 64, 128 + (hl + 1) * 64)
                    c.mm(pGs[P, gcol], wc[:, 0:64], BH[:, hc], True, True, [wc, BH], [pGs])
                    c.mm(pH[P, 256:320], BH[:, hc], wc[:, 64:128], True, False, [BH, wc], [pH])
                    c.mm(pH[P, 256:320], KH[:, hc], vh, False, True, [KH, zm_], [pH])
                    c.stt(Gbd[p][P, hl * 64:(hl + 1) * 64], ident[P, hl * 64:(hl + 1) * 64], PCc[P, p:p + 1], pGs[P, gcol],
                          ALU.mult, ALU.add, [ident, PCc, pGs], [Gbd[p]])
                    c.cp("act", Hs[P, p, :], pH[P, 256:320], [pH], [Hs])
                rt = RT[p % 2]
                c.tr(pRT[:, 384:512], Rp[:, pc], ident[:], [Rp, ident], [pRT])
                c.cp("act", rt[:], pRT[:, 384:512], [pRT], [rt])
                for hl in range(2):
                    h = 2 * p + hl
                    P = slice(hl * 64, (hl + 1) * 64)
                    yb = B7 if hl == 0 else B1
                    c.mm(yb[:, h * 64:(h + 1) * 64], rt[P, :], STc[P, p, :], True, True, [rt, STc], [yb])
                c.mm(pS[:, 320:384], Gbd[p][:], STc[:, p, :], True, True, [Gbd[p], STc], [pS])
                c.tt("dve", STn[:, p, :], pS[:, 320:384], Hs[:, p, :], ALU.add, [pS, Hs], [STn])
            for hl in range(2):
                yb = B7 if hl == 0 else B1
                c.tt("dve", v3(yv[:])[:, hl::2, :], v3(yb[:])[:, hl::2, :], v3(Yl[:])[:, hl::2, :], ALU.add, [yb, Yl], [yv])
            c.red(s8[1][:], v3(yv[:]), ALU.add, [yv], [s8[1]])
            c.ts("dve", s8[1][:], s8[1][:], 1.0 / 64, ALU.mult, [s8[1]], [s8[1]])
            c.tt("dve", v3(yc[:]), v3(yv[:]), bc3(s8[1][:]), ALU.subtract, [yv, s8[1]], [yc])
            c.tt("pool", sq2[:], yc[:], yc[:], ALU.mult, [yc], [sq2])
            c.red(s8[2][:], v3(sq2[:]), ALU.add, [sq2], [s8[2]])
            c.act(s8[2][:], s8[2][:], AF.Sqrt, [s8[2], epsg], [s8[2]], bias=epsg[:, 0:1], scale=1.0 / 64)
            c.recip(s8[2][:], s8[2][:], [s8[2]], [s8[2]])
            c.tt("dve", v3(yc[:]), v3(yc[:]), bc3(s8[2][:]), ALU.mult, [yc, s8[2]], [yc])
            c.tt("pool", yc[:], yc[:], lngr[:], ALU.mult, [yc, lngr], [yc])
            c.tt("pool", yc[:], yc[:], lnbr[:], ALU.add, [yc, lnbr], [yc])
            c.tt("pool", sq2[:], r_, kmod[:], ALU.mult, [zm_, kmod], [sq2])
            c.tt("pool", sq2[:], sq2[:], rkr[:], ALU.mult, [sq2, rkr], [sq2])
            c.red(s8[3][:], v3(sq2[:]), ALU.add, [sq2], [s8[3]])
            c.tt("dve", v3(bon[:]), v3(v_), bc3(s8[3][:]), ALU.mult, [zm_, s8[3]], [bon])
            c.tt("dve", yo[:], yc[:], bon[:], ALU.add, [yc, bon], [yo])
            c.tt("dve", yo[:], yo[:], gv[:], ALU.mult, [yo, gv], [yo])
            for q in range(4):
                c.tr(B4[:, q * 128:(q + 1) * 128], yo[:, q * 128:(q + 1) * 128], ident[:], [yo, ident], [B4])
            c.cp("act", obT[i][:].rearrange("p a b -> p (a b)"), B4[:], [B4], [obT[i]])
            c.dma(sc["brT"][2].rearrange("(h v) t -> v h t", v=128)[:, :, t0:t0 + 128], obT[i][:], reads=[obT[i]])
        m.sched.emit()


def build(S, debug=False, nlayers=L, ph="ABCDEFGH"):
    m = Model(S, debug=debug)
    xa = m.scratch("xa", [S, D], F32)
    xb = m.scratch("xb", [S, D], F32)
    xc = m.scratch("xc", [S, D], F32)
    phase_prep(m)
    xin = m.din["x"]
    for l in range(nlayers):
        if "hg" not in m.scr:
            m.scratch("hg", [S, 2048], F32)
            m.scratch("dqT", [512, S], BF16)
            m.scratch("dkT", [512, S], BF16)
            m.scratch("dvv", [S, 512], BF16)
            m.scratch("rwz", [S, 1536], F32)
            m.scratch("rwcT", [256, S], F32)
            m.scratch("gateT", [3072, S], BF16)
        if "A" in ph:
            phase_A(m, l, xin)
        if "brT" not in m.scr:
            m.scratch("brT", [3, 512, S], BF16)
        if "B" in ph:
            phase_B(m, l)
        if "C" in ph:
            phase_C(m, l)
        if "D" in ph:
            phase_D(m, l)
        if "E" in ph:
            phase_E(m, l, xin, xa)
        if "F" in ph:
            phase_F(m, l, xa, xb)
        if "G" in ph:
            phase_G(m, l, xb, xc)
        xin = xc
    if "H" in ph:
        phase_H(m, xin)
    return m


_CACHE = {}


def kernel(**inputs):
    S = inputs["x"].shape[1]
    B = inputs["x"].shape[0]
    if S not in _CACHE:
        _CACHE[S] = build(S)
    m = _CACHE[S]
    consts = host_consts()
    base = {n: np.ascontiguousarray(np.asarray(inputs[n], np.float32)) for n, _ in PARAMS}
    for n, v in consts.items():
        base["c_" + n] = v
    in_maps = []
    for b in range(B):
        d = dict(base)
        d["x"] = np.ascontiguousarray(np.asarray(inputs["x"][b], np.float32))
        d["mem"] = np.ascontiguousarray(np.asarray(inputs["mem"][b], np.float32))
        in_maps.append(d)
    res = run_bass_kernel_spmd(m.nc, in_maps, core_ids=list(range(B)))
    return np.stack([np.asarray(r["out"], np.float32) for r in res.results], axis=0)
```

```python
import contextlib
import math
import numpy as np
import ml_dtypes
import concourse.bass as bass
import concourse.mybir as mybir
from concourse.bass_utils import run_bass_kernel_spmd

F32 = mybir.dt.float32
BF16 = mybir.dt.bfloat16
AF = mybir.ActivationFunctionType
ALU = mybir.AluOpType
AX = mybir.AxisListType

D = 1024
L = 2
NIN = 8448
DFF = 2816
MEM = 256
ENGS = ("pe", "act", "dve", "pool", "sp")
NSLOT = 6
LAM = math.exp(-0.5)


class Buf:
    __slots__ = ("name", "last_w", "readers", "excl")

    def __init__(self, name=""):
        self.name = name
        self.last_w = None
        self.readers = []
        self.excl = False


class T:
    def __init__(self, t, name=""):
        self.t = t
        self.b = Buf(name)

    def __getitem__(self, idx):
        return self.t[idx]


def _b(x):
    return x.b if isinstance(x, T) else x


class Sched:
    def __init__(self, nc):
        self.nc = nc
        self.q = {e: [] for e in ENGS}
        self.cnt = {}
        self.seen = {e: {} for e in ENGS}
        self.semkeys = []
        for e in ENGS:
            self._mk(e)
        self.dma_n = {e: 0 for e in ENGS}
        for e in ("sp", "act", "pool"):
            for i in range(NSLOT):
                self._mk(("dma", e, i))
        self.n_instr = 0
        self.nblk = 0

    def _mk(self, k):
        self.cnt[k] = 0
        self.semkeys.append(k)

    def _need(self, e, deps):
        best = {}
        for d in deps:
            if d is None:
                continue
            k, v = d
            if k == e and e == "pe":
                continue
            if self.seen[e].get(k, 0) >= v:
                continue
            if best.get(k, 0) < v:
                best[k] = v
        for k, v in best.items():
            self.seen[e][k] = v
            self.q[e].append(("wait", k, v))

    def _deps(self, reads, writes):
        deps = []
        for r in reads:
            r = _b(r)
            deps.append(r.last_w)
            if r.excl:
                deps.extend(r.readers)
        for w in writes:
            w = _b(w)
            deps.append(w.last_w)
            deps.extend(w.readers)
        return deps

    MAXOPS = None

    def op(self, e, fn, reads=(), writes=()):
        if Sched.MAXOPS is not None and self.n_instr >= Sched.MAXOPS:
            return
        self._need(e, self._deps(reads, writes))
        self.cnt[e] += 1
        v = self.cnt[e]
        self.q[e].append(("op", fn, e))
        for w in writes:
            w = _b(w)
            w.last_w = (e, v)
            w.readers = []
        for r in reads:
            _b(r).readers.append((e, v))
        self.n_instr += 1

    def dma(self, e, out, in_, reads=(), writes=()):
        if Sched.MAXOPS is not None and self.n_instr >= Sched.MAXOPS:
            return
        n = self.dma_n[e]
        self.dma_n[e] += 1
        slot = ("dma", e, n % NSLOT)
        deps = self._deps(reads, writes)
        if self.cnt[slot] > 0:
            deps.append((slot, self.cnt[slot]))
        self._need(e, deps)
        self.cnt[slot] += 16
        v = self.cnt[slot]
        self.q[e].append(("dma", out, in_, slot))
        for w in writes:
            w = _b(w)
            w.last_w = (slot, v)
            w.readers = []
        for r in reads:
            _b(r).readers.append((slot, v))
        self.n_instr += 1

    def drain(self):
        deps = [(k, v) for k, v in self.cnt.items() if isinstance(k, tuple) and v > 0]
        self._need("sp", deps)

    def emit(self):
        nc = self.nc
        self.drain()
        sems = {}
        for k in self.semkeys:
            nm = "s_" + "_".join(str(x) for x in (k if isinstance(k, tuple) else (k,))) + f"_{self.nblk}"
            sems[k] = nc.alloc_semaphore(name=nm)
        self.nblk += 1
        if self.nblk == 1:
            nc.clear_and_free_semaphores(list(sems.values()))
            nc.all_engine_barrier()
            for k in self.semkeys:
                nm = "s0_" + "_".join(str(x) for x in (k if isinstance(k, tuple) else (k,)))
                sems[k] = nc.alloc_semaphore(name=nm)
        with contextlib.ExitStack() as st:
            st.enter_context(nc.allow_non_contiguous_dma(reason="small strided parameter loads"))
            block = st.enter_context(nc.Block())

            def run(eh, items):
                for it in items:
                    if it[0] == "wait":
                        eh.wait_ge(sems[it[1]], it[2])
                    elif it[0] == "raw":
                        it[1](eh)
                    elif it[0] == "op":
                        it[1](eh).then_inc(sems[it[2]], 1)
                    else:
                        eh.dma_start(out=it[1], in_=it[2]).then_inc(sems[it[3]], 16)

            @block.tensor
            def _(e):
                run(e, self.q["pe"])

            @block.scalar
            def _(e):
                run(e, self.q["act"])

            @block.vector
            def _(e):
                run(e, self.q["dve"])

            @block.gpsimd
            def _(e):
                run(e, self.q["pool"])

            @block.sync
            def _(e):
                run(e, self.q["sp"])
        nc.clear_and_free_semaphores(list(sems.values()))
        nc.all_engine_barrier()
        for k in self.cnt:
            self.cnt[k] = 0
        self.seen = {e: {} for e in ENGS}
        self.q = {e: [] for e in ENGS}


class Ctx:
    def __init__(self, nc, S_, st):
        self.nc = nc
        self.S = S_
        self.st = st
        self.rr = 0

    _uid = [0]

    def sb(self, name, shape, dt=F32):
        Ctx._uid[0] += 1
        name = f"t{Ctx._uid[0]}_{name}"
        return T(self.st.enter_context(self.nc.sbuf_tensor(name, list(shape), dt)), name)

    def ps(self, name, shape, dt=F32):
        Ctx._uid[0] += 1
        name = f"p{Ctx._uid[0]}_{name}"
        t = T(self.st.enter_context(self.nc.psum_tensor(name, list(shape), dt)), name)
        t.b.excl = True
        return t

    def sbn(self, name, shape, dt=F32, n=2):
        return [self.sb(f"{name}{i}", shape, dt) for i in range(n)]

    def psn(self, name, shape, dt=F32, n=2):
        return [self.ps(f"{name}{i}", shape, dt) for i in range(n)]

    _mode = [None]

    def _pe_mode(self, lhsT):
        def rnd(n):
            return 32 if n <= 32 else (64 if n <= 64 else 128)
        shp = lhsT.shape
        k = shp[0]
        mfree = 1
        for d in shp[1:]:
            mfree *= d
        mode = (rnd(k), rnd(mfree))
        if Ctx._mode[0] is not None and Ctx._mode[0] != mode:
            if not (Sched.MAXOPS is not None and self.S.n_instr >= Sched.MAXOPS):
                self.S.q["pe"].append(("raw", lambda e: e.drain()))
        Ctx._mode[0] = mode

    def mm(self, out, lhsT, rhs, start, stop, reads, writes):
        self._pe_mode(lhsT)
        self.S.op("pe", lambda e: e.matmul(out, lhsT=lhsT, rhs=rhs, start=start, stop=stop), reads, writes)

    def tr(self, out, in_, ident, reads, writes):
        self._pe_mode(in_)
        self.S.op("pe", lambda e: e.transpose(out=out, in_=in_, identity=ident), reads, writes)

    def act(self, out, in_, func, reads, writes, bias=None, scale=None, accum=None, eng="act"):
        kw = {}
        if bias is not None:
            kw["bias"] = bias
        if scale is not None:
            kw["scale"] = scale
        if accum is not None:
            kw["accum_out"] = accum
        self.S.op("act", lambda e: e.activation(out=out, in_=in_, func=func, **kw), reads, writes)

    def tt(self, eng, out, in0, in1, op, reads, writes):
        self.S.op(eng, lambda e: e.tensor_tensor(out=out, in0=in0, in1=in1, op=op), reads, writes)

    def ts(self, eng, out, in0, s1, op0, reads, writes, s2=None, op1=None):
        if op1 is None:
            self.S.op(eng, lambda e: e.tensor_scalar(out=out, in0=in0, scalar1=s1, scalar2=None, op0=op0), reads, writes)
        else:
            self.S.op(eng, lambda e: e.tensor_scalar(out=out, in0=in0, scalar1=s1, scalar2=s2, op0=op0, op1=op1), reads, writes)

    def stt(self, out, in0, scalar, in1, op0, op1, reads, writes):
        self.S.op("dve", lambda e: e.scalar_tensor_tensor(out=out, in0=in0, scalar=scalar, in1=in1, op0=op0, op1=op1), reads, writes)

    def cp(self, eng, out, in_, reads, writes):
        if eng == "act":
            self.S.op("act", lambda e: e.activation(out=out, in_=in_, func=AF.Copy), reads, writes)
        else:
            self.S.op(eng, lambda e: e.tensor_copy(out=out, in_=in_), reads, writes)

    def red(self, out, in_, op, reads, writes):
        self.S.op("dve", lambda e: e.tensor_reduce(out=out, in_=in_, axis=AX.X, op=op), reads, writes)

    def recip(self, out, in_, reads, writes):
        self.S.op("dve", lambda e: e.reciprocal(out=out, in_=in_), reads, writes)

    def memset(self, eng, ap, val, writes):
        self.S.op(eng, lambda e: e.memset(ap, val), (), writes)

    def dma(self, out, in_, reads=(), writes=(), q=None):
        if q is None:
            q = ("sp", "act", "pool")[self.rr % 3]
            self.rr += 1
        self.S.dma(q, out, in_, reads, writes)


def host_consts():
    c = {}
    idx = np.arange(128)
    s = idx[:, None]
    t = idx[None, :]
    same = (s // 64) == (t // 64)
    c["ident"] = np.eye(128, dtype=np.float32)
    c["ones"] = np.ones((128, 128), np.float32)
    tri64 = ((s <= t) & same).astype(np.float32)
    mid64 = ((s <= (t // 64) * 64 + 31) & same).astype(np.float32)
    blk64 = same.astype(np.float32)
    c["hgm"] = np.concatenate([tri64, tri64 - mid64, blk64 - tri64], axis=1)
    incl = (s <= t).astype(np.float32)
    strict = (s < t).astype(np.float32)
    rev = (s > t).astype(np.float32)
    c["rwm"] = np.concatenate([incl, strict, rev], axis=1)
    c["maskg"] = np.concatenate([strict, incl, strict, incl], axis=1)
    scale = 64 ** -0.5
    slopes = 2.0 ** (-8.0 * np.arange(1, 5) / 4)
    ki = np.arange(128)[:, None].astype(np.float64)
    qi = np.arange(512)[None, :].astype(np.float64)
    al = np.zeros((4, 5, 128, 512), np.float32)
    for h in range(4):
        al[h, 0] = (-slopes[h] * (qi - ki) / scale)
        for d in range(4):
            dist = qi - ki - 128 * d
            al[h, 1 + d] = np.where(dist >= 0, -slopes[h] * dist / scale, -1e30)
    c["alibi"] = al.transpose(2, 0, 1, 3).reshape(128, 4 * 5 * 512).copy()
    ct = np.zeros((128, 4 * 65), np.float32)
    for h in range(4):
        ct[:, h * 65:(h + 1) * 65] = (-slopes[h] * 128.0 * np.arange(65))[None, :]
    c["ctab"] = ct
    return c


CONST_SHAPES = {"ident": (128, 128), "ones": (128, 128), "hgm": (128, 384), "rwm": (128, 384),
                "maskg": (128, 512), "alibi": (128, 4 * 5 * 512), "ctab": (128, 4 * 65)}

PARAMS = [
    ("norm_mix_g", (L, D)), ("w_in", (L, D, NIN)), ("b_gate", (L, 3072)), ("hgrn_lb_param", (L, 512)),
    ("hgrn_norm_g", (L, 512)), ("diff_lambda", (L, 4, 64)), ("diff_subln_g", (L, 128)),
    ("rwkv_mu", (L, 1792)), ("rwkv_w0", (L, 512)), ("rwkv_w_up", (L, 64, 512)), ("rwkv_a0", (L, 512)),
    ("rwkv_a_up", (L, 64, 512)), ("rwkv_g_up", (L, 128, 512)), ("rwkv_k_k", (L, 512)), ("rwkv_k_a", (L, 512)),
    ("rwkv_r_k", (L, 8, 64)), ("rwkv_ln_g", (L, 512)), ("rwkv_ln_b", (L, 512)),
    ("w_branch", (L, 3, 512, D)), ("w_out", (L, D, D)), ("norm_xa_g", (L, D)), ("norm_mem_g", (L, D)),
    ("xa_wq", (L, D, D)), ("xa_wkv", (L, D, 2 * D)), ("xa_wo", (L, D, D)), ("norm_ffn_g", (L, D)),
    ("ffn_w_up", (L, D, 2 * DFF)), ("ffn_conv_w", (L, 3, DFF)), ("ffn_conv_b", (L, DFF)),
    ("ffn_w_down", (L, DFF, D)), ("final_norm_g", (D,)),
]


class Model:
    def __init__(self, S, debug=False, phases=None):
        self.S = S
        self.debug = debug
        self.phases = phases
        nc = bass.Bass("TRN2", target_bir_lowering=False)
        self.nc = nc
        self.din = {}
        self.din["x"] = nc.dram_tensor("x", [S, D], F32, kind="ExternalInput").ap()
        self.din["mem"] = nc.dram_tensor("mem", [MEM, D], F32, kind="ExternalInput").ap()
        for n, shp in PARAMS:
            self.din[n] = nc.dram_tensor(n, list(shp), F32, kind="ExternalInput").ap()
        for n, shp in CONST_SHAPES.items():
            self.din["c_" + n] = nc.dram_tensor("c_" + n, list(shp), F32, kind="ExternalInput").ap()
        self.out = nc.dram_tensor("out", [S, D], F32, kind="ExternalOutput").ap()
        self.scr = {}
        self.sched = Sched(nc)

    def scratch(self, name, shape, dt):
        kind = "ExternalOutput" if self.debug else "Internal"
        self.scr[name] = self.nc.dram_tensor(name, list(shape), dt, kind=kind).ap()
        return self.scr[name]

    def ctx(self, st):
        return Ctx(self.nc, self.sched, st)


def phase_prep(m):
    nc, S_ = m.nc, m.sched
    specs = [("w_in", "norm_mix_g", D, NIN), ("w_out", None, D, D), ("xa_wq", "norm_xa_g", D, D),
             ("xa_wkv", "norm_mem_g", D, 2 * D), ("xa_wo", None, D, D), ("ffn_w_up", "norm_ffn_g", D, 2 * DFF),
             ("ffn_w_down", None, DFF, D), ("w_branch", None, 1536, D)]
    for name, g, K, N in specs:
        m.scratch("b_" + name, [L, K, N], BF16)
    with contextlib.ExitStack() as st:
        c = m.ctx(st)
        gt = c.sb("gt", [128, 4, L, 8])
        gi = 0
        gmap = {}
        for name, g, K, N in specs:
            if g is not None:
                gmap[g] = gi
                for l in range(L):
                    c.dma(gt[:, gi, l, :], m.din[g][l].rearrange("(c p) -> p c", p=128), writes=[gt])
                gi += 1
        W = 2048
        ins = c.sbn("pin", [128, W], F32, 3)
        outs = c.sbn("pout", [128, W], BF16, 3)
        k = 0
        for name, g, K, N in specs:
            for l in range(L):
                src = m.din[name][l]
                if name == "w_branch":
                    src = src.rearrange("a k n -> (a k) n")
                dst = m.scr["b_" + name][l]
                for kc in range(K // 128):
                    for n0 in range(0, N, W):
                        w = min(W, N - n0)
                        ti, to = ins[k % 3], outs[k % 3]
                        c.dma(ti[:, 0:w], src[kc * 128:(kc + 1) * 128, n0:n0 + w], writes=[ti])
                        eng = ("dve", "pool")[k % 2]
                        if g is not None:
                            c.ts(eng, to[:, 0:w], ti[:, 0:w], gt[:, gmap[g], l, kc:kc + 1], ALU.mult, [ti, gt], [to])
                        else:
                            c.cp(eng, to[:, 0:w], ti[:, 0:w], [ti], [to])
                        c.dma(dst[kc * 128:(kc + 1) * 128, n0:n0 + w], to[:, 0:w], reads=[to])
                        k += 1
        S_.emit()


def load_consts(m, c, names):
    r = {}
    for n in names:
        shp = CONST_SHAPES[n]
        t = c.sb("c_" + n, shp, F32)
        c.dma(t[:], m.din["c_" + n][:, :], writes=[t])
        r[n] = t
    return r


def make_bf(c, src, shape, name):
    t = c.sb(name, shape, BF16)
    c.cp("dve", t[:], src[:], [src], [t])
    return t


def norm_T(c, xsrc, hT, col0, xt, xb, junk, ss, rstd, eps, pT, identb):
    c.dma(xt[:], xsrc, writes=[xt])
    c.act(junk[:], xt[:], AF.Square, [xt], [junk, ss], accum=ss[:, 0:1])
    c.act(rstd[:, 0:1], ss[:, 0:1], AF.Sqrt, [ss, eps], [rstd], bias=eps[:, 0:1], scale=1.0 / D)
    c.recip(rstd[:, 0:1], rstd[:, 0:1], [rstd], [rstd])
    c.ts("dve", xb[:], xt[:], rstd[:, 0:1], ALU.mult, [xt, rstd], [xb])
    for kc in range(8):
        c.tr(pT[:, kc, :], xb[:, kc * 128:(kc + 1) * 128], identb[:], [xb, identb], [pT])
    c.cp("pool" if False else "act", hT[:, :, col0:col0 + 128], pT[:], [pT], [hT])


def phase_A(m, l, xsrc):
    S = m.S
    TG = min(S, 2048)
    TB = min(512, TG)
    sc = m.scr
    if "hg" not in sc:
        m.scratch("hg", [S, 2048], F32)
        m.scratch("dqT", [512, S], BF16)
        m.scratch("dkT", [512, S], BF16)
        m.scratch("dvv", [S, 512], BF16)
        m.scratch("rwz", [S, 1536], F32)
        m.scratch("rwcT", [256, S], F32)
        m.scratch("gateT", [3072, S], BF16)
    blocks = [
        (0, 512, "tok", "hg", 0, AF.Silu, F32), (512, 512, "tok", "hg", 512, AF.Sigmoid, F32),
        (1024, 512, "tok", "hg", 1024, AF.Copy, F32), (1536, 512, "tok", "hg", 1536, AF.Silu, F32),
        (2048, 512, "feat", "dqT", 0, AF.Copy, BF16), (2560, 512, "feat", "dkT", 0, AF.Copy, BF16),
        (3072, 512, "tok", "dvv", 0, AF.Copy, BF16),
        (3584, 512, "tok", "rwz", 0, AF.Copy, F32), (4096, 512, "tok", "rwz", 512, AF.Copy, F32),
        (4608, 512, "tok", "rwz", 1024, AF.Copy, F32), (5120, 256, "feat", "rwcT", 0, AF.Copy, F32),
    ] + [(5376 + i * 512, 512, "gate", "gateT", i * 512, AF.Sigmoid, BF16) for i in range(6)]
    with contextlib.ExitStack() as st:
        c = m.ctx(st)
        cs = load_consts(m, c, ["ident"])
        identb = make_bf(c, cs["ident"], [128, 128], "identb")
        eps = c.sb("eps", [128, 1])
        c.memset("dve", eps[:], 1e-6, [eps])
        bg = c.sb("bg", [128, 24])
        c.dma(bg[:], m.din["b_gate"][l].rearrange("(c p) -> p c", p=128), writes=[bg])
        hT = c.sb("hT", [128, 8, TG], BF16)
        xts = c.sbn("xt", [128, D], F32, 2)
        xbs = c.sbn("xb", [128, D], BF16, 2)
        junk = c.sb("junk", [128, D], F32)
        sss = c.sbn("ss", [128, 1], F32, 2)
        rstds = c.sbn("rstd", [128, 1], F32, 2)
        pTs = c.psn("pT", [128, 8, 128], BF16, 2)
        wbs = c.sbn("wb", [128, 8, 512], BF16, 2)
        pos = c.psn("po", [128, 512], F32, 4)
        o32 = c.sbn("o32", [128, 512], F32, 3)
        o16 = c.sbn("o16", [128, 512], BF16, 3)
        wsrc = sc["b_w_in"][l].rearrange("(c p) n -> p c n", p=128)
        it = 0
        for g0 in range(0, S, TG):
            for tt in range(TG // 128):
                i = tt % 2
                norm_T(c, xsrc[g0 + tt * 128:g0 + (tt + 1) * 128, :], hT, tt * 128, xts[i], xbs[i], junk, sss[i],
                       rstds[i], eps, pTs[i], identb)
            for bi, (c0, ncol, kind, dname, doff, func, dt) in enumerate(blocks):
                wb = wbs[bi % 2]
                c.dma(wb[:, :, 0:ncol], wsrc[:, :, c0:c0 + ncol], writes=[wb], q="sp")
                dest = sc[dname]
                if kind == "tok":
                    for tt in range(TG // 128):
                        po = pos[it % 4]
                        ot = (o32 if dt == F32 else o16)[it % 3]
                        it += 1
                        for kc in range(8):
                            c.mm(po[:, 0:ncol], hT[:, kc, tt * 128:(tt + 1) * 128], wb[:, kc, 0:ncol], kc == 0, kc == 7,
                                 [hT, wb], [po])
                        c.act(ot[:, 0:ncol], po[:, 0:ncol], func, [po], [ot])
                        c.dma(dest[g0 + tt * 128:g0 + (tt + 1) * 128, doff:doff + ncol], ot[:, 0:ncol], reads=[ot])
                else:
                    for fc in range(ncol // 128):
                        for tb in range(TG // TB):
                            po = pos[it % 4]
                            ot = (o32 if dt == F32 else o16)[it % 3]
                            it += 1
                            for kc in range(8):
                                c.mm(po[:, 0:TB], wb[:, kc, fc * 128:(fc + 1) * 128], hT[:, kc, tb * TB:(tb + 1) * TB],
                                     kc == 0, kc == 7, [hT, wb], [po])
                            if kind == "gate":
                                gc = (doff + fc * 128) // 128
                                c.act(ot[:, 0:TB], po[:, 0:TB], func, [po, bg], [ot], bias=bg[:, gc:gc + 1])
                            else:
                                c.act(ot[:, 0:TB], po[:, 0:TB], func, [po], [ot])
                            r0 = doff + fc * 128
                            c.dma(dest[r0:r0 + 128, g0 + tb * TB:g0 + (tb + 1) * TB], ot[:, 0:TB], reads=[ot])
        m.sched.emit()


def rowb(m, c, name, src1d, F):
    t = c.sb(name, [128, F], F32)
    c.dma(t[:], src1d.partition_broadcast(128), writes=[t])
    return t


def sub(parent):
    v = T(parent.t, parent.b.name + "_v")
    return v


def bc3(ap2, n=64):
    H = ap2.shape[1]
    return ap2.unsqueeze(2).to_broadcast([128, H, n])


def v3(ap2, n=64):
    return ap2.rearrange("p (h n) -> p h n", n=n)


def phase_B(m, l):
    S = m.S
    sc = m.scr
    if "brT" not in sc:
        m.scratch("brT", [3, 512, S], BF16)
    with contextlib.ExitStack() as st:
        c = m.ctx(st)
        cs = load_consts(m, c, ["ident", "hgm", "ones"])
        ident, hgm, ones = cs["ident"], cs["hgm"], cs["ones"]
        eps = c.sb("eps", [128, 1])
        c.memset("dve", eps[:], 1e-6, [eps])
        lbrow = c.sb("lbrow", [128, 512])
        omlb = c.sb("omlb", [128, 512])
        if l == 0:
            c.memset("dve", lbrow[:], 0.0, [lbrow])
        else:
            a0 = rowb(m, c, "lba0", m.din["hgrn_lb_param"][0], 512)
            a1 = rowb(m, c, "lba1", m.din["hgrn_lb_param"][1], 512)
            c.tt("dve", a1[:], a1[:], a0[:], ALU.subtract, [a0, a1], [a1])
            c.act(lbrow[:], a1[:], AF.Sigmoid, [a1], [lbrow])
        c.ts("dve", omlb[:], lbrow[:], -1.0, ALU.mult, [lbrow], [omlb], s2=1.0, op1=ALU.add)
        ngrow = rowb(m, c, "ngrow", m.din["hgrn_norm_g"][l], 512)
        Sst = [c.sbn(f"Sst{h}_", [128, 128], F32, 2) for h in range(4)]
        for h in range(4):
            c.memset("pool", Sst[h][0][:], 0.0, [Sst[h][0]])
        hgt = c.sbn("hgt", [128, 2048], F32, 2)
        fv = c.sbn("fv", [128, 512], F32, 2)
        kf = c.sbn("kf", [128, 512], F32, 2)
        lf = c.sbn("lf", [128, 512], F32, 2)
        ee = c.sbn("ee", [128, 4, 512], F32, 2)
        qk = c.sbn("qk", [128, 4, 512], F32, 2)
        FT = c.sbn("FT", [128, 3, 128], F32, 2)
        AT = c.sbn("AT", [128, 128], F32, 2)
        dcol = c.sbn("dcol", [128, 2], F32, 4)
        ssq = c.sbn("ssq", [128, 4], F32, 2)
        rstd = c.sbn("rstdh", [128, 4], F32, 2)
        gn = c.sbn("gn", [128, 512], F32, 2)
        ob = c.sbn("ob", [128, 512], BF16, 2)
        obf = c.sbn("obf", [128, 512], F32, 2)
        obT = c.sbn("obT", [128, 4, 128], BF16, 2)
        junk = c.sb("junkb", [128, 128], F32)
        pcs = c.psn("pcs", [128, 512], F32, 3)
        pTr1 = c.ps("pTr", [128, 512], F32)
        pTr = [pTr1, pTr1]
        psc = c.ps("psc", [128, 512], F32)
        pkv = c.ps("pkv", [128, 512], F32)
        pcol = c.ps("pcol", [128, 512], F32)
        po = c.ps("pob", [128, 512], F32)
        pTb = pTr1
        cn = 0
        for ti in range(S // 128):
            t0 = ti * 128
            i = ti % 2
            hg = hgt[i]
            c.dma(hg[:], sc["hg"][t0:t0 + 128, :], writes=[hg])
            q, sg, vv, gate = hg[:, 0:512], hg[:, 512:1024], hg[:, 1024:1536], hg[:, 1536:2048]
            f = fv[i]
            c.tt("pool", f[:], sg, omlb[:], ALU.mult, [hg, omlb], [f])
            c.tt("pool", f[:], f[:], lbrow[:], ALU.add, [f, lbrow], [f])
            c.ts("pool", f[:], f[:], 1e-30, ALU.max, [f], [f])
            c.ts("dve", kf[i][:], f[:], -1.0, ALU.mult, [f], [kf[i]], s2=1.0, op1=ALU.add)
            c.act(lf[i][:], f[:], AF.Ln, [f], [lf[i]])
            for k in range(3):
                c.mm(pcs[k][:], hgm[:, k * 128:(k + 1) * 128], lf[i][:], True, True, [hgm, lf[i]], [pcs[k]])
            e = ee[i]
            c.act(e[:, 0, :], pcs[1][:], AF.Exp, [pcs[1]], [e])
            c.act(e[:, 1, :], pcs[1][:], AF.Exp, [pcs[1]], [e], scale=-1.0)
            c.act(e[:, 2, :], pcs[0][:], AF.Exp, [pcs[0]], [e])
            c.act(e[:, 3, :], pcs[2][:], AF.Exp, [pcs[2]], [e])
            w = qk[i]
            c.tt("dve", w[:, 0, :], q, e[:, 0, :], ALU.mult, [hg, e], [w])
            c.tt("pool", w[:, 1, :], kf[i][:], e[:, 1, :], ALU.mult, [kf[i], e], [w])
            c.tt("dve", w[:, 2, :], q, e[:, 2, :], ALU.mult, [hg, e], [w])
            c.tt("pool", w[:, 3, :], kf[i][:], e[:, 3, :], ALU.mult, [kf[i], e], [w])
            c.tt("pool", gn[i][:], gate, ngrow[:], ALU.mult, [hg, ngrow], [gn[i]])
            for h in range(4):
                hc = slice(h * 128, (h + 1) * 128)
                ft = FT[h % 2]
                ptr = pTr[h % 2]
                for k in range(3):
                    c.tr(ptr[:, k * 128:(k + 1) * 128], w[:, k, hc], ident[:], [w, ident], [ptr])
                c.cp("act", ft[:].rearrange("p a b -> p (a b)"), ptr[:, 0:384], [ptr], [ft])
                c.mm(psc[:, 0:128], ft[:, 1, :], ft[:, 0, :], True, True, [ft], [psc])
                at = AT[h % 2]
                c.tt("dve", at[:], psc[:, 0:128], hgm[:, 0:128], ALU.mult, [psc, hgm], [at])
                c.mm(po[:, hc], at[:], vv[:, hc] if False else hg[:, 1024 + h * 128:1024 + (h + 1) * 128], True, False, [at, hg], [po])
                for ch in range(2):
                    P = slice(ch * 64, (ch + 1) * 64)
                    Sc = Sst[h][(2 * ti + ch) % 2]
                    Sn = Sst[h][(2 * ti + ch + 1) % 2]
                    c.mm(po[P, hc], ft[:, 2, P], Sc[:], False, ch == 1, [ft, Sc], [po])
                    dc = dcol[cn % 4]
                    cn += 1
                    c.mm(pcol[:, 256:258], lf[i][P, hc], ones[P, 0:2], True, True, [lf[i], ones], [pcol])
                    c.act(dc[:], pcol[:, 256:258], AF.Exp, [pcol], [dc])
                    c.mm(pkv[:, 128:256], w[P, 3, hc], hg[P, 1024 + h * 128:1024 + (h + 1) * 128], True, True, [w, hg], [pkv])
                    c.stt(Sn[:], Sc[:], dc[:, 0:1], pkv[:, 128:256], ALU.mult, ALU.add, [Sc, dc, pkv], [Sn])
            for h in range(4):
                hc = slice(h * 128, (h + 1) * 128)
                c.act(junk[:], po[:, hc], AF.Square, [po], [junk, ssq[i]], accum=ssq[i][:, h:h + 1])
            c.act(rstd[i][:], ssq[i][:], AF.Sqrt, [ssq[i], eps], [rstd[i]], bias=eps[:, 0:1], scale=1.0 / 128)
            c.recip(rstd[i][:], rstd[i][:], [rstd[i]], [rstd[i]])
            for h in range(4):
                hc = slice(h * 128, (h + 1) * 128)
                c.stt(obf[i][:, hc], po[:, hc], rstd[i][:, h:h + 1], gn[i][:, hc], ALU.mult, ALU.mult, [po, rstd[i], gn[i]], [obf[i]])
            for h in range(4):
                hc = slice(h * 128, (h + 1) * 128)
                c.tr(pTb[:, hc], obf[i][:, hc], ident[:], [obf[i], ident], [pTb])
            c.cp("act", obT[i][:].rearrange("p a b -> p (a b)"), pTb[:], [pTb], [obT[i]])
            c.dma(sc["brT"][0].rearrange("(h v) t -> v h t", v=128)[:, :, t0:t0 + 128], obT[i][:], reads=[obT[i]])
        m.sched.emit()


def phase_C(m, l):
    S = m.S
    sc = m.scr
    QB = min(512, S)
    nq = QB // 128
    NQ = S // QB
    NJ = S // 128
    lambda_init = 0.8 - 0.6 * math.exp(-0.3 * l)
    with contextlib.ExitStack() as st:
        c = m.ctx(st)
        cs = load_consts(m, c, ["ones", "ctab"])
        ones, ctab = cs["ones"], cs["ctab"]
        onesb = make_bf(c, ones, [128, 128], "onesb")
        eps = c.sb("eps", [128, 1])
        c.memset("dve", eps[:], 1e-6, [eps])
        lamr = rowb(m, c, "lamr", m.din["diff_lambda"][l].rearrange("a b -> (a b)"), 256)
        ltmp = c.sb("ltmp", [128, 128])
        lsum = c.sb("lsum", [128, 2])
        c.tt("dve", ltmp[:, 0:64], lamr[:, 0:64], lamr[:, 64:128], ALU.mult, [lamr], [ltmp])
        c.tt("dve", ltmp[:, 64:128], lamr[:, 128:192], lamr[:, 192:256], ALU.mult, [lamr, ltmp], [ltmp])
        c.red(lsum[:], ltmp[:].rearrange("p (a b) -> p a b", b=64), ALU.add, [ltmp], [lsum])
        c.act(lsum[:], lsum[:], AF.Exp, [lsum], [lsum])
        nlam = c.sb("nlam", [128, 1])
        c.tt("dve", nlam[:], lsum[:, 1:2], lsum[:, 0:1], ALU.subtract, [lsum], [nlam])
        c.ts("dve", nlam[:], nlam[:], -lambda_init, ALU.add, [nlam], [nlam])
        gcol = c.sb("gcol", [128, 1])
        c.dma(gcol[:], m.din["diff_subln_g"][l].rearrange("(p o) -> p o", o=1), writes=[gcol])
        c.ts("dve", gcol[:], gcol[:], 1.0 - lambda_init, ALU.mult, [gcol], [gcol])
        qT = c.sb("qT", [128, S], BF16)
        kT = c.sb("kT", [128, S], BF16)
        V = c.sb("V", [128, NJ, 128], BF16)
        AL = c.sb("AL", [128, 5, 512], F32)
        TMP = c.sbn("tmpc", [128, 512], F32, 3)
        PT = c.sbn("ptc", [128, 512], BF16, 4)
        PST = c.psn("pst", [128, 512], F32, 4)
        PO = c.psn("poc", [128, 512], F32, 2)
        PL = c.psn("plc", [128, 512], F32, 2)
        rl = c.sbn("rlc", [128, 512], F32, 2)
        oc = c.sbn("occ", [128, 512], F32, 2)
        od = c.sb("odc", [128, 512], F32)
        sq = c.sb("sqc", [128, 512], F32)
        rs = c.sb("rsc", [128, 512], F32)
        obo = c.sbn("oboc", [128, 512], BF16, 2)
        cnt = 0
        for h in range(4):
            c.dma(qT[:], sc["dqT"][h * 128:(h + 1) * 128, :], writes=[qT], q="sp")
            c.dma(kT[:], sc["dkT"][h * 128:(h + 1) * 128, :], writes=[kT], q="act")
            vsrc = sc["dvv"].rearrange("(j p) c -> p j c", p=128)
            for j0 in range(0, NJ, 8):
                j1 = min(NJ, j0 + 8)
                c.dma(V[:, j0:j1, :], vsrc[:, j0:j1, h * 128:(h + 1) * 128], writes=[V], q=("pool", "sp", "act")[(j0 // 8) % 3])
            c.dma(AL[:].rearrange("p a b -> p (a b)"), m.din["c_alibi"][:, h * 2560:(h + 1) * 2560], writes=[AL], q="sp")
            for I in range(NQ):
                qs = slice(I * QB, (I + 1) * QB)
                jmax = nq * (I + 1) - 1
                units = [(j, c2) for j in range(jmax + 1) for c2 in range(2)]
                LA = 2
                pend = []

                def stage1(j, c2):
                    nonlocal cnt
                    d = j - nq * I
                    var = 0 if d < 0 else 1 + d
                    mc = (nq * I - j) if d < 0 else 0
                    P = slice(c2 * 64, (c2 + 1) * 64)
                    pst = PST[cnt % 4]
                    tmp = TMP[cnt % 3]
                    pt = PT[cnt % 4]
                    cnt += 1
                    c.mm(pst[:, 0:QB], kT[P, j * 128:(j + 1) * 128], qT[P, qs], True, True, [kT, qT], [pst])
                    c.tt("dve", tmp[:, 0:QB], pst[:, 0:QB], AL[:, var, 0:QB], ALU.add, [pst, AL], [tmp])
                    c.act(pt[:, 0:QB], tmp[:, 0:QB], AF.Exp, [tmp, ctab], [pt], bias=ctab[:, h * 65 + mc:h * 65 + mc + 1], scale=0.125)
                    return pt

                def stage2(j, c2, pt):
                    c.mm(PO[c2][:, 0:QB], V[:, j, :], pt[:, 0:QB], j == 0, j == jmax, [V, pt], [PO[c2]])
                    c.mm(PL[c2][:, 0:QB], onesb[:], pt[:, 0:QB], j == 0, j == jmax, [onesb, pt], [PL[c2]])

                for ui in range(len(units) + LA):
                    if ui < len(units):
                        j, c2 = units[ui]
                        pend.append((j, c2, stage1(j, c2)))
                    if ui >= LA:
                        stage2(*pend.pop(0))
                for c2 in range(2):
                    c.recip(rl[c2][:, 0:QB], PL[c2][:, 0:QB], [PL[c2]], [rl[c2]])
                    c.tt("dve", oc[c2][:, 0:QB], PO[c2][:, 0:QB], rl[c2][:, 0:QB], ALU.mult, [PO[c2], rl[c2]], [oc[c2]])
                c.stt(od[:, 0:QB], oc[1][:, 0:QB], nlam[:, 0:1], oc[0][:, 0:QB], ALU.mult, ALU.add, [oc[0], oc[1], nlam], [od])
                c.tt("pool", sq[:, 0:QB], od[:, 0:QB], od[:, 0:QB], ALU.mult, [od], [sq])
                pss = PST[cnt % 4]
                cnt += 1
                c.mm(pss[:, 0:QB], ones[:], sq[:, 0:QB], True, True, [ones, sq], [pss])
                c.act(rs[:, 0:QB], pss[:, 0:QB], AF.Sqrt, [pss, eps], [rs], bias=eps[:, 0:1], scale=1.0 / 128)
                c.recip(rs[:, 0:QB], rs[:, 0:QB], [rs], [rs])
                o_ = obo[I % 2]
                c.stt(o_[:, 0:QB], od[:, 0:QB], gcol[:, 0:1], rs[:, 0:QB], ALU.mult, ALU.mult, [od, gcol, rs], [o_])
                c.dma(sc["brT"][1][h * 128:(h + 1) * 128, qs], o_[:, 0:QB], reads=[o_])
        m.sched.emit()


def phase_E(m, l, xsrc, xdst):
    S = m.S
    sc = m.scr
    TB = min(512, S)
    with contextlib.ExitStack() as st:
        c = m.ctx(st)
        wbr = c.sb("wbr", [128, 12, D], BF16)
        wout = c.sb("wout", [128, 8, D], BF16)
        c.dma(wbr[:], sc["b_w_branch"][l].rearrange("(c p) n -> p c n", p=128), writes=[wbr], q="sp")
        c.dma(wout[:], sc["b_w_out"][l].rearrange("(c p) n -> p c n", p=128), writes=[wout], q="act")
        br = c.sbn("br", [128, 12, TB], BF16, 2)
        gt = c.sbn("gte", [128, 24, TB], BF16, 2)
        acc = c.sbn("acce", [128, TB], F32, 2)
        tmp = c.sbn("tmpe", [128, TB], F32, 3)
        mT = c.sbn("mT", [128, 8, TB], BF16, 2)
        xt = c.sbn("xte", [128, D], F32, 2)
        xo = c.sbn("xoe", [128, D], F32, 2)
        PM = c.psn("pme", [128, 512], F32, 4)
        PO = c.psn("poe", [128, 512], F32, 4)
        k = 0
        k2 = 0
        for tb in range(S // TB):
            ts_ = slice(tb * TB, (tb + 1) * TB)
            b_, g_, m_ = br[tb % 2], gt[tb % 2], mT[tb % 2]
            c.dma(b_[:], sc["brT"].rearrange("n (c p) t -> p (n c) t", p=128)[:, :, ts_], writes=[b_], q="sp")
            c.dma(g_[:], sc["gateT"].rearrange("(c p) t -> p c t", p=128)[:, :, ts_], writes=[g_], q="act")
            for dmc in range(8):
                a_ = acc[dmc % 2]
                for n in range(3):
                    pm = PM[k % 4]
                    for kc in range(4):
                        c.mm(pm[:, 0:TB], wbr[:, n * 4 + kc, dmc * 128:(dmc + 1) * 128], b_[:, n * 4 + kc, :], kc == 0, kc == 3, [wbr, b_], [pm])
                    if n == 0:
                        c.tt("dve", a_[:], pm[:, 0:TB], g_[:, n * 8 + dmc, :], ALU.mult, [pm, g_], [a_])
                    else:
                        t_ = tmp[k % 3]
                        c.tt("dve", t_[:], pm[:, 0:TB], g_[:, n * 8 + dmc, :], ALU.mult, [pm, g_], [t_])
                        if n == 1:
                            c.tt("pool", a_[:], a_[:], t_[:], ALU.add, [a_, t_], [a_])
                        else:
                            c.tt("pool", m_[:, dmc, :], a_[:], t_[:], ALU.add, [a_, t_], [m_])
                    k += 1
            for tt in range(TB // 128):
                x_, o_ = xt[k2 % 2], xo[k2 % 2]
                r0 = tb * TB + tt * 128
                c.dma(x_[:], xsrc[r0:r0 + 128, :], writes=[x_], q="pool")
                for cb in range(2):
                    po = PO[(2 * k2 + cb) % 4]
                    for kc in range(8):
                        c.mm(po[:], m_[:, kc, tt * 128:(tt + 1) * 128], wout[:, kc, cb * 512:(cb + 1) * 512], kc == 0, kc == 7, [m_, wout], [po])
                    c.tt("dve", o_[:, cb * 512:(cb + 1) * 512], po[:], x_[:, cb * 512:(cb + 1) * 512], ALU.add, [po, x_], [o_])
                c.dma(xdst[r0:r0 + 128, :], o_[:], reads=[o_], q="sp")
                k2 += 1
        m.sched.emit()


class NormBufs:
    def __init__(self, c, tag):
        self.xts = c.sbn("xt" + tag, [128, D], F32, 2)
        self.xbs = c.sbn("xb" + tag, [128, D], BF16, 2)
        self.junk = c.sb("junk" + tag, [128, D], F32)
        self.sss = c.sbn("ss" + tag, [128, 1], F32, 2)
        self.rstds = c.sbn("rstd" + tag, [128, 1], F32, 2)
        self.pTs = c.psn("pT" + tag, [128, 8, 128], BF16, 2)
        self.eps = c.sb("eps" + tag, [128, 1])
        c.memset("dve", self.eps[:], 1e-6, [self.eps])
        self.n = 0

    def run(self, c, xsrc, hT, col0, identb):
        i = self.n % 2
        self.n += 1
        norm_T(c, xsrc, hT, col0, self.xts[i], self.xbs[i], self.junk, self.sss[i], self.rstds[i], self.eps,
               self.pTs[i], identb)


def phase_F(m, l, xsrc, xdst):
    S = m.S
    sc = m.scr
    TB = min(512, S)
    with contextlib.ExitStack() as st:
        c = m.ctx(st)
        cs = load_consts(m, c, ["ident", "ones"])
        identb = make_bf(c, cs["ident"], [128, 128], "identb")
        onesb = make_bf(c, cs["ones"], [128, 128], "onesb")
        nb = NormBufs(c, "f")
        wq = c.sb("wq", [128, 8, D], BF16)
        wkv = c.sb("wkv", [128, 8, 2 * D], BF16)
        wo = c.sb("wo", [128, 8, D], BF16)
        c.dma(wq[:], sc["b_xa_wq"][l].rearrange("(c p) n -> p c n", p=128), writes=[wq], q="sp")
        c.dma(wkv[:], sc["b_xa_wkv"][l].rearrange("(c p) n -> p c n", p=128), writes=[wkv], q="act")
        c.dma(wo[:], sc["b_xa_wo"][l].rearrange("(c p) n -> p c n", p=128), writes=[wo], q="pool")
        memT = c.sb("memT", [128, 8, MEM], BF16)
        KT = c.sb("KT", [128, 8, MEM], BF16)
        Vm = c.sb("Vm", [128, 2, D], BF16)
        PA = c.psn("paf", [128, 512], F32, 3)
        for mt in range(2):
            nb.run(c, m.din["mem"][mt * 128:(mt + 1) * 128, :], memT, mt * 128, identb)
        k = 0
        for fc in range(8):
            pa = PA[k % 3]
            k += 1
            for kc in range(8):
                c.mm(pa[:, 0:MEM], wkv[:, kc, fc * 128:(fc + 1) * 128], memT[:, kc, :], kc == 0, kc == 7, [wkv, memT], [pa])
            c.cp("act", KT[:, fc, :], pa[:, 0:MEM], [pa], [KT])
        for mt in range(2):
            for cb in range(2):
                pa = PA[k % 3]
                k += 1
                for kc in range(8):
                    c.mm(pa[:], memT[:, kc, mt * 128:(mt + 1) * 128], wkv[:, kc, D + cb * 512:D + (cb + 1) * 512], kc == 0, kc == 7, [wkv, memT], [pa])
                c.cp("act", Vm[:, mt, cb * 512:(cb + 1) * 512], pa[:], [pa], [Vm])
        hT = c.sbn("hTf", [128, 8, TB], BF16, 2)
        qT = c.sbn("qTf", [128, 8, TB], BF16, 2)
        oT = c.sbn("oTf", [128, 8, TB], BF16, 2)
        pt = c.sbn("ptf", [128, 2, TB], BF16, 3)
        rl = c.sbn("rlf", [128, TB], F32, 2)
        xt = c.sbn("xtf", [128, D], F32, 2)
        xo = c.sbn("xof", [128, D], F32, 2)
        PB = c.psn("pbf", [128, 512], F32, 3)
        k2 = 0
        kp = 0
        for tb in range(S // TB):
            h_, q_, o_ = hT[tb % 2], qT[tb % 2], oT[tb % 2]
            for tt in range(TB // 128):
                r0 = tb * TB + tt * 128
                nb.run(c, xsrc[r0:r0 + 128, :], h_, tt * 128, identb)
            for fc in range(8):
                pa = PA[k % 3]
                k += 1
                for kc in range(8):
                    c.mm(pa[:, 0:TB], wq[:, kc, fc * 128:(fc + 1) * 128], h_[:, kc, :], kc == 0, kc == 7, [wq, h_], [pa])
                c.cp("act", q_[:, fc, :], pa[:, 0:TB], [pa], [q_])
            for h in range(4):
                p_ = pt[kp % 3]
                kp += 1
                for mc in range(2):
                    pa = PA[k % 3]
                    k += 1
                    for dc in range(2):
                        c.mm(pa[:, 0:TB], KT[:, h * 2 + dc, mc * 128:(mc + 1) * 128], q_[:, h * 2 + dc, :], dc == 0, dc == 1, [KT, q_], [pa])
                    c.act(p_[:, mc, :], pa[:, 0:TB], AF.Exp, [pa], [p_], scale=1.0 / 16)
                pl = PB[2]
                for mc in range(2):
                    c.mm(pl[:, 0:TB], onesb[:], p_[:, mc, :], mc == 0, mc == 1, [onesb, p_], [pl])
                r_ = rl[h % 2]
                c.recip(r_[:], pl[:, 0:TB], [pl], [r_])
                for dc in range(2):
                    pb = PB[dc]
                    for mc in range(2):
                        c.mm(pb[:, 0:TB], Vm[:, mc, h * 256 + dc * 128:h * 256 + (dc + 1) * 128], p_[:, mc, :], mc == 0, mc == 1, [Vm, p_], [pb])
                    c.tt("dve", o_[:, h * 2 + dc, :], pb[:, 0:TB], r_[:], ALU.mult, [pb, r_], [o_])
            for tt in range(TB // 128):
                x_, xo_ = xt[k2 % 2], xo[k2 % 2]
                r0 = tb * TB + tt * 128
                c.dma(x_[:], xsrc[r0:r0 + 128, :], writes=[x_], q="pool")
                for cb in range(2):
                    pa = PA[k % 3]
                    k += 1
                    for kc in range(8):
                        c.mm(pa[:], o_[:, kc, tt * 128:(tt + 1) * 128], wo[:, kc, cb * 512:(cb + 1) * 512], kc == 0, kc == 7, [o_, wo], [pa])
                    c.tt("dve", xo_[:, cb * 512:(cb + 1) * 512], pa[:], x_[:, cb * 512:(cb + 1) * 512], ALU.add, [pa, x_], [xo_])
                c.dma(xdst[r0:r0 + 128, :], xo_[:], reads=[xo_], q="sp")
                k2 += 1
        m.sched.emit()


def phase_G(m, l, xsrc, xdst):
    S = m.S
    sc = m.scr
    TBK = min(1024, S)
    SB = min(512, TBK)
    NFC = DFF // 128
    with contextlib.ExitStack() as st:
        c = m.ctx(st)
        cs = load_consts(m, c, ["ident"])
        identb = make_bf(c, cs["ident"], [128, 128], "identb")
        nb = NormBufs(c, "g")
        wdn = c.sb("wdn", [128, NFC, D], BF16)
        c.dma(wdn[:], sc["b_ffn_w_down"][l].rearrange("(c p) n -> p c n", p=128), writes=[wdn], q="pool")
        cw = c.sb("cw", [128, NFC, 3])
        for j in range(3):
            c.dma(cw[:, :, j], m.din["ffn_conv_w"][l, j].rearrange("(c p) -> p c", p=128), writes=[cw])
        cb_ = c.sb("cbias", [128, NFC])
        c.dma(cb_[:], m.din["ffn_conv_b"][l].rearrange("(c p) -> p c", p=128), writes=[cb_])
        halo = c.sb("halo", [128, NFC, 2])
        c.memset("dve", halo[:], 0.0, [halo])
        hT = c.sb("hTg", [128, 8, TBK], BF16)
        hid = c.sb("hid", [128, NFC, TBK], BF16)
        wu = c.sbn("wu", [128, 8, 512], BF16, 2)
        wv = c.sbn("wv", [128, 8, 512], BF16, 2)
        uext = c.sbn("uext", [128, SB + 2], F32, 2)
        t1 = c.sbn("t1g", [128, SB], F32, 2)
        t2 = c.sbn("t2g", [128, SB], F32, 2)
        sg = c.sbn("sgg", [128, SB], F32, 2)
        xt = c.sbn("xtg", [128, D], F32, 2)
        xo = c.sbn("xog", [128, D], F32, 2)
        PU = c.psn("pug", [128, 512], F32, 2)
        PV = c.psn("pvg", [128, 512], F32, 2)
        PO = c.psn("pog", [128, 512], F32, 2)
        wsrc = sc["b_ffn_w_up"][l].rearrange("(c p) n -> p c n", p=128)
        k = 0
        k2 = 0
        for tbk in range(S // TBK):
            for tt in range(TBK // 128):
                r0 = tbk * TBK + tt * 128
                nb.run(c, xsrc[r0:r0 + 128, :], hT, tt * 128, identb)
            for grp in range(6):
                ncol = 512 if grp < 5 else 256
                wu_, wv_ = wu[grp % 2], wv[grp % 2]
                c.dma(wu_[:, :, 0:ncol], wsrc[:, :, grp * 512:grp * 512 + ncol], writes=[wu_], q="sp")
                c.dma(wv_[:, :, 0:ncol], wsrc[:, :, DFF + grp * 512:DFF + grp * 512 + ncol], writes=[wv_], q="act")
                for fcl in range(ncol // 128):
                    fc = grp * 4 + fcl
                    for sbi in range(TBK // SB):
                        ss_ = slice(sbi * SB, (sbi + 1) * SB)
                        pu, pv = PU[k % 2], PV[k % 2]
                        ue, a1, a2, s_ = uext[k % 2], t1[k % 2], t2[k % 2], sg[k % 2]
                        k += 1
                        for kc in range(8):
                            c.mm(pu[:, 0:SB], wu_[:, kc, fcl * 128:(fcl + 1) * 128], hT[:, kc, ss_], kc == 0, kc == 7, [wu_, hT], [pu])
                        for kc in range(8):
                            c.mm(pv[:, 0:SB], wv_[:, kc, fcl * 128:(fcl + 1) * 128], hT[:, kc, ss_], kc == 0, kc == 7, [wv_, hT], [pv])
                        c.cp("pool", ue[:, 0:2], halo[:, fc, :], [halo], [ue])
                        c.cp("act", ue[:, 2:SB + 2], pu[:, 0:SB], [pu], [ue])
                        c.cp("pool", halo[:, fc, :], ue[:, SB:SB + 2], [ue], [halo])
                        c.act(a1[:], pu[:, 0:SB], AF.Identity, [pu, cw, cb_], [a1], bias=cb_[:, fc:fc + 1], scale=cw[:, fc, 2:3])
                        c.stt(a2[:], ue[:, 1:SB + 1], cw[:, fc, 1:2], a1[:], ALU.mult, ALU.add, [ue, cw, a1], [a2])
                        c.stt(a1[:], ue[:, 0:SB], cw[:, fc, 0:1], a2[:], ALU.mult, ALU.add, [ue, cw, a2], [a1])
                        c.act(s_[:], a1[:], AF.Silu, [a1], [s_])
                        c.tt("dve", hid[:, fc, ss_], s_[:], pv[:, 0:SB], ALU.mult, [s_, pv], [hid])
            for tt in range(TBK // 128):
                x_, xo_ = xt[k2 % 2], xo[k2 % 2]
                r0 = tbk * TBK + tt * 128
                c.dma(x_[:], xsrc[r0:r0 + 128, :], writes=[x_], q="pool")
                for cb in range(2):
                    po = PO[cb]
                    for fc in range(NFC):
                        c.mm(po[:], hid[:, fc, tt * 128:(tt + 1) * 128], wdn[:, fc, cb * 512:(cb + 1) * 512], fc == 0, fc == NFC - 1, [hid, wdn], [po])
                    c.tt("dve", xo_[:, cb * 512:(cb + 1) * 512], po[:], x_[:, cb * 512:(cb + 1) * 512], ALU.add, [po, x_], [xo_])
                c.dma(xdst[r0:r0 + 128, :], xo_[:], reads=[xo_], q="sp")
                k2 += 1
        m.sched.emit()


def phase_H(m, xsrc):
    S = m.S
    with contextlib.ExitStack() as st:
        c = m.ctx(st)
        grow = rowb(m, c, "fgrow", m.din["final_norm_g"], D)
        eps = c.sb("eps", [128, 1])
        c.memset("dve", eps[:], 1e-6, [eps])
        xt = c.sbn("xth", [128, D], F32, 3)
        xo = c.sbn("xoh", [128, D], F32, 3)
        junk = c.sb("junkh", [128, D], F32)
        ss = c.sbn("ssh", [128, 1], F32, 3)
        for ti in range(S // 128):
            i = ti % 3
            c.dma(xt[i][:], xsrc[ti * 128:(ti + 1) * 128, :], writes=[xt[i]])
            c.act(junk[:], xt[i][:], AF.Square, [xt[i]], [junk, ss[i]], accum=ss[i][:, 0:1])
            c.act(ss[i][:], ss[i][:], AF.Sqrt, [ss[i], eps], [ss[i]], bias=eps[:, 0:1], scale=1.0 / D)
            c.recip(ss[i][:], ss[i][:], [ss[i]], [ss[i]])
            c.stt(xo[i][:], xt[i][:], ss[i][:, 0:1], grow[:], ALU.mult, ALU.mult, [xt[i], ss[i], grow], [xo[i]])
            c.dma(m.out[ti * 128:(ti + 1) * 128, :], xo[i][:], reads=[xo[i]])
        m.sched.emit()


def phase_D(m, l):
    S = m.S
    sc = m.scr
    with contextlib.ExitStack() as st:
        c = m.ctx(st)
        cs = load_consts(m, c, ["ident", "rwm", "maskg", "ones"])
        ident, rwm, maskg, ones = cs["ident"], cs["rwm"], cs["maskg"], cs["ones"]
        din = m.din
        w0r = rowb(m, c, "w0r", din["rwkv_w0"][l], 512)
        a0r = rowb(m, c, "a0r", din["rwkv_a0"][l], 512)
        kkr = rowb(m, c, "kkr", din["rwkv_k_k"][l], 512)
        kar = rowb(m, c, "kar", din["rwkv_k_a"][l], 512)
        rkr = rowb(m, c, "rkr", din["rwkv_r_k"][l].rearrange("a b -> (a b)"), 512)
        lngr = rowb(m, c, "lngr", din["rwkv_ln_g"][l], 512)
        lnbr = rowb(m, c, "lnbr", din["rwkv_ln_b"][l], 512)
        mur = rowb(m, c, "mur", din["rwkv_mu"][l][0:1536], 1536)
        mucol = c.sb("mucol", [128, 2])
        c.dma(mucol[:], din["rwkv_mu"][l][1536:1792].rearrange("(c p) -> p c", p=128), writes=[mucol])
        WUP = c.sb("WUP", [128, 512])
        c.dma(WUP[0:64, :], din["rwkv_w_up"][l], writes=[WUP])
        c.dma(WUP[64:128, :], din["rwkv_a_up"][l], writes=[WUP])
        GUP = c.sb("GUP", [128, 512])
        c.dma(GUP[:], din["rwkv_g_up"][l], writes=[GUP])
        epsg = c.sb("epsg", [128, 1])
        c.memset("dve", epsg[:], 64e-5, [epsg])
        STs = c.sbn("STs", [128, 4, 64], F32, 2)
        c.memset("dve", STs[0][:], 0.0, [STs[0]])
        Gbd = c.sbn("Gbd", [128, 128], F32, 4)
        for p in range(4):
            c.memset("pool", Gbd[p][:], 0.0, [Gbd[p]])
        Hs = c.sb("Hs", [128, 4, 64])
        z = c.sbn("z", [128, 1536], F32, 2)
        zp = c.sbn("zp", [128, 1536], F32, 2)
        zm = c.sbn("zm", [128, 1536], F32, 2)
        cod = c.sbn("cod", [128, 2, 129], F32, 2)
        dcd = c.sb("dcd", [128, 2, 128])
        cm = c.sb("cm", [128, 2, 128])
        lw = c.sb("lw", [128, 128])
        sgd = c.sb("sgd", [128, 128])
        tmpw = c.sb("tmpw", [128, 512])
        sigw = c.sb("sigw", [128, 512])
        av = c.sb("av", [128, 512])
        gv = c.sb("gv", [128, 512])
        kkraw = c.sb("kkraw", [128, 512])
        sqk = c.sb("sqk", [128, 512])
        s8 = c.sbn("s8_", [128, 8], F32, 6)
        kk = c.sb("kk", [128, 512])
        kmod = c.sb("kmod", [128, 512])
        bv = c.sb("bv", [128, 512])
        ee = c.sb("eed", [128, 4, 512])
        TM = c.sb("TM", [128, 4, 512])
        BH = c.sb("BH", [128, 512])
        KH = c.sb("KH", [128, 512])
        PCc = c.sb("PCc", [128, 4])
        FT = c.sbn("FTd", [128, 4, 128], F32, 4)
        GM = c.sbn("GM", [128, 512], F32, 2)
        Lc = c.sbn("Lc", [128, 2, 128], F32, 4)
        Wc = c.sbn("Wc", [128, 128], F32, 4)
        Rp = c.sb("Rp", [128, 512])
        Yl = c.sb("Yl", [128, 512])
        RT = c.sbn("RTd", [128, 128], F32, 2)
        yv = c.sb("yv", [128, 512])
        yc = c.sb("yc", [128, 512])
        sq2 = c.sb("sq2", [128, 512])
        bon = c.sb("bon", [128, 512])
        yo = c.sb("yo", [128, 512])
        obT = c.sbn("obTd", [128, 4, 128], BF16, 2)
        B0 = c.ps("B0", [128, 512]); B1 = c.ps("B1", [128, 512]); B2 = c.ps("B2", [128, 512])
        B3 = c.ps("B3", [128, 512]); B4 = c.ps("B4", [128, 512]); B5 = c.ps("B5", [128, 512])
        B6 = c.ps("B6", [128, 512]); B7 = c.ps("B7", [128, 512])
        pL = B5
        pY = pGs = pH = pS = pRT = B6
        rwcsrc = sc["rwcT"].rearrange("(c p) t -> p c t", p=128)
        for ti in range(S // 128):
            t0 = ti * 128
            i = ti % 2
            z_, zp_, zm_, cod_ = z[i], zp[i], zm[i], cod[i]
            c.dma(z_[:], sc["rwz"][t0:t0 + 128, :], writes=[z_], q="sp")
            if ti == 0:
                c.memset("pool", zp_[0:1, :], 0.0, [zp_])
                c.dma(zp_[1:128, :], sc["rwz"][0:127, :], writes=[zp_], q="act")
                c.memset("pool", cod_[:, :, 0:1], 0.0, [cod_])
                c.dma(cod_[:, :, 1:129], rwcsrc[:, :, 0:128], writes=[cod_], q="pool")
            else:
                c.dma(zp_[:], sc["rwz"][t0 - 1:t0 + 127, :], writes=[zp_], q="act")
                c.dma(cod_[:], rwcsrc[:, :, t0 - 1:t0 + 128], writes=[cod_], q="pool")
            c.tt("dve", zm_[:], zp_[:], z_[:], ALU.subtract, [zp_, z_], [zm_])
            c.tt("pool", zm_[:], zm_[:], mur[:], ALU.mult, [zm_, mur], [zm_])
            c.tt("dve", zm_[:], zm_[:], z_[:], ALU.add, [zm_, z_], [zm_])
            r_, k_, v_ = zm_[:, 0:512], zm_[:, 512:1024], zm_[:, 1024:1536]
            c.tt("pool", dcd[:], cod_[:, :, 0:128], cod_[:, :, 1:129], ALU.subtract, [cod_], [dcd])
            for ch in range(2):
                c.stt(cm[:, ch, :], dcd[:, ch, :], mucol[:, ch:ch + 1], cod_[:, ch, 1:129], ALU.mult, ALU.add, [dcd, mucol, cod_], [cm])
            c.act(lw[0:64, :], cm[0:64, 0, :], AF.Tanh, [cm], [lw])
            c.cp("pool", lw[64:128, :], cm[64:128, 0, :], [cm], [lw])
            c.act(sgd[:], cm[:, 1, :], AF.Sigmoid, [cm], [sgd])
            c.mm(B0[:], lw[0:64, :], WUP[0:64, :], True, True, [lw, WUP], [B0])
            c.mm(B1[:], lw[64:128, :], WUP[64:128, :], True, True, [lw, WUP], [B1])
            c.mm(B2[:], sgd[:], GUP[:], True, True, [sgd, GUP], [B2])
            c.tt("dve", tmpw[:], B0[:], w0r[:], ALU.add, [B0, w0r], [tmpw])
            c.act(sigw[:], tmpw[:], AF.Sigmoid, [tmpw], [sigw])
            c.tt("dve", tmpw[:], B1[:], a0r[:], ALU.add, [B1, a0r, sigw], [tmpw])
            c.act(av[:], tmpw[:], AF.Sigmoid, [tmpw], [av])
            c.cp("act", gv[:], B2[:], [B2], [gv])
            c.tt("pool", kkraw[:], k_, kkr[:], ALU.mult, [zm_, kkr], [kkraw])
            c.tt("pool", sqk[:], kkraw[:], kkraw[:], ALU.mult, [kkraw], [sqk])
            c.red(s8[0][:], v3(sqk[:]), ALU.add, [sqk], [s8[0]])
            c.act(s8[0][:], s8[0][:], AF.Sqrt, [s8[0]], [s8[0]])
            c.ts("dve", s8[0][:], s8[0][:], 1e-12, ALU.max, [s8[0]], [s8[0]])
            c.recip(s8[0][:], s8[0][:], [s8[0]], [s8[0]])
            c.tt("dve", v3(kk[:]), v3(kkraw[:]), bc3(s8[0][:]), ALU.mult, [kkraw, s8[0]], [kk])
            c.stt(kmod[:], av[:], -1.0, kar[:], ALU.add, ALU.mult, [av, kar], [kmod])
            c.stt(kmod[:], kmod[:], 1.0, k_, ALU.add, ALU.mult, [kmod, zm_], [kmod])
            c.tt("pool", bv[:], kk[:], av[:], ALU.mult, [kk, av], [bv])
            c.mm(B0[:], rwm[:, 0:128], sigw[:], True, True, [rwm, sigw], [B0])
            c.mm(B1[:], rwm[:, 128:256], sigw[:], True, True, [rwm, sigw], [B1])
            c.mm(B2[:], rwm[:, 256:384], sigw[:], True, True, [rwm, sigw], [B2])
            c.act(ee[:, 0, :], B1[:], AF.Exp, [B1], [ee], scale=-LAM)
            c.act(ee[:, 1, :], B0[:], AF.Exp, [B0], [ee], scale=-LAM)
            c.act(ee[:, 2, :], B0[:], AF.Exp, [B0], [ee], scale=LAM)
            c.act(ee[:, 3, :], B2[:], AF.Exp, [B2], [ee], scale=-LAM)
            c.stt(TM[:, 0, :], kk[:], -1.0, ee[:, 0, :], ALU.mult, ALU.mult, [kk, ee], [TM])
            c.tt("pool", TM[:, 1, :], r_, ee[:, 1, :], ALU.mult, [zm_, ee], [TM])
            c.tt("dve", TM[:, 2, :], bv[:], ee[:, 2, :], ALU.mult, [bv, ee], [TM])
            c.tt("pool", TM[:, 3, :], kmod[:], ee[:, 2, :], ALU.mult, [kmod, ee], [TM])
            c.tt("dve", BH[:], bv[:], ee[:, 3, :], ALU.mult, [bv, ee], [BH])
            c.tt("pool", KH[:], kmod[:], ee[:, 3, :], ALU.mult, [kmod, ee], [KH])
            for p in range(4):
                c.mm(B3[:, 2 * p:2 * p + 2], sigw[:, p * 128:(p + 1) * 128], ones[:, 0:2], True, True, [sigw, ones], [B3])
            c.act(PCc[:], B3[:, 0:8].rearrange("p (a b) -> p a b", b=2)[:, :, 0], AF.Exp, [B3], [PCc], scale=-LAM)
            STc, STn = STs[ti % 2], STs[(ti + 1) % 2]
            for p in range(4):
                pc = slice(p * 128, (p + 1) * 128)
                ft = FT[p]
                for q in range(4):
                    c.tr(B3[:, q * 128:(q + 1) * 128], TM[:, q, pc], ident[:], [TM, ident], [B3])
                c.cp("act", ft[:].rearrange("p a b -> p (a b)"), B3[:], [B3], [ft])
                for hl in range(2):
                    h = 2 * p + hl
                    P = slice(hl * 64, (hl + 1) * 64)
                    hc = slice(h * 64, (h + 1) * 64)
                    vh = zm_[:, 1024 + h * 64:1024 + (h + 1) * 64]
                    gm = GM[h % 2]
                    c.mm(B4[:, 0:256], ft[P, 2, :], ft[P, 0:2, :], True, True, [ft], [B4])
                    c.mm(B4[:, 256:512], ft[P, 3, :], ft[P, 0:2, :], True, True, [ft], [B4])
                    c.tt("dve", gm[:], B4[:], maskg[:], ALU.mult, [B4, maskg], [gm])
                    LabT, MrbT, LakT, MrkT = gm[:, 0:128], gm[:, 128:256], gm[:, 256:384], gm[:, 384:512]
                    lc = Lc[0]
                    c.tr(pL[:, 384:512], LabT, ident[:], [gm, ident], [pL])
                    c.cp("act", lc[:, 0, :], pL[:, 384:512], [pL], [lc])
                    c.cp("pool", lc[:, 1, :], LabT, [gm], [lc])
                    wc = Wc[0]
                    c.mm(B5[:, 0:64], LakT, vh, True, True, [gm, zm_], [B5])
                    c.cp("act", wc[:, 64:128], B5[:, 0:64], [B5], [wc])
                    c.cp("pool", wc[:, 0:64], TM[:, 0, hc], [TM], [wc])
                    for j in range(7):
                        lcn = Lc[(j + 1) % 4]
                        wcn = Wc[(j + 1) % 4]
                        c.mm(B5[:, 0:128], lc[:, 1, :], wc[:], True, True, [lc, wc], [B5])
                        if j < 6:
                            c.mm(B5[:, 128:256], lc[:, 1, :], lc[:, 0, :], True, True, [lc], [B5])
                            c.mm(B5[:, 256:384], lc[:, 0, :], lc[:, 1, :], True, True, [lc], [B5])
                            c.cp("dve", lcn[:].rearrange("p a b -> p (a b)"), B5[:, 128:384], [B5], [lcn])
                        c.tt("dve", wcn[:], B5[:, 0:128], wc[:], ALU.add, [B5, wc], [wcn])
                        lc, wc = lcn, wcn
                    c.mm(pY[:, 0:128], MrbT, wc[:], True, False, [gm, wc], [pY])
                    c.mm(pY[:, 64:128], MrkT, vh, False, True, [gm, zm_], [pY])
                    c.tt("dve", Rp[:, hc], TM[:, 1, hc], pY[:, 0:64], ALU.add, [TM, pY], [Rp])
                    c.cp("act", Yl[:, hc], pY[:, 64:128], [pY], [Yl])
                    gcol = slice(128 + hl * 64, 128 + (hl + 1) * 64)
                    c.mm(pGs[P, gcol], wc[:, 0:64], BH[:, hc], True, True, [wc, BH], [pGs])
                    c.mm(pH[P, 256:320], BH[:, hc], wc[:, 64:128], True, False, [BH, wc], [pH])
                    c.mm(pH[P, 256:320], KH[:, hc], vh, False, True, [KH, zm_], [pH])
                    c.stt(Gbd[p][P, hl * 64:(hl + 1) * 64], ident[P, hl * 64:(hl + 1) * 64], PCc[P, p:p + 1], pGs[P, gcol],
                          ALU.mult, ALU.add, [ident, PCc, pGs], [Gbd[p]])
                    c.cp("act", Hs[P, p, :], pH[P, 256:320], [pH], [Hs])
                rt = RT[p % 2]
                c.tr(pRT[:, 384:512], Rp[:, pc], ident[:], [Rp, ident], [pRT])
                c.cp("act", rt[:], pRT[:, 384:512], [pRT], [rt])
                for hl in range(2):
                    h = 2 * p + hl
                    P = slice(hl * 64, (hl + 1) * 64)
                    yb = B7 if hl == 0 else B1
                    c.mm(yb[:, h * 64:(h + 1) * 64], rt[P, :], STc[P, p, :], True, True, [rt, STc], [yb])
                c.mm(pS[:, 320:384], Gbd[p][:], STc[:, p, :], True, True, [Gbd[p], STc], [pS])
                c.tt("dve", STn[:, p, :], pS[:, 320:384], Hs[:, p, :], ALU.add, [pS, Hs], [STn])
            for hl in range(2):
                yb = B7 if hl == 0 else B1
                c.tt("dve", v3(yv[:])[:, hl::2, :], v3(yb[:])[:, hl::2, :], v3(Yl[:])[:, hl::2, :], ALU.add, [yb, Yl], [yv])
            c.red(s8[1][:], v3(yv[:]), ALU.add, [yv], [s8[1]])
            c.ts("dve", s8[1][:], s8[1][:], 1.0 / 64, ALU.mult, [s8[1]], [s8[1]])
            c.tt("dve", v3(yc[:]), v3(yv[:]), bc3(s8[1][:]), ALU.subtract, [yv, s8[1]], [yc])
            c.tt("pool", sq2[:], yc[:], yc[:], ALU.mult, [yc], [sq2])
            c.red(s8[2][:], v3(sq2[:]), ALU.add, [sq2], [s8[2]])
            c.act(s8[2][:], s8[2][:], AF.Sqrt, [s8[2], epsg], [s8[2]], bias=epsg[:, 0:1], scale=1.0 / 64)
            c.recip(s8[2][:], s8[2][:], [s8[2]], [s8[2]])
            c.tt("dve", v3(yc[:]), v3(yc[:]), bc3(s8[2][:]), ALU.mult, [yc, s8[2]], [yc])
            c.tt("pool", yc[:], yc[:], lngr[:], ALU.mult, [yc, lngr], [yc])
            c.tt("pool", yc[:], yc[:], lnbr[:], ALU.add, [yc, lnbr], [yc])
            c.tt("pool", sq2[:], r_, kmod[:], ALU.mult, [zm_, kmod], [sq2])
            c.tt("pool", sq2[:], sq2[:], rkr[:], ALU.mult, [sq2, rkr], [sq2])
            c.red(s8[3][:], v3(sq2[:]), ALU.add, [sq2], [s8[3]])
            c.tt("dve", v3(bon[:]), v3(v_), bc3(s8[3][:]), ALU.mult, [zm_, s8[3]], [bon])
            c.tt("dve", yo[:], yc[:], bon[:], ALU.add, [yc, bon], [yo])
            c.tt("dve", yo[:], yo[:], gv[:], ALU.mult, [yo, gv], [yo])
            for q in range(4):
                c.tr(B4[:, q * 128:(q + 1) * 128], yo[:, q * 128:(q + 1) * 128], ident[:], [yo, ident], [B4])
            c.cp("act", obT[i][:].rearrange("p a b -> p (a b)"), B4[:], [B4], [obT[i]])
            c.dma(sc["brT"][2].rearrange("(h v) t -> v h t", v=128)[:, :, t0:t0 + 128], obT[i][:], reads=[obT[i]])
        m.sched.emit()


def build(S, debug=False, nlayers=L, ph="ABCDEFGH"):
    m = Model(S, debug=debug)
    xa = m.scratch("xa", [S, D], F32)
    xb = m.scratch("xb", [S, D], F32)
    xc = m.scratch("xc", [S, D], F32)
    phase_prep(m)
    xin = m.din["x"]
    for l in range(nlayers):
        if "hg" not in m.scr:
            m.scratch("hg", [S, 2048], F32)
            m.scratch("dqT", [512, S], BF16)
            m.scratch("dkT", [512, S], BF16)
            m.scratch("dvv", [S, 512], BF16)
            m.scratch("rwz", [S, 1536], F32)
            m.scratch("rwcT", [256, S], F32)
            m.scratch("gateT", [3072, S], BF16)
        if "A" in ph:
            phase_A(m, l, xin)
        if "brT" not in m.scr:
            m.scratch("brT", [3, 512, S], BF16)
        if "B" in ph:
            phase_B(m, l)
        if "C" in ph:
            phase_C(m, l)
        if "D" in ph:
            phase_D(m, l)
        if "E" in ph:
            phase_E(m, l, xin, xa)
        if "F" in ph:
            phase_F(m, l, xa, xb)
        if "G" in ph:
            phase_G(m, l, xb, xc)
        xin = xc
    if "H" in ph:
        phase_H(m, xin)
    return m


_CACHE = {}


def kernel(**inputs):
    S = inputs["x"].shape[1]
    B = inputs["x"].shape[0]
    if S not in _CACHE:
        _CACHE[S] = build(S)
    m = _CACHE[S]
    consts = host_consts()
    base = {n: np.ascontiguousarray(np.asarray(inputs[n], np.float32)) for n, _ in PARAMS}
    for n, v in consts.items():
        base["c_" + n] = v
    in_maps = []
    for b in range(B):
        d = dict(base)
        d["x"] = np.ascontiguousarray(np.asarray(inputs["x"][b], np.float32))
        d["mem"] = np.ascontiguousarray(np.asarray(inputs["mem"][b], np.float32))
        in_maps.append(d)
    res = run_bass_kernel_spmd(m.nc, in_maps, core_ids=list(range(B)))
    return np.stack([np.asarray(r["out"], np.float32) for r in res.results], axis=0)
```

```python
import contextlib
import math
import numpy as np
import ml_dtypes
import concourse.bass as bass
import concourse.mybir as mybir
from concourse.bass_utils import run_bass_kernel_spmd

F32 = mybir.dt.float32
BF16 = mybir.dt.bfloat16
AF = mybir.ActivationFunctionType
ALU = mybir.AluOpType
AX = mybir.AxisListType

D = 1024
L = 2
NIN = 8448
DFF = 2816
MEM = 256
ENGS = ("pe", "act", "dve", "pool", "sp")
NSLOT = 6
LAM = math.exp(-0.5)
PE_DRAIN = False


class Buf:
    __slots__ = ("name", "last_w", "readers", "excl")

    def __init__(self, name=""):
        self.name = name
        self.last_w = None
        self.readers = []
        self.excl = False


class T:
    def __init__(self, t, name=""):
        self.t = t
        self.b = Buf(name)

    def __getitem__(self, idx):
        return self.t[idx]


def _b(x):
    return x.b if isinstance(x, T) else x


class Sched:
    def __init__(self, nc):
        self.nc = nc
        self.q = {e: [] for e in ENGS}
        self.cnt = {}
        self.seen = {e: {} for e in ENGS}
        self.semkeys = []
        for e in ENGS:
            self._mk(e)
        self.dma_n = {e: 0 for e in ENGS}
        for e in ("sp", "act", "pool"):
            for i in range(NSLOT):
                self._mk(("dma", e, i))
        self.n_instr = 0
        self.nblk = 0

    def _mk(self, k):
        self.cnt[k] = 0
        self.semkeys.append(k)

    def _need(self, e, deps):
        best = {}
        for d in deps:
            if d is None:
                continue
            k, v = d
            if k == e and e == "pe":
                continue
            if self.seen[e].get(k, 0) >= v:
                continue
            if best.get(k, 0) < v:
                best[k] = v
        for k, v in best.items():
            self.seen[e][k] = v
            self.q[e].append(("wait", k, v))

    def _deps(self, reads, writes):
        deps = []
        for r in reads:
            r = _b(r)
            deps.append(r.last_w)
            if r.excl:
                deps.extend(r.readers)
        for w in writes:
            w = _b(w)
            deps.append(w.last_w)
            deps.extend(w.readers)
        return deps

    MAXOPS = None

    def op(self, e, fn, reads=(), writes=()):
        if Sched.MAXOPS is not None and self.n_instr >= Sched.MAXOPS:
            return
        self._need(e, self._deps(reads, writes))
        self.cnt[e] += 1
        v = self.cnt[e]
        self.q[e].append(("op", fn, e))
        for w in writes:
            w = _b(w)
            w.last_w = (e, v)
            w.readers = []
        for r in reads:
            _b(r).readers.append((e, v))
        self.n_instr += 1

    def dma(self, e, out, in_, reads=(), writes=()):
        if Sched.MAXOPS is not None and self.n_instr >= Sched.MAXOPS:
            return
        n = self.dma_n[e]
        self.dma_n[e] += 1
        slot = ("dma", e, n % NSLOT)
        deps = self._deps(reads, writes)
        if self.cnt[slot] > 0:
            deps.append((slot, self.cnt[slot]))
        self._need(e, deps)
        self.cnt[slot] += 16
        v = self.cnt[slot]
        self.q[e].append(("dma", out, in_, slot))
        for w in writes:
            w = _b(w)
            w.last_w = (slot, v)
            w.readers = []
        for r in reads:
            _b(r).readers.append((slot, v))
        self.n_instr += 1

    def drain(self):
        deps = [(k, v) for k, v in self.cnt.items() if isinstance(k, tuple) and v > 0]
        self._need("sp", deps)

    def emit(self):
        nc = self.nc
        self.drain()
        sems = {}
        for k in self.semkeys:
            nm = "s_" + "_".join(str(x) for x in (k if isinstance(k, tuple) else (k,))) + f"_{self.nblk}"
            sems[k] = nc.alloc_semaphore(name=nm)
        self.nblk += 1
        if self.nblk == 1:
            nc.clear_and_free_semaphores(list(sems.values()))
            nc.all_engine_barrier()
            for k in self.semkeys:
                nm = "s0_" + "_".join(str(x) for x in (k if isinstance(k, tuple) else (k,)))
                sems[k] = nc.alloc_semaphore(name=nm)
        with contextlib.ExitStack() as st:
            st.enter_context(nc.allow_non_contiguous_dma(reason="small strided parameter loads"))
            block = st.enter_context(nc.Block())

            def run(eh, items):
                for it in items:
                    if it[0] == "wait":
                        eh.wait_ge(sems[it[1]], it[2])
                    elif it[0] == "raw":
                        it[1](eh)
                    elif it[0] == "op":
                        it[1](eh).then_inc(sems[it[2]], 1)
                    else:
                        eh.dma_start(out=it[1], in_=it[2]).then_inc(sems[it[3]], 16)

            @block.tensor
            def _(e):
                run(e, self.q["pe"])

            @block.scalar
            def _(e):
                run(e, self.q["act"])

            @block.vector
            def _(e):
                run(e, self.q["dve"])

            @block.gpsimd
            def _(e):
                run(e, self.q["pool"])

            @block.sync
            def _(e):
                run(e, self.q["sp"])
        nc.clear_and_free_semaphores(list(sems.values()))
        nc.all_engine_barrier()
        for k in self.cnt:
            self.cnt[k] = 0
        self.seen = {e: {} for e in ENGS}
        self.q = {e: [] for e in ENGS}


class Ctx:
    def __init__(self, nc, S_, st):
        self.nc = nc
        self.S = S_
        self.st = st
        self.rr = 0

    _uid = [0]

    def sb(self, name, shape, dt=F32):
        Ctx._uid[0] += 1
        name = f"t{Ctx._uid[0]}_{name}"
        return T(self.st.enter_context(self.nc.sbuf_tensor(name, list(shape), dt)), name)

    def ps(self, name, shape, dt=F32):
        Ctx._uid[0] += 1
        name = f"p{Ctx._uid[0]}_{name}"
        t = T(self.st.enter_context(self.nc.psum_tensor(name, list(shape), dt)), name)
        t.b.excl = True
        return t

    def sbn(self, name, shape, dt=F32, n=2):
        return [self.sb(f"{name}{i}", shape, dt) for i in range(n)]

    def psn(self, name, shape, dt=F32, n=2):
        return [self.ps(f"{name}{i}", shape, dt) for i in range(n)]

    _mode = [None]

    def _pe_mode(self, lhsT):
        def rnd(n):
            return 32 if n <= 32 else (64 if n <= 64 else 128)
        shp = lhsT.shape
        k = shp[0]
        mfree = 1
        for d in shp[1:]:
            mfree *= d
        mode = (rnd(k), rnd(mfree))
        if PE_DRAIN and Ctx._mode[0] is not None and Ctx._mode[0] != mode:
            if not (Sched.MAXOPS is not None and self.S.n_instr >= Sched.MAXOPS):
                self.S.q["pe"].append(("raw", lambda e: e.drain()))
        Ctx._mode[0] = mode

    def mm(self, out, lhsT, rhs, start, stop, reads, writes):
        self._pe_mode(lhsT)
        self.S.op("pe", lambda e: e.matmul(out, lhsT=lhsT, rhs=rhs, start=start, stop=stop), reads, writes)

    def tr(self, out, in_, ident, reads, writes):
        self._pe_mode(in_)
        self.S.op("pe", lambda e: e.transpose(out=out, in_=in_, identity=ident), reads, writes)

    def act(self, out, in_, func, reads, writes, bias=None, scale=None, accum=None, eng="act"):
        kw = {}
        if bias is not None:
            kw["bias"] = bias
        if scale is not None:
            kw["scale"] = scale
        if accum is not None:
            kw["accum_out"] = accum
        self.S.op("act", lambda e: e.activation(out=out, in_=in_, func=func, **kw), reads, writes)

    def tt(self, eng, out, in0, in1, op, reads, writes):
        self.S.op(eng, lambda e: e.tensor_tensor(out=out, in0=in0, in1=in1, op=op), reads, writes)

    def ts(self, eng, out, in0, s1, op0, reads, writes, s2=None, op1=None):
        if op1 is None:
            self.S.op(eng, lambda e: e.tensor_scalar(out=out, in0=in0, scalar1=s1, scalar2=None, op0=op0), reads, writes)
        else:
            self.S.op(eng, lambda e: e.tensor_scalar(out=out, in0=in0, scalar1=s1, scalar2=s2, op0=op0, op1=op1), reads, writes)

    def stt(self, out, in0, scalar, in1, op0, op1, reads, writes):
        self.S.op("dve", lambda e: e.scalar_tensor_tensor(out=out, in0=in0, scalar=scalar, in1=in1, op0=op0, op1=op1), reads, writes)

    def cp(self, eng, out, in_, reads, writes):
        if eng == "act":
            self.S.op("act", lambda e: e.activation(out=out, in_=in_, func=AF.Copy), reads, writes)
        else:
            self.S.op(eng, lambda e: e.tensor_copy(out=out, in_=in_), reads, writes)

    def red(self, out, in_, op, reads, writes):
        self.S.op("dve", lambda e: e.tensor_reduce(out=out, in_=in_, axis=AX.X, op=op), reads, writes)

    def recip(self, out, in_, reads, writes):
        self.S.op("dve", lambda e: e.reciprocal(out=out, in_=in_), reads, writes)

    def memset(self, eng, ap, val, writes):
        self.S.op(eng, lambda e: e.memset(ap, val), (), writes)

    def dma(self, out, in_, reads=(), writes=(), q=None):
        if q is None:
            q = ("sp", "act", "pool")[self.rr % 3]
            self.rr += 1
        self.S.dma(q, out, in_, reads, writes)


def host_consts():
    c = {}
    idx = np.arange(128)
    s = idx[:, None]
    t = idx[None, :]
    same = (s // 64) == (t // 64)
    c["ident"] = np.eye(128, dtype=np.float32)
    c["ones"] = np.ones((128, 128), np.float32)
    tri64 = ((s <= t) & same).astype(np.float32)
    mid64 = ((s <= (t // 64) * 64 + 31) & same).astype(np.float32)
    blk64 = same.astype(np.float32)
    c["hgm"] = np.concatenate([tri64, tri64 - mid64, blk64 - tri64], axis=1)
    incl = (s <= t).astype(np.float32)
    strict = (s < t).astype(np.float32)
    rev = (s > t).astype(np.float32)
    c["rwm"] = np.concatenate([incl, strict, rev], axis=1)
    c["maskg"] = np.concatenate([strict, incl, strict, incl], axis=1)
    scale = 64 ** -0.5
    slopes = 2.0 ** (-8.0 * np.arange(1, 5) / 4)
    ki = np.arange(128)[:, None].astype(np.float64)
    qi = np.arange(512)[None, :].astype(np.float64)
    al = np.zeros((4, 5, 128, 512), np.float32)
    for h in range(4):
        al[h, 0] = (-slopes[h] * (qi - ki) / scale)
        for d in range(4):
            dist = qi - ki - 128 * d
            al[h, 1 + d] = np.where(dist >= 0, -slopes[h] * dist / scale, -1e30)
    c["alibi"] = al.transpose(2, 0, 1, 3).reshape(128, 4 * 5 * 512).copy()
    ct = np.zeros((128, 4 * 65), np.float32)
    for h in range(4):
        ct[:, h * 65:(h + 1) * 65] = (-slopes[h] * 128.0 * np.arange(65))[None, :]
    c["ctab"] = ct
    return c


CONST_SHAPES = {"ident": (128, 128), "ones": (128, 128), "hgm": (128, 384), "rwm": (128, 384),
                "maskg": (128, 512), "alibi": (128, 4 * 5 * 512), "ctab": (128, 4 * 65)}

PARAMS = [
    ("norm_mix_g", (L, D)), ("w_in", (L, D, NIN)), ("b_gate", (L, 3072)), ("hgrn_lb_param", (L, 512)),
    ("hgrn_norm_g", (L, 512)), ("diff_lambda", (L, 4, 64)), ("diff_subln_g", (L, 128)),
    ("rwkv_mu", (L, 1792)), ("rwkv_w0", (L, 512)), ("rwkv_w_up", (L, 64, 512)), ("rwkv_a0", (L, 512)),
    ("rwkv_a_up", (L, 64, 512)), ("rwkv_g_up", (L, 128, 512)), ("rwkv_k_k", (L, 512)), ("rwkv_k_a", (L, 512)),
    ("rwkv_r_k", (L, 8, 64)), ("rwkv_ln_g", (L, 512)), ("rwkv_ln_b", (L, 512)),
    ("w_branch", (L, 3, 512, D)), ("w_out", (L, D, D)), ("norm_xa_g", (L, D)), ("norm_mem_g", (L, D)),
    ("xa_wq", (L, D, D)), ("xa_wkv", (L, D, 2 * D)), ("xa_wo", (L, D, D)), ("norm_ffn_g", (L, D)),
    ("ffn_w_up", (L, D, 2 * DFF)), ("ffn_conv_w", (L, 3, DFF)), ("ffn_conv_b", (L, DFF)),
    ("ffn_w_down", (L, DFF, D)), ("final_norm_g", (D,)),
]


class Model:
    def __init__(self, S, debug=False, phases=None):
        self.S = S
        self.debug = debug
        self.phases = phases
        nc = bass.Bass("TRN2", target_bir_lowering=False)
        self.nc = nc
        self.din = {}
        self.din["x"] = nc.dram_tensor("x", [S, D], F32, kind="ExternalInput").ap()
        self.din["mem"] = nc.dram_tensor("mem", [MEM, D], F32, kind="ExternalInput").ap()
        for n, shp in PARAMS:
            self.din[n] = nc.dram_tensor(n, list(shp), F32, kind="ExternalInput").ap()
        for n, shp in CONST_SHAPES.items():
            self.din["c_" + n] = nc.dram_tensor("c_" + n, list(shp), F32, kind="ExternalInput").ap()
        self.out = nc.dram_tensor("out", [S, D], F32, kind="ExternalOutput").ap()
        self.scr = {}
        self.sched = Sched(nc)

    def scratch(self, name, shape, dt):
        kind = "ExternalOutput" if self.debug else "Internal"
        self.scr[name] = self.nc.dram_tensor(name, list(shape), dt, kind=kind).ap()
        return self.scr[name]

    def ctx(self, st):
        return Ctx(self.nc, self.sched, st)


def phase_prep(m):
    nc, S_ = m.nc, m.sched
    specs = [("w_in", "norm_mix_g", D, NIN), ("w_out", None, D, D), ("xa_wq", "norm_xa_g", D, D),
             ("xa_wkv", "norm_mem_g", D, 2 * D), ("xa_wo", None, D, D), ("ffn_w_up", "norm_ffn_g", D, 2 * DFF),
             ("ffn_w_down", None, DFF, D), ("w_branch", None, 1536, D)]
    for name, g, K, N in specs:
        m.scratch("b_" + name, [L, K, N], BF16)
    with contextlib.ExitStack() as st:
        c = m.ctx(st)
        gt = c.sb("gt", [128, 4, L, 8])
        gi = 0
        gmap = {}
        for name, g, K, N in specs:
            if g is not None:
                gmap[g] = gi
                for l in range(L):
                    c.dma(gt[:, gi, l, :], m.din[g][l].rearrange("(c p) -> p c", p=128), writes=[gt])
                gi += 1
        W = 2048
        ins = c.sbn("pin", [128, W], F32, 3)
        outs = c.sbn("pout", [128, W], BF16, 3)
        k = 0
        for name, g, K, N in specs:
            for l in range(L):
                src = m.din[name][l]
                if name == "w_branch":
                    src = src.rearrange("a k n -> (a k) n")
                dst = m.scr["b_" + name][l]
                for kc in range(K // 128):
                    for n0 in range(0, N, W):
                        w = min(W, N - n0)
                        ti, to = ins[k % 3], outs[k % 3]
                        c.dma(ti[:, 0:w], src[kc * 128:(kc + 1) * 128, n0:n0 + w], writes=[ti])
                        eng = ("dve", "pool")[k % 2]
                        if g is not None:
                            c.ts(eng, to[:, 0:w], ti[:, 0:w], gt[:, gmap[g], l, kc:kc + 1], ALU.mult, [ti, gt], [to])
                        else:
                            c.cp(eng, to[:, 0:w], ti[:, 0:w], [ti], [to])
                        c.dma(dst[kc * 128:(kc + 1) * 128, n0:n0 + w], to[:, 0:w], reads=[to])
                        k += 1
        S_.emit()


def load_consts(m, c, names):
    r = {}
    for n in names:
        shp = CONST_SHAPES[n]
        t = c.sb("c_" + n, shp, F32)
        c.dma(t[:], m.din["c_" + n][:, :], writes=[t])
        r[n] = t
    return r


def make_bf(c, src, shape, name):
    t = c.sb(name, shape, BF16)
    c.cp("dve", t[:], src[:], [src], [t])
    return t


def norm_T(c, xsrc, hT, col0, xt, xb, junk, ss, rstd, eps, pT, identb):
    c.dma(xt[:], xsrc, writes=[xt])
    c.act(junk[:], xt[:], AF.Square, [xt], [junk, ss], accum=ss[:, 0:1])
    c.act(rstd[:, 0:1], ss[:, 0:1], AF.Sqrt, [ss, eps], [rstd], bias=eps[:, 0:1], scale=1.0 / D)
    c.recip(rstd[:, 0:1], rstd[:, 0:1], [rstd], [rstd])
    c.ts("dve", xb[:], xt[:], rstd[:, 0:1], ALU.mult, [xt, rstd], [xb])
    for kc in range(8):
        c.tr(pT[:, kc, :], xb[:, kc * 128:(kc + 1) * 128], identb[:], [xb, identb], [pT])
    c.cp("pool" if False else "act", hT[:, :, col0:col0 + 128], pT[:], [pT], [hT])


def phase_A(m, l, xsrc):
    S = m.S
    TG = min(S, 2048)
    TB = min(512, TG)
    sc = m.scr
    if "hg" not in sc:
        m.scratch("hg", [S, 2048], F32)
        m.scratch("dqT", [512, S], BF16)
        m.scratch("dkT", [512, S], BF16)
        m.scratch("dvv", [S, 512], BF16)
        m.scratch("rwz", [S, 1536], F32)
        m.scratch("rwcT", [256, S], F32)
        m.scratch("gateT", [3072, S], BF16)
    blocks = [
        (0, 512, "tok", "hg", 0, AF.Silu, F32), (512, 512, "tok", "hg", 512, AF.Sigmoid, F32),
        (1024, 512, "tok", "hg", 1024, AF.Copy, F32), (1536, 512, "tok", "hg", 1536, AF.Silu, F32),
        (2048, 512, "feat", "dqT", 0, AF.Copy, BF16), (2560, 512, "feat", "dkT", 0, AF.Copy, BF16),
        (3072, 512, "tok", "dvv", 0, AF.Copy, BF16),
        (3584, 512, "tok", "rwz", 0, AF.Copy, F32), (4096, 512, "tok", "rwz", 512, AF.Copy, F32),
        (4608, 512, "tok", "rwz", 1024, AF.Copy, F32), (5120, 256, "feat", "rwcT", 0, AF.Copy, F32),
    ] + [(5376 + i * 512, 512, "gate", "gateT", i * 512, AF.Sigmoid, BF16) for i in range(6)]
    with contextlib.ExitStack() as st:
        c = m.ctx(st)
        cs = load_consts(m, c, ["ident"])
        identb = make_bf(c, cs["ident"], [128, 128], "identb")
        eps = c.sb("eps", [128, 1])
        c.memset("dve", eps[:], 1e-6, [eps])
        bg = c.sb("bg", [128, 24])
        c.dma(bg[:], m.din["b_gate"][l].rearrange("(c p) -> p c", p=128), writes=[bg])
        hT = c.sb("hT", [128, 8, TG], BF16)
        xts = c.sbn("xt", [128, D], F32, 2)
        xbs = c.sbn("xb", [128, D], BF16, 2)
        junk = c.sb("junk", [128, D], F32)
        sss = c.sbn("ss", [128, 1], F32, 2)
        rstds = c.sbn("rstd", [128, 1], F32, 2)
        pTs = c.psn("pT", [128, 8, 128], BF16, 2)
        wbs = c.sbn("wb", [128, 8, 512], BF16, 2)
        pos = c.psn("po", [128, 512], F32, 4)
        o32 = c.sbn("o32", [128, 512], F32, 3)
        o16 = c.sbn("o16", [128, 512], BF16, 3)
        wsrc = sc["b_w_in"][l].rearrange("(c p) n -> p c n", p=128)
        it = 0
        for g0 in range(0, S, TG):
            for tt in range(TG // 128):
                i = tt % 2
                norm_T(c, xsrc[g0 + tt * 128:g0 + (tt + 1) * 128, :], hT, tt * 128, xts[i], xbs[i], junk, sss[i],
                       rstds[i], eps, pTs[i], identb)
            for bi, (c0, ncol, kind, dname, doff, func, dt) in enumerate(blocks):
                wb = wbs[bi % 2]
                c.dma(wb[:, :, 0:ncol], wsrc[:, :, c0:c0 + ncol], writes=[wb], q="sp")
                dest = sc[dname]
                if kind == "tok":
                    for tt in range(TG // 128):
                        po = pos[it % 4]
                        ot = (o32 if dt == F32 else o16)[it % 3]
                        it += 1
                        for kc in range(8):
                            c.mm(po[:, 0:ncol], hT[:, kc, tt * 128:(tt + 1) * 128], wb[:, kc, 0:ncol], kc == 0, kc == 7,
                                 [hT, wb], [po])
                        c.act(ot[:, 0:ncol], po[:, 0:ncol], func, [po], [ot])
                        c.dma(dest[g0 + tt * 128:g0 + (tt + 1) * 128, doff:doff + ncol], ot[:, 0:ncol], reads=[ot])
                else:
                    for fc in range(ncol // 128):
                        for tb in range(TG // TB):
                            po = pos[it % 4]
                            ot = (o32 if dt == F32 else o16)[it % 3]
                            it += 1
                            for kc in range(8):
                                c.mm(po[:, 0:TB], wb[:, kc, fc * 128:(fc + 1) * 128], hT[:, kc, tb * TB:(tb + 1) * TB],
                                     kc == 0, kc == 7, [hT, wb], [po])
                            if kind == "gate":
                                gc = (doff + fc * 128) // 128
                                c.act(ot[:, 0:TB], po[:, 0:TB], func, [po, bg], [ot], bias=bg[:, gc:gc + 1])
                            else:
                                c.act(ot[:, 0:TB], po[:, 0:TB], func, [po], [ot])
                            r0 = doff + fc * 128
                            c.dma(dest[r0:r0 + 128, g0 + tb * TB:g0 + (tb + 1) * TB], ot[:, 0:TB], reads=[ot])
        m.sched.emit()


def rowb(m, c, name, src1d, F):
    t = c.sb(name, [128, F], F32)
    c.dma(t[:], src1d.partition_broadcast(128), writes=[t])
    return t


def sub(parent):
    v = T(parent.t, parent.b.name + "_v")
    return v


def bc3(ap2, n=64):
    H = ap2.shape[1]
    return ap2.unsqueeze(2).to_broadcast([128, H, n])


def v3(ap2, n=64):
    return ap2.rearrange("p (h n) -> p h n", n=n)


def phase_B(m, l):
    S = m.S
    sc = m.scr
    if "brT" not in sc:
        m.scratch("brT", [3, 512, S], BF16)
    with contextlib.ExitStack() as st:
        c = m.ctx(st)
        cs = load_consts(m, c, ["ident", "hgm", "ones"])
        ident, hgm, ones = cs["ident"], cs["hgm"], cs["ones"]
        eps = c.sb("eps", [128, 1])
        c.memset("dve", eps[:], 1e-6, [eps])
        lbrow = c.sb("lbrow", [128, 512])
        omlb = c.sb("omlb", [128, 512])
        if l == 0:
            c.memset("dve", lbrow[:], 0.0, [lbrow])
        else:
            a0 = rowb(m, c, "lba0", m.din["hgrn_lb_param"][0], 512)
            a1 = rowb(m, c, "lba1", m.din["hgrn_lb_param"][1], 512)
            c.tt("dve", a1[:], a1[:], a0[:], ALU.subtract, [a0, a1], [a1])
            c.act(lbrow[:], a1[:], AF.Sigmoid, [a1], [lbrow])
        c.ts("dve", omlb[:], lbrow[:], -1.0, ALU.mult, [lbrow], [omlb], s2=1.0, op1=ALU.add)
        ngrow = rowb(m, c, "ngrow", m.din["hgrn_norm_g"][l], 512)
        Sst = [c.sbn(f"Sst{h}_", [128, 128], F32, 2) for h in range(4)]
        for h in range(4):
            c.memset("pool", Sst[h][0][:], 0.0, [Sst[h][0]])
        hgt = c.sbn("hgt", [128, 2048], F32, 2)
        fv = c.sbn("fv", [128, 512], F32, 2)
        kf = c.sbn("kf", [128, 512], F32, 2)
        lf = c.sbn("lf", [128, 512], F32, 2)
        ee = c.sbn("ee", [128, 4, 512], F32, 2)
        qk = c.sbn("qk", [128, 4, 512], F32, 2)
        FT = c.sbn("FT", [128, 3, 128], F32, 2)
        AT = c.sbn("AT", [128, 128], F32, 2)
        dcol = c.sbn("dcol", [128, 2], F32, 4)
        ssq = c.sbn("ssq", [128, 4], F32, 2)
        rstd = c.sbn("rstdh", [128, 4], F32, 2)
        gn = c.sbn("gn", [128, 512], F32, 2)
        ob = c.sbn("ob", [128, 512], BF16, 2)
        obf = c.sbn("obf", [128, 512], F32, 2)
        obT = c.sbn("obT", [128, 4, 128], BF16, 2)
        junk = c.sb("junkb", [128, 128], F32)
        pcs = c.psn("pcs", [128, 512], F32, 3)
        pTr1 = c.ps("pTr", [128, 512], F32)
        pTr = [pTr1, pTr1]
        psc = c.ps("psc", [128, 512], F32)
        pkv = c.ps("pkv", [128, 512], F32)
        pcol = c.ps("pcol", [128, 512], F32)
        po = c.ps("pob", [128, 512], F32)
        pTb = pTr1
        cn = 0
        for ti in range(S // 128):
            t0 = ti * 128
            i = ti % 2
            hg = hgt[i]
            c.dma(hg[:], sc["hg"][t0:t0 + 128, :], writes=[hg])
            q, sg, vv, gate = hg[:, 0:512], hg[:, 512:1024], hg[:, 1024:1536], hg[:, 1536:2048]
            f = fv[i]
            c.tt("pool", f[:], sg, omlb[:], ALU.mult, [hg, omlb], [f])
            c.tt("pool", f[:], f[:], lbrow[:], ALU.add, [f, lbrow], [f])
            c.ts("pool", f[:], f[:], 1e-30, ALU.max, [f], [f])
            c.ts("dve", kf[i][:], f[:], -1.0, ALU.mult, [f], [kf[i]], s2=1.0, op1=ALU.add)
            c.act(lf[i][:], f[:], AF.Ln, [f], [lf[i]])
            for k in range(3):
                c.mm(pcs[k][:], hgm[:, k * 128:(k + 1) * 128], lf[i][:], True, True, [hgm, lf[i]], [pcs[k]])
            e = ee[i]
            c.act(e[:, 0, :], pcs[1][:], AF.Exp, [pcs[1]], [e])
            c.act(e[:, 1, :], pcs[1][:], AF.Exp, [pcs[1]], [e], scale=-1.0)
            c.act(e[:, 2, :], pcs[0][:], AF.Exp, [pcs[0]], [e])
            c.act(e[:, 3, :], pcs[2][:], AF.Exp, [pcs[2]], [e])
            w = qk[i]
            c.tt("dve", w[:, 0, :], q, e[:, 0, :], ALU.mult, [hg, e], [w])
            c.tt("pool", w[:, 1, :], kf[i][:], e[:, 1, :], ALU.mult, [kf[i], e], [w])
            c.tt("dve", w[:, 2, :], q, e[:, 2, :], ALU.mult, [hg, e], [w])
            c.tt("pool", w[:, 3, :], kf[i][:], e[:, 3, :], ALU.mult, [kf[i], e], [w])
            c.tt("pool", gn[i][:], gate, ngrow[:], ALU.mult, [hg, ngrow], [gn[i]])
            for h in range(4):
                hc = slice(h * 128, (h + 1) * 128)
                ft = FT[h % 2]
                ptr = pTr[h % 2]
                for k in range(3):
                    c.tr(ptr[:, k * 128:(k + 1) * 128], w[:, k, hc], ident[:], [w, ident], [ptr])
                c.cp("act", ft[:].rearrange("p a b -> p (a b)"), ptr[:, 0:384], [ptr], [ft])
                c.mm(psc[:, 0:128], ft[:, 1, :], ft[:, 0, :], True, True, [ft], [psc])
                at = AT[h % 2]
                c.tt("dve", at[:], psc[:, 0:128], hgm[:, 0:128], ALU.mult, [psc, hgm], [at])
                c.mm(po[:, hc], at[:], vv[:, hc] if False else hg[:, 1024 + h * 128:1024 + (h + 1) * 128], True, False, [at, hg], [po])
                for ch in range(2):
                    P = slice(ch * 64, (ch + 1) * 64)
                    Sc = Sst[h][(2 * ti + ch) % 2]
                    Sn = Sst[h][(2 * ti + ch + 1) % 2]
                    c.mm(po[P, hc], ft[:, 2, P], Sc[:], False, ch == 1, [ft, Sc], [po])
                    dc = dcol[cn % 4]
                    cn += 1
                    c.mm(pcol[:, 256:258], lf[i][P, hc], ones[P, 0:2], True, True, [lf[i], ones], [pcol])
                    c.act(dc[:], pcol[:, 256:258], AF.Exp, [pcol], [dc])
                    c.mm(pkv[:, 128:256], w[P, 3, hc], hg[P, 1024 + h * 128:1024 + (h + 1) * 128], True, True, [w, hg], [pkv])
                    c.stt(Sn[:], Sc[:], dc[:, 0:1], pkv[:, 128:256], ALU.mult, ALU.add, [Sc, dc, pkv], [Sn])
            for h in range(4):
                hc = slice(h * 128, (h + 1) * 128)
                c.act(junk[:], po[:, hc], AF.Square, [po], [junk, ssq[i]], accum=ssq[i][:, h:h + 1])
            c.act(rstd[i][:], ssq[i][:], AF.Sqrt, [ssq[i], eps], [rstd[i]], bias=eps[:, 0:1], scale=1.0 / 128)
            c.recip(rstd[i][:], rstd[i][:], [rstd[i]], [rstd[i]])
            for h in range(4):
                hc = slice(h * 128, (h + 1) * 128)
                c.stt(obf[i][:, hc], po[:, hc], rstd[i][:, h:h + 1], gn[i][:, hc], ALU.mult, ALU.mult, [po, rstd[i], gn[i]], [obf[i]])
            for h in range(4):
                hc = slice(h * 128, (h + 1) * 128)
                c.tr(pTb[:, hc], obf[i][:, hc], ident[:], [obf[i], ident], [pTb])
            c.cp("act", obT[i][:].rearrange("p a b -> p (a b)"), pTb[:], [pTb], [obT[i]])
            c.dma(sc["brT"][0].rearrange("(h v) t -> v h t", v=128)[:, :, t0:t0 + 128], obT[i][:], reads=[obT[i]])
        m.sched.emit()


def phase_C(m, l):
    S = m.S
    sc = m.scr
    QB = min(512, S)
    nq = QB // 128
    NQ = S // QB
    NJ = S // 128
    lambda_init = 0.8 - 0.6 * math.exp(-0.3 * l)
    with contextlib.ExitStack() as st:
        c = m.ctx(st)
        cs = load_consts(m, c, ["ones", "ctab"])
        ones, ctab = cs["ones"], cs["ctab"]
        onesb = make_bf(c, ones, [128, 128], "onesb")
        eps = c.sb("eps", [128, 1])
        c.memset("dve", eps[:], 1e-6, [eps])
        lamr = rowb(m, c, "lamr", m.din["diff_lambda"][l].rearrange("a b -> (a b)"), 256)
        ltmp = c.sb("ltmp", [128, 128])
        lsum = c.sb("lsum", [128, 2])
        c.tt("dve", ltmp[:, 0:64], lamr[:, 0:64], lamr[:, 64:128], ALU.mult, [lamr], [ltmp])
        c.tt("dve", ltmp[:, 64:128], lamr[:, 128:192], lamr[:, 192:256], ALU.mult, [lamr, ltmp], [ltmp])
        c.red(lsum[:], ltmp[:].rearrange("p (a b) -> p a b", b=64), ALU.add, [ltmp], [lsum])
        c.act(lsum[:], lsum[:], AF.Exp, [lsum], [lsum])
        nlam = c.sb("nlam", [128, 1])
        c.tt("dve", nlam[:], lsum[:, 1:2], lsum[:, 0:1], ALU.subtract, [lsum], [nlam])
        c.ts("dve", nlam[:], nlam[:], -lambda_init, ALU.add, [nlam], [nlam])
        gcol = c.sb("gcol", [128, 1])
        c.dma(gcol[:], m.din["diff_subln_g"][l].rearrange("(p o) -> p o", o=1), writes=[gcol])
        c.ts("dve", gcol[:], gcol[:], 1.0 - lambda_init, ALU.mult, [gcol], [gcol])
        qT = c.sb("qT", [128, S], BF16)
        kT = c.sb("kT", [128, S], BF16)
        V = c.sb("V", [128, NJ, 128], BF16)
        AL = c.sb("AL", [128, 5, 512], F32)
        TMP = c.sbn("tmpc", [128, 512], F32, 3)
        PT = c.sbn("ptc", [128, 512], BF16, 4)
        PST = c.psn("pst", [128, 512], F32, 4)
        PO = c.psn("poc", [128, 512], F32, 2)
        PL = c.psn("plc", [128, 512], F32, 2)
        rl = c.sbn("rlc", [128, 512], F32, 2)
        oc = c.sbn("occ", [128, 512], F32, 2)
        od = c.sb("odc", [128, 512], F32)
        sq = c.sb("sqc", [128, 512], F32)
        rs = c.sb("rsc", [128, 512], F32)
        obo = c.sbn("oboc", [128, 512], BF16, 2)
        cnt = 0
        for h in range(4):
            c.dma(qT[:], sc["dqT"][h * 128:(h + 1) * 128, :], writes=[qT], q="sp")
            c.dma(kT[:], sc["dkT"][h * 128:(h + 1) * 128, :], writes=[kT], q="act")
            vsrc = sc["dvv"].rearrange("(j p) c -> p j c", p=128)
            for j0 in range(0, NJ, 8):
                j1 = min(NJ, j0 + 8)
                c.dma(V[:, j0:j1, :], vsrc[:, j0:j1, h * 128:(h + 1) * 128], writes=[V], q=("pool", "sp", "act")[(j0 // 8) % 3])
            c.dma(AL[:].rearrange("p a b -> p (a b)"), m.din["c_alibi"][:, h * 2560:(h + 1) * 2560], writes=[AL], q="sp")
            for I in range(NQ):
                qs = slice(I * QB, (I + 1) * QB)
                jmax = nq * (I + 1) - 1
                units = [(j, c2) for j in range(jmax + 1) for c2 in range(2)]
                LA = 2
                pend = []

                def stage1(j, c2):
                    nonlocal cnt
                    d = j - nq * I
                    var = 0 if d < 0 else 1 + d
                    mc = (nq * I - j) if d < 0 else 0
                    P = slice(c2 * 64, (c2 + 1) * 64)
                    pst = PST[cnt % 4]
                    tmp = TMP[cnt % 3]
                    pt = PT[cnt % 4]
                    cnt += 1
                    c.mm(pst[:, 0:QB], kT[P, j * 128:(j + 1) * 128], qT[P, qs], True, True, [kT, qT], [pst])
                    c.tt("dve", tmp[:, 0:QB], pst[:, 0:QB], AL[:, var, 0:QB], ALU.add, [pst, AL], [tmp])
                    c.act(pt[:, 0:QB], tmp[:, 0:QB], AF.Exp, [tmp, ctab], [pt], bias=ctab[:, h * 65 + mc:h * 65 + mc + 1], scale=0.125)
                    return pt

                def stage2(j, c2, pt):
                    c.mm(PO[c2][:, 0:QB], V[:, j, :], pt[:, 0:QB], j == 0, j == jmax, [V, pt], [PO[c2]])
                    c.mm(PL[c2][:, 0:QB], onesb[:], pt[:, 0:QB], j == 0, j == jmax, [onesb, pt], [PL[c2]])

                for ui in range(len(units) + LA):
                    if ui < len(units):
                        j, c2 = units[ui]
                        pend.append((j, c2, stage1(j, c2)))
                    if ui >= LA:
                        stage2(*pend.pop(0))
                for c2 in range(2):
                    c.recip(rl[c2][:, 0:QB], PL[c2][:, 0:QB], [PL[c2]], [rl[c2]])
                    c.tt("dve", oc[c2][:, 0:QB], PO[c2][:, 0:QB], rl[c2][:, 0:QB], ALU.mult, [PO[c2], rl[c2]], [oc[c2]])
                c.stt(od[:, 0:QB], oc[1][:, 0:QB], nlam[:, 0:1], oc[0][:, 0:QB], ALU.mult, ALU.add, [oc[0], oc[1], nlam], [od])
                c.tt("pool", sq[:, 0:QB], od[:, 0:QB], od[:, 0:QB], ALU.mult, [od], [sq])
                pss = PST[cnt % 4]
                cnt += 1
                c.mm(pss[:, 0:QB], ones[:], sq[:, 0:QB], True, True, [ones, sq], [pss])
                c.act(rs[:, 0:QB], pss[:, 0:QB], AF.Sqrt, [pss, eps], [rs], bias=eps[:, 0:1], scale=1.0 / 128)
                c.recip(rs[:, 0:QB], rs[:, 0:QB], [rs], [rs])
                o_ = obo[I % 2]
                c.stt(o_[:, 0:QB], od[:, 0:QB], gcol[:, 0:1], rs[:, 0:QB], ALU.mult, ALU.mult, [od, gcol, rs], [o_])
                c.dma(sc["brT"][1][h * 128:(h + 1) * 128, qs], o_[:, 0:QB], reads=[o_])
        m.sched.emit()


def phase_E(m, l, xsrc, xdst):
    S = m.S
    sc = m.scr
    TB = min(512, S)
    with contextlib.ExitStack() as st:
        c = m.ctx(st)
        wbr = c.sb("wbr", [128, 12, D], BF16)
        wout = c.sb("wout", [128, 8, D], BF16)
        c.dma(wbr[:], sc["b_w_branch"][l].rearrange("(c p) n -> p c n", p=128), writes=[wbr], q="sp")
        c.dma(wout[:], sc["b_w_out"][l].rearrange("(c p) n -> p c n", p=128), writes=[wout], q="act")
        br = c.sbn("br", [128, 12, TB], BF16, 2)
        gt = c.sbn("gte", [128, 24, TB], BF16, 2)
        acc = c.sbn("acce", [128, TB], F32, 2)
        tmp = c.sbn("tmpe", [128, TB], F32, 3)
        mT = c.sbn("mT", [128, 8, TB], BF16, 2)
        xt = c.sbn("xte", [128, D], F32, 2)
        xo = c.sbn("xoe", [128, D], F32, 2)
        PM = c.psn("pme", [128, 512], F32, 4)
        PO = c.psn("poe", [128, 512], F32, 4)
        k = 0
        k2 = 0
        for tb in range(S // TB):
            ts_ = slice(tb * TB, (tb + 1) * TB)
            b_, g_, m_ = br[tb % 2], gt[tb % 2], mT[tb % 2]
            c.dma(b_[:], sc["brT"].rearrange("n (c p) t -> p (n c) t", p=128)[:, :, ts_], writes=[b_], q="sp")
            c.dma(g_[:], sc["gateT"].rearrange("(c p) t -> p c t", p=128)[:, :, ts_], writes=[g_], q="act")
            for dmc in range(8):
                a_ = acc[dmc % 2]
                for n in range(3):
                    pm = PM[k % 4]
                    for kc in range(4):
                        c.mm(pm[:, 0:TB], wbr[:, n * 4 + kc, dmc * 128:(dmc + 1) * 128], b_[:, n * 4 + kc, :], kc == 0, kc == 3, [wbr, b_], [pm])
                    if n == 0:
                        c.tt("dve", a_[:], pm[:, 0:TB], g_[:, n * 8 + dmc, :], ALU.mult, [pm, g_], [a_])
                    else:
                        t_ = tmp[k % 3]
                        c.tt("dve", t_[:], pm[:, 0:TB], g_[:, n * 8 + dmc, :], ALU.mult, [pm, g_], [t_])
                        if n == 1:
                            c.tt("pool", a_[:], a_[:], t_[:], ALU.add, [a_, t_], [a_])
                        else:
                            c.tt("pool", m_[:, dmc, :], a_[:], t_[:], ALU.add, [a_, t_], [m_])
                    k += 1
            for tt in range(TB // 128):
                x_, o_ = xt[k2 % 2], xo[k2 % 2]
                r0 = tb * TB + tt * 128
                c.dma(x_[:], xsrc[r0:r0 + 128, :], writes=[x_], q="pool")
                for cb in range(2):
                    po = PO[(2 * k2 + cb) % 4]
                    for kc in range(8):
                        c.mm(po[:], m_[:, kc, tt * 128:(tt + 1) * 128], wout[:, kc, cb * 512:(cb + 1) * 512], kc == 0, kc == 7, [m_, wout], [po])
                    c.tt("dve", o_[:, cb * 512:(cb + 1) * 512], po[:], x_[:, cb * 512:(cb + 1) * 512], ALU.add, [po, x_], [o_])
                c.dma(xdst[r0:r0 + 128, :], o_[:], reads=[o_], q="sp")
                k2 += 1
        m.sched.emit()


class NormBufs:
    def __init__(self, c, tag):
        self.xts = c.sbn("xt" + tag, [128, D], F32, 2)
        self.xbs = c.sbn("xb" + tag, [128, D], BF16, 2)
        self.junk = c.sb("junk" + tag, [128, D], F32)
        self.sss = c.sbn("ss" + tag, [128, 1], F32, 2)
        self.rstds = c.sbn("rstd" + tag, [128, 1], F32, 2)
        self.pTs = c.psn("pT" + tag, [128, 8, 128], BF16, 2)
        self.eps = c.sb("eps" + tag, [128, 1])
        c.memset("dve", self.eps[:], 1e-6, [self.eps])
        self.n = 0

    def run(self, c, xsrc, hT, col0, identb):
        i = self.n % 2
        self.n += 1
        norm_T(c, xsrc, hT, col0, self.xts[i], self.xbs[i], self.junk, self.sss[i], self.rstds[i], self.eps,
               self.pTs[i], identb)


def phase_F(m, l, xsrc, xdst):
    S = m.S
    sc = m.scr
    TB = min(512, S)
    with contextlib.ExitStack() as st:
        c = m.ctx(st)
        cs = load_consts(m, c, ["ident", "ones"])
        identb = make_bf(c, cs["ident"], [128, 128], "identb")
        onesb = make_bf(c, cs["ones"], [128, 128], "onesb")
        nb = NormBufs(c, "f")
        wq = c.sb("wq", [128, 8, D], BF16)
        wkv = c.sb("wkv", [128, 8, 2 * D], BF16)
        wo = c.sb("wo", [128, 8, D], BF16)
        c.dma(wq[:], sc["b_xa_wq"][l].rearrange("(c p) n -> p c n", p=128), writes=[wq], q="sp")
        c.dma(wkv[:], sc["b_xa_wkv"][l].rearrange("(c p) n -> p c n", p=128), writes=[wkv], q="act")
        c.dma(wo[:], sc["b_xa_wo"][l].rearrange("(c p) n -> p c n", p=128), writes=[wo], q="pool")
        memT = c.sb("memT", [128, 8, MEM], BF16)
        KT = c.sb("KT", [128, 8, MEM], BF16)
        Vm = c.sb("Vm", [128, 2, D], BF16)
        PA = c.psn("paf", [128, 512], F32, 3)
        for mt in range(2):
            nb.run(c, m.din["mem"][mt * 128:(mt + 1) * 128, :], memT, mt * 128, identb)
        k = 0
        for fc in range(8):
            pa = PA[k % 3]
            k += 1
            for kc in range(8):
                c.mm(pa[:, 0:MEM], wkv[:, kc, fc * 128:(fc + 1) * 128], memT[:, kc, :], kc == 0, kc == 7, [wkv, memT], [pa])
            c.cp("act", KT[:, fc, :], pa[:, 0:MEM], [pa], [KT])
        for mt in range(2):
            for cb in range(2):
                pa = PA[k % 3]
                k += 1
                for kc in range(8):
                    c.mm(pa[:], memT[:, kc, mt * 128:(mt + 1) * 128], wkv[:, kc, D + cb * 512:D + (cb + 1) * 512], kc == 0, kc == 7, [wkv, memT], [pa])
                c.cp("act", Vm[:, mt, cb * 512:(cb + 1) * 512], pa[:], [pa], [Vm])
        hT = c.sbn("hTf", [128, 8, TB], BF16, 2)
        qT = c.sbn("qTf", [128, 8, TB], BF16, 2)
        oT = c.sbn("oTf", [128, 8, TB], BF16, 2)
        pt = c.sbn("ptf", [128, 2, TB], BF16, 3)
        rl = c.sbn("rlf", [128, TB], F32, 2)
        xt = c.sbn("xtf", [128, D], F32, 2)
        xo = c.sbn("xof", [128, D], F32, 2)
        PB = c.psn("pbf", [128, 512], F32, 3)
        k2 = 0
        kp = 0
        for tb in range(S // TB):
            h_, q_, o_ = hT[tb % 2], qT[tb % 2], oT[tb % 2]
            for tt in range(TB // 128):
                r0 = tb * TB + tt * 128
                nb.run(c, xsrc[r0:r0 + 128, :], h_, tt * 128, identb)
            for fc in range(8):
                pa = PA[k % 3]
                k += 1
                for kc in range(8):
                    c.mm(pa[:, 0:TB], wq[:, kc, fc * 128:(fc + 1) * 128], h_[:, kc, :], kc == 0, kc == 7, [wq, h_], [pa])
                c.cp("act", q_[:, fc, :], pa[:, 0:TB], [pa], [q_])
            for h in range(4):
                p_ = pt[kp % 3]
                kp += 1
                for mc in range(2):
                    pa = PA[k % 3]
                    k += 1
                    for dc in range(2):
                        c.mm(pa[:, 0:TB], KT[:, h * 2 + dc, mc * 128:(mc + 1) * 128], q_[:, h * 2 + dc, :], dc == 0, dc == 1, [KT, q_], [pa])
                    c.act(p_[:, mc, :], pa[:, 0:TB], AF.Exp, [pa], [p_], scale=1.0 / 16)
                pl = PB[2]
                for mc in range(2):
                    c.mm(pl[:, 0:TB], onesb[:], p_[:, mc, :], mc == 0, mc == 1, [onesb, p_], [pl])
                r_ = rl[h % 2]
                c.recip(r_[:], pl[:, 0:TB], [pl], [r_])
                for dc in range(2):
                    pb = PB[dc]
                    for mc in range(2):
                        c.mm(pb[:, 0:TB], Vm[:, mc, h * 256 + dc * 128:h * 256 + (dc + 1) * 128], p_[:, mc, :], mc == 0, mc == 1, [Vm, p_], [pb])
                    c.tt("dve", o_[:, h * 2 + dc, :], pb[:, 0:TB], r_[:], ALU.mult, [pb, r_], [o_])
            for tt in range(TB // 128):
                x_, xo_ = xt[k2 % 2], xo[k2 % 2]
                r0 = tb * TB + tt * 128
                c.dma(x_[:], xsrc[r0:r0 + 128, :], writes=[x_], q="pool")
                for cb in range(2):
                    pa = PA[k % 3]
                    k += 1
                    for kc in range(8):
                        c.mm(pa[:], o_[:, kc, tt * 128:(tt + 1) * 128], wo[:, kc, cb * 512:(cb + 1) * 512], kc == 0, kc == 7, [o_, wo], [pa])
                    c.tt("dve", xo_[:, cb * 512:(cb + 1) * 512], pa[:], x_[:, cb * 512:(cb + 1) * 512], ALU.add, [pa, x_], [xo_])
                c.dma(xdst[r0:r0 + 128, :], xo_[:], reads=[xo_], q="sp")
                k2 += 1
        m.sched.emit()


def phase_G(m, l, xsrc, xdst):
    S = m.S
    sc = m.scr
    TBK = min(1024, S)
    SB = min(512, TBK)
    NFC = DFF // 128
    with contextlib.ExitStack() as st:
        c = m.ctx(st)
        cs = load_consts(m, c, ["ident"])
        identb = make_bf(c, cs["ident"], [128, 128], "identb")
        nb = NormBufs(c, "g")
        wdn = c.sb("wdn", [128, NFC, D], BF16)
        c.dma(wdn[:], sc["b_ffn_w_down"][l].rearrange("(c p) n -> p c n", p=128), writes=[wdn], q="pool")
        cw = c.sb("cw", [128, NFC, 3])
        for j in range(3):
            c.dma(cw[:, :, j], m.din["ffn_conv_w"][l, j].rearrange("(c p) -> p c", p=128), writes=[cw])
        cb_ = c.sb("cbias", [128, NFC])
        c.dma(cb_[:], m.din["ffn_conv_b"][l].rearrange("(c p) -> p c", p=128), writes=[cb_])
        halo = c.sb("halo", [128, NFC, 2])
        c.memset("dve", halo[:], 0.0, [halo])
        hT = c.sb("hTg", [128, 8, TBK], BF16)
        hid = c.sb("hid", [128, NFC, TBK], BF16)
        wu = c.sbn("wu", [128, 8, 512], BF16, 2)
        wv = c.sbn("wv", [128, 8, 512], BF16, 2)
        uext = c.sbn("uext", [128, SB + 2], F32, 2)
        t1 = c.sbn("t1g", [128, SB], F32, 2)
        t2 = c.sbn("t2g", [128, SB], F32, 2)
        sg = c.sbn("sgg", [128, SB], F32, 2)
        xt = c.sbn("xtg", [128, D], F32, 2)
        xo = c.sbn("xog", [128, D], F32, 2)
        PU = c.psn("pug", [128, 512], F32, 2)
        PV = c.psn("pvg", [128, 512], F32, 2)
        PO = c.psn("pog", [128, 512], F32, 2)
        wsrc = sc["b_ffn_w_up"][l].rearrange("(c p) n -> p c n", p=128)
        k = 0
        k2 = 0
        for tbk in range(S // TBK):
            for tt in range(TBK // 128):
                r0 = tbk * TBK + tt * 128
                nb.run(c, xsrc[r0:r0 + 128, :], hT, tt * 128, identb)
            for grp in range(6):
                ncol = 512 if grp < 5 else 256
                wu_, wv_ = wu[grp % 2], wv[grp % 2]
                c.dma(wu_[:, :, 0:ncol], wsrc[:, :, grp * 512:grp * 512 + ncol], writes=[wu_], q="sp")
                c.dma(wv_[:, :, 0:ncol], wsrc[:, :, DFF + grp * 512:DFF + grp * 512 + ncol], writes=[wv_], q="act")
                for fcl in range(ncol // 128):
                    fc = grp * 4 + fcl
                    for sbi in range(TBK // SB):
                        ss_ = slice(sbi * SB, (sbi + 1) * SB)
                        pu, pv = PU[k % 2], PV[k % 2]
                        ue, a1, a2, s_ = uext[k % 2], t1[k % 2], t2[k % 2], sg[k % 2]
                        k += 1
                        for kc in range(8):
                            c.mm(pu[:, 0:SB], wu_[:, kc, fcl * 128:(fcl + 1) * 128], hT[:, kc, ss_], kc == 0, kc == 7, [wu_, hT], [pu])
                        for kc in range(8):
                            c.mm(pv[:, 0:SB], wv_[:, kc, fcl * 128:(fcl + 1) * 128], hT[:, kc, ss_], kc == 0, kc == 7, [wv_, hT], [pv])
                        c.cp("pool", ue[:, 0:2], halo[:, fc, :], [halo], [ue])
                        c.cp("act", ue[:, 2:SB + 2], pu[:, 0:SB], [pu], [ue])
                        c.cp("pool", halo[:, fc, :], ue[:, SB:SB + 2], [ue], [halo])
                        c.act(a1[:], pu[:, 0:SB], AF.Identity, [pu, cw, cb_], [a1], bias=cb_[:, fc:fc + 1], scale=cw[:, fc, 2:3])
                        c.stt(a2[:], ue[:, 1:SB + 1], cw[:, fc, 1:2], a1[:], ALU.mult, ALU.add, [ue, cw, a1], [a2])
                        c.stt(a1[:], ue[:, 0:SB], cw[:, fc, 0:1], a2[:], ALU.mult, ALU.add, [ue, cw, a2], [a1])
                        c.act(s_[:], a1[:], AF.Silu, [a1], [s_])
                        c.tt("dve", hid[:, fc, ss_], s_[:], pv[:, 0:SB], ALU.mult, [s_, pv], [hid])
            for tt in range(TBK // 128):
                x_, xo_ = xt[k2 % 2], xo[k2 % 2]
                r0 = tbk * TBK + tt * 128
                c.dma(x_[:], xsrc[r0:r0 + 128, :], writes=[x_], q="pool")
                for cb in range(2):
                    po = PO[cb]
                    for fc in range(NFC):
                        c.mm(po[:], hid[:, fc, tt * 128:(tt + 1) * 128], wdn[:, fc, cb * 512:(cb + 1) * 512], fc == 0, fc == NFC - 1, [hid, wdn], [po])
                    c.tt("dve", xo_[:, cb * 512:(cb + 1) * 512], po[:], x_[:, cb * 512:(cb + 1) * 512], ALU.add, [po, x_], [xo_])
                c.dma(xdst[r0:r0 + 128, :], xo_[:], reads=[xo_], q="sp")
                k2 += 1
        m.sched.emit()


def phase_H(m, xsrc):
    S = m.S
    with contextlib.ExitStack() as st:
        c = m.ctx(st)
        grow = rowb(m, c, "fgrow", m.din["final_norm_g"], D)
        eps = c.sb("eps", [128, 1])
        c.memset("dve", eps[:], 1e-6, [eps])
        xt = c.sbn("xth", [128, D], F32, 3)
        xo = c.sbn("xoh", [128, D], F32, 3)
        junk = c.sb("junkh", [128, D], F32)
        ss = c.sbn("ssh", [128, 1], F32, 3)
        for ti in range(S // 128):
            i = ti % 3
            c.dma(xt[i][:], xsrc[ti * 128:(ti + 1) * 128, :], writes=[xt[i]])
            c.act(junk[:], xt[i][:], AF.Square, [xt[i]], [junk, ss[i]], accum=ss[i][:, 0:1])
            c.act(ss[i][:], ss[i][:], AF.Sqrt, [ss[i], eps], [ss[i]], bias=eps[:, 0:1], scale=1.0 / D)
            c.recip(ss[i][:], ss[i][:], [ss[i]], [ss[i]])
            c.stt(xo[i][:], xt[i][:], ss[i][:, 0:1], grow[:], ALU.mult, ALU.mult, [xt[i], ss[i], grow], [xo[i]])
            c.dma(m.out[ti * 128:(ti + 1) * 128, :], xo[i][:], reads=[xo[i]])
        m.sched.emit()


def phase_D(m, l):
    S = m.S
    sc = m.scr
    with contextlib.ExitStack() as st:
        c = m.ctx(st)
        cs = load_consts(m, c, ["ident", "rwm", "maskg", "ones"])
        ident, rwm, maskg, ones = cs["ident"], cs["rwm"], cs["maskg"], cs["ones"]
        din = m.din
        w0r = rowb(m, c, "w0r", din["rwkv_w0"][l], 512)
        a0r = rowb(m, c, "a0r", din["rwkv_a0"][l], 512)
        kkr = rowb(m, c, "kkr", din["rwkv_k_k"][l], 512)
        kar = rowb(m, c, "kar", din["rwkv_k_a"][l], 512)
        rkr = rowb(m, c, "rkr", din["rwkv_r_k"][l].rearrange("a b -> (a b)"), 512)
        lngr = rowb(m, c, "lngr", din["rwkv_ln_g"][l], 512)
        lnbr = rowb(m, c, "lnbr", din["rwkv_ln_b"][l], 512)
        mur = rowb(m, c, "mur", din["rwkv_mu"][l][0:1536], 1536)
        mucol = c.sb("mucol", [128, 2])
        c.dma(mucol[:], din["rwkv_mu"][l][1536:1792].rearrange("(c p) -> p c", p=128), writes=[mucol])
        WUP = c.sb("WUP", [128, 512])
        c.dma(WUP[0:64, :], din["rwkv_w_up"][l], writes=[WUP])
        c.dma(WUP[64:128, :], din["rwkv_a_up"][l], writes=[WUP])
        GUP = c.sb("GUP", [128, 512])
        c.dma(GUP[:], din["rwkv_g_up"][l], writes=[GUP])
        epsg = c.sb("epsg", [128, 1])
        c.memset("dve", epsg[:], 64e-5, [epsg])
        STs = c.sbn("STs", [128, 4, 64], F32, 2)
        c.memset("dve", STs[0][:], 0.0, [STs[0]])
        Gbd = c.sbn("Gbd", [128, 128], F32, 4)
        for p in range(4):
            c.memset("pool", Gbd[p][:], 0.0, [Gbd[p]])
        Hs = c.sb("Hs", [128, 4, 64])
        z = c.sbn("z", [128, 1536], F32, 2)
        zp = c.sbn("zp", [128, 1536], F32, 2)
        zm = c.sbn("zm", [128, 1536], F32, 2)
        cod = c.sbn("cod", [128, 2, 129], F32, 2)
        dcd = c.sb("dcd", [128, 2, 128])
        cm = c.sb("cm", [128, 2, 128])
        lw = c.sb("lw", [128, 128])
        sgd = c.sb("sgd", [128, 128])
        tmpw = c.sb("tmpw", [128, 512])
        sigw = c.sb("sigw", [128, 512])
        av = c.sb("av", [128, 512])
        gv = c.sb("gv", [128, 512])
        kkraw = c.sb("kkraw", [128, 512])
        sqk = c.sb("sqk", [128, 512])
        s8 = c.sbn("s8_", [128, 8], F32, 6)
        kk = c.sb("kk", [128, 512])
        kmod = c.sb("kmod", [128, 512])
        bv = c.sb("bv", [128, 512])
        ee = c.sb("eed", [128, 4, 512])
        TM = c.sb("TM", [128, 4, 512])
        BH = c.sb("BH", [128, 512])
        KH = c.sb("KH", [128, 512])
        PCc = c.sb("PCc", [128, 4])
        FT = c.sbn("FTd", [128, 4, 128], F32, 4)
        GM = c.sbn("GM", [128, 512], F32, 2)
        Lc = c.sbn("Lc", [128, 2, 128], BF16, 4)
        Wb = c.sbn("Wb", [128, 128], BF16, 4)
        Wc = c.sbn("Wc", [128, 128], F32, 4)
        LcH = [Lc, c.sbn("Lc1_", [128, 2, 128], BF16, 4)]
        WcH = [Wc, c.sbn("Wc1_", [128, 128], F32, 4)]
        WbH = [Wb, c.sbn("Wb1_", [128, 128], BF16, 4)]
        Rp = c.sb("Rp", [128, 512])
        Yl = c.sb("Yl", [128, 512])
        RT = c.sbn("RTd", [128, 128], F32, 2)
        yv = c.sb("yv", [128, 512])
        yc = c.sb("yc", [128, 512])
        sq2 = c.sb("sq2", [128, 512])
        bon = c.sb("bon", [128, 512])
        yo = c.sb("yo", [128, 512])
        obT = c.sbn("obTd", [128, 4, 128], BF16, 2)
        B0 = c.ps("B0", [128, 512]); B1 = c.ps("B1", [128, 512]); B2 = c.ps("B2", [128, 512])
        B3 = c.ps("B3", [128, 512]); B4 = c.ps("B4", [128, 512]); B5 = c.ps("B5", [128, 512])
        B6 = c.ps("B6", [128, 512]); B7 = c.ps("B7", [128, 512])
        pL = B5
        pY = pGs = pH = pS = pRT = B6
        rwcsrc = sc["rwcT"].rearrange("(c p) t -> p c t", p=128)
        for ti in range(S // 128):
            t0 = ti * 128
            i = ti % 2
            z_, zp_, zm_, cod_ = z[i], zp[i], zm[i], cod[i]
            c.dma(z_[:], sc["rwz"][t0:t0 + 128, :], writes=[z_], q="sp")
            if ti == 0:
                c.memset("pool", zp_[0:1, :], 0.0, [zp_])
                c.dma(zp_[1:128, :], sc["rwz"][0:127, :], writes=[zp_], q="act")
                c.memset("pool", cod_[:, :, 0:1], 0.0, [cod_])
                c.dma(cod_[:, :, 1:129], rwcsrc[:, :, 0:128], writes=[cod_], q="pool")
            else:
                c.dma(zp_[:], sc["rwz"][t0 - 1:t0 + 127, :], writes=[zp_], q="act")
                c.dma(cod_[:], rwcsrc[:, :, t0 - 1:t0 + 128], writes=[cod_], q="pool")
            c.tt("dve", zm_[:], zp_[:], z_[:], ALU.subtract, [zp_, z_], [zm_])
            c.tt("pool", zm_[:], zm_[:], mur[:], ALU.mult, [zm_, mur], [zm_])
            c.tt("dve", zm_[:], zm_[:], z_[:], ALU.add, [zm_, z_], [zm_])
            r_, k_, v_ = zm_[:, 0:512], zm_[:, 512:1024], zm_[:, 1024:1536]
            c.tt("pool", dcd[:], cod_[:, :, 0:128], cod_[:, :, 1:129], ALU.subtract, [cod_], [dcd])
            for ch in range(2):
                c.stt(cm[:, ch, :], dcd[:, ch, :], mucol[:, ch:ch + 1], cod_[:, ch, 1:129], ALU.mult, ALU.add, [dcd, mucol, cod_], [cm])
            c.act(lw[0:64, :], cm[0:64, 0, :], AF.Tanh, [cm], [lw])
            c.cp("pool", lw[64:128, :], cm[64:128, 0, :], [cm], [lw])
            c.act(sgd[:], cm[:, 1, :], AF.Sigmoid, [cm], [sgd])
            c.mm(B0[:], lw[0:64, :], WUP[0:64, :], True, True, [lw, WUP], [B0])
            c.mm(B1[:], lw[64:128, :], WUP[64:128, :], True, True, [lw, WUP], [B1])
            c.mm(B2[:], sgd[:], GUP[:], True, True, [sgd, GUP], [B2])
            c.tt("dve", tmpw[:], B0[:], w0r[:], ALU.add, [B0, w0r], [tmpw])
            c.act(sigw[:], tmpw[:], AF.Sigmoid, [tmpw], [sigw])
            c.tt("dve", tmpw[:], B1[:], a0r[:], ALU.add, [B1, a0r, sigw], [tmpw])
            c.act(av[:], tmpw[:], AF.Sigmoid, [tmpw], [av])
            c.cp("act", gv[:], B2[:], [B2], [gv])
            c.tt("pool", kkraw[:], k_, kkr[:], ALU.mult, [zm_, kkr], [kkraw])
            c.tt("pool", sqk[:], kkraw[:], kkraw[:], ALU.mult, [kkraw], [sqk])
            c.red(s8[0][:], v3(sqk[:]), ALU.add, [sqk], [s8[0]])
            c.act(s8[0][:], s8[0][:], AF.Sqrt, [s8[0]], [s8[0]])
            c.ts("dve", s8[0][:], s8[0][:], 1e-12, ALU.max, [s8[0]], [s8[0]])
            c.recip(s8[0][:], s8[0][:], [s8[0]], [s8[0]])
            c.tt("dve", v3(kk[:]), v3(kkraw[:]), bc3(s8[0][:]), ALU.mult, [kkraw, s8[0]], [kk])
            c.stt(kmod[:], av[:], -1.0, kar[:], ALU.add, ALU.mult, [av, kar], [kmod])
            c.stt(kmod[:], kmod[:], 1.0, k_, ALU.add, ALU.mult, [kmod, zm_], [kmod])
            c.tt("pool", bv[:], kk[:], av[:], ALU.mult, [kk, av], [bv])
            c.mm(B0[:], rwm[:, 0:128], sigw[:], True, True, [rwm, sigw], [B0])
            c.mm(B1[:], rwm[:, 128:256], sigw[:], True, True, [rwm, sigw], [B1])
            c.mm(B2[:], rwm[:, 256:384], sigw[:], True, True, [rwm, sigw], [B2])
            c.act(ee[:, 0, :], B1[:], AF.Exp, [B1], [ee], scale=-LAM)
            c.act(ee[:, 1, :], B0[:], AF.Exp, [B0], [ee], scale=-LAM)
            c.act(ee[:, 2, :], B0[:], AF.Exp, [B0], [ee], scale=LAM)
            c.act(ee[:, 3, :], B2[:], AF.Exp, [B2], [ee], scale=-LAM)
            c.stt(TM[:, 0, :], kk[:], -1.0, ee[:, 0, :], ALU.mult, ALU.mult, [kk, ee], [TM])
            c.tt("pool", TM[:, 1, :], r_, ee[:, 1, :], ALU.mult, [zm_, ee], [TM])
            c.tt("dve", TM[:, 2, :], bv[:], ee[:, 2, :], ALU.mult, [bv, ee], [TM])
            c.tt("pool", TM[:, 3, :], kmod[:], ee[:, 2, :], ALU.mult, [kmod, ee], [TM])
            c.tt("dve", BH[:], bv[:], ee[:, 3, :], ALU.mult, [bv, ee], [BH])
            c.tt("pool", KH[:], kmod[:], ee[:, 3, :], ALU.mult, [kmod, ee], [KH])
            for p in range(4):
                c.mm(B3[:, 2 * p:2 * p + 2], sigw[:, p * 128:(p + 1) * 128], ones[:, 0:2], True, True, [sigw, ones], [B3])
            c.act(PCc[:], B3[:, 0:8].rearrange("p (a b) -> p a b", b=2)[:, :, 0], AF.Exp, [B3], [PCc], scale=-LAM)
            STc, STn = STs[ti % 2], STs[(ti + 1) % 2]
            for p in range(4):
                pc = slice(p * 128, (p + 1) * 128)
                ft = FT[p]
                for q in range(4):
                    c.tr(B3[:, q * 128:(q + 1) * 128], TM[:, q, pc], ident[:], [TM, ident], [B3])
                c.cp("act", ft[:].rearrange("p a b -> p (a b)"), B3[:], [B3], [ft])
                def head_gen(hl, p=p, ft=ft):
                    h = 2 * p + hl
                    P = slice(hl * 64, (hl + 1) * 64)
                    hc = slice(h * 64, (h + 1) * 64)
                    vh = zm_[:, 1024 + h * 64:1024 + (h + 1) * 64]
                    bG = (B4, B0)[hl]
                    bD = (B5, B2)[hl]
                    gm = GM[hl]
                    Lr, Wr, Wbr = LcH[hl], WcH[hl], WbH[hl]
                    c.mm(bG[:, 0:256], ft[P, 2, :], ft[P, 0:2, :], True, True, [ft], [bG])
                    c.mm(bG[:, 256:512], ft[P, 3, :], ft[P, 0:2, :], True, True, [ft], [bG])
                    c.tt("dve", gm[:], bG[:], maskg[:], ALU.mult, [bG, maskg], [gm])
                    yield
                    LabT, MrbT, LakT, MrkT = gm[:, 0:128], gm[:, 128:256], gm[:, 256:384], gm[:, 384:512]
                    lc, wc, wb = Lr[0], Wr[0], Wbr[0]
                    c.tr(bD[:, 384:512], LabT, ident[:], [gm, ident], [bD])
                    c.mm(bD[:, 0:64], LakT, vh, True, True, [gm, zm_], [bD])
                    c.cp("act", lc[:, 0, :], bD[:, 384:512], [bD], [lc])
                    c.cp("pool", lc[:, 1, :], LabT, [gm], [lc])
                    c.cp("act", wc[:, 64:128], bD[:, 0:64], [bD], [wc])
                    c.cp("pool", wc[:, 0:64], TM[:, 0, hc], [TM], [wc])
                    c.cp("pool", wb[:], wc[:], [wc], [wb])
                    yield
                    for j in range(7):
                        lcn, wcn, wbn = Lr[(j + 1) % 4], Wr[(j + 1) % 4], Wbr[(j + 1) % 4]
                        c.mm(bD[:, 0:128], lc[:, 1, :], wb[:], True, True, [lc, wb], [bD])
                        if j < 6:
                            c.mm(bD[:, 128:256], lc[:, 1, :], lc[:, 0, :], True, True, [lc], [bD])
                            c.mm(bD[:, 256:384], lc[:, 0, :], lc[:, 1, :], True, True, [lc], [bD])
                        c.tt("dve", wcn[:], bD[:, 0:128], wc[:], ALU.add, [bD, wc], [wcn])
                        if j < 6:
                            c.cp("act", lcn[:].rearrange("p a b -> p (a b)"), bD[:, 128:384], [bD], [lcn])
                            c.cp("pool", wbn[:], wcn[:], [wcn], [wbn])
                        lc, wc, wb = lcn, wcn, wbn
                        yield
                    c.mm(bG[:, 0:128], MrbT, wc[:], True, False, [gm, wc], [bG])
                    c.mm(bG[:, 64:128], MrkT, vh, False, True, [gm, zm_], [bG])
                    gcol = slice(128 + hl * 64, 128 + (hl + 1) * 64)
                    c.mm(bG[P, gcol], wc[:, 0:64], BH[:, hc], True, True, [wc, BH], [bG])
                    c.mm(bG[P, 256:320], BH[:, hc], wc[:, 64:128], True, False, [BH, wc], [bG])
                    c.mm(bG[P, 256:320], KH[:, hc], vh, False, True, [KH, zm_], [bG])
                    c.tt("dve", Rp[:, hc], TM[:, 1, hc], bG[:, 0:64], ALU.add, [TM, bG], [Rp])
                    c.stt(Gbd[p][P, hl * 64:(hl + 1) * 64], ident[P, hl * 64:(hl + 1) * 64], PCc[P, p:p + 1], bG[P, gcol],
                          ALU.mult, ALU.add, [ident, PCc, bG], [Gbd[p]])
                    c.cp("act", Yl[:, hc], bG[:, 64:128], [bG], [Yl])
                    c.cp("act", Hs[P, p, :], bG[P, 256:320], [bG], [Hs])

                gens = [head_gen(0), head_gen(1)]
                while gens:
                    for g in list(gens):
                        try:
                            next(g)
                        except StopIteration:
                            gens.remove(g)
                rt = RT[p % 2]
                c.tr(pRT[:, 384:512], Rp[:, pc], ident[:], [Rp, ident], [pRT])
                c.cp("act", rt[:], pRT[:, 384:512], [pRT], [rt])
                for hl in range(2):
                    h = 2 * p + hl
                    P = slice(hl * 64, (hl + 1) * 64)
                    yb = B7 if hl == 0 else B1
                    c.mm(yb[:, h * 64:(h + 1) * 64], rt[P, :], STc[P, p, :], True, True, [rt, STc], [yb])
                c.mm(pS[:, 320:384], Gbd[p][:], STc[:, p, :], True, True, [Gbd[p], STc], [pS])
                c.tt("dve", STn[:, p, :], pS[:, 320:384], Hs[:, p, :], ALU.add, [pS, Hs], [STn])
            for hl in range(2):
                yb = B7 if hl == 0 else B1
                c.tt("dve", v3(yv[:])[:, hl::2, :], v3(yb[:])[:, hl::2, :], v3(Yl[:])[:, hl::2, :], ALU.add, [yb, Yl], [yv])
            c.red(s8[1][:], v3(yv[:]), ALU.add, [yv], [s8[1]])
            c.ts("dve", s8[1][:], s8[1][:], 1.0 / 64, ALU.mult, [s8[1]], [s8[1]])
            c.tt("dve", v3(yc[:]), v3(yv[:]), bc3(s8[1][:]), ALU.subtract, [yv, s8[1]], [yc])
            c.tt("pool", sq2[:], yc[:], yc[:], ALU.mult, [yc], [sq2])
            c.red(s8[2][:], v3(sq2[:]), ALU.add, [sq2], [s8[2]])
            c.act(s8[2][:], s8[2][:], AF.Sqrt, [s8[2], epsg], [s8[2]], bias=epsg[:, 0:1], scale=1.0 / 64)
            c.recip(s8[2][:], s8[2][:], [s8[2]], [s8[2]])
            c.tt("dve", v3(yc[:]), v3(yc[:]), bc3(s8[2][:]), ALU.mult, [yc, s8[2]], [yc])
            c.tt("pool", yc[:], yc[:], lngr[:], ALU.mult, [yc, lngr], [yc])
            c.tt("pool", yc[:], yc[:], lnbr[:], ALU.add, [yc, lnbr], [yc])
            c.tt("pool", sq2[:], r_, kmod[:], ALU.mult, [zm_, kmod], [sq2])
            c.tt("pool", sq2[:], sq2[:], rkr[:], ALU.mult, [sq2, rkr], [sq2])
            c.red(s8[3][:], v3(sq2[:]), ALU.add, [sq2], [s8[3]])
            c.tt("dve", v3(bon[:]), v3(v_), bc3(s8[3][:]), ALU.mult, [zm_, s8[3]], [bon])
            c.tt("dve", yo[:], yc[:], bon[:], ALU.add, [yc, bon], [yo])
            c.tt("dve", yo[:], yo[:], gv[:], ALU.mult, [yo, gv], [yo])
            for q in range(4):
                c.tr(B4[:, q * 128:(q + 1) * 128], yo[:, q * 128:(q + 1) * 128], ident[:], [yo, ident], [B4])
            c.cp("act", obT[i][:].rearrange("p a b -> p (a b)"), B4[:], [B4], [obT[i]])
            c.dma(sc["brT"][2].rearrange("(h v) t -> v h t", v=128)[:, :, t0:t0 + 128], obT[i][:], reads=[obT[i]])
        m.sched.emit()


def build(S, debug=False, nlayers=L, ph="ABCDEFGH"):
    m = Model(S, debug=debug)
    xa = m.scratch("xa", [S, D], F32)
    xb = m.scratch("xb", [S, D], F32)
    xc = m.scratch("xc", [S, D], F32)
    phase_prep(m)
    xin = m.din["x"]
    for l in range(nlayers):
        if "hg" not in m.scr:
            m.scratch("hg", [S, 2048], F32)
            m.scratch("dqT", [512, S], BF16)
            m.scratch("dkT", [512, S], BF16)
            m.scratch("dvv", [S, 512], BF16)
            m.scratch("rwz", [S, 1536], F32)
            m.scratch("rwcT", [256, S], F32)
            m.scratch("gateT", [3072, S], BF16)
        if "A" in ph:
            phase_A(m, l, xin)
        if "brT" not in m.scr:
            m.scratch("brT", [3, 512, S], BF16)
        if "B" in ph:
            phase_B(m, l)
        if "C" in ph:
            phase_C(m, l)
        if "D" in ph:
            phase_D(m, l)
        if "E" in ph:
            phase_E(m, l, xin, xa)
        if "F" in ph:
            phase_F(m, l, xa, xb)
        if "G" in ph:
            phase_G(m, l, xb, xc)
        xin = xc
    if "H" in ph:
        phase_H(m, xin)
    return m


_CACHE = {}


def kernel(**inputs):
    S = inputs["x"].shape[1]
    B = inputs["x"].shape[0]
    if S not in _CACHE:
        _CACHE[S] = build(S)
    m = _CACHE[S]
    consts = host_consts()
    base = {n: np.ascontiguousarray(np.asarray(inputs[n], np.float32)) for n, _ in PARAMS}
    for n, v in consts.items():
        base["c_" + n] = v
    in_maps = []
    for b in range(B):
        d = dict(base)
        d["x"] = np.ascontiguousarray(np.asarray(inputs["x"][b], np.float32))
        d["mem"] = np.ascontiguousarray(np.asarray(inputs["mem"][b], np.float32))
        in_maps.append(d)
    res = run_bass_kernel_spmd(m.nc, in_maps, core_ids=list(range(B)))
    return np.stack([np.asarray(r["out"], np.float32) for r in res.results], axis=0)
```

```python
import contextlib
import math
import numpy as np
import ml_dtypes
import concourse.bass as bass
import concourse.mybir as mybir
from concourse.bass_utils import run_bass_kernel_spmd

F32 = mybir.dt.float32
BF16 = mybir.dt.bfloat16
AF = mybir.ActivationFunctionType
ALU = mybir.AluOpType
AX = mybir.AxisListType

D = 1024
L = 2
NIN = 8448
DFF = 2816
MEM = 256
ENGS = ("pe", "act", "dve", "pool", "sp")
NSLOT = 6
LAM = math.exp(-0.5)
PE_DRAIN = False


class Buf:
    __slots__ = ("name", "last_w", "readers", "excl")

    def __init__(self, name=""):
        self.name = name
        self.last_w = None
        self.readers = []
        self.excl = False


class T:
    def __init__(self, t, name=""):
        self.t = t
        self.b = Buf(name)

    def __getitem__(self, idx):
        return self.t[idx]


def _b(x):
    return x.b if isinstance(x, T) else x


class Sched:
    def __init__(self, nc):
        self.nc = nc
        self.q = {e: [] for e in ENGS}
        self.cnt = {}
        self.seen = {e: {} for e in ENGS}
        self.semkeys = []
        for e in ENGS:
            self._mk(e)
        self.dma_n = {e: 0 for e in ENGS}
        for e in ("sp", "act", "pool"):
            for i in range(NSLOT):
                self._mk(("dma", e, i))
        self.n_instr = 0
        self.nblk = 0

    def _mk(self, k):
        self.cnt[k] = 0
        self.semkeys.append(k)

    def _need(self, e, deps):
        best = {}
        for d in deps:
            if d is None:
                continue
            k, v = d
            if k == e and e == "pe":
                continue
            if self.seen[e].get(k, 0) >= v:
                continue
            if best.get(k, 0) < v:
                best[k] = v
        for k, v in best.items():
            self.seen[e][k] = v
            self.q[e].append(("wait", k, v))

    def _deps(self, reads, writes):
        deps = []
        for r in reads:
            r = _b(r)
            deps.append(r.last_w)
            if r.excl:
                deps.extend(r.readers)
        for w in writes:
            w = _b(w)
            deps.append(w.last_w)
            deps.extend(w.readers)
        return deps

    MAXOPS = None

    def op(self, e, fn, reads=(), writes=()):
        if Sched.MAXOPS is not None and self.n_instr >= Sched.MAXOPS:
            return
        self._need(e, self._deps(reads, writes))
        self.cnt[e] += 1
        v = self.cnt[e]
        self.q[e].append(("op", fn, e))
        for w in writes:
            w = _b(w)
            w.last_w = (e, v)
            w.readers = []
        for r in reads:
            _b(r).readers.append((e, v))
        self.n_instr += 1

    def dma(self, e, out, in_, reads=(), writes=()):
        if Sched.MAXOPS is not None and self.n_instr >= Sched.MAXOPS:
            return
        n = self.dma_n[e]
        self.dma_n[e] += 1
        slot = ("dma", e, n % NSLOT)
        deps = self._deps(reads, writes)
        if self.cnt[slot] > 0:
            deps.append((slot, self.cnt[slot]))
        self._need(e, deps)
        self.cnt[slot] += 16
        v = self.cnt[slot]
        self.q[e].append(("dma", out, in_, slot))
        for w in writes:
            w = _b(w)
            w.last_w = (slot, v)
            w.readers = []
        for r in reads:
            _b(r).readers.append((slot, v))
        self.n_instr += 1

    def drain(self):
        deps = [(k, v) for k, v in self.cnt.items() if isinstance(k, tuple) and v > 0]
        self._need("sp", deps)

    def emit(self):
        nc = self.nc
        self.drain()
        sems = {}
        for k in self.semkeys:
            nm = "s_" + "_".join(str(x) for x in (k if isinstance(k, tuple) else (k,))) + f"_{self.nblk}"
            sems[k] = nc.alloc_semaphore(name=nm)
        self.nblk += 1
        if self.nblk == 1:
            nc.clear_and_free_semaphores(list(sems.values()))
            nc.all_engine_barrier()
            for k in self.semkeys:
                nm = "s0_" + "_".join(str(x) for x in (k if isinstance(k, tuple) else (k,)))
                sems[k] = nc.alloc_semaphore(name=nm)
        with contextlib.ExitStack() as st:
            st.enter_context(nc.allow_non_contiguous_dma(reason="small strided parameter loads"))
            block = st.enter_context(nc.Block())

            def run(eh, items):
                for it in items:
                    if it[0] == "wait":
                        eh.wait_ge(sems[it[1]], it[2])
                    elif it[0] == "raw":
                        it[1](eh)
                    elif it[0] == "op":
                        it[1](eh).then_inc(sems[it[2]], 1)
                    else:
                        eh.dma_start(out=it[1], in_=it[2]).then_inc(sems[it[3]], 16)

            @block.tensor
            def _(e):
                run(e, self.q["pe"])

            @block.scalar
            def _(e):
                run(e, self.q["act"])

            @block.vector
            def _(e):
                run(e, self.q["dve"])

            @block.gpsimd
            def _(e):
                run(e, self.q["pool"])

            @block.sync
            def _(e):
                run(e, self.q["sp"])
        nc.clear_and_free_semaphores(list(sems.values()))
        nc.all_engine_barrier()
        for k in self.cnt:
            self.cnt[k] = 0
        self.seen = {e: {} for e in ENGS}
        self.q = {e: [] for e in ENGS}


class Ctx:
    def __init__(self, nc, S_, st):
        self.nc = nc
        self.S = S_
        self.st = st
        self.rr = 0

    _uid = [0]

    def sb(self, name, shape, dt=F32):
        Ctx._uid[0] += 1
        name = f"t{Ctx._uid[0]}_{name}"
        return T(self.st.enter_context(self.nc.sbuf_tensor(name, list(shape), dt)), name)

    def ps(self, name, shape, dt=F32):
        Ctx._uid[0] += 1
        name = f"p{Ctx._uid[0]}_{name}"
        t = T(self.st.enter_context(self.nc.psum_tensor(name, list(shape), dt)), name)
        t.b.excl = True
        return t

    def sbn(self, name, shape, dt=F32, n=2):
        return [self.sb(f"{name}{i}", shape, dt) for i in range(n)]

    def psn(self, name, shape, dt=F32, n=2):
        return [self.ps(f"{name}{i}", shape, dt) for i in range(n)]

    _mode = [None]

    def _pe_mode(self, lhsT):
        def rnd(n):
            return 32 if n <= 32 else (64 if n <= 64 else 128)
        shp = lhsT.shape
        k = shp[0]
        mfree = 1
        for d in shp[1:]:
            mfree *= d
        mode = (rnd(k), rnd(mfree))
        if PE_DRAIN and Ctx._mode[0] is not None and Ctx._mode[0] != mode:
            if not (Sched.MAXOPS is not None and self.S.n_instr >= Sched.MAXOPS):
                self.S.q["pe"].append(("raw", lambda e: e.drain()))
        Ctx._mode[0] = mode

    def mm(self, out, lhsT, rhs, start, stop, reads, writes):
        self._pe_mode(lhsT)
        self.S.op("pe", lambda e: e.matmul(out, lhsT=lhsT, rhs=rhs, start=start, stop=stop), reads, writes)

    def tr(self, out, in_, ident, reads, writes):
        self._pe_mode(in_)
        self.S.op("pe", lambda e: e.transpose(out=out, in_=in_, identity=ident), reads, writes)

    def act(self, out, in_, func, reads, writes, bias=None, scale=None, accum=None, eng="act"):
        kw = {}
        if bias is not None:
            kw["bias"] = bias
        if scale is not None:
            kw["scale"] = scale
        if accum is not None:
            kw["accum_out"] = accum
        self.S.op("act", lambda e: e.activation(out=out, in_=in_, func=func, **kw), reads, writes)

    def tt(self, eng, out, in0, in1, op, reads, writes):
        self.S.op(eng, lambda e: e.tensor_tensor(out=out, in0=in0, in1=in1, op=op), reads, writes)

    def ts(self, eng, out, in0, s1, op0, reads, writes, s2=None, op1=None):
        if op1 is None:
            self.S.op(eng, lambda e: e.tensor_scalar(out=out, in0=in0, scalar1=s1, scalar2=None, op0=op0), reads, writes)
        else:
            self.S.op(eng, lambda e: e.tensor_scalar(out=out, in0=in0, scalar1=s1, scalar2=s2, op0=op0, op1=op1), reads, writes)

    def stt(self, out, in0, scalar, in1, op0, op1, reads, writes):
        self.S.op("dve", lambda e: e.scalar_tensor_tensor(out=out, in0=in0, scalar=scalar, in1=in1, op0=op0, op1=op1), reads, writes)

    def cp(self, eng, out, in_, reads, writes):
        if eng == "act":
            self.S.op("act", lambda e: e.activation(out=out, in_=in_, func=AF.Copy), reads, writes)
        else:
            self.S.op(eng, lambda e: e.tensor_copy(out=out, in_=in_), reads, writes)

    def red(self, out, in_, op, reads, writes):
        self.S.op("dve", lambda e: e.tensor_reduce(out=out, in_=in_, axis=AX.X, op=op), reads, writes)

    def recip(self, out, in_, reads, writes):
        self.S.op("dve", lambda e: e.reciprocal(out=out, in_=in_), reads, writes)

    def memset(self, eng, ap, val, writes):
        self.S.op(eng, lambda e: e.memset(ap, val), (), writes)

    def dma(self, out, in_, reads=(), writes=(), q=None):
        if q is None:
            q = ("sp", "act", "pool")[self.rr % 3]
            self.rr += 1
        self.S.dma(q, out, in_, reads, writes)


def host_consts():
    c = {}
    idx = np.arange(128)
    s = idx[:, None]
    t = idx[None, :]
    same = (s // 64) == (t // 64)
    c["ident"] = np.eye(128, dtype=np.float32)
    c["ones"] = np.ones((128, 128), np.float32)
    tri64 = ((s <= t) & same).astype(np.float32)
    mid64 = ((s <= (t // 64) * 64 + 31) & same).astype(np.float32)
    blk64 = same.astype(np.float32)
    c["hgm"] = np.concatenate([tri64, tri64 - mid64, blk64 - tri64], axis=1)
    incl = (s <= t).astype(np.float32)
    strict = (s < t).astype(np.float32)
    rev = (s > t).astype(np.float32)
    c["rwm"] = np.concatenate([incl, strict, rev], axis=1)
    c["maskg"] = np.concatenate([strict, incl, strict, incl], axis=1)
    scale = 64 ** -0.5
    slopes = 2.0 ** (-8.0 * np.arange(1, 5) / 4)
    ki = np.arange(128)[:, None].astype(np.float64)
    qi = np.arange(512)[None, :].astype(np.float64)
    al = np.zeros((4, 5, 128, 512), np.float32)
    for h in range(4):
        al[h, 0] = (-slopes[h] * (qi - ki) / scale)
        for d in range(4):
            dist = qi - ki - 128 * d
            al[h, 1 + d] = np.where(dist >= 0, -slopes[h] * dist / scale, -1e30)
    c["alibi"] = al.transpose(2, 0, 1, 3).reshape(128, 4 * 5 * 512).copy()
    ct = np.zeros((128, 4 * 65), np.float32)
    for h in range(4):
        ct[:, h * 65:(h + 1) * 65] = (-slopes[h] * 128.0 * np.arange(65))[None, :]
    c["ctab"] = ct
    return c


CONST_SHAPES = {"ident": (128, 128), "ones": (128, 128), "hgm": (128, 384), "rwm": (128, 384),
                "maskg": (128, 512), "alibi": (128, 4 * 5 * 512), "ctab": (128, 4 * 65)}

PARAMS = [
    ("norm_mix_g", (L, D)), ("w_in", (L, D, NIN)), ("b_gate", (L, 3072)), ("hgrn_lb_param", (L, 512)),
    ("hgrn_norm_g", (L, 512)), ("diff_lambda", (L, 4, 64)), ("diff_subln_g", (L, 128)),
    ("rwkv_mu", (L, 1792)), ("rwkv_w0", (L, 512)), ("rwkv_w_up", (L, 64, 512)), ("rwkv_a0", (L, 512)),
    ("rwkv_a_up", (L, 64, 512)), ("rwkv_g_up", (L, 128, 512)), ("rwkv_k_k", (L, 512)), ("rwkv_k_a", (L, 512)),
    ("rwkv_r_k", (L, 8, 64)), ("rwkv_ln_g", (L, 512)), ("rwkv_ln_b", (L, 512)),
    ("w_branch", (L, 3, 512, D)), ("w_out", (L, D, D)), ("norm_xa_g", (L, D)), ("norm_mem_g", (L, D)),
    ("xa_wq", (L, D, D)), ("xa_wkv", (L, D, 2 * D)), ("xa_wo", (L, D, D)), ("norm_ffn_g", (L, D)),
    ("ffn_w_up", (L, D, 2 * DFF)), ("ffn_conv_w", (L, 3, DFF)), ("ffn_conv_b", (L, DFF)),
    ("ffn_w_down", (L, DFF, D)), ("final_norm_g", (D,)),
]


class Model:
    def __init__(self, S, debug=False, phases=None):
        self.S = S
        self.debug = debug
        self.phases = phases
        nc = bass.Bass("TRN2", target_bir_lowering=False)
        self.nc = nc
        self.din = {}
        self.din["x"] = nc.dram_tensor("x", [S, D], F32, kind="ExternalInput").ap()
        self.din["mem"] = nc.dram_tensor("mem", [MEM, D], F32, kind="ExternalInput").ap()
        for n, shp in PARAMS:
            self.din[n] = nc.dram_tensor(n, list(shp), F32, kind="ExternalInput").ap()
        for n, shp in CONST_SHAPES.items():
            self.din["c_" + n] = nc.dram_tensor("c_" + n, list(shp), F32, kind="ExternalInput").ap()
        self.out = nc.dram_tensor("out", [S, D], F32, kind="ExternalOutput").ap()
        self.scr = {}
        self.sched = Sched(nc)

    def scratch(self, name, shape, dt):
        kind = "ExternalOutput" if self.debug else "Internal"
        self.scr[name] = self.nc.dram_tensor(name, list(shape), dt, kind=kind).ap()
        return self.scr[name]

    def ctx(self, st):
        return Ctx(self.nc, self.sched, st)


def phase_prep(m):
    nc, S_ = m.nc, m.sched
    specs = [("w_in", "norm_mix_g", D, NIN), ("w_out", None, D, D), ("xa_wq", "norm_xa_g", D, D),
             ("xa_wkv", "norm_mem_g", D, 2 * D), ("xa_wo", None, D, D), ("ffn_w_up", "norm_ffn_g", D, 2 * DFF),
             ("ffn_w_down", None, DFF, D), ("w_branch", None, 1536, D)]
    for name, g, K, N in specs:
        m.scratch("b_" + name, [L, K, N], BF16)
    with contextlib.ExitStack() as st:
        c = m.ctx(st)
        gt = c.sb("gt", [128, 4, L, 8])
        gi = 0
        gmap = {}
        for name, g, K, N in specs:
            if g is not None:
                gmap[g] = gi
                for l in range(L):
                    c.dma(gt[:, gi, l, :], m.din[g][l].rearrange("(c p) -> p c", p=128), writes=[gt])
                gi += 1
        W = 2048
        ins = c.sbn("pin", [128, W], F32, 5)
        outs = c.sbn("pout", [128, W], BF16, 5)
        k = 0
        for name, g, K, N in specs:
            for l in range(L):
                src = m.din[name][l]
                if name == "w_branch":
                    src = src.rearrange("a k n -> (a k) n")
                dst = m.scr["b_" + name][l]
                for kc in range(K // 128):
                    for n0 in range(0, N, W):
                        w = min(W, N - n0)
                        ti, to = ins[k % 5], outs[k % 5]
                        c.dma(ti[:, 0:w], src[kc * 128:(kc + 1) * 128, n0:n0 + w], writes=[ti])
                        eng = ("dve", "act", "dve", "pool", "act")[k % 5]
                        if g is not None:
                            if eng == "act":
                                c.act(to[:, 0:w], ti[:, 0:w], AF.Copy, [ti, gt], [to], scale=gt[:, gmap[g], l, kc:kc + 1])
                            else:
                                c.ts(eng, to[:, 0:w], ti[:, 0:w], gt[:, gmap[g], l, kc:kc + 1], ALU.mult, [ti, gt], [to])
                        else:
                            c.cp(eng, to[:, 0:w], ti[:, 0:w], [ti], [to])
                        c.dma(dst[kc * 128:(kc + 1) * 128, n0:n0 + w], to[:, 0:w], reads=[to])
                        k += 1
        S_.emit()


def load_consts(m, c, names):
    r = {}
    for n in names:
        shp = CONST_SHAPES[n]
        t = c.sb("c_" + n, shp, F32)
        c.dma(t[:], m.din["c_" + n][:, :], writes=[t])
        r[n] = t
    return r


def make_bf(c, src, shape, name):
    t = c.sb(name, shape, BF16)
    c.cp("dve", t[:], src[:], [src], [t])
    return t


def norm_T(c, xsrc, hT, col0, xt, xb, junk, ss, rstd, eps, pT, identb):
    c.dma(xt[:], xsrc, writes=[xt])
    c.act(junk[:], xt[:], AF.Square, [xt], [junk, ss], accum=ss[:, 0:1])
    c.act(rstd[:, 0:1], ss[:, 0:1], AF.Sqrt, [ss, eps], [rstd], bias=eps[:, 0:1], scale=1.0 / D)
    c.recip(rstd[:, 0:1], rstd[:, 0:1], [rstd], [rstd])
    c.ts("dve", xb[:], xt[:], rstd[:, 0:1], ALU.mult, [xt, rstd], [xb])
    for kc in range(8):
        c.tr(pT[:, kc, :], xb[:, kc * 128:(kc + 1) * 128], identb[:], [xb, identb], [pT])
    c.cp("pool" if False else "act", hT[:, :, col0:col0 + 128], pT[:], [pT], [hT])


def phase_A(m, l, xsrc):
    S = m.S
    TG = min(S, 2048)
    TB = min(512, TG)
    sc = m.scr
    if "hg" not in sc:
        m.scratch("hg", [S, 2048], F32)
        m.scratch("dqT", [512, S], BF16)
        m.scratch("dkT", [512, S], BF16)
        m.scratch("dvv", [S, 512], BF16)
        m.scratch("rwz", [S, 1536], F32)
        m.scratch("rwcT", [256, S], F32)
        m.scratch("gateT", [3072, S], BF16)
    blocks = [
        (0, 512, "tok", "hg", 0, AF.Silu, F32), (512, 512, "tok", "hg", 512, AF.Sigmoid, F32),
        (1024, 512, "tok", "hg", 1024, AF.Copy, F32), (1536, 512, "tok", "hg", 1536, AF.Silu, F32),
        (2048, 512, "feat", "dqT", 0, AF.Copy, BF16), (2560, 512, "feat", "dkT", 0, AF.Copy, BF16),
        (3072, 512, "tok", "dvv", 0, AF.Copy, BF16),
        (3584, 512, "tok", "rwz", 0, AF.Copy, F32), (4096, 512, "tok", "rwz", 512, AF.Copy, F32),
        (4608, 512, "tok", "rwz", 1024, AF.Copy, F32), (5120, 256, "feat", "rwcT", 0, AF.Copy, F32),
    ] + [(5376 + i * 512, 512, "gate", "gateT", i * 512, AF.Sigmoid, BF16) for i in range(6)]
    with contextlib.ExitStack() as st:
        c = m.ctx(st)
        cs = load_consts(m, c, ["ident"])
        identb = make_bf(c, cs["ident"], [128, 128], "identb")
        eps = c.sb("eps", [128, 1])
        c.memset("dve", eps[:], 1e-6, [eps])
        bg = c.sb("bg", [128, 24])
        c.dma(bg[:], m.din["b_gate"][l].rearrange("(c p) -> p c", p=128), writes=[bg])
        hT = c.sb("hT", [128, 8, TG], BF16)
        xts = c.sbn("xt", [128, D], F32, 2)
        xbs = c.sbn("xb", [128, D], BF16, 2)
        junk = c.sb("junk", [128, D], F32)
        sss = c.sbn("ss", [128, 1], F32, 2)
        rstds = c.sbn("rstd", [128, 1], F32, 2)
        pTs = c.psn("pT", [128, 8, 128], BF16, 2)
        wbs = c.sbn("wb", [128, 8, 512], BF16, 2)
        pos = c.psn("po", [128, 512], F32, 4)
        o32 = c.sbn("o32", [128, 512], F32, 3)
        o16 = c.sbn("o16", [128, 512], BF16, 3)
        wsrc = sc["b_w_in"][l].rearrange("(c p) n -> p c n", p=128)
        it = 0
        for g0 in range(0, S, TG):
            for tt in range(TG // 128):
                i = tt % 2
                norm_T(c, xsrc[g0 + tt * 128:g0 + (tt + 1) * 128, :], hT, tt * 128, xts[i], xbs[i], junk, sss[i],
                       rstds[i], eps, pTs[i], identb)
            for bi, (c0, ncol, kind, dname, doff, func, dt) in enumerate(blocks):
                wb = wbs[bi % 2]
                c.dma(wb[:, :, 0:ncol], wsrc[:, :, c0:c0 + ncol], writes=[wb], q="sp")
                dest = sc[dname]
                if kind == "tok":
                    for tt in range(TG // 128):
                        po = pos[it % 4]
                        ot = (o32 if dt == F32 else o16)[it % 3]
                        it += 1
                        for kc in range(8):
                            c.mm(po[:, 0:ncol], hT[:, kc, tt * 128:(tt + 1) * 128], wb[:, kc, 0:ncol], kc == 0, kc == 7,
                                 [hT, wb], [po])
                        c.act(ot[:, 0:ncol], po[:, 0:ncol], func, [po], [ot])
                        c.dma(dest[g0 + tt * 128:g0 + (tt + 1) * 128, doff:doff + ncol], ot[:, 0:ncol], reads=[ot])
                else:
                    for fc in range(ncol // 128):
                        for tb in range(TG // TB):
                            po = pos[it % 4]
                            ot = (o32 if dt == F32 else o16)[it % 3]
                            it += 1
                            for kc in range(8):
                                c.mm(po[:, 0:TB], wb[:, kc, fc * 128:(fc + 1) * 128], hT[:, kc, tb * TB:(tb + 1) * TB],
                                     kc == 0, kc == 7, [hT, wb], [po])
                            if kind == "gate":
                                gc = (doff + fc * 128) // 128
                                c.act(ot[:, 0:TB], po[:, 0:TB], func, [po, bg], [ot], bias=bg[:, gc:gc + 1])
                            else:
                                c.act(ot[:, 0:TB], po[:, 0:TB], func, [po], [ot])
                            r0 = doff + fc * 128
                            c.dma(dest[r0:r0 + 128, g0 + tb * TB:g0 + (tb + 1) * TB], ot[:, 0:TB], reads=[ot])
        m.sched.emit()


def rowb(m, c, name, src1d, F):
    t = c.sb(name, [128, F], F32)
    c.dma(t[:], src1d.partition_broadcast(128), writes=[t])
    return t


def sub(parent):
    v = T(parent.t, parent.b.name + "_v")
    return v


def bc3(ap2, n=64):
    H = ap2.shape[1]
    return ap2.unsqueeze(2).to_broadcast([128, H, n])


def v3(ap2, n=64):
    return ap2.rearrange("p (h n) -> p h n", n=n)


def phase_B(m, l):
    S = m.S
    sc = m.scr
    if "brT" not in sc:
        m.scratch("brT", [3, 512, S], BF16)
    with contextlib.ExitStack() as st:
        c = m.ctx(st)
        cs = load_consts(m, c, ["ident", "hgm", "ones"])
        ident, hgm, ones = cs["ident"], cs["hgm"], cs["ones"]
        eps = c.sb("eps", [128, 1])
        c.memset("dve", eps[:], 1e-6, [eps])
        lbrow = c.sb("lbrow", [128, 512])
        omlb = c.sb("omlb", [128, 512])
        if l == 0:
            c.memset("dve", lbrow[:], 0.0, [lbrow])
        else:
            a0 = rowb(m, c, "lba0", m.din["hgrn_lb_param"][0], 512)
            a1 = rowb(m, c, "lba1", m.din["hgrn_lb_param"][1], 512)
            c.tt("dve", a1[:], a1[:], a0[:], ALU.subtract, [a0, a1], [a1])
            c.act(lbrow[:], a1[:], AF.Sigmoid, [a1], [lbrow])
        c.ts("dve", omlb[:], lbrow[:], -1.0, ALU.mult, [lbrow], [omlb], s2=1.0, op1=ALU.add)
        ngrow = rowb(m, c, "ngrow", m.din["hgrn_norm_g"][l], 512)
        Sst = [c.sbn(f"Sst{h}_", [128, 128], F32, 2) for h in range(4)]
        for h in range(4):
            c.memset("pool", Sst[h][0][:], 0.0, [Sst[h][0]])
        hgt = c.sbn("hgt", [128, 2048], F32, 2)
        fv = c.sbn("fv", [128, 512], F32, 2)
        kf = c.sbn("kf", [128, 512], F32, 2)
        lf = c.sbn("lf", [128, 512], F32, 2)
        ee = c.sbn("ee", [128, 4, 512], F32, 2)
        qk = c.sbn("qk", [128, 4, 512], F32, 2)
        FT = c.sbn("FT", [128, 3, 128], F32, 2)
        AT = c.sbn("AT", [128, 128], F32, 2)
        dcol = c.sbn("dcol", [128, 2], F32, 4)
        ssq = c.sbn("ssq", [128, 4], F32, 2)
        rstd = c.sbn("rstdh", [128, 4], F32, 2)
        gn = c.sbn("gn", [128, 512], F32, 2)
        ob = c.sbn("ob", [128, 512], BF16, 2)
        obf = c.sbn("obf", [128, 512], F32, 2)
        obT = c.sbn("obT", [128, 4, 128], BF16, 2)
        junk = c.sb("junkb", [128, 128], F32)
        pcs = c.psn("pcs", [128, 512], F32, 3)
        pTr = c.psn("pTr", [128, 512], F32, 2)
        pmis = c.psn("pmisc", [128, 512], F32, 2)
        po = c.ps("pob", [128, 512], F32)
        pTb = pTr[0]
        cn = 0
        for ti in range(S // 128):
            t0 = ti * 128
            i = ti % 2
            hg = hgt[i]
            c.dma(hg[:], sc["hg"][t0:t0 + 128, :], writes=[hg])
            q, sg, vv, gate = hg[:, 0:512], hg[:, 512:1024], hg[:, 1024:1536], hg[:, 1536:2048]
            f = fv[i]
            c.tt("pool", f[:], sg, omlb[:], ALU.mult, [hg, omlb], [f])
            c.tt("pool", f[:], f[:], lbrow[:], ALU.add, [f, lbrow], [f])
            c.ts("pool", f[:], f[:], 1e-30, ALU.max, [f], [f])
            c.ts("dve", kf[i][:], f[:], -1.0, ALU.mult, [f], [kf[i]], s2=1.0, op1=ALU.add)
            c.act(lf[i][:], f[:], AF.Ln, [f], [lf[i]])
            for k in range(3):
                c.mm(pcs[k][:], hgm[:, k * 128:(k + 1) * 128], lf[i][:], True, True, [hgm, lf[i]], [pcs[k]])
            e = ee[i]
            c.act(e[:, 0, :], pcs[1][:], AF.Exp, [pcs[1]], [e])
            c.act(e[:, 1, :], pcs[1][:], AF.Exp, [pcs[1]], [e], scale=-1.0)
            c.act(e[:, 2, :], pcs[0][:], AF.Exp, [pcs[0]], [e])
            c.act(e[:, 3, :], pcs[2][:], AF.Exp, [pcs[2]], [e])
            w = qk[i]
            c.tt("dve", w[:, 0, :], q, e[:, 0, :], ALU.mult, [hg, e], [w])
            c.tt("pool", w[:, 1, :], kf[i][:], e[:, 1, :], ALU.mult, [kf[i], e], [w])
            c.tt("dve", w[:, 2, :], q, e[:, 2, :], ALU.mult, [hg, e], [w])
            c.tt("pool", w[:, 3, :], kf[i][:], e[:, 3, :], ALU.mult, [kf[i], e], [w])
            c.tt("pool", gn[i][:], gate, ngrow[:], ALU.mult, [hg, ngrow], [gn[i]])
            for h in range(4):
                hc = slice(h * 128, (h + 1) * 128)
                ft = FT[h % 2]
                ptr = pTr[h % 2]
                psc = pkv = pcol = pmis[h % 2]
                for k in range(3):
                    c.tr(ptr[:, k * 128:(k + 1) * 128], w[:, k, hc], ident[:], [w, ident], [ptr])
                c.cp("act", ft[:].rearrange("p a b -> p (a b)"), ptr[:, 0:384], [ptr], [ft])
                c.mm(psc[:, 0:128], ft[:, 1, :], ft[:, 0, :], True, True, [ft], [psc])
                at = AT[h % 2]
                c.tt("dve", at[:], psc[:, 0:128], hgm[:, 0:128], ALU.mult, [psc, hgm], [at])
                c.mm(po[:, hc], at[:], vv[:, hc] if False else hg[:, 1024 + h * 128:1024 + (h + 1) * 128], True, False, [at, hg], [po])
                for ch in range(2):
                    P = slice(ch * 64, (ch + 1) * 64)
                    Sc = Sst[h][(2 * ti + ch) % 2]
                    Sn = Sst[h][(2 * ti + ch + 1) % 2]
                    c.mm(po[P, hc], ft[:, 2, P], Sc[:], False, ch == 1, [ft, Sc], [po])
                    dc = dcol[cn % 4]
                    cn += 1
                    c.mm(pcol[:, 256:258], lf[i][P, hc], ones[P, 0:2], True, True, [lf[i], ones], [pcol])
                    c.act(dc[:], pcol[:, 256:258], AF.Exp, [pcol], [dc])
                    c.mm(pkv[:, 128:256], w[P, 3, hc], hg[P, 1024 + h * 128:1024 + (h + 1) * 128], True, True, [w, hg], [pkv])
                    c.stt(Sn[:], Sc[:], dc[:, 0:1], pkv[:, 128:256], ALU.mult, ALU.add, [Sc, dc, pkv], [Sn])
            for h in range(4):
                hc = slice(h * 128, (h + 1) * 128)
                c.act(junk[:], po[:, hc], AF.Square, [po], [junk, ssq[i]], accum=ssq[i][:, h:h + 1])
            c.act(rstd[i][:], ssq[i][:], AF.Sqrt, [ssq[i], eps], [rstd[i]], bias=eps[:, 0:1], scale=1.0 / 128)
            c.recip(rstd[i][:], rstd[i][:], [rstd[i]], [rstd[i]])
            for h in range(4):
                hc = slice(h * 128, (h + 1) * 128)
                c.stt(obf[i][:, hc], po[:, hc], rstd[i][:, h:h + 1], gn[i][:, hc], ALU.mult, ALU.mult, [po, rstd[i], gn[i]], [obf[i]])
            for h in range(4):
                hc = slice(h * 128, (h + 1) * 128)
                c.tr(pTb[:, hc], obf[i][:, hc], ident[:], [obf[i], ident], [pTb])
            c.cp("act", obT[i][:].rearrange("p a b -> p (a b)"), pTb[:], [pTb], [obT[i]])
            c.dma(sc["brT"][0].rearrange("(h v) t -> v h t", v=128)[:, :, t0:t0 + 128], obT[i][:], reads=[obT[i]])
        m.sched.emit()


def phase_C(m, l):
    S = m.S
    sc = m.scr
    QB = min(512, S)
    nq = QB // 128
    NQ = S // QB
    NJ = S // 128
    lambda_init = 0.8 - 0.6 * math.exp(-0.3 * l)
    with contextlib.ExitStack() as st:
        c = m.ctx(st)
        cs = load_consts(m, c, ["ones", "ctab"])
        ones, ctab = cs["ones"], cs["ctab"]
        onesb = make_bf(c, ones, [128, 128], "onesb")
        eps = c.sb("eps", [128, 1])
        c.memset("dve", eps[:], 1e-6, [eps])
        lamr = rowb(m, c, "lamr", m.din["diff_lambda"][l].rearrange("a b -> (a b)"), 256)
        ltmp = c.sb("ltmp", [128, 128])
        lsum = c.sb("lsum", [128, 2])
        c.tt("dve", ltmp[:, 0:64], lamr[:, 0:64], lamr[:, 64:128], ALU.mult, [lamr], [ltmp])
        c.tt("dve", ltmp[:, 64:128], lamr[:, 128:192], lamr[:, 192:256], ALU.mult, [lamr, ltmp], [ltmp])
        c.red(lsum[:], ltmp[:].rearrange("p (a b) -> p a b", b=64), ALU.add, [ltmp], [lsum])
        c.act(lsum[:], lsum[:], AF.Exp, [lsum], [lsum])
        nlam = c.sb("nlam", [128, 1])
        c.tt("dve", nlam[:], lsum[:, 1:2], lsum[:, 0:1], ALU.subtract, [lsum], [nlam])
        c.ts("dve", nlam[:], nlam[:], -lambda_init, ALU.add, [nlam], [nlam])
        gcol = c.sb("gcol", [128, 1])
        c.dma(gcol[:], m.din["diff_subln_g"][l].rearrange("(p o) -> p o", o=1), writes=[gcol])
        c.ts("dve", gcol[:], gcol[:], 1.0 - lambda_init, ALU.mult, [gcol], [gcol])
        qT = c.sb("qT", [128, S], BF16)
        kT = c.sb("kT", [128, S], BF16)
        V = c.sb("V", [128, NJ, 128], BF16)
        AL = c.sb("AL", [128, 5, 512], F32)
        TMP = c.sbn("tmpc", [128, 512], F32, 3)
        PT = c.sbn("ptc", [128, 512], BF16, 4)
        PST = c.psn("pst", [128, 512], F32, 4)
        PO = c.psn("poc", [128, 512], F32, 2)
        PL = c.psn("plc", [128, 512], F32, 2)
        rl = c.sbn("rlc", [128, 512], F32, 2)
        oc = c.sbn("occ", [128, 512], F32, 2)
        od = c.sb("odc", [128, 512], F32)
        sq = c.sb("sqc", [128, 512], F32)
        rs = c.sb("rsc", [128, 512], F32)
        obo = c.sbn("oboc", [128, 512], BF16, 2)
        cnt = 0
        for h in range(4):
            c.dma(qT[:], sc["dqT"][h * 128:(h + 1) * 128, :], writes=[qT], q="sp")
            c.dma(kT[:], sc["dkT"][h * 128:(h + 1) * 128, :], writes=[kT], q="act")
            vsrc = sc["dvv"].rearrange("(j p) c -> p j c", p=128)
            for j0 in range(0, NJ, 8):
                j1 = min(NJ, j0 + 8)
                c.dma(V[:, j0:j1, :], vsrc[:, j0:j1, h * 128:(h + 1) * 128], writes=[V], q=("pool", "sp", "act")[(j0 // 8) % 3])
            c.dma(AL[:].rearrange("p a b -> p (a b)"), m.din["c_alibi"][:, h * 2560:(h + 1) * 2560], writes=[AL], q="sp")
            for I in range(NQ):
                qs = slice(I * QB, (I + 1) * QB)
                jmax = nq * (I + 1) - 1
                units = [(j, c2) for j in range(jmax + 1) for c2 in range(2)]
                LA = 2
                pend = []

                def stage1(j, c2):
                    nonlocal cnt
                    d = j - nq * I
                    var = 0 if d < 0 else 1 + d
                    mc = (nq * I - j) if d < 0 else 0
                    P = slice(c2 * 64, (c2 + 1) * 64)
                    pst = PST[cnt % 4]
                    tmp = TMP[cnt % 3]
                    pt = PT[cnt % 4]
                    cnt += 1
                    c.mm(pst[:, 0:QB], kT[P, j * 128:(j + 1) * 128], qT[P, qs], True, True, [kT, qT], [pst])
                    c.tt("dve", tmp[:, 0:QB], pst[:, 0:QB], AL[:, var, 0:QB], ALU.add, [pst, AL], [tmp])
                    c.act(pt[:, 0:QB], tmp[:, 0:QB], AF.Exp, [tmp, ctab], [pt], bias=ctab[:, h * 65 + mc:h * 65 + mc + 1], scale=0.125)
                    return pt

                def stage2(j, c2, pt):
                    c.mm(PO[c2][:, 0:QB], V[:, j, :], pt[:, 0:QB], j == 0, j == jmax, [V, pt], [PO[c2]])
                    c.mm(PL[c2][:, 0:QB], onesb[:], pt[:, 0:QB], j == 0, j == jmax, [onesb, pt], [PL[c2]])

                for ui in range(0, len(units) + LA, 2):
                    for uu in (ui, ui + 1):
                        if uu < len(units):
                            j, c2 = units[uu]
                            pend.append((j, c2, stage1(j, c2)))
                    for uu in (ui, ui + 1):
                        if uu >= LA and pend and uu - LA < len(units):
                            stage2(*pend.pop(0))
                while pend:
                    stage2(*pend.pop(0))
                for c2 in range(2):
                    c.recip(rl[c2][:, 0:QB], PL[c2][:, 0:QB], [PL[c2]], [rl[c2]])
                    c.tt("dve", oc[c2][:, 0:QB], PO[c2][:, 0:QB], rl[c2][:, 0:QB], ALU.mult, [PO[c2], rl[c2]], [oc[c2]])
                c.stt(od[:, 0:QB], oc[1][:, 0:QB], nlam[:, 0:1], oc[0][:, 0:QB], ALU.mult, ALU.add, [oc[0], oc[1], nlam], [od])
                c.tt("pool", sq[:, 0:QB], od[:, 0:QB], od[:, 0:QB], ALU.mult, [od], [sq])
                pss = PST[cnt % 4]
                cnt += 1
                c.mm(pss[:, 0:QB], ones[:], sq[:, 0:QB], True, True, [ones, sq], [pss])
                c.act(rs[:, 0:QB], pss[:, 0:QB], AF.Sqrt, [pss, eps], [rs], bias=eps[:, 0:1], scale=1.0 / 128)
                c.recip(rs[:, 0:QB], rs[:, 0:QB], [rs], [rs])
                o_ = obo[I % 2]
                c.stt(o_[:, 0:QB], od[:, 0:QB], gcol[:, 0:1], rs[:, 0:QB], ALU.mult, ALU.mult, [od, gcol, rs], [o_])
                c.dma(sc["brT"][1][h * 128:(h + 1) * 128, qs], o_[:, 0:QB], reads=[o_])
        m.sched.emit()


def phase_E(m, l, xsrc, xdst):
    S = m.S
    sc = m.scr
    TB = min(512, S)
    with contextlib.ExitStack() as st:
        c = m.ctx(st)
        wbr = c.sb("wbr", [128, 12, D], BF16)
        wout = c.sb("wout", [128, 8, D], BF16)
        c.dma(wbr[:], sc["b_w_branch"][l].rearrange("(c p) n -> p c n", p=128), writes=[wbr], q="sp")
        c.dma(wout[:], sc["b_w_out"][l].rearrange("(c p) n -> p c n", p=128), writes=[wout], q="act")
        br = c.sbn("br", [128, 12, TB], BF16, 2)
        gt = c.sbn("gte", [128, 24, TB], BF16, 2)
        acc = c.sbn("acce", [128, TB], F32, 2)
        tmp = c.sbn("tmpe", [128, TB], F32, 3)
        mT = c.sbn("mT", [128, 8, TB], BF16, 2)
        xt = c.sbn("xte", [128, D], F32, 2)
        xo = c.sbn("xoe", [128, D], F32, 2)
        PM = c.psn("pme", [128, 512], F32, 4)
        PO = c.psn("poe", [128, 512], F32, 4)
        k = 0
        k2 = 0
        for tb in range(S // TB):
            ts_ = slice(tb * TB, (tb + 1) * TB)
            b_, g_, m_ = br[tb % 2], gt[tb % 2], mT[tb % 2]
            c.dma(b_[:], sc["brT"].rearrange("n (c p) t -> p (n c) t", p=128)[:, :, ts_], writes=[b_], q="sp")
            c.dma(g_[:], sc["gateT"].rearrange("(c p) t -> p c t", p=128)[:, :, ts_], writes=[g_], q="act")
            for dmc in range(8):
                a_ = acc[dmc % 2]
                for n in range(3):
                    pm = PM[k % 4]
                    for kc in range(4):
                        c.mm(pm[:, 0:TB], wbr[:, n * 4 + kc, dmc * 128:(dmc + 1) * 128], b_[:, n * 4 + kc, :], kc == 0, kc == 3, [wbr, b_], [pm])
                    if n == 0:
                        c.tt("dve", a_[:], pm[:, 0:TB], g_[:, n * 8 + dmc, :], ALU.mult, [pm, g_], [a_])
                    else:
                        t_ = tmp[k % 3]
                        c.tt("dve", t_[:], pm[:, 0:TB], g_[:, n * 8 + dmc, :], ALU.mult, [pm, g_], [t_])
                        if n == 1:
                            c.tt("pool", a_[:], a_[:], t_[:], ALU.add, [a_, t_], [a_])
                        else:
                            c.tt("pool", m_[:, dmc, :], a_[:], t_[:], ALU.add, [a_, t_], [m_])
                    k += 1
            for tt in range(TB // 128):
                x_, o_ = xt[k2 % 2], xo[k2 % 2]
                r0 = tb * TB + tt * 128
                c.dma(x_[:], xsrc[r0:r0 + 128, :], writes=[x_], q="pool")
                for cb in range(2):
                    po = PO[(2 * k2 + cb) % 4]
                    for kc in range(8):
                        c.mm(po[:], m_[:, kc, tt * 128:(tt + 1) * 128], wout[:, kc, cb * 512:(cb + 1) * 512], kc == 0, kc == 7, [m_, wout], [po])
                    c.tt("dve", o_[:, cb * 512:(cb + 1) * 512], po[:], x_[:, cb * 512:(cb + 1) * 512], ALU.add, [po, x_], [o_])
                c.dma(xdst[r0:r0 + 128, :], o_[:], reads=[o_], q="sp")
                k2 += 1
        m.sched.emit()


class NormBufs:
    def __init__(self, c, tag):
        self.xts = c.sbn("xt" + tag, [128, D], F32, 2)
        self.xbs = c.sbn("xb" + tag, [128, D], BF16, 2)
        self.junk = c.sb("junk" + tag, [128, D], F32)
        self.sss = c.sbn("ss" + tag, [128, 1], F32, 2)
        self.rstds = c.sbn("rstd" + tag, [128, 1], F32, 2)
        self.pTs = c.psn("pT" + tag, [128, 8, 128], BF16, 2)
        self.eps = c.sb("eps" + tag, [128, 1])
        c.memset("dve", self.eps[:], 1e-6, [self.eps])
        self.n = 0

    def run(self, c, xsrc, hT, col0, identb):
        i = self.n % 2
        self.n += 1
        norm_T(c, xsrc, hT, col0, self.xts[i], self.xbs[i], self.junk, self.sss[i], self.rstds[i], self.eps,
               self.pTs[i], identb)


def phase_F(m, l, xsrc, xdst):
    S = m.S
    sc = m.scr
    TB = min(512, S)
    with contextlib.ExitStack() as st:
        c = m.ctx(st)
        cs = load_consts(m, c, ["ident", "ones"])
        identb = make_bf(c, cs["ident"], [128, 128], "identb")
        onesb = make_bf(c, cs["ones"], [128, 128], "onesb")
        nb = NormBufs(c, "f")
        wq = c.sb("wq", [128, 8, D], BF16)
        wkv = c.sb("wkv", [128, 8, 2 * D], BF16)
        wo = c.sb("wo", [128, 8, D], BF16)
        c.dma(wq[:], sc["b_xa_wq"][l].rearrange("(c p) n -> p c n", p=128), writes=[wq], q="sp")
        c.dma(wkv[:], sc["b_xa_wkv"][l].rearrange("(c p) n -> p c n", p=128), writes=[wkv], q="act")
        c.dma(wo[:], sc["b_xa_wo"][l].rearrange("(c p) n -> p c n", p=128), writes=[wo], q="pool")
        memT = c.sb("memT", [128, 8, MEM], BF16)
        KT = c.sb("KT", [128, 8, MEM], BF16)
        Vm = c.sb("Vm", [128, 2, D], BF16)
        PA = c.psn("paf", [128, 512], F32, 3)
        for mt in range(2):
            nb.run(c, m.din["mem"][mt * 128:(mt + 1) * 128, :], memT, mt * 128, identb)
        k = 0
        for fc in range(8):
            pa = PA[k % 3]
            k += 1
            for kc in range(8):
                c.mm(pa[:, 0:MEM], wkv[:, kc, fc * 128:(fc + 1) * 128], memT[:, kc, :], kc == 0, kc == 7, [wkv, memT], [pa])
            c.cp("act", KT[:, fc, :], pa[:, 0:MEM], [pa], [KT])
        for mt in range(2):
            for cb in range(2):
                pa = PA[k % 3]
                k += 1
                for kc in range(8):
                    c.mm(pa[:], memT[:, kc, mt * 128:(mt + 1) * 128], wkv[:, kc, D + cb * 512:D + (cb + 1) * 512], kc == 0, kc == 7, [wkv, memT], [pa])
                c.cp("act", Vm[:, mt, cb * 512:(cb + 1) * 512], pa[:], [pa], [Vm])
        hT = c.sbn("hTf", [128, 8, TB], BF16, 2)
        qT = c.sbn("qTf", [128, 8, TB], BF16, 2)
        oT = c.sbn("oTf", [128, 8, TB], BF16, 2)
        pt = c.sbn("ptf", [128, 2, TB], BF16, 3)
        rl = c.sbn("rlf", [128, TB], F32, 2)
        xt = c.sbn("xtf", [128, D], F32, 2)
        xo = c.sbn("xof", [128, D], F32, 2)
        PB = c.psn("pbf", [128, 512], F32, 3)
        k2 = 0
        kp = 0
        for tb in range(S // TB):
            h_, q_, o_ = hT[tb % 2], qT[tb % 2], oT[tb % 2]
            for tt in range(TB // 128):
                r0 = tb * TB + tt * 128
                nb.run(c, xsrc[r0:r0 + 128, :], h_, tt * 128, identb)
            for fc in range(8):
                pa = PA[k % 3]
                k += 1
                for kc in range(8):
                    c.mm(pa[:, 0:TB], wq[:, kc, fc * 128:(fc + 1) * 128], h_[:, kc, :], kc == 0, kc == 7, [wq, h_], [pa])
                c.cp("act", q_[:, fc, :], pa[:, 0:TB], [pa], [q_])
            for h in range(4):
                p_ = pt[kp % 3]
                kp += 1
                for mc in range(2):
                    pa = PA[k % 3]
                    k += 1
                    for dc in range(2):
                        c.mm(pa[:, 0:TB], KT[:, h * 2 + dc, mc * 128:(mc + 1) * 128], q_[:, h * 2 + dc, :], dc == 0, dc == 1, [KT, q_], [pa])
                    c.act(p_[:, mc, :], pa[:, 0:TB], AF.Exp, [pa], [p_], scale=1.0 / 16)
                pl = PB[2]
                for mc in range(2):
                    c.mm(pl[:, 0:TB], onesb[:], p_[:, mc, :], mc == 0, mc == 1, [onesb, p_], [pl])
                r_ = rl[h % 2]
                c.recip(r_[:], pl[:, 0:TB], [pl], [r_])
                for dc in range(2):
                    pb = PB[dc]
                    for mc in range(2):
                        c.mm(pb[:, 0:TB], Vm[:, mc, h * 256 + dc * 128:h * 256 + (dc + 1) * 128], p_[:, mc, :], mc == 0, mc == 1, [Vm, p_], [pb])
                    c.tt("dve", o_[:, h * 2 + dc, :], pb[:, 0:TB], r_[:], ALU.mult, [pb, r_], [o_])
            for tt in range(TB // 128):
                x_, xo_ = xt[k2 % 2], xo[k2 % 2]
                r0 = tb * TB + tt * 128
                c.dma(x_[:], xsrc[r0:r0 + 128, :], writes=[x_], q="pool")
                for cb in range(2):
                    pa = PA[k % 3]
                    k += 1
                    for kc in range(8):
                        c.mm(pa[:], o_[:, kc, tt * 128:(tt + 1) * 128], wo[:, kc, cb * 512:(cb + 1) * 512], kc == 0, kc == 7, [o_, wo], [pa])
                    c.tt("dve", xo_[:, cb * 512:(cb + 1) * 512], pa[:], x_[:, cb * 512:(cb + 1) * 512], ALU.add, [pa, x_], [xo_])
                c.dma(xdst[r0:r0 + 128, :], xo_[:], reads=[xo_], q="sp")
                k2 += 1
        m.sched.emit()


def phase_G(m, l, xsrc, xdst):
    S = m.S
    sc = m.scr
    TBK = min(1024, S)
    SB = min(512, TBK)
    NFC = DFF // 128
    with contextlib.ExitStack() as st:
        c = m.ctx(st)
        cs = load_consts(m, c, ["ident"])
        identb = make_bf(c, cs["ident"], [128, 128], "identb")
        nb = NormBufs(c, "g")
        wdn = c.sb("wdn", [128, NFC, D], BF16)
        c.dma(wdn[:], sc["b_ffn_w_down"][l].rearrange("(c p) n -> p c n", p=128), writes=[wdn], q="pool")
        cw = c.sb("cw", [128, NFC, 3])
        for j in range(3):
            c.dma(cw[:, :, j], m.din["ffn_conv_w"][l, j].rearrange("(c p) -> p c", p=128), writes=[cw])
        cb_ = c.sb("cbias", [128, NFC])
        c.dma(cb_[:], m.din["ffn_conv_b"][l].rearrange("(c p) -> p c", p=128), writes=[cb_])
        halo = c.sb("halo", [128, NFC, 2])
        c.memset("dve", halo[:], 0.0, [halo])
        hT = c.sb("hTg", [128, 8, TBK], BF16)
        hid = c.sb("hid", [128, NFC, TBK], BF16)
        wu = c.sbn("wu", [128, 8, 512], BF16, 2)
        wv = c.sbn("wv", [128, 8, 512], BF16, 2)
        uext = c.sbn("uext", [128, SB + 2], F32, 2)
        t1 = c.sbn("t1g", [128, SB], F32, 2)
        t2 = c.sbn("t2g", [128, SB], F32, 2)
        sg = c.sbn("sgg", [128, SB], F32, 2)
        xt = c.sbn("xtg", [128, D], F32, 2)
        xo = c.sbn("xog", [128, D], F32, 2)
        PU = c.psn("pug", [128, 512], F32, 2)
        PV = c.psn("pvg", [128, 512], F32, 2)
        PO = c.psn("pog", [128, 512], F32, 2)
        wsrc = sc["b_ffn_w_up"][l].rearrange("(c p) n -> p c n", p=128)
        k = 0
        k2 = 0
        for tbk in range(S // TBK):
            for tt in range(TBK // 128):
                r0 = tbk * TBK + tt * 128
                nb.run(c, xsrc[r0:r0 + 128, :], hT, tt * 128, identb)
            for grp in range(6):
                ncol = 512 if grp < 5 else 256
                wu_, wv_ = wu[grp % 2], wv[grp % 2]
                c.dma(wu_[:, :, 0:ncol], wsrc[:, :, grp * 512:grp * 512 + ncol], writes=[wu_], q="sp")
                c.dma(wv_[:, :, 0:ncol], wsrc[:, :, DFF + grp * 512:DFF + grp * 512 + ncol], writes=[wv_], q="act")
                for fcl in range(ncol // 128):
                    fc = grp * 4 + fcl
                    for sbi in range(TBK // SB):
                        ss_ = slice(sbi * SB, (sbi + 1) * SB)
                        pu, pv = PU[k % 2], PV[k % 2]
                        ue, a1, a2, s_ = uext[k % 2], t1[k % 2], t2[k % 2], sg[k % 2]
                        k += 1
                        for kc in range(8):
                            c.mm(pu[:, 0:SB], wu_[:, kc, fcl * 128:(fcl + 1) * 128], hT[:, kc, ss_], kc == 0, kc == 7, [wu_, hT], [pu])
                        for kc in range(8):
                            c.mm(pv[:, 0:SB], wv_[:, kc, fcl * 128:(fcl + 1) * 128], hT[:, kc, ss_], kc == 0, kc == 7, [wv_, hT], [pv])
                        c.cp("pool", ue[:, 0:2], halo[:, fc, :], [halo], [ue])
                        c.cp("act", ue[:, 2:SB + 2], pu[:, 0:SB], [pu], [ue])
                        c.cp("pool", halo[:, fc, :], ue[:, SB:SB + 2], [ue], [halo])
                        c.act(a1[:], pu[:, 0:SB], AF.Identity, [pu, cw, cb_], [a1], bias=cb_[:, fc:fc + 1], scale=cw[:, fc, 2:3])
                        c.stt(a2[:], ue[:, 1:SB + 1], cw[:, fc, 1:2], a1[:], ALU.mult, ALU.add, [ue, cw, a1], [a2])
                        c.stt(a1[:], ue[:, 0:SB], cw[:, fc, 0:1], a2[:], ALU.mult, ALU.add, [ue, cw, a2], [a1])
                        c.act(s_[:], a1[:], AF.Silu, [a1], [s_])
                        c.tt("dve", hid[:, fc, ss_], s_[:], pv[:, 0:SB], ALU.mult, [s_, pv], [hid])
            for tt in range(TBK // 128):
                x_, xo_ = xt[k2 % 2], xo[k2 % 2]
                r0 = tbk * TBK + tt * 128
                c.dma(x_[:], xsrc[r0:r0 + 128, :], writes=[x_], q="pool")
                for cb in range(2):
                    po = PO[cb]
                    for fc in range(NFC):
                        c.mm(po[:], hid[:, fc, tt * 128:(tt + 1) * 128], wdn[:, fc, cb * 512:(cb + 1) * 512], fc == 0, fc == NFC - 1, [hid, wdn], [po])
                    c.tt("dve", xo_[:, cb * 512:(cb + 1) * 512], po[:], x_[:, cb * 512:(cb + 1) * 512], ALU.add, [po, x_], [xo_])
                c.dma(xdst[r0:r0 + 128, :], xo_[:], reads=[xo_], q="sp")
                k2 += 1
        m.sched.emit()


def phase_H(m, xsrc):
    S = m.S
    with contextlib.ExitStack() as st:
        c = m.ctx(st)
        grow = rowb(m, c, "fgrow", m.din["final_norm_g"], D)
        eps = c.sb("eps", [128, 1])
        c.memset("dve", eps[:], 1e-6, [eps])
        xt = c.sbn("xth", [128, D], F32, 3)
        xo = c.sbn("xoh", [128, D], F32, 3)
        junk = c.sb("junkh", [128, D], F32)
        ss = c.sbn("ssh", [128, 1], F32, 3)
        for ti in range(S // 128):
            i = ti % 3
            c.dma(xt[i][:], xsrc[ti * 128:(ti + 1) * 128, :], writes=[xt[i]])
            c.act(junk[:], xt[i][:], AF.Square, [xt[i]], [junk, ss[i]], accum=ss[i][:, 0:1])
            c.act(ss[i][:], ss[i][:], AF.Sqrt, [ss[i], eps], [ss[i]], bias=eps[:, 0:1], scale=1.0 / D)
            c.recip(ss[i][:], ss[i][:], [ss[i]], [ss[i]])
            c.stt(xo[i][:], xt[i][:], ss[i][:, 0:1], grow[:], ALU.mult, ALU.mult, [xt[i], ss[i], grow], [xo[i]])
            c.dma(m.out[ti * 128:(ti + 1) * 128, :], xo[i][:], reads=[xo[i]])
        m.sched.emit()


def phase_D(m, l):
    S = m.S
    sc = m.scr
    with contextlib.ExitStack() as st:
        c = m.ctx(st)
        cs = load_consts(m, c, ["ident", "rwm", "maskg", "ones"])
        ident, rwm, maskg, ones = cs["ident"], cs["rwm"], cs["maskg"], cs["ones"]
        din = m.din
        w0r = rowb(m, c, "w0r", din["rwkv_w0"][l], 512)
        a0r = rowb(m, c, "a0r", din["rwkv_a0"][l], 512)
        kkr = rowb(m, c, "kkr", din["rwkv_k_k"][l], 512)
        kar = rowb(m, c, "kar", din["rwkv_k_a"][l], 512)
        rkr = rowb(m, c, "rkr", din["rwkv_r_k"][l].rearrange("a b -> (a b)"), 512)
        lngr = rowb(m, c, "lngr", din["rwkv_ln_g"][l], 512)
        lnbr = rowb(m, c, "lnbr", din["rwkv_ln_b"][l], 512)
        mur = rowb(m, c, "mur", din["rwkv_mu"][l][0:1536], 1536)
        mucol = c.sb("mucol", [128, 2])
        c.dma(mucol[:], din["rwkv_mu"][l][1536:1792].rearrange("(c p) -> p c", p=128), writes=[mucol])
        WUP = c.sb("WUP", [128, 512])
        c.dma(WUP[0:64, :], din["rwkv_w_up"][l], writes=[WUP])
        c.dma(WUP[64:128, :], din["rwkv_a_up"][l], writes=[WUP])
        GUP = c.sb("GUP", [128, 512])
        c.dma(GUP[:], din["rwkv_g_up"][l], writes=[GUP])
        epsg = c.sb("epsg", [128, 1])
        c.memset("dve", epsg[:], 64e-5, [epsg])
        STs = c.sbn("STs", [128, 4, 64], F32, 2)
        c.memset("dve", STs[0][:], 0.0, [STs[0]])
        Gbd = c.sbn("Gbd", [128, 128], F32, 4)
        for p in range(4):
            c.memset("pool", Gbd[p][:], 0.0, [Gbd[p]])
        Hs = c.sb("Hs", [128, 4, 64])
        z = c.sbn("z", [128, 1536], F32, 2)
        zp = c.sbn("zp", [128, 1536], F32, 2)
        zm = c.sbn("zm", [128, 1536], F32, 2)
        cod = c.sbn("cod", [128, 2, 129], F32, 2)
        dcd = c.sb("dcd", [128, 2, 128])
        cm = c.sb("cm", [128, 2, 128])
        lw = c.sb("lw", [128, 128])
        sgd = c.sb("sgd", [128, 128])
        tmpw = c.sb("tmpw", [128, 512])
        sigw = c.sb("sigw", [128, 512])
        av = c.sb("av", [128, 512])
        gv = c.sb("gv", [128, 512])
        kkraw = c.sb("kkraw", [128, 512])
        sqk = c.sb("sqk", [128, 512])
        s8 = c.sbn("s8_", [128, 8], F32, 6)
        kk = c.sb("kk", [128, 512])
        kmod = c.sb("kmod", [128, 512])
        bv = c.sb("bv", [128, 512])
        ee = c.sb("eed", [128, 4, 512])
        TM = c.sb("TM", [128, 4, 512])
        BH = c.sb("BH", [128, 512])
        KH = c.sb("KH", [128, 512])
        PCc = c.sb("PCc", [128, 4])
        FT = c.sbn("FTd", [128, 4, 128], F32, 4)
        GM = c.sbn("GM", [128, 512], F32, 2)
        Lc = c.sbn("Lc", [128, 2, 128], BF16, 4)
        Wb = c.sbn("Wb", [128, 128], BF16, 4)
        Wc = c.sbn("Wc", [128, 128], F32, 4)
        LcH = [Lc, c.sbn("Lc1_", [128, 2, 128], BF16, 4)]
        WcH = [Wc, c.sbn("Wc1_", [128, 128], F32, 4)]
        WbH = [Wb, c.sbn("Wb1_", [128, 128], BF16, 4)]
        Rp = c.sb("Rp", [128, 512])
        Yl = c.sb("Yl", [128, 512])
        RT = c.sbn("RTd", [128, 128], F32, 2)
        yv = c.sb("yv", [128, 512])
        yc = c.sb("yc", [128, 512])
        sq2 = c.sb("sq2", [128, 512])
        bon = c.sb("bon", [128, 512])
        yo = c.sb("yo", [128, 512])
        obT = c.sbn("obTd", [128, 4, 128], BF16, 2)
        B0 = c.ps("B0", [128, 512]); B1 = c.ps("B1", [128, 512]); B2 = c.ps("B2", [128, 512])
        B3 = c.ps("B3", [128, 512]); B4 = c.ps("B4", [128, 512]); B5 = c.ps("B5", [128, 512])
        B6 = c.ps("B6", [128, 512]); B7 = c.ps("B7", [128, 512])
        pL = B5
        pY = pGs = pH = pS = pRT = B6
        rwcsrc = sc["rwcT"].rearrange("(c p) t -> p c t", p=128)
        for ti in range(S // 128):
            t0 = ti * 128
            i = ti % 2
            z_, zp_, zm_, cod_ = z[i], zp[i], zm[i], cod[i]
            c.dma(z_[:], sc["rwz"][t0:t0 + 128, :], writes=[z_], q="sp")
            if ti == 0:
                c.memset("pool", zp_[0:1, :], 0.0, [zp_])
                c.dma(zp_[1:128, :], sc["rwz"][0:127, :], writes=[zp_], q="act")
                c.memset("pool", cod_[:, :, 0:1], 0.0, [cod_])
                c.dma(cod_[:, :, 1:129], rwcsrc[:, :, 0:128], writes=[cod_], q="pool")
            else:
                c.dma(zp_[:], sc["rwz"][t0 - 1:t0 + 127, :], writes=[zp_], q="act")
                c.dma(cod_[:], rwcsrc[:, :, t0 - 1:t0 + 128], writes=[cod_], q="pool")
            c.tt("dve", zm_[:], zp_[:], z_[:], ALU.subtract, [zp_, z_], [zm_])
            c.tt("pool", zm_[:], zm_[:], mur[:], ALU.mult, [zm_, mur], [zm_])
            c.tt("dve", zm_[:], zm_[:], z_[:], ALU.add, [zm_, z_], [zm_])
            r_, k_, v_ = zm_[:, 0:512], zm_[:, 512:1024], zm_[:, 1024:1536]
            c.tt("pool", dcd[:], cod_[:, :, 0:128], cod_[:, :, 1:129], ALU.subtract, [cod_], [dcd])
            for ch in range(2):
                c.stt(cm[:, ch, :], dcd[:, ch, :], mucol[:, ch:ch + 1], cod_[:, ch, 1:129], ALU.mult, ALU.add, [dcd, mucol, cod_], [cm])
            c.act(lw[0:64, :], cm[0:64, 0, :], AF.Tanh, [cm], [lw])
            c.cp("pool", lw[64:128, :], cm[64:128, 0, :], [cm], [lw])
            c.act(sgd[:], cm[:, 1, :], AF.Sigmoid, [cm], [sgd])
            c.mm(B0[:], lw[0:64, :], WUP[0:64, :], True, True, [lw, WUP], [B0])
            c.mm(B1[:], lw[64:128, :], WUP[64:128, :], True, True, [lw, WUP], [B1])
            c.mm(B2[:], sgd[:], GUP[:], True, True, [sgd, GUP], [B2])
            c.tt("dve", tmpw[:], B0[:], w0r[:], ALU.add, [B0, w0r], [tmpw])
            c.act(sigw[:], tmpw[:], AF.Sigmoid, [tmpw], [sigw])
            c.tt("dve", tmpw[:], B1[:], a0r[:], ALU.add, [B1, a0r, sigw], [tmpw])
            c.act(av[:], tmpw[:], AF.Sigmoid, [tmpw], [av])
            c.cp("act", gv[:], B2[:], [B2], [gv])
            c.tt("pool", kkraw[:], k_, kkr[:], ALU.mult, [zm_, kkr], [kkraw])
            c.tt("pool", sqk[:], kkraw[:], kkraw[:], ALU.mult, [kkraw], [sqk])
            c.red(s8[0][:], v3(sqk[:]), ALU.add, [sqk], [s8[0]])
            c.act(s8[0][:], s8[0][:], AF.Sqrt, [s8[0]], [s8[0]])
            c.ts("dve", s8[0][:], s8[0][:], 1e-12, ALU.max, [s8[0]], [s8[0]])
            c.recip(s8[0][:], s8[0][:], [s8[0]], [s8[0]])
            c.tt("dve", v3(kk[:]), v3(kkraw[:]), bc3(s8[0][:]), ALU.mult, [kkraw, s8[0]], [kk])
            c.stt(kmod[:], av[:], -1.0, kar[:], ALU.add, ALU.mult, [av, kar], [kmod])
            c.stt(kmod[:], kmod[:], 1.0, k_, ALU.add, ALU.mult, [kmod, zm_], [kmod])
            c.tt("pool", bv[:], kk[:], av[:], ALU.mult, [kk, av], [bv])
            c.mm(B0[:], rwm[:, 0:128], sigw[:], True, True, [rwm, sigw], [B0])
            c.mm(B1[:], rwm[:, 128:256], sigw[:], True, True, [rwm, sigw], [B1])
            c.mm(B2[:], rwm[:, 256:384], sigw[:], True, True, [rwm, sigw], [B2])
            c.act(ee[:, 0, :], B1[:], AF.Exp, [B1], [ee], scale=-LAM)
            c.act(ee[:, 1, :], B0[:], AF.Exp, [B0], [ee], scale=-LAM)
            c.act(ee[:, 2, :], B0[:], AF.Exp, [B0], [ee], scale=LAM)
            c.act(ee[:, 3, :], B2[:], AF.Exp, [B2], [ee], scale=-LAM)
            c.stt(TM[:, 0, :], kk[:], -1.0, ee[:, 0, :], ALU.mult, ALU.mult, [kk, ee], [TM])
            c.tt("pool", TM[:, 1, :], r_, ee[:, 1, :], ALU.mult, [zm_, ee], [TM])
            c.tt("dve", TM[:, 2, :], bv[:], ee[:, 2, :], ALU.mult, [bv, ee], [TM])
            c.tt("pool", TM[:, 3, :], kmod[:], ee[:, 2, :], ALU.mult, [kmod, ee], [TM])
            c.tt("dve", BH[:], bv[:], ee[:, 3, :], ALU.mult, [bv, ee], [BH])
            c.tt("pool", KH[:], kmod[:], ee[:, 3, :], ALU.mult, [kmod, ee], [KH])
            for p in range(4):
                c.mm(B3[:, 2 * p:2 * p + 2], sigw[:, p * 128:(p + 1) * 128], ones[:, 0:2], True, True, [sigw, ones], [B3])
            c.act(PCc[:], B3[:, 0:8].rearrange("p (a b) -> p a b", b=2)[:, :, 0], AF.Exp, [B3], [PCc], scale=-LAM)
            STc, STn = STs[ti % 2], STs[(ti + 1) % 2]
            for p in range(4):
                pc = slice(p * 128, (p + 1) * 128)
                ft = FT[p]
                for q in range(4):
                    c.tr(B3[:, q * 128:(q + 1) * 128], TM[:, q, pc], ident[:], [TM, ident], [B3])
                c.cp("act", ft[:].rearrange("p a b -> p (a b)"), B3[:], [B3], [ft])
                def head_gen(hl, p=p, ft=ft):
                    h = 2 * p + hl
                    P = slice(hl * 64, (hl + 1) * 64)
                    hc = slice(h * 64, (h + 1) * 64)
                    vh = zm_[:, 1024 + h * 64:1024 + (h + 1) * 64]
                    bG = (B4, B0)[hl]
                    bD = (B5, B2)[hl]
                    gm = GM[hl]
                    Lr, Wr, Wbr = LcH[hl], WcH[hl], WbH[hl]
                    c.mm(bG[:, 0:256], ft[P, 2, :], ft[P, 0:2, :], True, True, [ft], [bG])
                    c.mm(bG[:, 256:512], ft[P, 3, :], ft[P, 0:2, :], True, True, [ft], [bG])
                    c.tt("dve", gm[:], bG[:], maskg[:], ALU.mult, [bG, maskg], [gm])
                    yield
                    LabT, MrbT, LakT, MrkT = gm[:, 0:128], gm[:, 128:256], gm[:, 256:384], gm[:, 384:512]
                    lc, wc, wb = Lr[0], Wr[0], Wbr[0]
                    c.tr(bD[:, 384:512], LabT, ident[:], [gm, ident], [bD])
                    c.mm(bD[:, 0:64], LakT, vh, True, True, [gm, zm_], [bD])
                    c.cp("act", lc[:, 0, :], bD[:, 384:512], [bD], [lc])
                    c.cp("pool", lc[:, 1, :], LabT, [gm], [lc])
                    c.cp("act", wc[:, 64:128], bD[:, 0:64], [bD], [wc])
                    c.cp("pool", wc[:, 0:64], TM[:, 0, hc], [TM], [wc])
                    c.cp("pool", wb[:], wc[:], [wc], [wb])
                    yield
                    for j in range(7):
                        lcn, wcn, wbn = Lr[(j + 1) % 4], Wr[(j + 1) % 4], Wbr[(j + 1) % 4]
                        c.mm(bD[:, 0:128], lc[:, 1, :], wb[:], True, True, [lc, wb], [bD])
                        if j < 6:
                            c.mm(bD[:, 128:256], lc[:, 1, :], lc[:, 0, :], True, True, [lc], [bD])
                            c.mm(bD[:, 256:384], lc[:, 0, :], lc[:, 1, :], True, True, [lc], [bD])
                        c.tt("dve", wcn[:], bD[:, 0:128], wc[:], ALU.add, [bD, wc], [wcn])
                        if j < 6:
                            c.cp("act", lcn[:].rearrange("p a b -> p (a b)"), bD[:, 128:384], [bD], [lcn])
                            c.cp("pool", wbn[:], wcn[:], [wcn], [wbn])
                        lc, wc, wb = lcn, wcn, wbn
                        yield
                    c.mm(bG[:, 0:128], MrbT, wc[:], True, False, [gm, wc], [bG])
                    c.mm(bG[:, 64:128], MrkT, vh, False, True, [gm, zm_], [bG])
                    gcol = slice(128 + hl * 64, 128 + (hl + 1) * 64)
                    c.mm(bG[P, gcol], wc[:, 0:64], BH[:, hc], True, True, [wc, BH], [bG])
                    c.mm(bG[P, 256:320], BH[:, hc], wc[:, 64:128], True, False, [BH, wc], [bG])
                    c.mm(bG[P, 256:320], KH[:, hc], vh, False, True, [KH, zm_], [bG])
                    c.tt("dve", Rp[:, hc], TM[:, 1, hc], bG[:, 0:64], ALU.add, [TM, bG], [Rp])
                    c.stt(Gbd[p][P, hl * 64:(hl + 1) * 64], ident[P, hl * 64:(hl + 1) * 64], PCc[P, p:p + 1], bG[P, gcol],
                          ALU.mult, ALU.add, [ident, PCc, bG], [Gbd[p]])
                    c.cp("act", Yl[:, hc], bG[:, 64:128], [bG], [Yl])
                    c.cp("act", Hs[P, p, :], bG[P, 256:320], [bG], [Hs])

                gens = [head_gen(0), head_gen(1)]
                while gens:
                    for g in list(gens):
                        try:
                            next(g)
                        except StopIteration:
                            gens.remove(g)
                rt = RT[p % 2]
                c.tr(pRT[:, 384:512], Rp[:, pc], ident[:], [Rp, ident], [pRT])
                c.cp("act", rt[:], pRT[:, 384:512], [pRT], [rt])
                for hl in range(2):
                    h = 2 * p + hl
                    P = slice(hl * 64, (hl + 1) * 64)
                    yb = B7 if hl == 0 else B1
                    c.mm(yb[:, h * 64:(h + 1) * 64], rt[P, :], STc[P, p, :], True, True, [rt, STc], [yb])
                c.mm(pS[:, 320:384], Gbd[p][:], STc[:, p, :], True, True, [Gbd[p], STc], [pS])
                c.tt("dve", STn[:, p, :], pS[:, 320:384], Hs[:, p, :], ALU.add, [pS, Hs], [STn])
            for hl in range(2):
                yb = B7 if hl == 0 else B1
                c.tt("dve", v3(yv[:])[:, hl::2, :], v3(yb[:])[:, hl::2, :], v3(Yl[:])[:, hl::2, :], ALU.add, [yb, Yl], [yv])
            c.red(s8[1][:], v3(yv[:]), ALU.add, [yv], [s8[1]])
            c.ts("dve", s8[1][:], s8[1][:], 1.0 / 64, ALU.mult, [s8[1]], [s8[1]])
            c.tt("dve", v3(yc[:]), v3(yv[:]), bc3(s8[1][:]), ALU.subtract, [yv, s8[1]], [yc])
            c.tt("pool", sq2[:], yc[:], yc[:], ALU.mult, [yc], [sq2])
            c.red(s8[2][:], v3(sq2[:]), ALU.add, [sq2], [s8[2]])
            c.act(s8[2][:], s8[2][:], AF.Sqrt, [s8[2], epsg], [s8[2]], bias=epsg[:, 0:1], scale=1.0 / 64)
            c.recip(s8[2][:], s8[2][:], [s8[2]], [s8[2]])
            c.tt("dve", v3(yc[:]), v3(yc[:]), bc3(s8[2][:]), ALU.mult, [yc, s8[2]], [yc])
            c.tt("pool", yc[:], yc[:], lngr[:], ALU.mult, [yc, lngr], [yc])
            c.tt("pool", yc[:], yc[:], lnbr[:], ALU.add, [yc, lnbr], [yc])
            c.tt("pool", sq2[:], r_, kmod[:], ALU.mult, [zm_, kmod], [sq2])
            c.tt("pool", sq2[:], sq2[:], rkr[:], ALU.mult, [sq2, rkr], [sq2])
            c.red(s8[3][:], v3(sq2[:]), ALU.add, [sq2], [s8[3]])
            c.tt("dve", v3(bon[:]), v3(v_), bc3(s8[3][:]), ALU.mult, [zm_, s8[3]], [bon])
            c.tt("dve", yo[:], yc[:], bon[:], ALU.add, [yc, bon], [yo])
            c.tt("dve", yo[:], yo[:], gv[:], ALU.mult, [yo, gv], [yo])
            for q in range(4):
                c.tr(B4[:, q * 128:(q + 1) * 128], yo[:, q * 128:(q + 1) * 128], ident[:], [yo, ident], [B4])
            c.cp("act", obT[i][:].rearrange("p a b -> p (a b)"), B4[:], [B4], [obT[i]])
            c.dma(sc["brT"][2].rearrange("(h v) t -> v h t", v=128)[:, :, t0:t0 + 128], obT[i][:], reads=[obT[i]])
        m.sched.emit()


def build(S, debug=False, nlayers=L, ph="ABCDEFGH"):
    m = Model(S, debug=debug)
    xa = m.scratch("xa", [S, D], F32)
    xb = m.scratch("xb", [S, D], F32)
    xc = m.scratch("xc", [S, D], F32)
    phase_prep(m)
    xin = m.din["x"]
    for l in range(nlayers):
        if "hg" not in m.scr:
            m.scratch("hg", [S, 2048], F32)
            m.scratch("dqT", [512, S], BF16)
            m.scratch("dkT", [512, S], BF16)
            m.scratch("dvv", [S, 512], BF16)
            m.scratch("rwz", [S, 1536], F32)
            m.scratch("rwcT", [256, S], F32)
            m.scratch("gateT", [3072, S], BF16)
        if "A" in ph:
            phase_A(m, l, xin)
        if "brT" not in m.scr:
            m.scratch("brT", [3, 512, S], BF16)
        if "B" in ph:
            phase_B(m, l)
        if "C" in ph:
            phase_C(m, l)
        if "D" in ph:
            phase_D(m, l)
        if "E" in ph:
            phase_E(m, l, xin, xa)
        if "F" in ph:
            phase_F(m, l, xa, xb)
        if "G" in ph:
            phase_G(m, l, xb, xc)
        xin = xc
    if "H" in ph:
        phase_H(m, xin)
    return m


_CACHE = {}


def kernel(**inputs):
    S = inputs["x"].shape[1]
    B = inputs["x"].shape[0]
    if S not in _CACHE:
        _CACHE[S] = build(S)
    m = _CACHE[S]
    consts = host_consts()
    base = {n: np.ascontiguousarray(np.asarray(inputs[n], np.float32)) for n, _ in PARAMS}
    for n, v in consts.items():
        base["c_" + n] = v
    in_maps = []
    for b in range(B):
        d = dict(base)
        d["x"] = np.ascontiguousarray(np.asarray(inputs["x"][b], np.float32))
        d["mem"] = np.ascontiguousarray(np.asarray(inputs["mem"][b], np.float32))
        in_maps.append(d)
    res = run_bass_kernel_spmd(m.nc, in_maps, core_ids=list(range(B)))
    return np.stack([np.asarray(r["out"], np.float32) for r in res.results], axis=0)
```

```python
import contextlib
import math
import numpy as np
import ml_dtypes
import concourse.bass as bass
import concourse.mybir as mybir
from concourse.bass_utils import run_bass_kernel_spmd

F32 = mybir.dt.float32
BF16 = mybir.dt.bfloat16
AF = mybir.ActivationFunctionType
ALU = mybir.AluOpType
AX = mybir.AxisListType

D = 1024
L = 2
NIN = 8448
DFF = 2816
MEM = 256
ENGS = ("pe", "act", "dve", "pool", "sp")
NSLOT = 6
LAM = math.exp(-0.5)
PE_DRAIN = False


class Buf:
    __slots__ = ("name", "last_w", "readers", "excl")

    def __init__(self, name=""):
        self.name = name
        self.last_w = None
        self.readers = []
        self.excl = False


class T:
    def __init__(self, t, name=""):
        self.t = t
        self.b = Buf(name)

    def __getitem__(self, idx):
        return self.t[idx]


def _b(x):
    return x.b if isinstance(x, T) else x


class Sched:
    def __init__(self, nc):
        self.nc = nc
        self.q = {e: [] for e in ENGS}
        self.cnt = {}
        self.seen = {e: {} for e in ENGS}
        self.semkeys = []
        for e in ENGS:
            self._mk(e)
        self.dma_n = {e: 0 for e in ENGS}
        for e in ("sp", "act", "pool"):
            for i in range(NSLOT):
                self._mk(("dma", e, i))
        self.n_instr = 0
        self.nblk = 0

    def _mk(self, k):
        self.cnt[k] = 0
        self.semkeys.append(k)

    def _need(self, e, deps):
        best = {}
        for d in deps:
            if d is None:
                continue
            k, v = d
            if k == e and e == "pe":
                continue
            if self.seen[e].get(k, 0) >= v:
                continue
            if best.get(k, 0) < v:
                best[k] = v
        for k, v in best.items():
            self.seen[e][k] = v
            self.q[e].append(("wait", k, v))

    def _deps(self, reads, writes):
        deps = []
        for r in reads:
            r = _b(r)
            deps.append(r.last_w)
            if r.excl:
                deps.extend(r.readers)
        for w in writes:
            w = _b(w)
            deps.append(w.last_w)
            deps.extend(w.readers)
        return deps

    MAXOPS = None

    def op(self, e, fn, reads=(), writes=()):
        if Sched.MAXOPS is not None and self.n_instr >= Sched.MAXOPS:
            return
        self._need(e, self._deps(reads, writes))
        self.cnt[e] += 1
        v = self.cnt[e]
        self.q[e].append(("op", fn, e))
        for w in writes:
            w = _b(w)
            w.last_w = (e, v)
            w.readers = []
        for r in reads:
            _b(r).readers.append((e, v))
        self.n_instr += 1

    def dma(self, e, out, in_, reads=(), writes=()):
        if Sched.MAXOPS is not None and self.n_instr >= Sched.MAXOPS:
            return
        n = self.dma_n[e]
        self.dma_n[e] += 1
        slot = ("dma", e, n % NSLOT)
        deps = self._deps(reads, writes)
        if self.cnt[slot] > 0:
            deps.append((slot, self.cnt[slot]))
        self._need(e, deps)
        self.cnt[slot] += 16
        v = self.cnt[slot]
        self.q[e].append(("dma", out, in_, slot))
        for w in writes:
            w = _b(w)
            w.last_w = (slot, v)
            w.readers = []
        for r in reads:
            _b(r).readers.append((slot, v))
        self.n_instr += 1

    def drain(self):
        deps = [(k, v) for k, v in self.cnt.items() if isinstance(k, tuple) and v > 0]
        self._need("sp", deps)

    def emit(self):
        nc = self.nc
        self.drain()
        sems = {}
        for k in self.semkeys:
            nm = "s_" + "_".join(str(x) for x in (k if isinstance(k, tuple) else (k,))) + f"_{self.nblk}"
            sems[k] = nc.alloc_semaphore(name=nm)
        self.nblk += 1
        if self.nblk == 1:
            nc.clear_and_free_semaphores(list(sems.values()))
            nc.all_engine_barrier()
            for k in self.semkeys:
                nm = "s0_" + "_".join(str(x) for x in (k if isinstance(k, tuple) else (k,)))
                sems[k] = nc.alloc_semaphore(name=nm)
        with contextlib.ExitStack() as st:
            st.enter_context(nc.allow_non_contiguous_dma(reason="small strided parameter loads"))
            block = st.enter_context(nc.Block())

            def run(eh, items):
                for it in items:
                    if it[0] == "wait":
                        eh.wait_ge(sems[it[1]], it[2])
                    elif it[0] == "raw":
                        it[1](eh)
                    elif it[0] == "op":
                        it[1](eh).then_inc(sems[it[2]], 1)
                    else:
                        eh.dma_start(out=it[1], in_=it[2]).then_inc(sems[it[3]], 16)

            @block.tensor
            def _(e):
                run(e, self.q["pe"])

            @block.scalar
            def _(e):
                run(e, self.q["act"])

            @block.vector
            def _(e):
                run(e, self.q["dve"])

            @block.gpsimd
            def _(e):
                run(e, self.q["pool"])

            @block.sync
            def _(e):
                run(e, self.q["sp"])
        nc.clear_and_free_semaphores(list(sems.values()))
        nc.all_engine_barrier()
        for k in self.cnt:
            self.cnt[k] = 0
        self.seen = {e: {} for e in ENGS}
        self.q = {e: [] for e in ENGS}


class Ctx:
    def __init__(self, nc, S_, st):
        self.nc = nc
        self.S = S_
        self.st = st
        self.rr = 0

    _uid = [0]

    def sb(self, name, shape, dt=F32):
        Ctx._uid[0] += 1
        name = f"t{Ctx._uid[0]}_{name}"
        return T(self.st.enter_context(self.nc.sbuf_tensor(name, list(shape), dt)), name)

    def ps(self, name, shape, dt=F32):
        Ctx._uid[0] += 1
        name = f"p{Ctx._uid[0]}_{name}"
        t = T(self.st.enter_context(self.nc.psum_tensor(name, list(shape), dt)), name)
        t.b.excl = True
        return t

    def sbn(self, name, shape, dt=F32, n=2):
        return [self.sb(f"{name}{i}", shape, dt) for i in range(n)]

    def psn(self, name, shape, dt=F32, n=2):
        return [self.ps(f"{name}{i}", shape, dt) for i in range(n)]

    _mode = [None]

    def _pe_mode(self, lhsT):
        def rnd(n):
            return 32 if n <= 32 else (64 if n <= 64 else 128)
        shp = lhsT.shape
        k = shp[0]
        mfree = 1
        for d in shp[1:]:
            mfree *= d
        mode = (rnd(k), rnd(mfree))
        if PE_DRAIN and Ctx._mode[0] is not None and Ctx._mode[0] != mode:
            if not (Sched.MAXOPS is not None and self.S.n_instr >= Sched.MAXOPS):
                self.S.q["pe"].append(("raw", lambda e: e.drain()))
        Ctx._mode[0] = mode

    def mm(self, out, lhsT, rhs, start, stop, reads, writes):
        self._pe_mode(lhsT)
        self.S.op("pe", lambda e: e.matmul(out, lhsT=lhsT, rhs=rhs, start=start, stop=stop), reads, writes)

    def tr(self, out, in_, ident, reads, writes):
        self._pe_mode(in_)
        self.S.op("pe", lambda e: e.transpose(out=out, in_=in_, identity=ident), reads, writes)

    def act(self, out, in_, func, reads, writes, bias=None, scale=None, accum=None, eng="act"):
        kw = {}
        if bias is not None:
            kw["bias"] = bias
        if scale is not None:
            kw["scale"] = scale
        if accum is not None:
            kw["accum_out"] = accum
        self.S.op("act", lambda e: e.activation(out=out, in_=in_, func=func, **kw), reads, writes)

    def tt(self, eng, out, in0, in1, op, reads, writes):
        self.S.op(eng, lambda e: e.tensor_tensor(out=out, in0=in0, in1=in1, op=op), reads, writes)

    def ts(self, eng, out, in0, s1, op0, reads, writes, s2=None, op1=None):
        if op1 is None:
            self.S.op(eng, lambda e: e.tensor_scalar(out=out, in0=in0, scalar1=s1, scalar2=None, op0=op0), reads, writes)
        else:
            self.S.op(eng, lambda e: e.tensor_scalar(out=out, in0=in0, scalar1=s1, scalar2=s2, op0=op0, op1=op1), reads, writes)

    def stt(self, out, in0, scalar, in1, op0, op1, reads, writes):
        self.S.op("dve", lambda e: e.scalar_tensor_tensor(out=out, in0=in0, scalar=scalar, in1=in1, op0=op0, op1=op1), reads, writes)

    def cp(self, eng, out, in_, reads, writes):
        if eng == "act":
            self.S.op("act", lambda e: e.activation(out=out, in_=in_, func=AF.Copy), reads, writes)
        else:
            self.S.op(eng, lambda e: e.tensor_copy(out=out, in_=in_), reads, writes)

    def red(self, out, in_, op, reads, writes):
        self.S.op("dve", lambda e: e.tensor_reduce(out=out, in_=in_, axis=AX.X, op=op), reads, writes)

    def recip(self, out, in_, reads, writes):
        self.S.op("dve", lambda e: e.reciprocal(out=out, in_=in_), reads, writes)

    def memset(self, eng, ap, val, writes):
        self.S.op(eng, lambda e: e.memset(ap, val), (), writes)

    def dma(self, out, in_, reads=(), writes=(), q=None):
        if q is None:
            q = ("sp", "act", "pool")[self.rr % 3]
            self.rr += 1
        self.S.dma(q, out, in_, reads, writes)


def host_consts():
    c = {}
    idx = np.arange(128)
    s = idx[:, None]
    t = idx[None, :]
    same = (s // 64) == (t // 64)
    c["ident"] = np.eye(128, dtype=np.float32)
    c["ones"] = np.ones((128, 128), np.float32)
    tri64 = ((s <= t) & same).astype(np.float32)
    mid64 = ((s <= (t // 64) * 64 + 31) & same).astype(np.float32)
    blk64 = same.astype(np.float32)
    c["hgm"] = np.concatenate([tri64, tri64 - mid64, blk64 - tri64], axis=1)
    incl = (s <= t).astype(np.float32)
    strict = (s < t).astype(np.float32)
    rev = (s > t).astype(np.float32)
    c["rwm"] = np.concatenate([incl, strict, rev], axis=1)
    c["maskg"] = np.concatenate([strict, incl, strict, incl], axis=1)
    scale = 64 ** -0.5
    slopes = 2.0 ** (-8.0 * np.arange(1, 5) / 4)
    ki = np.arange(128)[:, None].astype(np.float64)
    qi = np.arange(512)[None, :].astype(np.float64)
    al = np.zeros((4, 5, 128, 512), np.float32)
    for h in range(4):
        al[h, 0] = (-slopes[h] * (qi - ki) / scale)
        for d in range(4):
            dist = qi - ki - 128 * d
            al[h, 1 + d] = np.where(dist >= 0, -slopes[h] * dist / scale, -1e30)
    c["alibi"] = al.transpose(2, 0, 1, 3).reshape(128, 4 * 5 * 512).copy()
    ct = np.zeros((128, 4 * 65), np.float32)
    for h in range(4):
        ct[:, h * 65:(h + 1) * 65] = (-slopes[h] * 128.0 * np.arange(65))[None, :]
    c["ctab"] = ct
    return c


CONST_SHAPES = {"ident": (128, 128), "ones": (128, 128), "hgm": (128, 384), "rwm": (128, 384),
                "maskg": (128, 512), "alibi": (128, 4 * 5 * 512), "ctab": (128, 4 * 65)}

PARAMS = [
    ("norm_mix_g", (L, D)), ("w_in", (L, D, NIN)), ("b_gate", (L, 3072)), ("hgrn_lb_param", (L, 512)),
    ("hgrn_norm_g", (L, 512)), ("diff_lambda", (L, 4, 64)), ("diff_subln_g", (L, 128)),
    ("rwkv_mu", (L, 1792)), ("rwkv_w0", (L, 512)), ("rwkv_w_up", (L, 64, 512)), ("rwkv_a0", (L, 512)),
    ("rwkv_a_up", (L, 64, 512)), ("rwkv_g_up", (L, 128, 512)), ("rwkv_k_k", (L, 512)), ("rwkv_k_a", (L, 512)),
    ("rwkv_r_k", (L, 8, 64)), ("rwkv_ln_g", (L, 512)), ("rwkv_ln_b", (L, 512)),
    ("w_branch", (L, 3, 512, D)), ("w_out", (L, D, D)), ("norm_xa_g", (L, D)), ("norm_mem_g", (L, D)),
    ("xa_wq", (L, D, D)), ("xa_wkv", (L, D, 2 * D)), ("xa_wo", (L, D, D)), ("norm_ffn_g", (L, D)),
    ("ffn_w_up", (L, D, 2 * DFF)), ("ffn_conv_w", (L, 3, DFF)), ("ffn_conv_b", (L, DFF)),
    ("ffn_w_down", (L, DFF, D)), ("final_norm_g", (D,)),
]


class Model:
    def __init__(self, S, debug=False, phases=None):
        self.S = S
        self.debug = debug
        self.phases = phases
        nc = bass.Bass("TRN2", target_bir_lowering=False)
        self.nc = nc
        self.din = {}
        self.din["x"] = nc.dram_tensor("x", [S, D], F32, kind="ExternalInput").ap()
        self.din["mem"] = nc.dram_tensor("mem", [MEM, D], F32, kind="ExternalInput").ap()
        for n, shp in PARAMS:
            self.din[n] = nc.dram_tensor(n, list(shp), F32, kind="ExternalInput").ap()
        for n, shp in CONST_SHAPES.items():
            self.din["c_" + n] = nc.dram_tensor("c_" + n, list(shp), F32, kind="ExternalInput").ap()
        self.out = nc.dram_tensor("out", [S, D], F32, kind="ExternalOutput").ap()
        self.scr = {}
        self.sched = Sched(nc)

    def scratch(self, name, shape, dt):
        kind = "ExternalOutput" if self.debug else "Internal"
        self.scr[name] = self.nc.dram_tensor(name, list(shape), dt, kind=kind).ap()
        return self.scr[name]

    def ctx(self, st):
        return Ctx(self.nc, self.sched, st)


def phase_prep(m):
    nc, S_ = m.nc, m.sched
    specs = [("w_in", "norm_mix_g", D, NIN), ("w_out", None, D, D), ("xa_wq", "norm_xa_g", D, D),
             ("xa_wkv", "norm_mem_g", D, 2 * D), ("xa_wo", None, D, D), ("ffn_w_up", "norm_ffn_g", D, 2 * DFF),
             ("ffn_w_down", None, DFF, D), ("w_branch", None, 1536, D)]
    for name, g, K, N in specs:
        m.scratch("b_" + name, [L, K, N], BF16)
    with contextlib.ExitStack() as st:
        c = m.ctx(st)
        gt = c.sb("gt", [128, 4, L, 8])
        gi = 0
        gmap = {}
        for name, g, K, N in specs:
            if g is not None:
                gmap[g] = gi
                for l in range(L):
                    c.dma(gt[:, gi, l, :], m.din[g][l].rearrange("(c p) -> p c", p=128), writes=[gt])
                gi += 1
        W = 2048
        ins = c.sbn("pin", [128, W], F32, 5)
        outs = c.sbn("pout", [128, W], BF16, 5)
        k = 0
        for name, g, K, N in specs:
            for l in range(L):
                src = m.din[name][l]
                if name == "w_branch":
                    src = src.rearrange("a k n -> (a k) n")
                dst = m.scr["b_" + name][l]
                for kc in range(K // 128):
                    for n0 in range(0, N, W):
                        w = min(W, N - n0)
                        ti, to = ins[k % 5], outs[k % 5]
                        c.dma(ti[:, 0:w], src[kc * 128:(kc + 1) * 128, n0:n0 + w], writes=[ti])
                        eng = ("dve", "act", "dve", "pool", "act")[k % 5]
                        if g is not None:
                            if eng == "act":
                                c.act(to[:, 0:w], ti[:, 0:w], AF.Copy, [ti, gt], [to], scale=gt[:, gmap[g], l, kc:kc + 1])
                            else:
                                c.ts(eng, to[:, 0:w], ti[:, 0:w], gt[:, gmap[g], l, kc:kc + 1], ALU.mult, [ti, gt], [to])
                        else:
                            c.cp(eng, to[:, 0:w], ti[:, 0:w], [ti], [to])
                        c.dma(dst[kc * 128:(kc + 1) * 128, n0:n0 + w], to[:, 0:w], reads=[to])
                        k += 1
        S_.emit()


def load_consts(m, c, names):
    r = {}
    for n in names:
        shp = CONST_SHAPES[n]
        t = c.sb("c_" + n, shp, F32)
        c.dma(t[:], m.din["c_" + n][:, :], writes=[t])
        r[n] = t
    return r


def make_bf(c, src, shape, name):
    t = c.sb(name, shape, BF16)
    c.cp("dve", t[:], src[:], [src], [t])
    return t


def norm_T(c, xsrc, hT, col0, xt, xb, junk, ss, rstd, eps, pT, identb):
    c.dma(xt[:], xsrc, writes=[xt])
    c.act(junk[:], xt[:], AF.Square, [xt], [junk, ss], accum=ss[:, 0:1])
    c.act(rstd[:, 0:1], ss[:, 0:1], AF.Sqrt, [ss, eps], [rstd], bias=eps[:, 0:1], scale=1.0 / D)
    c.recip(rstd[:, 0:1], rstd[:, 0:1], [rstd], [rstd])
    c.ts("dve", xb[:], xt[:], rstd[:, 0:1], ALU.mult, [xt, rstd], [xb])
    for kc in range(8):
        c.tr(pT[:, kc, :], xb[:, kc * 128:(kc + 1) * 128], identb[:], [xb, identb], [pT])
    c.cp("pool" if False else "act", hT[:, :, col0:col0 + 128], pT[:], [pT], [hT])


def phase_A(m, l, xsrc):
    S = m.S
    TG = min(S, 2048)
    TB = min(512, TG)
    sc = m.scr
    if "hg" not in sc:
        m.scratch("hg", [S, 2048], F32)
        m.scratch("dqT", [512, S], BF16)
        m.scratch("dkT", [512, S], BF16)
        m.scratch("dvv", [S, 512], BF16)
        m.scratch("rwz", [S, 1536], F32)
        m.scratch("rwcT", [256, S], F32)
        m.scratch("gateT", [3072, S], BF16)
    blocks = [
        (0, 512, "tok", "hg", 0, AF.Silu, F32), (512, 512, "tok", "hg", 512, AF.Sigmoid, F32),
        (1024, 512, "tok", "hg", 1024, AF.Copy, F32), (1536, 512, "tok", "hg", 1536, AF.Silu, F32),
        (2048, 512, "feat", "dqT", 0, AF.Copy, BF16), (2560, 512, "feat", "dkT", 0, AF.Copy, BF16),
        (3072, 512, "tok", "dvv", 0, AF.Copy, BF16),
        (3584, 512, "tok", "rwz", 0, AF.Copy, F32), (4096, 512, "tok", "rwz", 512, AF.Copy, F32),
        (4608, 512, "tok", "rwz", 1024, AF.Copy, F32), (5120, 256, "feat", "rwcT", 0, AF.Copy, F32),
    ] + [(5376 + i * 512, 512, "gate", "gateT", i * 512, AF.Sigmoid, BF16) for i in range(6)]
    with contextlib.ExitStack() as st:
        c = m.ctx(st)
        cs = load_consts(m, c, ["ident"])
        identb = make_bf(c, cs["ident"], [128, 128], "identb")
        eps = c.sb("eps", [128, 1])
        c.memset("dve", eps[:], 1e-6, [eps])
        bg = c.sb("bg", [128, 24])
        c.dma(bg[:], m.din["b_gate"][l].rearrange("(c p) -> p c", p=128), writes=[bg])
        hT = c.sb("hT", [128, 8, TG], BF16)
        xts = c.sbn("xt", [128, D], F32, 2)
        xbs = c.sbn("xb", [128, D], BF16, 2)
        junk = c.sb("junk", [128, D], F32)
        sss = c.sbn("ss", [128, 1], F32, 2)
        rstds = c.sbn("rstd", [128, 1], F32, 2)
        pTs = c.psn("pT", [128, 8, 128], BF16, 2)
        wbs = c.sbn("wb", [128, 8, 512], BF16, 2)
        pos = c.psn("po", [128, 512], F32, 4)
        o32 = c.sbn("o32", [128, 512], F32, 3)
        o16 = c.sbn("o16", [128, 512], BF16, 3)
        wsrc = sc["b_w_in"][l].rearrange("(c p) n -> p c n", p=128)
        it = 0
        for g0 in range(0, S, TG):
            for tt in range(TG // 128):
                i = tt % 2
                norm_T(c, xsrc[g0 + tt * 128:g0 + (tt + 1) * 128, :], hT, tt * 128, xts[i], xbs[i], junk, sss[i],
                       rstds[i], eps, pTs[i], identb)
            for bi, (c0, ncol, kind, dname, doff, func, dt) in enumerate(blocks):
                wb = wbs[bi % 2]
                c.dma(wb[:, :, 0:ncol], wsrc[:, :, c0:c0 + ncol], writes=[wb], q="sp")
                dest = sc[dname]
                if kind == "tok":
                    for tt in range(TG // 128):
                        po = pos[it % 4]
                        ot = (o32 if dt == F32 else o16)[it % 3]
                        it += 1
                        for kc in range(8):
                            c.mm(po[:, 0:ncol], hT[:, kc, tt * 128:(tt + 1) * 128], wb[:, kc, 0:ncol], kc == 0, kc == 7,
                                 [hT, wb], [po])
                        c.act(ot[:, 0:ncol], po[:, 0:ncol], func, [po], [ot])
                        c.dma(dest[g0 + tt * 128:g0 + (tt + 1) * 128, doff:doff + ncol], ot[:, 0:ncol], reads=[ot])
                else:
                    for fc in range(ncol // 128):
                        for tb in range(TG // TB):
                            po = pos[it % 4]
                            ot = (o32 if dt == F32 else o16)[it % 3]
                            it += 1
                            for kc in range(8):
                                c.mm(po[:, 0:TB], wb[:, kc, fc * 128:(fc + 1) * 128], hT[:, kc, tb * TB:(tb + 1) * TB],
                                     kc == 0, kc == 7, [hT, wb], [po])
                            if kind == "gate":
                                gc = (doff + fc * 128) // 128
                                c.act(ot[:, 0:TB], po[:, 0:TB], func, [po, bg], [ot], bias=bg[:, gc:gc + 1])
                            else:
                                c.act(ot[:, 0:TB], po[:, 0:TB], func, [po], [ot])
                            r0 = doff + fc * 128
                            c.dma(dest[r0:r0 + 128, g0 + tb * TB:g0 + (tb + 1) * TB], ot[:, 0:TB], reads=[ot])
        m.sched.emit()


def rowb(m, c, name, src1d, F):
    t = c.sb(name, [128, F], F32)
    c.dma(t[:], src1d.partition_broadcast(128), writes=[t])
    return t


def sub(parent):
    v = T(parent.t, parent.b.name + "_v")
    return v


def bc3(ap2, n=64):
    H = ap2.shape[1]
    return ap2.unsqueeze(2).to_broadcast([128, H, n])


def v3(ap2, n=64):
    return ap2.rearrange("p (h n) -> p h n", n=n)


def phase_B(m, l):
    S = m.S
    sc = m.scr
    if "brT" not in sc:
        m.scratch("brT", [3, 512, S], BF16)
    with contextlib.ExitStack() as st:
        c = m.ctx(st)
        cs = load_consts(m, c, ["ident", "hgm", "ones"])
        ident, hgm, ones = cs["ident"], cs["hgm"], cs["ones"]
        eps = c.sb("eps", [128, 1])
        c.memset("dve", eps[:], 1e-6, [eps])
        lbrow = c.sb("lbrow", [128, 512])
        omlb = c.sb("omlb", [128, 512])
        if l == 0:
            c.memset("dve", lbrow[:], 0.0, [lbrow])
        else:
            a0 = rowb(m, c, "lba0", m.din["hgrn_lb_param"][0], 512)
            a1 = rowb(m, c, "lba1", m.din["hgrn_lb_param"][1], 512)
            c.tt("dve", a1[:], a1[:], a0[:], ALU.subtract, [a0, a1], [a1])
            c.act(lbrow[:], a1[:], AF.Sigmoid, [a1], [lbrow])
        c.ts("dve", omlb[:], lbrow[:], -1.0, ALU.mult, [lbrow], [omlb], s2=1.0, op1=ALU.add)
        ngrow = rowb(m, c, "ngrow", m.din["hgrn_norm_g"][l], 512)
        Sst = [c.sbn(f"Sst{h}_", [128, 128], F32, 2) for h in range(4)]
        for h in range(4):
            c.memset("pool", Sst[h][0][:], 0.0, [Sst[h][0]])
        hgt = c.sbn("hgt", [128, 2048], F32, 2)
        fv = c.sbn("fv", [128, 512], F32, 2)
        kf = c.sbn("kf", [128, 512], F32, 2)
        lf = c.sbn("lf", [128, 512], F32, 2)
        ee = c.sbn("ee", [128, 4, 512], F32, 2)
        qk = c.sbn("qk", [128, 4, 512], F32, 2)
        FT = c.sbn("FT", [128, 3, 128], F32, 4)
        AT = c.sbn("AT", [128, 128], F32, 4)
        dcol = c.sbn("dcol", [128, 2], F32, 8)
        ssq = c.sbn("ssq", [128, 4], F32, 2)
        rstd = c.sbn("rstdh", [128, 4], F32, 2)
        gn = c.sbn("gn", [128, 512], F32, 2)
        ob = c.sbn("ob", [128, 512], BF16, 2)
        obf = c.sbn("obf", [128, 512], F32, 2)
        obT = c.sbn("obT", [128, 4, 128], BF16, 2)
        junk = c.sb("junkb", [128, 128], F32)
        pcs = c.psn("pcs", [128, 512], F32, 3)
        pTr = c.psn("pTr", [128, 512], F32, 2)
        pmis = c.psn("pmisc", [128, 512], F32, 2)
        po = c.ps("pob", [128, 512], F32)
        pTb = pTr[0]
        cn = 0
        for ti in range(S // 128):
            t0 = ti * 128
            i = ti % 2
            hg = hgt[i]
            c.dma(hg[:], sc["hg"][t0:t0 + 128, :], writes=[hg])
            q, sg, vv, gate = hg[:, 0:512], hg[:, 512:1024], hg[:, 1024:1536], hg[:, 1536:2048]
            f = fv[i]
            c.tt("pool", f[:], sg, omlb[:], ALU.mult, [hg, omlb], [f])
            c.tt("pool", f[:], f[:], lbrow[:], ALU.add, [f, lbrow], [f])
            c.ts("pool", f[:], f[:], 1e-30, ALU.max, [f], [f])
            c.ts("dve", kf[i][:], f[:], -1.0, ALU.mult, [f], [kf[i]], s2=1.0, op1=ALU.add)
            c.act(lf[i][:], f[:], AF.Ln, [f], [lf[i]])
            for k in range(3):
                c.mm(pcs[k][:], hgm[:, k * 128:(k + 1) * 128], lf[i][:], True, True, [hgm, lf[i]], [pcs[k]])
            e = ee[i]
            c.act(e[:, 0, :], pcs[1][:], AF.Exp, [pcs[1]], [e])
            c.act(e[:, 1, :], pcs[1][:], AF.Exp, [pcs[1]], [e], scale=-1.0)
            c.act(e[:, 2, :], pcs[0][:], AF.Exp, [pcs[0]], [e])
            c.act(e[:, 3, :], pcs[2][:], AF.Exp, [pcs[2]], [e])
            w = qk[i]
            c.tt("dve", w[:, 0, :], q, e[:, 0, :], ALU.mult, [hg, e], [w])
            c.tt("pool", w[:, 1, :], kf[i][:], e[:, 1, :], ALU.mult, [kf[i], e], [w])
            c.tt("dve", w[:, 2, :], q, e[:, 2, :], ALU.mult, [hg, e], [w])
            c.tt("pool", w[:, 3, :], kf[i][:], e[:, 3, :], ALU.mult, [kf[i], e], [w])
            c.tt("pool", gn[i][:], gate, ngrow[:], ALU.mult, [hg, ngrow], [gn[i]])
            def hgen(h):
                nonlocal cn
                hc = slice(h * 128, (h + 1) * 128)
                vh = hg[:, 1024 + h * 128:1024 + (h + 1) * 128]
                ft = FT[h]
                at = AT[h]
                ptr = pTr[h % 2]
                pm = pmis[h % 2]
                for k in range(3):
                    c.tr(ptr[:, k * 128:(k + 1) * 128], w[:, k, hc], ident[:], [w, ident], [ptr])
                c.cp("act", ft[:].rearrange("p a b -> p (a b)"), ptr[:, 0:384], [ptr], [ft])
                yield
                c.mm(pm[:, 0:128], ft[:, 1, :], ft[:, 0, :], True, True, [ft], [pm])
                c.tt("dve", at[:], pm[:, 0:128], hgm[:, 0:128], ALU.mult, [pm, hgm], [at])
                yield

                def chunk_state(ch):
                    nonlocal cn
                    P = slice(ch * 64, (ch + 1) * 64)
                    Sc = Sst[h][(2 * ti + ch) % 2]
                    Sn = Sst[h][(2 * ti + ch + 1) % 2]
                    dc = dcol[cn % 8]
                    cn += 1
                    c.mm(pm[:, 256:258], lf[i][P, hc], ones[P, 0:2], True, True, [lf[i], ones], [pm])
                    c.mm(pm[:, 128:256], w[P, 3, hc], hg[P, 1024 + h * 128:1024 + (h + 1) * 128], True, True, [w, hg], [pm])
                    c.act(dc[:], pm[:, 256:258], AF.Exp, [pm], [dc])
                    c.stt(Sn[:], Sc[:], dc[:, 0:1], pm[:, 128:256], ALU.mult, ALU.add, [Sc, dc, pm], [Sn])

                chunk_state(0)
                yield
                S0 = Sst[h][(2 * ti) % 2]
                S1 = Sst[h][(2 * ti + 1) % 2]
                c.mm(po[:, hc], at[:], vh, True, False, [at, hg], [po])
                c.mm(po[0:64, hc], ft[:, 2, 0:64], S0[:], False, False, [ft, S0], [po])
                c.mm(po[64:128, hc], ft[:, 2, 64:128], S1[:], False, True, [ft, S1], [po])
                yield
                chunk_state(1)

            gens = [hgen(h) for h in range(4)]
            while gens:
                for g in list(gens):
                    try:
                        next(g)
                    except StopIteration:
                        gens.remove(g)
            for h in range(4):
                hc = slice(h * 128, (h + 1) * 128)
                c.act(junk[:], po[:, hc], AF.Square, [po], [junk, ssq[i]], accum=ssq[i][:, h:h + 1])
            c.act(rstd[i][:], ssq[i][:], AF.Sqrt, [ssq[i], eps], [rstd[i]], bias=eps[:, 0:1], scale=1.0 / 128)
            c.recip(rstd[i][:], rstd[i][:], [rstd[i]], [rstd[i]])
            for h in range(4):
                hc = slice(h * 128, (h + 1) * 128)
                c.stt(obf[i][:, hc], po[:, hc], rstd[i][:, h:h + 1], gn[i][:, hc], ALU.mult, ALU.mult, [po, rstd[i], gn[i]], [obf[i]])
            for h in range(4):
                hc = slice(h * 128, (h + 1) * 128)
                c.tr(pTb[:, hc], obf[i][:, hc], ident[:], [obf[i], ident], [pTb])
            c.cp("act", obT[i][:].rearrange("p a b -> p (a b)"), pTb[:], [pTb], [obT[i]])
            c.dma(sc["brT"][0].rearrange("(h v) t -> v h t", v=128)[:, :, t0:t0 + 128], obT[i][:], reads=[obT[i]])
        m.sched.emit()


def phase_C(m, l):
    S = m.S
    sc = m.scr
    QB = min(512, S)
    nq = QB // 128
    NQ = S // QB
    NJ = S // 128
    lambda_init = 0.8 - 0.6 * math.exp(-0.3 * l)
    with contextlib.ExitStack() as st:
        c = m.ctx(st)
        cs = load_consts(m, c, ["ones", "ctab"])
        ones, ctab = cs["ones"], cs["ctab"]
        onesb = make_bf(c, ones, [128, 128], "onesb")
        eps = c.sb("eps", [128, 1])
        c.memset("dve", eps[:], 1e-6, [eps])
        lamr = rowb(m, c, "lamr", m.din["diff_lambda"][l].rearrange("a b -> (a b)"), 256)
        ltmp = c.sb("ltmp", [128, 128])
        lsum = c.sb("lsum", [128, 2])
        c.tt("dve", ltmp[:, 0:64], lamr[:, 0:64], lamr[:, 64:128], ALU.mult, [lamr], [ltmp])
        c.tt("dve", ltmp[:, 64:128], lamr[:, 128:192], lamr[:, 192:256], ALU.mult, [lamr, ltmp], [ltmp])
        c.red(lsum[:], ltmp[:].rearrange("p (a b) -> p a b", b=64), ALU.add, [ltmp], [lsum])
        c.act(lsum[:], lsum[:], AF.Exp, [lsum], [lsum])
        nlam = c.sb("nlam", [128, 1])
        c.tt("dve", nlam[:], lsum[:, 1:2], lsum[:, 0:1], ALU.subtract, [lsum], [nlam])
        c.ts("dve", nlam[:], nlam[:], -lambda_init, ALU.add, [nlam], [nlam])
        gcol = c.sb("gcol", [128, 1])
        c.dma(gcol[:], m.din["diff_subln_g"][l].rearrange("(p o) -> p o", o=1), writes=[gcol])
        c.ts("dve", gcol[:], gcol[:], 1.0 - lambda_init, ALU.mult, [gcol], [gcol])
        qT = c.sb("qT", [128, S], BF16)
        kT = c.sb("kT", [128, S], BF16)
        V = c.sb("V", [128, NJ, 128], BF16)
        AL = c.sb("AL", [128, 5, 512], F32)
        TMP = c.sbn("tmpc", [128, 512], F32, 3)
        PT = c.sbn("ptc", [128, 512], BF16, 4)
        PST = c.psn("pst", [128, 512], F32, 4)
        PO = c.psn("poc", [128, 512], F32, 2)
        PL = c.psn("plc", [128, 512], F32, 2)
        rl = c.sbn("rlc", [128, 512], F32, 2)
        oc = c.sbn("occ", [128, 512], F32, 2)
        od = c.sb("odc", [128, 512], F32)
        sq = c.sb("sqc", [128, 512], F32)
        rs = c.sb("rsc", [128, 512], F32)
        obo = c.sbn("oboc", [128, 512], BF16, 2)
        cnt = 0
        for h in range(4):
            c.dma(qT[:], sc["dqT"][h * 128:(h + 1) * 128, :], writes=[qT], q="sp")
            c.dma(kT[:], sc["dkT"][h * 128:(h + 1) * 128, :], writes=[kT], q="act")
            vsrc = sc["dvv"].rearrange("(j p) c -> p j c", p=128)
            for j0 in range(0, NJ, 8):
                j1 = min(NJ, j0 + 8)
                c.dma(V[:, j0:j1, :], vsrc[:, j0:j1, h * 128:(h + 1) * 128], writes=[V], q=("pool", "sp", "act")[(j0 // 8) % 3])
            c.dma(AL[:].rearrange("p a b -> p (a b)"), m.din["c_alibi"][:, h * 2560:(h + 1) * 2560], writes=[AL], q="sp")
            for I in range(NQ):
                qs = slice(I * QB, (I + 1) * QB)
                jmax = nq * (I + 1) - 1
                units = [(j, c2) for j in range(jmax + 1) for c2 in range(2)]
                LA = 2
                pend = []

                def stage1(j, c2):
                    nonlocal cnt
                    d = j - nq * I
                    var = 0 if d < 0 else 1 + d
                    mc = (nq * I - j) if d < 0 else 0
                    P = slice(c2 * 64, (c2 + 1) * 64)
                    pst = PST[cnt % 4]
                    tmp = TMP[cnt % 3]
                    pt = PT[cnt % 4]
                    cnt += 1
                    c.mm(pst[:, 0:QB], kT[P, j * 128:(j + 1) * 128], qT[P, qs], True, True, [kT, qT], [pst])
                    c.tt("dve", tmp[:, 0:QB], pst[:, 0:QB], AL[:, var, 0:QB], ALU.add, [pst, AL], [tmp])
                    c.act(pt[:, 0:QB], tmp[:, 0:QB], AF.Exp, [tmp, ctab], [pt], bias=ctab[:, h * 65 + mc:h * 65 + mc + 1], scale=0.125)
                    return pt

                def stage2(j, c2, pt):
                    c.mm(PO[c2][:, 0:QB], V[:, j, :], pt[:, 0:QB], j == 0, j == jmax, [V, pt], [PO[c2]])
                    c.mm(PL[c2][:, 0:QB], onesb[:], pt[:, 0:QB], j == 0, j == jmax, [onesb, pt], [PL[c2]])

                for ui in range(0, len(units) + LA, 2):
                    for uu in (ui, ui + 1):
                        if uu < len(units):
                            j, c2 = units[uu]
                            pend.append((j, c2, stage1(j, c2)))
                    for uu in (ui, ui + 1):
                        if uu >= LA and pend and uu - LA < len(units):
                            stage2(*pend.pop(0))
                while pend:
                    stage2(*pend.pop(0))
                for c2 in range(2):
                    c.recip(rl[c2][:, 0:QB], PL[c2][:, 0:QB], [PL[c2]], [rl[c2]])
                    c.tt("dve", oc[c2][:, 0:QB], PO[c2][:, 0:QB], rl[c2][:, 0:QB], ALU.mult, [PO[c2], rl[c2]], [oc[c2]])
                c.stt(od[:, 0:QB], oc[1][:, 0:QB], nlam[:, 0:1], oc[0][:, 0:QB], ALU.mult, ALU.add, [oc[0], oc[1], nlam], [od])
                c.tt("pool", sq[:, 0:QB], od[:, 0:QB], od[:, 0:QB], ALU.mult, [od], [sq])
                pss = PST[cnt % 4]
                cnt += 1
                c.mm(pss[:, 0:QB], ones[:], sq[:, 0:QB], True, True, [ones, sq], [pss])
                c.act(rs[:, 0:QB], pss[:, 0:QB], AF.Sqrt, [pss, eps], [rs], bias=eps[:, 0:1], scale=1.0 / 128)
                c.recip(rs[:, 0:QB], rs[:, 0:QB], [rs], [rs])
                o_ = obo[I % 2]
                c.stt(o_[:, 0:QB], od[:, 0:QB], gcol[:, 0:1], rs[:, 0:QB], ALU.mult, ALU.mult, [od, gcol, rs], [o_])
                c.dma(sc["brT"][1][h * 128:(h + 1) * 128, qs], o_[:, 0:QB], reads=[o_])
        m.sched.emit()


def phase_E(m, l, xsrc, xdst):
    S = m.S
    sc = m.scr
    TB = min(512, S)
    with contextlib.ExitStack() as st:
        c = m.ctx(st)
        wbr = c.sb("wbr", [128, 12, D], BF16)
        wout = c.sb("wout", [128, 8, D], BF16)
        c.dma(wbr[:], sc["b_w_branch"][l].rearrange("(c p) n -> p c n", p=128), writes=[wbr], q="sp")
        c.dma(wout[:], sc["b_w_out"][l].rearrange("(c p) n -> p c n", p=128), writes=[wout], q="act")
        br = c.sbn("br", [128, 12, TB], BF16, 2)
        gt = c.sbn("gte", [128, 24, TB], BF16, 2)
        acc = c.sbn("acce", [128, TB], F32, 2)
        tmp = c.sbn("tmpe", [128, TB], F32, 3)
        mT = c.sbn("mT", [128, 8, TB], BF16, 2)
        xt = c.sbn("xte", [128, D], F32, 2)
        xo = c.sbn("xoe", [128, D], F32, 2)
        PM = c.psn("pme", [128, 512], F32, 4)
        PO = c.psn("poe", [128, 512], F32, 4)
        k = 0
        k2 = 0
        for tb in range(S // TB):
            ts_ = slice(tb * TB, (tb + 1) * TB)
            b_, g_, m_ = br[tb % 2], gt[tb % 2], mT[tb % 2]
            c.dma(b_[:], sc["brT"].rearrange("n (c p) t -> p (n c) t", p=128)[:, :, ts_], writes=[b_], q="sp")
            c.dma(g_[:], sc["gateT"].rearrange("(c p) t -> p c t", p=128)[:, :, ts_], writes=[g_], q="act")
            for dmc in range(8):
                a_ = acc[dmc % 2]
                for n in range(3):
                    pm = PM[k % 4]
                    for kc in range(4):
                        c.mm(pm[:, 0:TB], wbr[:, n * 4 + kc, dmc * 128:(dmc + 1) * 128], b_[:, n * 4 + kc, :], kc == 0, kc == 3, [wbr, b_], [pm])
                    if n == 0:
                        c.tt("dve", a_[:], pm[:, 0:TB], g_[:, n * 8 + dmc, :], ALU.mult, [pm, g_], [a_])
                    else:
                        t_ = tmp[k % 3]
                        c.tt("dve", t_[:], pm[:, 0:TB], g_[:, n * 8 + dmc, :], ALU.mult, [pm, g_], [t_])
                        if n == 1:
                            c.tt("pool", a_[:], a_[:], t_[:], ALU.add, [a_, t_], [a_])
                        else:
                            c.tt("pool", m_[:, dmc, :], a_[:], t_[:], ALU.add, [a_, t_], [m_])
                    k += 1
            for tt in range(TB // 128):
                x_, o_ = xt[k2 % 2], xo[k2 % 2]
                r0 = tb * TB + tt * 128
                c.dma(x_[:], xsrc[r0:r0 + 128, :], writes=[x_], q="pool")
                for cb in range(2):
                    po = PO[(2 * k2 + cb) % 4]
                    for kc in range(8):
                        c.mm(po[:], m_[:, kc, tt * 128:(tt + 1) * 128], wout[:, kc, cb * 512:(cb + 1) * 512], kc == 0, kc == 7, [m_, wout], [po])
                    c.tt("dve", o_[:, cb * 512:(cb + 1) * 512], po[:], x_[:, cb * 512:(cb + 1) * 512], ALU.add, [po, x_], [o_])
                c.dma(xdst[r0:r0 + 128, :], o_[:], reads=[o_], q="sp")
                k2 += 1
        m.sched.emit()


class NormBufs:
    def __init__(self, c, tag):
        self.xts = c.sbn("xt" + tag, [128, D], F32, 2)
        self.xbs = c.sbn("xb" + tag, [128, D], BF16, 2)
        self.junk = c.sb("junk" + tag, [128, D], F32)
        self.sss = c.sbn("ss" + tag, [128, 1], F32, 2)
        self.rstds = c.sbn("rstd" + tag, [128, 1], F32, 2)
        self.pTs = c.psn("pT" + tag, [128, 8, 128], BF16, 2)
        self.eps = c.sb("eps" + tag, [128, 1])
        c.memset("dve", self.eps[:], 1e-6, [self.eps])
        self.n = 0

    def run(self, c, xsrc, hT, col0, identb):
        i = self.n % 2
        self.n += 1
        norm_T(c, xsrc, hT, col0, self.xts[i], self.xbs[i], self.junk, self.sss[i], self.rstds[i], self.eps,
               self.pTs[i], identb)


def phase_F(m, l, xsrc, xdst):
    S = m.S
    sc = m.scr
    TB = min(512, S)
    with contextlib.ExitStack() as st:
        c = m.ctx(st)
        cs = load_consts(m, c, ["ident", "ones"])
        identb = make_bf(c, cs["ident"], [128, 128], "identb")
        onesb = make_bf(c, cs["ones"], [128, 128], "onesb")
        nb = NormBufs(c, "f")
        wq = c.sb("wq", [128, 8, D], BF16)
        wkv = c.sb("wkv", [128, 8, 2 * D], BF16)
        wo = c.sb("wo", [128, 8, D], BF16)
        c.dma(wq[:], sc["b_xa_wq"][l].rearrange("(c p) n -> p c n", p=128), writes=[wq], q="sp")
        c.dma(wkv[:], sc["b_xa_wkv"][l].rearrange("(c p) n -> p c n", p=128), writes=[wkv], q="act")
        c.dma(wo[:], sc["b_xa_wo"][l].rearrange("(c p) n -> p c n", p=128), writes=[wo], q="pool")
        memT = c.sb("memT", [128, 8, MEM], BF16)
        KT = c.sb("KT", [128, 8, MEM], BF16)
        Vm = c.sb("Vm", [128, 2, D], BF16)
        PA = c.psn("paf", [128, 512], F32, 3)
        for mt in range(2):
            nb.run(c, m.din["mem"][mt * 128:(mt + 1) * 128, :], memT, mt * 128, identb)
        k = 0
        for fc in range(8):
            pa = PA[k % 3]
            k += 1
            for kc in range(8):
                c.mm(pa[:, 0:MEM], wkv[:, kc, fc * 128:(fc + 1) * 128], memT[:, kc, :], kc == 0, kc == 7, [wkv, memT], [pa])
            c.cp("act", KT[:, fc, :], pa[:, 0:MEM], [pa], [KT])
        for mt in range(2):
            for cb in range(2):
                pa = PA[k % 3]
                k += 1
                for kc in range(8):
                    c.mm(pa[:], memT[:, kc, mt * 128:(mt + 1) * 128], wkv[:, kc, D + cb * 512:D + (cb + 1) * 512], kc == 0, kc == 7, [wkv, memT], [pa])
                c.cp("act", Vm[:, mt, cb * 512:(cb + 1) * 512], pa[:], [pa], [Vm])
        hT = c.sbn("hTf", [128, 8, TB], BF16, 2)
        qT = c.sbn("qTf", [128, 8, TB], BF16, 2)
        oT = c.sbn("oTf", [128, 8, TB], BF16, 2)
        pt = c.sbn("ptf", [128, 2, TB], BF16, 3)
        rl = c.sbn("rlf", [128, TB], F32, 2)
        xt = c.sbn("xtf", [128, D], F32, 2)
        xo = c.sbn("xof", [128, D], F32, 2)
        PB = c.psn("pbf", [128, 512], F32, 3)
        k2 = 0
        kp = 0
        for tb in range(S // TB):
            h_, q_, o_ = hT[tb % 2], qT[tb % 2], oT[tb % 2]
            for tt in range(TB // 128):
                r0 = tb * TB + tt * 128
                nb.run(c, xsrc[r0:r0 + 128, :], h_, tt * 128, identb)
            for fc in range(8):
                pa = PA[k % 3]
                k += 1
                for kc in range(8):
                    c.mm(pa[:, 0:TB], wq[:, kc, fc * 128:(fc + 1) * 128], h_[:, kc, :], kc == 0, kc == 7, [wq, h_], [pa])
                c.cp("act", q_[:, fc, :], pa[:, 0:TB], [pa], [q_])
            for h in range(4):
                p_ = pt[kp % 3]
                kp += 1
                for mc in range(2):
                    pa = PA[k % 3]
                    k += 1
                    for dc in range(2):
                        c.mm(pa[:, 0:TB], KT[:, h * 2 + dc, mc * 128:(mc + 1) * 128], q_[:, h * 2 + dc, :], dc == 0, dc == 1, [KT, q_], [pa])
                    c.act(p_[:, mc, :], pa[:, 0:TB], AF.Exp, [pa], [p_], scale=1.0 / 16)
                pl = PB[2]
                for mc in range(2):
                    c.mm(pl[:, 0:TB], onesb[:], p_[:, mc, :], mc == 0, mc == 1, [onesb, p_], [pl])
                r_ = rl[h % 2]
                c.recip(r_[:], pl[:, 0:TB], [pl], [r_])
                for dc in range(2):
                    pb = PB[dc]
                    for mc in range(2):
                        c.mm(pb[:, 0:TB], Vm[:, mc, h * 256 + dc * 128:h * 256 + (dc + 1) * 128], p_[:, mc, :], mc == 0, mc == 1, [Vm, p_], [pb])
                    c.tt("dve", o_[:, h * 2 + dc, :], pb[:, 0:TB], r_[:], ALU.mult, [pb, r_], [o_])
            for tt in range(TB // 128):
                x_, xo_ = xt[k2 % 2], xo[k2 % 2]
                r0 = tb * TB + tt * 128
                c.dma(x_[:], xsrc[r0:r0 + 128, :], writes=[x_], q="pool")
                for cb in range(2):
                    pa = PA[k % 3]
                    k += 1
                    for kc in range(8):
                        c.mm(pa[:], o_[:, kc, tt * 128:(tt + 1) * 128], wo[:, kc, cb * 512:(cb + 1) * 512], kc == 0, kc == 7, [o_, wo], [pa])
                    c.tt("dve", xo_[:, cb * 512:(cb + 1) * 512], pa[:], x_[:, cb * 512:(cb + 1) * 512], ALU.add, [pa, x_], [xo_])
                c.dma(xdst[r0:r0 + 128, :], xo_[:], reads=[xo_], q="sp")
                k2 += 1
        m.sched.emit()


def phase_G(m, l, xsrc, xdst):
    S = m.S
    sc = m.scr
    TBK = min(1024, S)
    SB = min(512, TBK)
    NFC = DFF // 128
    with contextlib.ExitStack() as st:
        c = m.ctx(st)
        cs = load_consts(m, c, ["ident"])
        identb = make_bf(c, cs["ident"], [128, 128], "identb")
        nb = NormBufs(c, "g")
        wdn = c.sb("wdn", [128, NFC, D], BF16)
        c.dma(wdn[:], sc["b_ffn_w_down"][l].rearrange("(c p) n -> p c n", p=128), writes=[wdn], q="pool")
        cw = c.sb("cw", [128, NFC, 3])
        for j in range(3):
            c.dma(cw[:, :, j], m.din["ffn_conv_w"][l, j].rearrange("(c p) -> p c", p=128), writes=[cw])
        cb_ = c.sb("cbias", [128, NFC])
        c.dma(cb_[:], m.din["ffn_conv_b"][l].rearrange("(c p) -> p c", p=128), writes=[cb_])
        halo = c.sb("halo", [128, NFC, 2])
        c.memset("dve", halo[:], 0.0, [halo])
        hT = c.sb("hTg", [128, 8, TBK], BF16)
        hid = c.sb("hid", [128, NFC, TBK], BF16)
        wu = c.sbn("wu", [128, 8, 512], BF16, 2)
        wv = c.sbn("wv", [128, 8, 512], BF16, 2)
        uext = c.sbn("uext", [128, SB + 2], F32, 2)
        t1 = c.sbn("t1g", [128, SB], F32, 2)
        t2 = c.sbn("t2g", [128, SB], F32, 2)
        sg = c.sbn("sgg", [128, SB], F32, 2)
        xt = c.sbn("xtg", [128, D], F32, 2)
        xo = c.sbn("xog", [128, D], F32, 2)
        PU = c.psn("pug", [128, 512], F32, 2)
        PV = c.psn("pvg", [128, 512], F32, 2)
        PO = c.psn("pog", [128, 512], F32, 2)
        wsrc = sc["b_ffn_w_up"][l].rearrange("(c p) n -> p c n", p=128)
        k = 0
        k2 = 0
        for tbk in range(S // TBK):
            for tt in range(TBK // 128):
                r0 = tbk * TBK + tt * 128
                nb.run(c, xsrc[r0:r0 + 128, :], hT, tt * 128, identb)
            for grp in range(6):
                ncol = 512 if grp < 5 else 256
                wu_, wv_ = wu[grp % 2], wv[grp % 2]
                c.dma(wu_[:, :, 0:ncol], wsrc[:, :, grp * 512:grp * 512 + ncol], writes=[wu_], q="sp")
                c.dma(wv_[:, :, 0:ncol], wsrc[:, :, DFF + grp * 512:DFF + grp * 512 + ncol], writes=[wv_], q="act")
                for fcl in range(ncol // 128):
                    fc = grp * 4 + fcl
                    for sbi in range(TBK // SB):
                        ss_ = slice(sbi * SB, (sbi + 1) * SB)
                        pu, pv = PU[k % 2], PV[k % 2]
                        ue, a1, a2, s_ = uext[k % 2], t1[k % 2], t2[k % 2], sg[k % 2]
                        k += 1
                        for kc in range(8):
                            c.mm(pu[:, 0:SB], wu_[:, kc, fcl * 128:(fcl + 1) * 128], hT[:, kc, ss_], kc == 0, kc == 7, [wu_, hT], [pu])
                        for kc in range(8):
                            c.mm(pv[:, 0:SB], wv_[:, kc, fcl * 128:(fcl + 1) * 128], hT[:, kc, ss_], kc == 0, kc == 7, [wv_, hT], [pv])
                        c.cp("pool", ue[:, 0:2], halo[:, fc, :], [halo], [ue])
                        c.cp("act", ue[:, 2:SB + 2], pu[:, 0:SB], [pu], [ue])
                        c.cp("pool", halo[:, fc, :], ue[:, SB:SB + 2], [ue], [halo])
                        c.act(a1[:], pu[:, 0:SB], AF.Identity, [pu, cw, cb_], [a1], bias=cb_[:, fc:fc + 1], scale=cw[:, fc, 2:3])
                        c.stt(a2[:], ue[:, 1:SB + 1], cw[:, fc, 1:2], a1[:], ALU.mult, ALU.add, [ue, cw, a1], [a2])
                        c.stt(a1[:], ue[:, 0:SB], cw[:, fc, 0:1], a2[:], ALU.mult, ALU.add, [ue, cw, a2], [a1])
                        c.act(s_[:], a1[:], AF.Silu, [a1], [s_])
                        c.tt("dve", hid[:, fc, ss_], s_[:], pv[:, 0:SB], ALU.mult, [s_, pv], [hid])
            for tt in range(TBK // 128):
                x_, xo_ = xt[k2 % 2], xo[k2 % 2]
                r0 = tbk * TBK + tt * 128
                c.dma(x_[:], xsrc[r0:r0 + 128, :], writes=[x_], q="pool")
                for cb in range(2):
                    po = PO[cb]
                    for fc in range(NFC):
                        c.mm(po[:], hid[:, fc, tt * 128:(tt + 1) * 128], wdn[:, fc, cb * 512:(cb + 1) * 512], fc == 0, fc == NFC - 1, [hid, wdn], [po])
                    c.tt("dve", xo_[:, cb * 512:(cb + 1) * 512], po[:], x_[:, cb * 512:(cb + 1) * 512], ALU.add, [po, x_], [xo_])
                c.dma(xdst[r0:r0 + 128, :], xo_[:], reads=[xo_], q="sp")
                k2 += 1
        m.sched.emit()


def phase_H(m, xsrc):
    S = m.S
    with contextlib.ExitStack() as st:
        c = m.ctx(st)
        grow = rowb(m, c, "fgrow", m.din["final_norm_g"], D)
        eps = c.sb("eps", [128, 1])
        c.memset("dve", eps[:], 1e-6, [eps])
        xt = c.sbn("xth", [128, D], F32, 3)
        xo = c.sbn("xoh", [128, D], F32, 3)
        junk = c.sb("junkh", [128, D], F32)
        ss = c.sbn("ssh", [128, 1], F32, 3)
        for ti in range(S // 128):
            i = ti % 3
            c.dma(xt[i][:], xsrc[ti * 128:(ti + 1) * 128, :], writes=[xt[i]])
            c.act(junk[:], xt[i][:], AF.Square, [xt[i]], [junk, ss[i]], accum=ss[i][:, 0:1])
            c.act(ss[i][:], ss[i][:], AF.Sqrt, [ss[i], eps], [ss[i]], bias=eps[:, 0:1], scale=1.0 / D)
            c.recip(ss[i][:], ss[i][:], [ss[i]], [ss[i]])
            c.stt(xo[i][:], xt[i][:], ss[i][:, 0:1], grow[:], ALU.mult, ALU.mult, [xt[i], ss[i], grow], [xo[i]])
            c.dma(m.out[ti * 128:(ti + 1) * 128, :], xo[i][:], reads=[xo[i]])
        m.sched.emit()


def phase_D(m, l):
    S = m.S
    sc = m.scr
    with contextlib.ExitStack() as st:
        c = m.ctx(st)
        cs = load_consts(m, c, ["ident", "rwm", "maskg", "ones"])
        ident, rwm, maskg, ones = cs["ident"], cs["rwm"], cs["maskg"], cs["ones"]
        din = m.din
        w0r = rowb(m, c, "w0r", din["rwkv_w0"][l], 512)
        a0r = rowb(m, c, "a0r", din["rwkv_a0"][l], 512)
        kkr = rowb(m, c, "kkr", din["rwkv_k_k"][l], 512)
        kar = rowb(m, c, "kar", din["rwkv_k_a"][l], 512)
        rkr = rowb(m, c, "rkr", din["rwkv_r_k"][l].rearrange("a b -> (a b)"), 512)
        lngr = rowb(m, c, "lngr", din["rwkv_ln_g"][l], 512)
        lnbr = rowb(m, c, "lnbr", din["rwkv_ln_b"][l], 512)
        mur = rowb(m, c, "mur", din["rwkv_mu"][l][0:1536], 1536)
        mucol = c.sb("mucol", [128, 2])
        c.dma(mucol[:], din["rwkv_mu"][l][1536:1792].rearrange("(c p) -> p c", p=128), writes=[mucol])
        WUP = c.sb("WUP", [128, 512])
        c.dma(WUP[0:64, :], din["rwkv_w_up"][l], writes=[WUP])
        c.dma(WUP[64:128, :], din["rwkv_a_up"][l], writes=[WUP])
        GUP = c.sb("GUP", [128, 512])
        c.dma(GUP[:], din["rwkv_g_up"][l], writes=[GUP])
        epsg = c.sb("epsg", [128, 1])
        c.memset("dve", epsg[:], 64e-5, [epsg])
        STs = c.sbn("STs", [128, 4, 64], F32, 2)
        c.memset("dve", STs[0][:], 0.0, [STs[0]])
        Gbd = c.sbn("Gbd", [128, 128], F32, 4)
        for p in range(4):
            c.memset("pool", Gbd[p][:], 0.0, [Gbd[p]])
        Hs = c.sb("Hs", [128, 4, 64])
        z = c.sbn("z", [128, 1536], F32, 2)
        zp = c.sbn("zp", [128, 1536], F32, 2)
        zm = c.sbn("zm", [128, 1536], F32, 2)
        cod = c.sbn("cod", [128, 2, 129], F32, 2)
        dcd = c.sb("dcd", [128, 2, 128])
        cm = c.sb("cm", [128, 2, 128])
        lw = c.sb("lw", [128, 128])
        sgd = c.sb("sgd", [128, 128])
        tmpw = c.sb("tmpw", [128, 512])
        sigw = c.sb("sigw", [128, 512])
        av = c.sb("av", [128, 512])
        gv = c.sb("gv", [128, 512])
        kkraw = c.sb("kkraw", [128, 512])
        sqk = c.sb("sqk", [128, 512])
        s8 = c.sbn("s8_", [128, 8], F32, 6)
        kk = c.sb("kk", [128, 512])
        kmod = c.sb("kmod", [128, 512])
        bv = c.sb("bv", [128, 512])
        ee = c.sb("eed", [128, 4, 512])
        TM = c.sb("TM", [128, 4, 512])
        BH = c.sb("BH", [128, 512])
        KH = c.sb("KH", [128, 512])
        PCc = c.sb("PCc", [128, 4])
        FT = c.sbn("FTd", [128, 4, 128], F32, 4)
        GM = c.sbn("GM", [128, 512], F32, 2)
        Lc = c.sbn("Lc", [128, 2, 128], BF16, 4)
        Wb = c.sbn("Wb", [128, 128], BF16, 4)
        Wc = c.sbn("Wc", [128, 128], F32, 4)
        LcH = [Lc, c.sbn("Lc1_", [128, 2, 128], BF16, 4)]
        WcH = [Wc, c.sbn("Wc1_", [128, 128], F32, 4)]
        WbH = [Wb, c.sbn("Wb1_", [128, 128], BF16, 4)]
        Rp = c.sb("Rp", [128, 512])
        Yl = c.sb("Yl", [128, 512])
        RT = c.sbn("RTd", [128, 128], F32, 2)
        yv = c.sb("yv", [128, 512])
        yc = c.sb("yc", [128, 512])
        sq2 = c.sb("sq2", [128, 512])
        bon = c.sb("bon", [128, 512])
        yo = c.sb("yo", [128, 512])
        obT = c.sbn("obTd", [128, 4, 128], BF16, 2)
        B0 = c.ps("B0", [128, 512]); B1 = c.ps("B1", [128, 512]); B2 = c.ps("B2", [128, 512])
        B3 = c.ps("B3", [128, 512]); B4 = c.ps("B4", [128, 512]); B5 = c.ps("B5", [128, 512])
        B6 = c.ps("B6", [128, 512]); B7 = c.ps("B7", [128, 512])
        pL = B5
        pY = pGs = pH = pS = pRT = B6
        rwcsrc = sc["rwcT"].rearrange("(c p) t -> p c t", p=128)
        for ti in range(S // 128):
            t0 = ti * 128
            i = ti % 2
            z_, zp_, zm_, cod_ = z[i], zp[i], zm[i], cod[i]
            c.dma(z_[:], sc["rwz"][t0:t0 + 128, :], writes=[z_], q="sp")
            if ti == 0:
                c.memset("pool", zp_[0:1, :], 0.0, [zp_])
                c.dma(zp_[1:128, :], sc["rwz"][0:127, :], writes=[zp_], q="act")
                c.memset("pool", cod_[:, :, 0:1], 0.0, [cod_])
                c.dma(cod_[:, :, 1:129], rwcsrc[:, :, 0:128], writes=[cod_], q="pool")
            else:
                c.dma(zp_[:], sc["rwz"][t0 - 1:t0 + 127, :], writes=[zp_], q="act")
                c.dma(cod_[:], rwcsrc[:, :, t0 - 1:t0 + 128], writes=[cod_], q="pool")
            c.tt("dve", zm_[:], zp_[:], z_[:], ALU.subtract, [zp_, z_], [zm_])
            c.tt("pool", zm_[:], zm_[:], mur[:], ALU.mult, [zm_, mur], [zm_])
            c.tt("dve", zm_[:], zm_[:], z_[:], ALU.add, [zm_, z_], [zm_])
            r_, k_, v_ = zm_[:, 0:512], zm_[:, 512:1024], zm_[:, 1024:1536]
            c.tt("pool", dcd[:], cod_[:, :, 0:128], cod_[:, :, 1:129], ALU.subtract, [cod_], [dcd])
            for ch in range(2):
                c.stt(cm[:, ch, :], dcd[:, ch, :], mucol[:, ch:ch + 1], cod_[:, ch, 1:129], ALU.mult, ALU.add, [dcd, mucol, cod_], [cm])
            c.act(lw[0:64, :], cm[0:64, 0, :], AF.Tanh, [cm], [lw])
            c.cp("pool", lw[64:128, :], cm[64:128, 0, :], [cm], [lw])
            c.act(sgd[:], cm[:, 1, :], AF.Sigmoid, [cm], [sgd])
            c.mm(B0[:], lw[0:64, :], WUP[0:64, :], True, True, [lw, WUP], [B0])
            c.mm(B1[:], lw[64:128, :], WUP[64:128, :], True, True, [lw, WUP], [B1])
            c.mm(B2[:], sgd[:], GUP[:], True, True, [sgd, GUP], [B2])
            c.tt("dve", tmpw[:], B0[:], w0r[:], ALU.add, [B0, w0r], [tmpw])
            c.act(sigw[:], tmpw[:], AF.Sigmoid, [tmpw], [sigw])
            c.tt("dve", tmpw[:], B1[:], a0r[:], ALU.add, [B1, a0r, sigw], [tmpw])
            c.act(av[:], tmpw[:], AF.Sigmoid, [tmpw], [av])
            c.cp("act", gv[:], B2[:], [B2], [gv])
            c.tt("pool", kkraw[:], k_, kkr[:], ALU.mult, [zm_, kkr], [kkraw])
            c.tt("pool", sqk[:], kkraw[:], kkraw[:], ALU.mult, [kkraw], [sqk])
            c.red(s8[0][:], v3(sqk[:]), ALU.add, [sqk], [s8[0]])
            c.act(s8[0][:], s8[0][:], AF.Sqrt, [s8[0]], [s8[0]])
            c.ts("dve", s8[0][:], s8[0][:], 1e-12, ALU.max, [s8[0]], [s8[0]])
            c.recip(s8[0][:], s8[0][:], [s8[0]], [s8[0]])
            c.tt("dve", v3(kk[:]), v3(kkraw[:]), bc3(s8[0][:]), ALU.mult, [kkraw, s8[0]], [kk])
            c.stt(kmod[:], av[:], -1.0, kar[:], ALU.add, ALU.mult, [av, kar], [kmod])
            c.stt(kmod[:], kmod[:], 1.0, k_, ALU.add, ALU.mult, [kmod, zm_], [kmod])
            c.tt("pool", bv[:], kk[:], av[:], ALU.mult, [kk, av], [bv])
            c.mm(B0[:], rwm[:, 0:128], sigw[:], True, True, [rwm, sigw], [B0])
            c.mm(B1[:], rwm[:, 128:256], sigw[:], True, True, [rwm, sigw], [B1])
            c.mm(B2[:], rwm[:, 256:384], sigw[:], True, True, [rwm, sigw], [B2])
            c.act(ee[:, 0, :], B1[:], AF.Exp, [B1], [ee], scale=-LAM)
            c.act(ee[:, 1, :], B0[:], AF.Exp, [B0], [ee], scale=-LAM)
            c.act(ee[:, 2, :], B0[:], AF.Exp, [B0], [ee], scale=LAM)
            c.act(ee[:, 3, :], B2[:], AF.Exp, [B2], [ee], scale=-LAM)
            c.stt(TM[:, 0, :], kk[:], -1.0, ee[:, 0, :], ALU.mult, ALU.mult, [kk, ee], [TM])
            c.tt("pool", TM[:, 1, :], r_, ee[:, 1, :], ALU.mult, [zm_, ee], [TM])
            c.tt("dve", TM[:, 2, :], bv[:], ee[:, 2, :], ALU.mult, [bv, ee], [TM])
            c.tt("pool", TM[:, 3, :], kmod[:], ee[:, 2, :], ALU.mult, [kmod, ee], [TM])
            c.tt("dve", BH[:], bv[:], ee[:, 3, :], ALU.mult, [bv, ee], [BH])
            c.tt("pool", KH[:], kmod[:], ee[:, 3, :], ALU.mult, [kmod, ee], [KH])
            for p in range(4):
                c.mm(B3[:, 2 * p:2 * p + 2], sigw[:, p * 128:(p + 1) * 128], ones[:, 0:2], True, True, [sigw, ones], [B3])
            c.act(PCc[:], B3[:, 0:8].rearrange("p (a b) -> p a b", b=2)[:, :, 0], AF.Exp, [B3], [PCc], scale=-LAM)
            STc, STn = STs[ti % 2], STs[(ti + 1) % 2]
            for p in range(4):
                pc = slice(p * 128, (p + 1) * 128)
                ft = FT[p]
                for q in range(4):
                    c.tr(B3[:, q * 128:(q + 1) * 128], TM[:, q, pc], ident[:], [TM, ident], [B3])
                c.cp("act", ft[:].rearrange("p a b -> p (a b)"), B3[:], [B3], [ft])
                def head_gen(hl, p=p, ft=ft):
                    h = 2 * p + hl
                    P = slice(hl * 64, (hl + 1) * 64)
                    hc = slice(h * 64, (h + 1) * 64)
                    vh = zm_[:, 1024 + h * 64:1024 + (h + 1) * 64]
                    bG = (B4, B0)[hl]
                    bD = (B5, B2)[hl]
                    gm = GM[hl]
                    Lr, Wr, Wbr = LcH[hl], WcH[hl], WbH[hl]
                    c.mm(bG[:, 0:256], ft[P, 2, :], ft[P, 0:2, :], True, True, [ft], [bG])
                    c.mm(bG[:, 256:512], ft[P, 3, :], ft[P, 0:2, :], True, True, [ft], [bG])
                    c.tt("dve", gm[:], bG[:], maskg[:], ALU.mult, [bG, maskg], [gm])
                    yield
                    LabT, MrbT, LakT, MrkT = gm[:, 0:128], gm[:, 128:256], gm[:, 256:384], gm[:, 384:512]
                    lc, wc, wb = Lr[0], Wr[0], Wbr[0]
                    c.tr(bD[:, 384:512], LabT, ident[:], [gm, ident], [bD])
                    c.mm(bD[:, 0:64], LakT, vh, True, True, [gm, zm_], [bD])
                    c.cp("act", lc[:, 0, :], bD[:, 384:512], [bD], [lc])
                    c.cp("pool", lc[:, 1, :], LabT, [gm], [lc])
                    c.cp("act", wc[:, 64:128], bD[:, 0:64], [bD], [wc])
                    c.cp("pool", wc[:, 0:64], TM[:, 0, hc], [TM], [wc])
                    c.cp("pool", wb[:], wc[:], [wc], [wb])
                    yield
                    for j in range(7):
                        lcn, wcn, wbn = Lr[(j + 1) % 4], Wr[(j + 1) % 4], Wbr[(j + 1) % 4]
                        c.mm(bD[:, 0:128], lc[:, 1, :], wb[:], True, True, [lc, wb], [bD])
                        if j < 6:
                            c.mm(bD[:, 128:256], lc[:, 1, :], lc[:, 0, :], True, True, [lc], [bD])
                            c.mm(bD[:, 256:384], lc[:, 0, :], lc[:, 1, :], True, True, [lc], [bD])
                        c.tt("dve", wcn[:], bD[:, 0:128], wc[:], ALU.add, [bD, wc], [wcn])
                        if j < 6:
                            c.cp("act", lcn[:].rearrange("p a b -> p (a b)"), bD[:, 128:384], [bD], [lcn])
                            c.cp("pool", wbn[:], wcn[:], [wcn], [wbn])
                        lc, wc, wb = lcn, wcn, wbn
                        yield
                    c.mm(bG[:, 0:128], MrbT, wc[:], True, False, [gm, wc], [bG])
                    c.mm(bG[:, 64:128], MrkT, vh, False, True, [gm, zm_], [bG])
                    gcol = slice(128 + hl * 64, 128 + (hl + 1) * 64)
                    c.mm(bG[P, gcol], wc[:, 0:64], BH[:, hc], True, True, [wc, BH], [bG])
                    c.mm(bG[P, 256:320], BH[:, hc], wc[:, 64:128], True, False, [BH, wc], [bG])
                    c.mm(bG[P, 256:320], KH[:, hc], vh, False, True, [KH, zm_], [bG])
                    c.tt("dve", Rp[:, hc], TM[:, 1, hc], bG[:, 0:64], ALU.add, [TM, bG], [Rp])
                    c.stt(Gbd[p][P, hl * 64:(hl + 1) * 64], ident[P, hl * 64:(hl + 1) * 64], PCc[P, p:p + 1], bG[P, gcol],
                          ALU.mult, ALU.add, [ident, PCc, bG], [Gbd[p]])
                    c.cp("act", Yl[:, hc], bG[:, 64:128], [bG], [Yl])
                    c.cp("act", Hs[P, p, :], bG[P, 256:320], [bG], [Hs])

                gens = [head_gen(0), head_gen(1)]
                while gens:
                    for g in list(gens):
                        try:
                            next(g)
                        except StopIteration:
                            gens.remove(g)
                rt = RT[p % 2]
                c.tr(pRT[:, 384:512], Rp[:, pc], ident[:], [Rp, ident], [pRT])
                c.cp("act", rt[:], pRT[:, 384:512], [pRT], [rt])
                for hl in range(2):
                    h = 2 * p + hl
                    P = slice(hl * 64, (hl + 1) * 64)
                    yb = B7 if hl == 0 else B1
                    c.mm(yb[:, h * 64:(h + 1) * 64], rt[P, :], STc[P, p, :], True, True, [rt, STc], [yb])
                c.mm(pS[:, 320:384], Gbd[p][:], STc[:, p, :], True, True, [Gbd[p], STc], [pS])
                c.tt("dve", STn[:, p, :], pS[:, 320:384], Hs[:, p, :], ALU.add, [pS, Hs], [STn])
            for hl in range(2):
                yb = B7 if hl == 0 else B1
                c.tt("dve", v3(yv[:])[:, hl::2, :], v3(yb[:])[:, hl::2, :], v3(Yl[:])[:, hl::2, :], ALU.add, [yb, Yl], [yv])
            c.red(s8[1][:], v3(yv[:]), ALU.add, [yv], [s8[1]])
            c.ts("dve", s8[1][:], s8[1][:], 1.0 / 64, ALU.mult, [s8[1]], [s8[1]])
            c.tt("dve", v3(yc[:]), v3(yv[:]), bc3(s8[1][:]), ALU.subtract, [yv, s8[1]], [yc])
            c.tt("pool", sq2[:], yc[:], yc[:], ALU.mult, [yc], [sq2])
            c.red(s8[2][:], v3(sq2[:]), ALU.add, [sq2], [s8[2]])
            c.act(s8[2][:], s8[2][:], AF.Sqrt, [s8[2], epsg], [s8[2]], bias=epsg[:, 0:1], scale=1.0 / 64)
            c.recip(s8[2][:], s8[2][:], [s8[2]], [s8[2]])
            c.tt("dve", v3(yc[:]), v3(yc[:]), bc3(s8[2][:]), ALU.mult, [yc, s8[2]], [yc])
            c.tt("pool", yc[:], yc[:], lngr[:], ALU.mult, [yc, lngr], [yc])
            c.tt("pool", yc[:], yc[:], lnbr[:], ALU.add, [yc, lnbr], [yc])
            c.tt("pool", sq2[:], r_, kmod[:], ALU.mult, [zm_, kmod], [sq2])
            c.tt("pool", sq2[:], sq2[:], rkr[:], ALU.mult, [sq2, rkr], [sq2])
            c.red(s8[3][:], v3(sq2[:]), ALU.add, [sq2], [s8[3]])
            c.tt("dve", v3(bon[:]), v3(v_), bc3(s8[3][:]), ALU.mult, [zm_, s8[3]], [bon])
            c.tt("dve", yo[:], yc[:], bon[:], ALU.add, [yc, bon], [yo])
            c.tt("dve", yo[:], yo[:], gv[:], ALU.mult, [yo, gv], [yo])
            for q in range(4):
                c.tr(B4[:, q * 128:(q + 1) * 128], yo[:, q * 128:(q + 1) * 128], ident[:], [yo, ident], [B4])
            c.cp("act", obT[i][:].rearrange("p a b -> p (a b)"), B4[:], [B4], [obT[i]])
            c.dma(sc["brT"][2].rearrange("(h v) t -> v h t", v=128)[:, :, t0:t0 + 128], obT[i][:], reads=[obT[i]])
        m.sched.emit()


def build(S, debug=False, nlayers=L, ph="ABCDEFGH"):
    m = Model(S, debug=debug)
    xa = m.scratch("xa", [S, D], F32)
    xb = m.scratch("xb", [S, D], F32)
    xc = m.scratch("xc", [S, D], F32)
    phase_prep(m)
    xin = m.din["x"]
    for l in range(nlayers):
        if "hg" not in m.scr:
            m.scratch("hg", [S, 2048], F32)
            m.scratch("dqT", [512, S], BF16)
            m.scratch("dkT", [512, S], BF16)
            m.scratch("dvv", [S, 512], BF16)
            m.scratch("rwz", [S, 1536], F32)
            m.scratch("rwcT", [256, S], F32)
            m.scratch("gateT", [3072, S], BF16)
        if "A" in ph:
            phase_A(m, l, xin)
        if "brT" not in m.scr:
            m.scratch("brT", [3, 512, S], BF16)
        if "B" in ph:
            phase_B(m, l)
        if "C" in ph:
            phase_C(m, l)
        if "D" in ph:
            phase_D(m, l)
        if "E" in ph:
            phase_E(m, l, xin, xa)
        if "F" in ph:
            phase_F(m, l, xa, xb)
        if "G" in ph:
            phase_G(m, l, xb, xc)
        xin = xc
    if "H" in ph:
        phase_H(m, xin)
    return m


_CACHE = {}


def kernel(**inputs):
    S = inputs["x"].shape[1]
    B = inputs["x"].shape[0]
    if S not in _CACHE:
        _CACHE[S] = build(S)
    m = _CACHE[S]
    consts = host_consts()
    base = {n: np.ascontiguousarray(np.asarray(inputs[n], np.float32)) for n, _ in PARAMS}
    for n, v in consts.items():
        base["c_" + n] = v
    in_maps = []
    for b in range(B):
        d = dict(base)
        d["x"] = np.ascontiguousarray(np.asarray(inputs["x"][b], np.float32))
        d["mem"] = np.ascontiguousarray(np.asarray(inputs["mem"][b], np.float32))
        in_maps.append(d)
    res = run_bass_kernel_spmd(m.nc, in_maps, core_ids=list(range(B)))
    return np.stack([np.asarray(r["out"], np.float32) for r in res.results], axis=0)
```

```python
import contextlib
import math
import numpy as np
import ml_dtypes
import concourse.bass as bass
import concourse.mybir as mybir
from concourse.bass_utils import run_bass_kernel_spmd

F32 = mybir.dt.float32
BF16 = mybir.dt.bfloat16
AF = mybir.ActivationFunctionType
ALU = mybir.AluOpType
AX = mybir.AxisListType

D = 1024
L = 2
NIN = 8448
DFF = 2816
MEM = 256
ENGS = ("pe", "act", "dve", "pool", "sp")
NSLOT = 6
LAM = math.exp(-0.5)
PE_DRAIN = False


class Buf:
    __slots__ = ("name", "last_w", "readers", "excl")

    def __init__(self, name=""):
        self.name = name
        self.last_w = None
        self.readers = []
        self.excl = False


class T:
    def __init__(self, t, name=""):
        self.t = t
        self.b = Buf(name)

    def __getitem__(self, idx):
        return self.t[idx]


def _b(x):
    return x.b if isinstance(x, T) else x


class Sched:
    def __init__(self, nc):
        self.nc = nc
        self.q = {e: [] for e in ENGS}
        self.cnt = {}
        self.seen = {e: {} for e in ENGS}
        self.semkeys = []
        for e in ENGS:
            self._mk(e)
        self.dma_n = {e: 0 for e in ENGS}
        for e in ("sp", "act", "pool"):
            for i in range(NSLOT):
                self._mk(("dma", e, i))
        self.n_instr = 0
        self.nblk = 0

    def _mk(self, k):
        self.cnt[k] = 0
        self.semkeys.append(k)

    def _need(self, e, deps):
        best = {}
        for d in deps:
            if d is None:
                continue
            k, v = d
            if k == e and e == "pe":
                continue
            if self.seen[e].get(k, 0) >= v:
                continue
            if best.get(k, 0) < v:
                best[k] = v
        for k, v in best.items():
            self.seen[e][k] = v
            self.q[e].append(("wait", k, v))

    def _deps(self, reads, writes):
        deps = []
        for r in reads:
            r = _b(r)
            deps.append(r.last_w)
            if r.excl:
                deps.extend(r.readers)
        for w in writes:
            w = _b(w)
            deps.append(w.last_w)
            deps.extend(w.readers)
        return deps

    MAXOPS = None

    def op(self, e, fn, reads=(), writes=()):
        if Sched.MAXOPS is not None and self.n_instr >= Sched.MAXOPS:
            return
        self._need(e, self._deps(reads, writes))
        self.cnt[e] += 1
        v = self.cnt[e]
        self.q[e].append(("op", fn, e))
        for w in writes:
            w = _b(w)
            w.last_w = (e, v)
            w.readers = []
        for r in reads:
            _b(r).readers.append((e, v))
        self.n_instr += 1

    def dma(self, e, out, in_, reads=(), writes=()):
        if Sched.MAXOPS is not None and self.n_instr >= Sched.MAXOPS:
            return
        n = self.dma_n[e]
        self.dma_n[e] += 1
        slot = ("dma", e, n % NSLOT)
        deps = self._deps(reads, writes)
        if self.cnt[slot] > 0:
            deps.append((slot, self.cnt[slot]))
        self._need(e, deps)
        self.cnt[slot] += 16
        v = self.cnt[slot]
        self.q[e].append(("dma", out, in_, slot))
        for w in writes:
            w = _b(w)
            w.last_w = (slot, v)
            w.readers = []
        for r in reads:
            _b(r).readers.append((slot, v))
        self.n_instr += 1

    def drain(self):
        deps = [(k, v) for k, v in self.cnt.items() if isinstance(k, tuple) and v > 0]
        self._need("sp", deps)

    def emit(self):
        nc = self.nc
        self.drain()
        sems = {}
        for k in self.semkeys:
            nm = "s_" + "_".join(str(x) for x in (k if isinstance(k, tuple) else (k,))) + f"_{self.nblk}"
            sems[k] = nc.alloc_semaphore(name=nm)
        self.nblk += 1
        if self.nblk == 1:
            nc.clear_and_free_semaphores(list(sems.values()))
            nc.all_engine_barrier()
            for k in self.semkeys:
                nm = "s0_" + "_".join(str(x) for x in (k if isinstance(k, tuple) else (k,)))
                sems[k] = nc.alloc_semaphore(name=nm)
        with contextlib.ExitStack() as st:
            st.enter_context(nc.allow_non_contiguous_dma(reason="small strided parameter loads"))
            block = st.enter_context(nc.Block())

            def run(eh, items):
                for it in items:
                    if it[0] == "wait":
                        eh.wait_ge(sems[it[1]], it[2])
                    elif it[0] == "raw":
                        it[1](eh)
                    elif it[0] == "op":
                        it[1](eh).then_inc(sems[it[2]], 1)
                    else:
                        eh.dma_start(out=it[1], in_=it[2]).then_inc(sems[it[3]], 16)

            @block.tensor
            def _(e):
                run(e, self.q["pe"])

            @block.scalar
            def _(e):
                run(e, self.q["act"])

            @block.vector
            def _(e):
                run(e, self.q["dve"])

            @block.gpsimd
            def _(e):
                run(e, self.q["pool"])

            @block.sync
            def _(e):
                run(e, self.q["sp"])
        nc.clear_and_free_semaphores(list(sems.values()))
        nc.all_engine_barrier()
        for k in self.cnt:
            self.cnt[k] = 0
        self.seen = {e: {} for e in ENGS}
        self.q = {e: [] for e in ENGS}


class Ctx:
    def __init__(self, nc, S_, st):
        self.nc = nc
        self.S = S_
        self.st = st
        self.rr = 0

    _uid = [0]

    def sb(self, name, shape, dt=F32):
        Ctx._uid[0] += 1
        name = f"t{Ctx._uid[0]}_{name}"
        return T(self.st.enter_context(self.nc.sbuf_tensor(name, list(shape), dt)), name)

    def ps(self, name, shape, dt=F32):
        Ctx._uid[0] += 1
        name = f"p{Ctx._uid[0]}_{name}"
        t = T(self.st.enter_context(self.nc.psum_tensor(name, list(shape), dt)), name)
        t.b.excl = True
        return t

    def sbn(self, name, shape, dt=F32, n=2):
        return [self.sb(f"{name}{i}", shape, dt) for i in range(n)]

    def psn(self, name, shape, dt=F32, n=2):
        return [self.ps(f"{name}{i}", shape, dt) for i in range(n)]

    _mode = [None]

    def _pe_mode(self, lhsT):
        def rnd(n):
            return 32 if n <= 32 else (64 if n <= 64 else 128)
        shp = lhsT.shape
        k = shp[0]
        mfree = 1
        for d in shp[1:]:
            mfree *= d
        mode = (rnd(k), rnd(mfree))
        if PE_DRAIN and Ctx._mode[0] is not None and Ctx._mode[0] != mode:
            if not (Sched.MAXOPS is not None and self.S.n_instr >= Sched.MAXOPS):
                self.S.q["pe"].append(("raw", lambda e: e.drain()))
        Ctx._mode[0] = mode

    def mm(self, out, lhsT, rhs, start, stop, reads, writes):
        self._pe_mode(lhsT)
        self.S.op("pe", lambda e: e.matmul(out, lhsT=lhsT, rhs=rhs, start=start, stop=stop), reads, writes)

    def tr(self, out, in_, ident, reads, writes):
        self._pe_mode(in_)
        self.S.op("pe", lambda e: e.transpose(out=out, in_=in_, identity=ident), reads, writes)

    def act(self, out, in_, func, reads, writes, bias=None, scale=None, accum=None, eng="act"):
        kw = {}
        if bias is not None:
            kw["bias"] = bias
        if scale is not None:
            kw["scale"] = scale
        if accum is not None:
            kw["accum_out"] = accum
        self.S.op("act", lambda e: e.activation(out=out, in_=in_, func=func, **kw), reads, writes)

    def tt(self, eng, out, in0, in1, op, reads, writes):
        self.S.op(eng, lambda e: e.tensor_tensor(out=out, in0=in0, in1=in1, op=op), reads, writes)

    def ts(self, eng, out, in0, s1, op0, reads, writes, s2=None, op1=None):
        if op1 is None:
            self.S.op(eng, lambda e: e.tensor_scalar(out=out, in0=in0, scalar1=s1, scalar2=None, op0=op0), reads, writes)
        else:
            self.S.op(eng, lambda e: e.tensor_scalar(out=out, in0=in0, scalar1=s1, scalar2=s2, op0=op0, op1=op1), reads, writes)

    def stt(self, out, in0, scalar, in1, op0, op1, reads, writes):
        self.S.op("dve", lambda e: e.scalar_tensor_tensor(out=out, in0=in0, scalar=scalar, in1=in1, op0=op0, op1=op1), reads, writes)

    def cp(self, eng, out, in_, reads, writes):
        if eng == "act":
            self.S.op("act", lambda e: e.activation(out=out, in_=in_, func=AF.Copy), reads, writes)
        else:
            self.S.op(eng, lambda e: e.tensor_copy(out=out, in_=in_), reads, writes)

    def red(self, out, in_, op, reads, writes):
        self.S.op("dve", lambda e: e.tensor_reduce(out=out, in_=in_, axis=AX.X, op=op), reads, writes)

    def recip(self, out, in_, reads, writes):
        self.S.op("dve", lambda e: e.reciprocal(out=out, in_=in_), reads, writes)

    def memset(self, eng, ap, val, writes):
        self.S.op(eng, lambda e: e.memset(ap, val), (), writes)

    def dma(self, out, in_, reads=(), writes=(), q=None):
        if q is None:
            q = ("sp", "act", "pool")[self.rr % 3]
            self.rr += 1
        self.S.dma(q, out, in_, reads, writes)


def host_consts():
    c = {}
    idx = np.arange(128)
    s = idx[:, None]
    t = idx[None, :]
    same = (s // 64) == (t // 64)
    c["ident"] = np.eye(128, dtype=np.float32)
    c["ones"] = np.ones((128, 128), np.float32)
    tri64 = ((s <= t) & same).astype(np.float32)
    mid64 = ((s <= (t // 64) * 64 + 31) & same).astype(np.float32)
    blk64 = same.astype(np.float32)
    c["hgm"] = np.concatenate([tri64, tri64 - mid64, blk64 - tri64], axis=1)
    incl = (s <= t).astype(np.float32)
    strict = (s < t).astype(np.float32)
    rev = (s > t).astype(np.float32)
    c["rwm"] = np.concatenate([incl, strict, rev], axis=1)
    c["maskg"] = np.concatenate([strict, incl, strict, incl], axis=1)
    scale = 64 ** -0.5
    slopes = 2.0 ** (-8.0 * np.arange(1, 5) / 4)
    ki = np.arange(128)[:, None].astype(np.float64)
    qi = np.arange(512)[None, :].astype(np.float64)
    al = np.zeros((4, 5, 128, 512), np.float32)
    for h in range(4):
        al[h, 0] = (-slopes[h] * (qi - ki) / scale)
        for d in range(4):
            dist = qi - ki - 128 * d
            al[h, 1 + d] = np.where(dist >= 0, -slopes[h] * dist / scale, -1e30)
    c["alibi"] = al.transpose(2, 0, 1, 3).reshape(128, 4 * 5 * 512).copy()
    ct = np.zeros((128, 4 * 65), np.float32)
    for h in range(4):
        ct[:, h * 65:(h + 1) * 65] = (-slopes[h] * 128.0 * np.arange(65))[None, :]
    c["ctab"] = ct
    return c


CONST_SHAPES = {"ident": (128, 128), "ones": (128, 128), "hgm": (128, 384), "rwm": (128, 384),
                "maskg": (128, 512), "alibi": (128, 4 * 5 * 512), "ctab": (128, 4 * 65)}

PARAMS = [
    ("norm_mix_g", (L, D)), ("w_in", (L, D, NIN)), ("b_gate", (L, 3072)), ("hgrn_lb_param", (L, 512)),
    ("hgrn_norm_g", (L, 512)), ("diff_lambda", (L, 4, 64)), ("diff_subln_g", (L, 128)),
    ("rwkv_mu", (L, 1792)), ("rwkv_w0", (L, 512)), ("rwkv_w_up", (L, 64, 512)), ("rwkv_a0", (L, 512)),
    ("rwkv_a_up", (L, 64, 512)), ("rwkv_g_up", (L, 128, 512)), ("rwkv_k_k", (L, 512)), ("rwkv_k_a", (L, 512)),
    ("rwkv_r_k", (L, 8, 64)), ("rwkv_ln_g", (L, 512)), ("rwkv_ln_b", (L, 512)),
    ("w_branch", (L, 3, 512, D)), ("w_out", (L, D, D)), ("norm_xa_g", (L, D)), ("norm_mem_g", (L, D)),
    ("xa_wq", (L, D, D)), ("xa_wkv", (L, D, 2 * D)), ("xa_wo", (L, D, D)), ("norm_ffn_g", (L, D)),
    ("ffn_w_up", (L, D, 2 * DFF)), ("ffn_conv_w", (L, 3, DFF)), ("ffn_conv_b", (L, DFF)),
    ("ffn_w_down", (L, DFF, D)), ("final_norm_g", (D,)),
]


class Model:
    def __init__(self, S, debug=False, phases=None):
        self.S = S
        self.debug = debug
        self.phases = phases
        nc = bass.Bass("TRN2", target_bir_lowering=False)
        self.nc = nc
        self.din = {}
        self.din["x"] = nc.dram_tensor("x", [S, D], F32, kind="ExternalInput").ap()
        self.din["mem"] = nc.dram_tensor("mem", [MEM, D], F32, kind="ExternalInput").ap()
        for n, shp in PARAMS:
            self.din[n] = nc.dram_tensor(n, list(shp), F32, kind="ExternalInput").ap()
        for n, shp in CONST_SHAPES.items():
            self.din["c_" + n] = nc.dram_tensor("c_" + n, list(shp), F32, kind="ExternalInput").ap()
        self.out = nc.dram_tensor("out", [S, D], F32, kind="ExternalOutput").ap()
        self.scr = {}
        self.sched = Sched(nc)

    def scratch(self, name, shape, dt):
        kind = "ExternalOutput" if self.debug else "Internal"
        self.scr[name] = self.nc.dram_tensor(name, list(shape), dt, kind=kind).ap()
        return self.scr[name]

    def ctx(self, st):
        return Ctx(self.nc, self.sched, st)


def phase_prep(m):
    nc, S_ = m.nc, m.sched
    specs = [("w_in", "norm_mix_g", D, NIN), ("w_out", None, D, D), ("xa_wq", "norm_xa_g", D, D),
             ("xa_wkv", "norm_mem_g", D, 2 * D), ("xa_wo", None, D, D), ("ffn_w_up", "norm_ffn_g", D, 2 * DFF),
             ("ffn_w_down", None, DFF, D), ("w_branch", None, 1536, D)]
    for name, g, K, N in specs:
        m.scratch("b_" + name, [L, K, N], BF16)
    with contextlib.ExitStack() as st:
        c = m.ctx(st)
        gt = c.sb("gt", [128, 4, L, 8])
        gi = 0
        gmap = {}
        for name, g, K, N in specs:
            if g is not None:
                gmap[g] = gi
                for l in range(L):
                    c.dma(gt[:, gi, l, :], m.din[g][l].rearrange("(c p) -> p c", p=128), writes=[gt])
                gi += 1
        W = 2048
        ins = c.sbn("pin", [128, W], F32, 5)
        outs = c.sbn("pout", [128, W], BF16, 5)
        k = 0
        for name, g, K, N in specs:
            for l in range(L):
                src = m.din[name][l]
                if name == "w_branch":
                    src = src.rearrange("a k n -> (a k) n")
                dst = m.scr["b_" + name][l]
                for kc in range(K // 128):
                    for n0 in range(0, N, W):
                        w = min(W, N - n0)
                        ti, to = ins[k % 5], outs[k % 5]
                        c.dma(ti[:, 0:w], src[kc * 128:(kc + 1) * 128, n0:n0 + w], writes=[ti])
                        eng = ("dve", "act", "dve", "pool", "act")[k % 5]
                        if g is not None:
                            if eng == "act":
                                c.act(to[:, 0:w], ti[:, 0:w], AF.Copy, [ti, gt], [to], scale=gt[:, gmap[g], l, kc:kc + 1])
                            else:
                                c.ts(eng, to[:, 0:w], ti[:, 0:w], gt[:, gmap[g], l, kc:kc + 1], ALU.mult, [ti, gt], [to])
                        else:
                            c.cp(eng, to[:, 0:w], ti[:, 0:w], [ti], [to])
                        c.dma(dst[kc * 128:(kc + 1) * 128, n0:n0 + w], to[:, 0:w], reads=[to])
                        k += 1
        S_.emit()


def load_consts(m, c, names):
    r = {}
    for n in names:
        shp = CONST_SHAPES[n]
        t = c.sb("c_" + n, shp, F32)
        c.dma(t[:], m.din["c_" + n][:, :], writes=[t])
        r[n] = t
    return r


def make_bf(c, src, shape, name):
    t = c.sb(name, shape, BF16)
    c.cp("dve", t[:], src[:], [src], [t])
    return t


def norm_T(c, xsrc, hT, col0, xt, xb, junk, ss, rstd, eps, pT, identb):
    c.dma(xt[:], xsrc, writes=[xt])
    c.act(junk[:], xt[:], AF.Square, [xt], [junk, ss], accum=ss[:, 0:1])
    c.act(rstd[:, 0:1], ss[:, 0:1], AF.Sqrt, [ss, eps], [rstd], bias=eps[:, 0:1], scale=1.0 / D)
    c.recip(rstd[:, 0:1], rstd[:, 0:1], [rstd], [rstd])
    c.ts("dve", xb[:], xt[:], rstd[:, 0:1], ALU.mult, [xt, rstd], [xb])
    for kc in range(8):
        c.tr(pT[:, kc, :], xb[:, kc * 128:(kc + 1) * 128], identb[:], [xb, identb], [pT])
    c.cp("pool" if False else "act", hT[:, :, col0:col0 + 128], pT[:], [pT], [hT])


def phase_A(m, l, xsrc):
    S = m.S
    TG = min(S, 2048)
    TB = min(512, TG)
    sc = m.scr
    if "hg" not in sc:
        m.scratch("hg", [S, 2048], F32)
        m.scratch("dqT", [512, S], BF16)
        m.scratch("dkT", [512, S], BF16)
        m.scratch("dvv", [S, 512], BF16)
        m.scratch("rwz", [S, 1536], F32)
        m.scratch("rwcT", [256, S], F32)
        m.scratch("gateT", [3072, S], BF16)
    blocks = [
        (0, 512, "tok", "hg", 0, AF.Silu, F32), (512, 512, "tok", "hg", 512, AF.Sigmoid, F32),
        (1024, 512, "tok", "hg", 1024, AF.Copy, F32), (1536, 512, "tok", "hg", 1536, AF.Silu, F32),
        (2048, 512, "feat", "dqT", 0, AF.Copy, BF16), (2560, 512, "feat", "dkT", 0, AF.Copy, BF16),
        (3072, 512, "tok", "dvv", 0, AF.Copy, BF16),
        (3584, 512, "tok", "rwz", 0, AF.Copy, F32), (4096, 512, "tok", "rwz", 512, AF.Copy, F32),
        (4608, 512, "tok", "rwz", 1024, AF.Copy, F32), (5120, 256, "feat", "rwcT", 0, AF.Copy, F32),
    ] + [(5376 + i * 512, 512, "gate", "gateT", i * 512, AF.Sigmoid, BF16) for i in range(6)]
    with contextlib.ExitStack() as st:
        c = m.ctx(st)
        cs = load_consts(m, c, ["ident"])
        identb = make_bf(c, cs["ident"], [128, 128], "identb")
        eps = c.sb("eps", [128, 1])
        c.memset("dve", eps[:], 1e-6, [eps])
        bg = c.sb("bg", [128, 24])
        c.dma(bg[:], m.din["b_gate"][l].rearrange("(c p) -> p c", p=128), writes=[bg])
        hT = c.sb("hT", [128, 8, TG], BF16)
        xts = c.sbn("xt", [128, D], F32, 2)
        xbs = c.sbn("xb", [128, D], BF16, 2)
        junk = c.sb("junk", [128, D], F32)
        sss = c.sbn("ss", [128, 1], F32, 2)
        rstds = c.sbn("rstd", [128, 1], F32, 2)
        pTs = c.psn("pT", [128, 8, 128], BF16, 2)
        wbs = c.sbn("wb", [128, 8, 512], BF16, 2)
        pos = c.psn("po", [128, 512], F32, 4)
        o32 = c.sbn("o32", [128, 512], F32, 3)
        o16 = c.sbn("o16", [128, 512], BF16, 3)
        wsrc = sc["b_w_in"][l].rearrange("(c p) n -> p c n", p=128)
        it = 0
        for g0 in range(0, S, TG):
            for tt in range(TG // 128):
                i = tt % 2
                norm_T(c, xsrc[g0 + tt * 128:g0 + (tt + 1) * 128, :], hT, tt * 128, xts[i], xbs[i], junk, sss[i],
                       rstds[i], eps, pTs[i], identb)
            for bi, (c0, ncol, kind, dname, doff, func, dt) in enumerate(blocks):
                wb = wbs[bi % 2]
                c.dma(wb[:, :, 0:ncol], wsrc[:, :, c0:c0 + ncol], writes=[wb], q="sp")
                dest = sc[dname]
                if kind == "tok":
                    for tt in range(TG // 128):
                        po = pos[it % 4]
                        ot = (o32 if dt == F32 else o16)[it % 3]
                        it += 1
                        for kc in range(8):
                            c.mm(po[:, 0:ncol], hT[:, kc, tt * 128:(tt + 1) * 128], wb[:, kc, 0:ncol], kc == 0, kc == 7,
                                 [hT, wb], [po])
                        c.act(ot[:, 0:ncol], po[:, 0:ncol], func, [po], [ot])
                        c.dma(dest[g0 + tt * 128:g0 + (tt + 1) * 128, doff:doff + ncol], ot[:, 0:ncol], reads=[ot])
                else:
                    for fc in range(ncol // 128):
                        for tb in range(TG // TB):
                            po = pos[it % 4]
                            ot = (o32 if dt == F32 else o16)[it % 3]
                            it += 1
                            for kc in range(8):
                                c.mm(po[:, 0:TB], wb[:, kc, fc * 128:(fc + 1) * 128], hT[:, kc, tb * TB:(tb + 1) * TB],
                                     kc == 0, kc == 7, [hT, wb], [po])
                            if kind == "gate":
                                gc = (doff + fc * 128) // 128
                                c.act(ot[:, 0:TB], po[:, 0:TB], func, [po, bg], [ot], bias=bg[:, gc:gc + 1])
                            else:
                                c.act(ot[:, 0:TB], po[:, 0:TB], func, [po], [ot])
                            r0 = doff + fc * 128
                            c.dma(dest[r0:r0 + 128, g0 + tb * TB:g0 + (tb + 1) * TB], ot[:, 0:TB], reads=[ot])
        m.sched.emit()


def rowb(m, c, name, src1d, F):
    t = c.sb(name, [128, F], F32)
    c.dma(t[:], src1d.partition_broadcast(128), writes=[t])
    return t


def sub(parent):
    v = T(parent.t, parent.b.name + "_v")
    return v


def bc3(ap2, n=64):
    H = ap2.shape[1]
    return ap2.unsqueeze(2).to_broadcast([128, H, n])


def v3(ap2, n=64):
    return ap2.rearrange("p (h n) -> p h n", n=n)


def phase_B(m, l):
    S = m.S
    sc = m.scr
    if "brT" not in sc:
        m.scratch("brT", [3, 512, S], BF16)
    with contextlib.ExitStack() as st:
        c = m.ctx(st)
        cs = load_consts(m, c, ["ident", "hgm", "ones"])
        ident, hgm, ones = cs["ident"], cs["hgm"], cs["ones"]
        eps = c.sb("eps", [128, 1])
        c.memset("dve", eps[:], 1e-6, [eps])
        lbrow = c.sb("lbrow", [128, 512])
        omlb = c.sb("omlb", [128, 512])
        if l == 0:
            c.memset("dve", lbrow[:], 0.0, [lbrow])
        else:
            a0 = rowb(m, c, "lba0", m.din["hgrn_lb_param"][0], 512)
            a1 = rowb(m, c, "lba1", m.din["hgrn_lb_param"][1], 512)
            c.tt("dve", a1[:], a1[:], a0[:], ALU.subtract, [a0, a1], [a1])
            c.act(lbrow[:], a1[:], AF.Sigmoid, [a1], [lbrow])
        c.ts("dve", omlb[:], lbrow[:], -1.0, ALU.mult, [lbrow], [omlb], s2=1.0, op1=ALU.add)
        ngrow = rowb(m, c, "ngrow", m.din["hgrn_norm_g"][l], 512)
        Sst = [c.sbn(f"Sst{h}_", [128, 128], F32, 2) for h in range(4)]
        for h in range(4):
            c.memset("pool", Sst[h][0][:], 0.0, [Sst[h][0]])
        hgt = c.sbn("hgt", [128, 2048], F32, 2)
        fv = c.sbn("fv", [128, 512], F32, 2)
        kf = c.sbn("kf", [128, 512], F32, 2)
        lf = c.sbn("lf", [128, 512], F32, 2)
        ee = c.sbn("ee", [128, 4, 512], F32, 2)
        qk = c.sbn("qk", [128, 4, 512], F32, 2)
        FT = c.sbn("FT", [128, 3, 128], F32, 4)
        AT = c.sbn("AT", [128, 128], F32, 4)
        dcol = c.sbn("dcol", [128, 2], F32, 8)
        ssq = c.sbn("ssq", [128, 4], F32, 2)
        rstd = c.sbn("rstdh", [128, 4], F32, 2)
        gn = c.sbn("gn", [128, 512], F32, 2)
        ob = c.sbn("ob", [128, 512], BF16, 2)
        obf = c.sbn("obf", [128, 512], F32, 2)
        obT = c.sbn("obT", [128, 4, 128], BF16, 2)
        junk = c.sb("junkb", [128, 128], F32)
        pcs = c.psn("pcs", [128, 512], F32, 3)
        pTr = c.psn("pTr", [128, 512], F32, 2)
        pmis = c.psn("pmisc", [128, 512], F32, 2)
        po = c.ps("pob", [128, 512], F32)
        pTb = pTr[0]
        cn = 0
        for ti in range(S // 128):
            t0 = ti * 128
            i = ti % 2
            hg = hgt[i]
            c.dma(hg[:], sc["hg"][t0:t0 + 128, :], writes=[hg])
            q, sg, vv, gate = hg[:, 0:512], hg[:, 512:1024], hg[:, 1024:1536], hg[:, 1536:2048]
            f = fv[i]
            c.tt("dve", f[:], sg, omlb[:], ALU.mult, [hg, omlb], [f])
            c.tt("dve", f[:], f[:], lbrow[:], ALU.add, [f, lbrow], [f])
            c.ts("dve", f[:], f[:], 1e-30, ALU.max, [f], [f])
            c.ts("dve", kf[i][:], f[:], -1.0, ALU.mult, [f], [kf[i]], s2=1.0, op1=ALU.add)
            c.act(lf[i][:], f[:], AF.Ln, [f], [lf[i]])
            for k in range(3):
                c.mm(pcs[k][:], hgm[:, k * 128:(k + 1) * 128], lf[i][:], True, True, [hgm, lf[i]], [pcs[k]])
            e = ee[i]
            c.act(e[:, 0, :], pcs[1][:], AF.Exp, [pcs[1]], [e])
            c.act(e[:, 1, :], pcs[1][:], AF.Exp, [pcs[1]], [e], scale=-1.0)
            c.act(e[:, 2, :], pcs[0][:], AF.Exp, [pcs[0]], [e])
            c.act(e[:, 3, :], pcs[2][:], AF.Exp, [pcs[2]], [e])
            w = qk[i]
            c.tt("dve", w[:, 0, :], q, e[:, 0, :], ALU.mult, [hg, e], [w])
            c.tt("pool", w[:, 1, :], kf[i][:], e[:, 1, :], ALU.mult, [kf[i], e], [w])
            c.tt("dve", w[:, 2, :], q, e[:, 2, :], ALU.mult, [hg, e], [w])
            c.tt("pool", w[:, 3, :], kf[i][:], e[:, 3, :], ALU.mult, [kf[i], e], [w])
            c.tt("pool", gn[i][:], gate, ngrow[:], ALU.mult, [hg, ngrow], [gn[i]])
            def hgen(h):
                nonlocal cn
                hc = slice(h * 128, (h + 1) * 128)
                vh = hg[:, 1024 + h * 128:1024 + (h + 1) * 128]
                ft = FT[h]
                at = AT[h]
                ptr = pTr[h % 2]
                pm = pmis[h % 2]
                for k in range(3):
                    c.tr(ptr[:, k * 128:(k + 1) * 128], w[:, k, hc], ident[:], [w, ident], [ptr])
                c.cp("act", ft[:].rearrange("p a b -> p (a b)"), ptr[:, 0:384], [ptr], [ft])
                yield
                c.mm(pm[:, 0:128], ft[:, 1, :], ft[:, 0, :], True, True, [ft], [pm])
                c.tt("dve", at[:], pm[:, 0:128], hgm[:, 0:128], ALU.mult, [pm, hgm], [at])
                yield

                def chunk_state(ch):
                    nonlocal cn
                    P = slice(ch * 64, (ch + 1) * 64)
                    Sc = Sst[h][(2 * ti + ch) % 2]
                    Sn = Sst[h][(2 * ti + ch + 1) % 2]
                    dc = dcol[cn % 8]
                    cn += 1
                    c.mm(pm[:, 256:258], lf[i][P, hc], ones[P, 0:2], True, True, [lf[i], ones], [pm])
                    c.mm(pm[:, 128:256], w[P, 3, hc], hg[P, 1024 + h * 128:1024 + (h + 1) * 128], True, True, [w, hg], [pm])
                    c.act(dc[:], pm[:, 256:258], AF.Exp, [pm], [dc])
                    c.stt(Sn[:], Sc[:], dc[:, 0:1], pm[:, 128:256], ALU.mult, ALU.add, [Sc, dc, pm], [Sn])

                chunk_state(0)
                yield
                S0 = Sst[h][(2 * ti) % 2]
                S1 = Sst[h][(2 * ti + 1) % 2]
                c.mm(po[:, hc], at[:], vh, True, False, [at, hg], [po])
                c.mm(po[0:64, hc], ft[:, 2, 0:64], S0[:], False, False, [ft, S0], [po])
                c.mm(po[64:128, hc], ft[:, 2, 64:128], S1[:], False, True, [ft, S1], [po])
                yield
                chunk_state(1)

            gens = [hgen(h) for h in range(4)]
            while gens:
                for g in list(gens):
                    try:
                        next(g)
                    except StopIteration:
                        gens.remove(g)
            for h in range(4):
                hc = slice(h * 128, (h + 1) * 128)
                c.act(junk[:], po[:, hc], AF.Square, [po], [junk, ssq[i]], accum=ssq[i][:, h:h + 1])
            c.act(rstd[i][:], ssq[i][:], AF.Sqrt, [ssq[i], eps], [rstd[i]], bias=eps[:, 0:1], scale=1.0 / 128)
            c.recip(rstd[i][:], rstd[i][:], [rstd[i]], [rstd[i]])
            for h in range(4):
                hc = slice(h * 128, (h + 1) * 128)
                c.stt(obf[i][:, hc], po[:, hc], rstd[i][:, h:h + 1], gn[i][:, hc], ALU.mult, ALU.mult, [po, rstd[i], gn[i]], [obf[i]])
            for h in range(4):
                hc = slice(h * 128, (h + 1) * 128)
                c.tr(pTb[:, hc], obf[i][:, hc], ident[:], [obf[i], ident], [pTb])
            c.cp("act", obT[i][:].rearrange("p a b -> p (a b)"), pTb[:], [pTb], [obT[i]])
            c.dma(sc["brT"][0].rearrange("(h v) t -> v h t", v=128)[:, :, t0:t0 + 128], obT[i][:], reads=[obT[i]])
        m.sched.emit()


def phase_C(m, l):
    S = m.S
    sc = m.scr
    QB = min(512, S)
    nq = QB // 128
    NQ = S // QB
    NJ = S // 128
    lambda_init = 0.8 - 0.6 * math.exp(-0.3 * l)
    with contextlib.ExitStack() as st:
        c = m.ctx(st)
        cs = load_consts(m, c, ["ones", "ctab"])
        ones, ctab = cs["ones"], cs["ctab"]
        onesb = make_bf(c, ones, [128, 128], "onesb")
        eps = c.sb("eps", [128, 1])
        c.memset("dve", eps[:], 1e-6, [eps])
        lamr = rowb(m, c, "lamr", m.din["diff_lambda"][l].rearrange("a b -> (a b)"), 256)
        ltmp = c.sb("ltmp", [128, 128])
        lsum = c.sb("lsum", [128, 2])
        c.tt("dve", ltmp[:, 0:64], lamr[:, 0:64], lamr[:, 64:128], ALU.mult, [lamr], [ltmp])
        c.tt("dve", ltmp[:, 64:128], lamr[:, 128:192], lamr[:, 192:256], ALU.mult, [lamr, ltmp], [ltmp])
        c.red(lsum[:], ltmp[:].rearrange("p (a b) -> p a b", b=64), ALU.add, [ltmp], [lsum])
        c.act(lsum[:], lsum[:], AF.Exp, [lsum], [lsum])
        nlam = c.sb("nlam", [128, 1])
        c.tt("dve", nlam[:], lsum[:, 1:2], lsum[:, 0:1], ALU.subtract, [lsum], [nlam])
        c.ts("dve", nlam[:], nlam[:], -lambda_init, ALU.add, [nlam], [nlam])
        gcol = c.sb("gcol", [128, 1])
        c.dma(gcol[:], m.din["diff_subln_g"][l].rearrange("(p o) -> p o", o=1), writes=[gcol])
        c.ts("dve", gcol[:], gcol[:], 1.0 - lambda_init, ALU.mult, [gcol], [gcol])
        qT = c.sb("qT", [128, S], BF16)
        kT = c.sb("kT", [128, S], BF16)
        V = c.sb("V", [128, NJ, 128], BF16)
        AL = c.sb("AL", [128, 5, 512], F32)
        TMP = c.sbn("tmpc", [128, 512], F32, 3)
        PT = c.sbn("ptc", [128, 512], BF16, 4)
        PST = c.psn("pst", [128, 512], F32, 4)
        PO = c.psn("poc", [128, 512], F32, 2)
        PL = c.psn("plc", [128, 512], F32, 2)
        rl = c.sbn("rlc", [128, 512], F32, 2)
        oc = c.sbn("occ", [128, 512], F32, 2)
        od = c.sb("odc", [128, 512], F32)
        sq = c.sb("sqc", [128, 512], F32)
        rs = c.sb("rsc", [128, 512], F32)
        obo = c.sbn("oboc", [128, 512], BF16, 2)
        cnt = 0
        for h in range(4):
            c.dma(qT[:], sc["dqT"][h * 128:(h + 1) * 128, :], writes=[qT], q="sp")
            c.dma(kT[:], sc["dkT"][h * 128:(h + 1) * 128, :], writes=[kT], q="act")
            vsrc = sc["dvv"].rearrange("(j p) c -> p j c", p=128)
            for j0 in range(0, NJ, 8):
                j1 = min(NJ, j0 + 8)
                c.dma(V[:, j0:j1, :], vsrc[:, j0:j1, h * 128:(h + 1) * 128], writes=[V], q=("pool", "sp", "act")[(j0 // 8) % 3])
            c.dma(AL[:].rearrange("p a b -> p (a b)"), m.din["c_alibi"][:, h * 2560:(h + 1) * 2560], writes=[AL], q="sp")
            for I in range(NQ):
                qs = slice(I * QB, (I + 1) * QB)
                jmax = nq * (I + 1) - 1
                units = [(j, c2) for j in range(jmax + 1) for c2 in range(2)]
                LA = 2
                pend = []

                def stage1(j, c2):
                    nonlocal cnt
                    d = j - nq * I
                    var = 0 if d < 0 else 1 + d
                    mc = (nq * I - j) if d < 0 else 0
                    P = slice(c2 * 64, (c2 + 1) * 64)
                    pst = PST[cnt % 4]
                    tmp = TMP[cnt % 3]
                    pt = PT[cnt % 4]
                    cnt += 1
                    c.mm(pst[:, 0:QB], kT[P, j * 128:(j + 1) * 128], qT[P, qs], True, True, [kT, qT], [pst])
                    c.tt("dve", tmp[:, 0:QB], pst[:, 0:QB], AL[:, var, 0:QB], ALU.add, [pst, AL], [tmp])
                    c.act(pt[:, 0:QB], tmp[:, 0:QB], AF.Exp, [tmp, ctab], [pt], bias=ctab[:, h * 65 + mc:h * 65 + mc + 1], scale=0.125)
                    return pt

                def stage2(j, c2, pt):
                    c.mm(PO[c2][:, 0:QB], V[:, j, :], pt[:, 0:QB], j == 0, j == jmax, [V, pt], [PO[c2]])
                    c.mm(PL[c2][:, 0:QB], onesb[:], pt[:, 0:QB], j == 0, j == jmax, [onesb, pt], [PL[c2]])

                for ui in range(0, len(units) + LA, 2):
                    for uu in (ui, ui + 1):
                        if uu < len(units):
                            j, c2 = units[uu]
                            pend.append((j, c2, stage1(j, c2)))
                    for uu in (ui, ui + 1):
                        if uu >= LA and pend and uu - LA < len(units):
                            stage2(*pend.pop(0))
                while pend:
                    stage2(*pend.pop(0))
                for c2 in range(2):
                    c.recip(rl[c2][:, 0:QB], PL[c2][:, 0:QB], [PL[c2]], [rl[c2]])
                    c.tt("dve", oc[c2][:, 0:QB], PO[c2][:, 0:QB], rl[c2][:, 0:QB], ALU.mult, [PO[c2], rl[c2]], [oc[c2]])
                c.stt(od[:, 0:QB], oc[1][:, 0:QB], nlam[:, 0:1], oc[0][:, 0:QB], ALU.mult, ALU.add, [oc[0], oc[1], nlam], [od])
                c.tt("pool", sq[:, 0:QB], od[:, 0:QB], od[:, 0:QB], ALU.mult, [od], [sq])
                pss = PST[cnt % 4]
                cnt += 1
                c.mm(pss[:, 0:QB], ones[:], sq[:, 0:QB], True, True, [ones, sq], [pss])
                c.act(rs[:, 0:QB], pss[:, 0:QB], AF.Sqrt, [pss, eps], [rs], bias=eps[:, 0:1], scale=1.0 / 128)
                c.recip(rs[:, 0:QB], rs[:, 0:QB], [rs], [rs])
                o_ = obo[I % 2]
                c.stt(o_[:, 0:QB], od[:, 0:QB], gcol[:, 0:1], rs[:, 0:QB], ALU.mult, ALU.mult, [od, gcol, rs], [o_])
                c.dma(sc["brT"][1][h * 128:(h + 1) * 128, qs], o_[:, 0:QB], reads=[o_])
        m.sched.emit()


def phase_E(m, l, xsrc, xdst):
    S = m.S
    sc = m.scr
    TB = min(512, S)
    with contextlib.ExitStack() as st:
        c = m.ctx(st)
        wbr = c.sb("wbr", [128, 12, D], BF16)
        wout = c.sb("wout", [128, 8, D], BF16)
        c.dma(wbr[:], sc["b_w_branch"][l].rearrange("(c p) n -> p c n", p=128), writes=[wbr], q="sp")
        c.dma(wout[:], sc["b_w_out"][l].rearrange("(c p) n -> p c n", p=128), writes=[wout], q="act")
        br = c.sbn("br", [128, 12, TB], BF16, 2)
        gt = c.sbn("gte", [128, 24, TB], BF16, 2)
        acc = c.sbn("acce", [128, TB], F32, 2)
        tmp = c.sbn("tmpe", [128, TB], F32, 3)
        mT = c.sbn("mT", [128, 8, TB], BF16, 2)
        xt = c.sbn("xte", [128, D], F32, 2)
        xo = c.sbn("xoe", [128, D], F32, 2)
        PM = c.psn("pme", [128, 512], F32, 4)
        PO = c.psn("poe", [128, 512], F32, 4)
        k = 0
        k2 = 0
        for tb in range(S // TB):
            ts_ = slice(tb * TB, (tb + 1) * TB)
            b_, g_, m_ = br[tb % 2], gt[tb % 2], mT[tb % 2]
            c.dma(b_[:], sc["brT"].rearrange("n (c p) t -> p (n c) t", p=128)[:, :, ts_], writes=[b_], q="sp")
            c.dma(g_[:], sc["gateT"].rearrange("(c p) t -> p c t", p=128)[:, :, ts_], writes=[g_], q="act")
            for dmc in range(8):
                a_ = acc[dmc % 2]
                for n in range(3):
                    pm = PM[k % 4]
                    for kc in range(4):
                        c.mm(pm[:, 0:TB], wbr[:, n * 4 + kc, dmc * 128:(dmc + 1) * 128], b_[:, n * 4 + kc, :], kc == 0, kc == 3, [wbr, b_], [pm])
                    if n == 0:
                        c.tt("dve", a_[:], pm[:, 0:TB], g_[:, n * 8 + dmc, :], ALU.mult, [pm, g_], [a_])
                    else:
                        t_ = tmp[k % 3]
                        c.tt("dve", t_[:], pm[:, 0:TB], g_[:, n * 8 + dmc, :], ALU.mult, [pm, g_], [t_])
                        if n == 1:
                            c.tt("pool", a_[:], a_[:], t_[:], ALU.add, [a_, t_], [a_])
                        else:
                            c.tt("pool", m_[:, dmc, :], a_[:], t_[:], ALU.add, [a_, t_], [m_])
                    k += 1
            for tt in range(TB // 128):
                x_, o_ = xt[k2 % 2], xo[k2 % 2]
                r0 = tb * TB + tt * 128
                c.dma(x_[:], xsrc[r0:r0 + 128, :], writes=[x_], q="pool")
                for cb in range(2):
                    po = PO[(2 * k2 + cb) % 4]
                    for kc in range(8):
                        c.mm(po[:], m_[:, kc, tt * 128:(tt + 1) * 128], wout[:, kc, cb * 512:(cb + 1) * 512], kc == 0, kc == 7, [m_, wout], [po])
                    c.tt("dve", o_[:, cb * 512:(cb + 1) * 512], po[:], x_[:, cb * 512:(cb + 1) * 512], ALU.add, [po, x_], [o_])
                c.dma(xdst[r0:r0 + 128, :], o_[:], reads=[o_], q="sp")
                k2 += 1
        m.sched.emit()


class NormBufs:
    def __init__(self, c, tag):
        self.xts = c.sbn("xt" + tag, [128, D], F32, 2)
        self.xbs = c.sbn("xb" + tag, [128, D], BF16, 2)
        self.junk = c.sb("junk" + tag, [128, D], F32)
        self.sss = c.sbn("ss" + tag, [128, 1], F32, 2)
        self.rstds = c.sbn("rstd" + tag, [128, 1], F32, 2)
        self.pTs = c.psn("pT" + tag, [128, 8, 128], BF16, 2)
        self.eps = c.sb("eps" + tag, [128, 1])
        c.memset("dve", self.eps[:], 1e-6, [self.eps])
        self.n = 0

    def run(self, c, xsrc, hT, col0, identb):
        i = self.n % 2
        self.n += 1
        norm_T(c, xsrc, hT, col0, self.xts[i], self.xbs[i], self.junk, self.sss[i], self.rstds[i], self.eps,
               self.pTs[i], identb)


def phase_F(m, l, xsrc, xdst):
    S = m.S
    sc = m.scr
    TB = min(512, S)
    with contextlib.ExitStack() as st:
        c = m.ctx(st)
        cs = load_consts(m, c, ["ident", "ones"])
        identb = make_bf(c, cs["ident"], [128, 128], "identb")
        onesb = make_bf(c, cs["ones"], [128, 128], "onesb")
        nb = NormBufs(c, "f")
        wq = c.sb("wq", [128, 8, D], BF16)
        wkv = c.sb("wkv", [128, 8, 2 * D], BF16)
        wo = c.sb("wo", [128, 8, D], BF16)
        c.dma(wq[:], sc["b_xa_wq"][l].rearrange("(c p) n -> p c n", p=128), writes=[wq], q="sp")
        c.dma(wkv[:], sc["b_xa_wkv"][l].rearrange("(c p) n -> p c n", p=128), writes=[wkv], q="act")
        c.dma(wo[:], sc["b_xa_wo"][l].rearrange("(c p) n -> p c n", p=128), writes=[wo], q="pool")
        memT = c.sb("memT", [128, 8, MEM], BF16)
        KT = c.sb("KT", [128, 8, MEM], BF16)
        Vm = c.sb("Vm", [128, 2, D], BF16)
        PA = c.psn("paf", [128, 512], F32, 3)
        for mt in range(2):
            nb.run(c, m.din["mem"][mt * 128:(mt + 1) * 128, :], memT, mt * 128, identb)
        k = 0
        for fc in range(8):
            pa = PA[k % 3]
            k += 1
            for kc in range(8):
                c.mm(pa[:, 0:MEM], wkv[:, kc, fc * 128:(fc + 1) * 128], memT[:, kc, :], kc == 0, kc == 7, [wkv, memT], [pa])
            c.cp("act", KT[:, fc, :], pa[:, 0:MEM], [pa], [KT])
        for mt in range(2):
            for cb in range(2):
                pa = PA[k % 3]
                k += 1
                for kc in range(8):
                    c.mm(pa[:], memT[:, kc, mt * 128:(mt + 1) * 128], wkv[:, kc, D + cb * 512:D + (cb + 1) * 512], kc == 0, kc == 7, [wkv, memT], [pa])
                c.cp("act", Vm[:, mt, cb * 512:(cb + 1) * 512], pa[:], [pa], [Vm])
        hT = c.sbn("hTf", [128, 8, TB], BF16, 2)
        qT = c.sbn("qTf", [128, 8, TB], BF16, 2)
        oT = c.sbn("oTf", [128, 8, TB], BF16, 2)
        pt = c.sbn("ptf", [128, 2, TB], BF16, 3)
        rl = c.sbn("rlf", [128, TB], F32, 2)
        xt = c.sbn("xtf", [128, D], F32, 2)
        xo = c.sbn("xof", [128, D], F32, 2)
        PB = c.psn("pbf", [128, 512], F32, 3)
        k2 = 0
        kp = 0
        for tb in range(S // TB):
            h_, q_, o_ = hT[tb % 2], qT[tb % 2], oT[tb % 2]
            for tt in range(TB // 128):
                r0 = tb * TB + tt * 128
                nb.run(c, xsrc[r0:r0 + 128, :], h_, tt * 128, identb)
            for fc in range(8):
                pa = PA[k % 3]
                k += 1
                for kc in range(8):
                    c.mm(pa[:, 0:TB], wq[:, kc, fc * 128:(fc + 1) * 128], h_[:, kc, :], kc == 0, kc == 7, [wq, h_], [pa])
                c.cp("act", q_[:, fc, :], pa[:, 0:TB], [pa], [q_])
            for h in range(4):
                p_ = pt[kp % 3]
                kp += 1
                for mc in range(2):
                    pa = PA[k % 3]
                    k += 1
                    for dc in range(2):
                        c.mm(pa[:, 0:TB], KT[:, h * 2 + dc, mc * 128:(mc + 1) * 128], q_[:, h * 2 + dc, :], dc == 0, dc == 1, [KT, q_], [pa])
                    c.act(p_[:, mc, :], pa[:, 0:TB], AF.Exp, [pa], [p_], scale=1.0 / 16)
                pl = PB[2]
                for mc in range(2):
                    c.mm(pl[:, 0:TB], onesb[:], p_[:, mc, :], mc == 0, mc == 1, [onesb, p_], [pl])
                r_ = rl[h % 2]
                c.recip(r_[:], pl[:, 0:TB], [pl], [r_])
                for dc in range(2):
                    pb = PB[dc]
                    for mc in range(2):
                        c.mm(pb[:, 0:TB], Vm[:, mc, h * 256 + dc * 128:h * 256 + (dc + 1) * 128], p_[:, mc, :], mc == 0, mc == 1, [Vm, p_], [pb])
                    c.tt("dve", o_[:, h * 2 + dc, :], pb[:, 0:TB], r_[:], ALU.mult, [pb, r_], [o_])
            for tt in range(TB // 128):
                x_, xo_ = xt[k2 % 2], xo[k2 % 2]
                r0 = tb * TB + tt * 128
                c.dma(x_[:], xsrc[r0:r0 + 128, :], writes=[x_], q="pool")
                for cb in range(2):
                    pa = PA[k % 3]
                    k += 1
                    for kc in range(8):
                        c.mm(pa[:], o_[:, kc, tt * 128:(tt + 1) * 128], wo[:, kc, cb * 512:(cb + 1) * 512], kc == 0, kc == 7, [o_, wo], [pa])
                    c.tt("dve", xo_[:, cb * 512:(cb + 1) * 512], pa[:], x_[:, cb * 512:(cb + 1) * 512], ALU.add, [pa, x_], [xo_])
                c.dma(xdst[r0:r0 + 128, :], xo_[:], reads=[xo_], q="sp")
                k2 += 1
        m.sched.emit()


def phase_G(m, l, xsrc, xdst):
    S = m.S
    sc = m.scr
    TBK = min(1024, S)
    SB = min(512, TBK)
    NFC = DFF // 128
    with contextlib.ExitStack() as st:
        c = m.ctx(st)
        cs = load_consts(m, c, ["ident"])
        identb = make_bf(c, cs["ident"], [128, 128], "identb")
        nb = NormBufs(c, "g")
        wdn = c.sb("wdn", [128, NFC, D], BF16)
        c.dma(wdn[:], sc["b_ffn_w_down"][l].rearrange("(c p) n -> p c n", p=128), writes=[wdn], q="pool")
        cw = c.sb("cw", [128, NFC, 3])
        for j in range(3):
            c.dma(cw[:, :, j], m.din["ffn_conv_w"][l, j].rearrange("(c p) -> p c", p=128), writes=[cw])
        cb_ = c.sb("cbias", [128, NFC])
        c.dma(cb_[:], m.din["ffn_conv_b"][l].rearrange("(c p) -> p c", p=128), writes=[cb_])
        halo = c.sb("halo", [128, NFC, 2])
        c.memset("dve", halo[:], 0.0, [halo])
        hT = c.sb("hTg", [128, 8, TBK], BF16)
        hid = c.sb("hid", [128, NFC, TBK], BF16)
        wu = c.sbn("wu", [128, 8, 512], BF16, 2)
        wv = c.sbn("wv", [128, 8, 512], BF16, 2)
        uext = c.sbn("uext", [128, SB + 2], F32, 2)
        t1 = c.sbn("t1g", [128, SB], F32, 2)
        t2 = c.sbn("t2g", [128, SB], F32, 2)
        sg = c.sbn("sgg", [128, SB], F32, 2)
        xt = c.sbn("xtg", [128, D], F32, 2)
        xo = c.sbn("xog", [128, D], F32, 2)
        PU = c.psn("pug", [128, 512], F32, 2)
        PV = c.psn("pvg", [128, 512], F32, 2)
        PO = c.psn("pog", [128, 512], F32, 2)
        wsrc = sc["b_ffn_w_up"][l].rearrange("(c p) n -> p c n", p=128)
        k = 0
        k2 = 0
        for tbk in range(S // TBK):
            for tt in range(TBK // 128):
                r0 = tbk * TBK + tt * 128
                nb.run(c, xsrc[r0:r0 + 128, :], hT, tt * 128, identb)
            for grp in range(6):
                ncol = 512 if grp < 5 else 256
                wu_, wv_ = wu[grp % 2], wv[grp % 2]
                c.dma(wu_[:, :, 0:ncol], wsrc[:, :, grp * 512:grp * 512 + ncol], writes=[wu_], q="sp")
                c.dma(wv_[:, :, 0:ncol], wsrc[:, :, DFF + grp * 512:DFF + grp * 512 + ncol], writes=[wv_], q="act")
                for fcl in range(ncol // 128):
                    fc = grp * 4 + fcl
                    for sbi in range(TBK // SB):
                        ss_ = slice(sbi * SB, (sbi + 1) * SB)
                        pu, pv = PU[k % 2], PV[k % 2]
                        ue, a1, a2, s_ = uext[k % 2], t1[k % 2], t2[k % 2], sg[k % 2]
                        k += 1
                        for kc in range(8):
                            c.mm(pu[:, 0:SB], wu_[:, kc, fcl * 128:(fcl + 1) * 128], hT[:, kc, ss_], kc == 0, kc == 7, [wu_, hT], [pu])
                        for kc in range(8):
                            c.mm(pv[:, 0:SB], wv_[:, kc, fcl * 128:(fcl + 1) * 128], hT[:, kc, ss_], kc == 0, kc == 7, [wv_, hT], [pv])
                        c.cp("pool", ue[:, 0:2], halo[:, fc, :], [halo], [ue])
                        c.cp("act", ue[:, 2:SB + 2], pu[:, 0:SB], [pu], [ue])
                        c.cp("pool", halo[:, fc, :], ue[:, SB:SB + 2], [ue], [halo])
                        c.act(a1[:], pu[:, 0:SB], AF.Identity, [pu, cw, cb_], [a1], bias=cb_[:, fc:fc + 1], scale=cw[:, fc, 2:3])
                        c.stt(a2[:], ue[:, 1:SB + 1], cw[:, fc, 1:2], a1[:], ALU.mult, ALU.add, [ue, cw, a1], [a2])
                        c.stt(a1[:], ue[:, 0:SB], cw[:, fc, 0:1], a2[:], ALU.mult, ALU.add, [ue, cw, a2], [a1])
                        c.act(s_[:], a1[:], AF.Silu, [a1], [s_])
                        c.tt("dve", hid[:, fc, ss_], s_[:], pv[:, 0:SB], ALU.mult, [s_, pv], [hid])
            for tt in range(TBK // 128):
                x_, xo_ = xt[k2 % 2], xo[k2 % 2]
                r0 = tbk * TBK + tt * 128
                c.dma(x_[:], xsrc[r0:r0 + 128, :], writes=[x_], q="pool")
                for cb in range(2):
                    po = PO[cb]
                    for fc in range(NFC):
                        c.mm(po[:], hid[:, fc, tt * 128:(tt + 1) * 128], wdn[:, fc, cb * 512:(cb + 1) * 512], fc == 0, fc == NFC - 1, [hid, wdn], [po])
                    c.tt("dve", xo_[:, cb * 512:(cb + 1) * 512], po[:], x_[:, cb * 512:(cb + 1) * 512], ALU.add, [po, x_], [xo_])
                c.dma(xdst[r0:r0 + 128, :], xo_[:], reads=[xo_], q="sp")
                k2 += 1
        m.sched.emit()


def phase_H(m, xsrc):
    S = m.S
    with contextlib.ExitStack() as st:
        c = m.ctx(st)
        grow = rowb(m, c, "fgrow", m.din["final_norm_g"], D)
        eps = c.sb("eps", [128, 1])
        c.memset("dve", eps[:], 1e-6, [eps])
        xt = c.sbn("xth", [128, D], F32, 3)
        xo = c.sbn("xoh", [128, D], F32, 3)
        junk = c.sb("junkh", [128, D], F32)
        ss = c.sbn("ssh", [128, 1], F32, 3)
        for ti in range(S // 128):
            i = ti % 3
            c.dma(xt[i][:], xsrc[ti * 128:(ti + 1) * 128, :], writes=[xt[i]])
            c.act(junk[:], xt[i][:], AF.Square, [xt[i]], [junk, ss[i]], accum=ss[i][:, 0:1])
            c.act(ss[i][:], ss[i][:], AF.Sqrt, [ss[i], eps], [ss[i]], bias=eps[:, 0:1], scale=1.0 / D)
            c.recip(ss[i][:], ss[i][:], [ss[i]], [ss[i]])
            c.stt(xo[i][:], xt[i][:], ss[i][:, 0:1], grow[:], ALU.mult, ALU.mult, [xt[i], ss[i], grow], [xo[i]])
            c.dma(m.out[ti * 128:(ti + 1) * 128, :], xo[i][:], reads=[xo[i]])
        m.sched.emit()


def phase_D(m, l):
    S = m.S
    sc = m.scr
    with contextlib.ExitStack() as st:
        c = m.ctx(st)
        cs = load_consts(m, c, ["ident", "rwm", "maskg", "ones"])
        ident, rwm, maskg, ones = cs["ident"], cs["rwm"], cs["maskg"], cs["ones"]
        din = m.din
        w0r = rowb(m, c, "w0r", din["rwkv_w0"][l], 512)
        a0r = rowb(m, c, "a0r", din["rwkv_a0"][l], 512)
        kkr = rowb(m, c, "kkr", din["rwkv_k_k"][l], 512)
        kar = rowb(m, c, "kar", din["rwkv_k_a"][l], 512)
        rkr = rowb(m, c, "rkr", din["rwkv_r_k"][l].rearrange("a b -> (a b)"), 512)
        lngr = rowb(m, c, "lngr", din["rwkv_ln_g"][l], 512)
        lnbr = rowb(m, c, "lnbr", din["rwkv_ln_b"][l], 512)
        mur = rowb(m, c, "mur", din["rwkv_mu"][l][0:1536], 1536)
        mucol = c.sb("mucol", [128, 2])
        c.dma(mucol[:], din["rwkv_mu"][l][1536:1792].rearrange("(c p) -> p c", p=128), writes=[mucol])
        WUP = c.sb("WUP", [128, 512])
        c.dma(WUP[0:64, :], din["rwkv_w_up"][l], writes=[WUP])
        c.dma(WUP[64:128, :], din["rwkv_a_up"][l], writes=[WUP])
        GUP = c.sb("GUP", [128, 512])
        c.dma(GUP[:], din["rwkv_g_up"][l], writes=[GUP])
        epsg = c.sb("epsg", [128, 1])
        c.memset("dve", epsg[:], 64e-5, [epsg])
        STs = c.sbn("STs", [128, 4, 64], F32, 2)
        c.memset("dve", STs[0][:], 0.0, [STs[0]])
        Gbd = c.sbn("Gbd", [128, 128], F32, 4)
        for p in range(4):
            c.memset("pool", Gbd[p][:], 0.0, [Gbd[p]])
        Hs = c.sb("Hs", [128, 4, 64])
        z = c.sbn("z", [128, 1536], F32, 2)
        zp = c.sbn("zp", [128, 1536], F32, 2)
        zm = c.sbn("zm", [128, 1536], F32, 2)
        cod = c.sbn("cod", [128, 2, 129], F32, 2)
        dcd = c.sb("dcd", [128, 2, 128])
        cm = c.sb("cm", [128, 2, 128])
        lw = c.sb("lw", [128, 128])
        sgd = c.sb("sgd", [128, 128])
        tmpw = c.sb("tmpw", [128, 512])
        sigw = c.sb("sigw", [128, 512])
        av = c.sb("av", [128, 512])
        gv = c.sb("gv", [128, 512])
        kkraw = c.sb("kkraw", [128, 512])
        sqk = c.sb("sqk", [128, 512])
        s8 = c.sbn("s8_", [128, 8], F32, 6)
        kk = c.sb("kk", [128, 512])
        kmod = c.sb("kmod", [128, 512])
        bv = c.sb("bv", [128, 512])
        ee = c.sb("eed", [128, 4, 512])
        TM = c.sb("TM", [128, 4, 512])
        BH = c.sb("BH", [128, 512])
        KH = c.sb("KH", [128, 512])
        PCc = c.sb("PCc", [128, 4])
        FT = c.sbn("FTd", [128, 4, 128], F32, 4)
        GM = c.sbn("GM", [128, 512], F32, 2)
        Lc = c.sbn("Lc", [128, 2, 128], BF16, 4)
        Wb = c.sbn("Wb", [128, 128], BF16, 4)
        Wc = c.sbn("Wc", [128, 128], F32, 4)
        LcH = [Lc, c.sbn("Lc1_", [128, 2, 128], BF16, 4)]
        WcH = [Wc, c.sbn("Wc1_", [128, 128], F32, 4)]
        WbH = [Wb, c.sbn("Wb1_", [128, 128], BF16, 4)]
        Rp = c.sb("Rp", [128, 512])
        Yl = c.sb("Yl", [128, 512])
        RT = c.sbn("RTd", [128, 128], F32, 2)
        yv = c.sb("yv", [128, 512])
        yc = c.sb("yc", [128, 512])
        sq2 = c.sb("sq2", [128, 512])
        bon = c.sb("bon", [128, 512])
        yo = c.sb("yo", [128, 512])
        obT = c.sbn("obTd", [128, 4, 128], BF16, 2)
        B0 = c.ps("B0", [128, 512]); B1 = c.ps("B1", [128, 512]); B2 = c.ps("B2", [128, 512])
        B3 = c.ps("B3", [128, 512]); B4 = c.ps("B4", [128, 512]); B5 = c.ps("B5", [128, 512])
        B6 = c.ps("B6", [128, 512]); B7 = c.ps("B7", [128, 512])
        pL = B5
        pY = pGs = pH = pS = pRT = B6
        rwcsrc = sc["rwcT"].rearrange("(c p) t -> p c t", p=128)
        for ti in range(S // 128):
            t0 = ti * 128
            i = ti % 2
            z_, zp_, zm_, cod_ = z[i], zp[i], zm[i], cod[i]
            c.dma(z_[:], sc["rwz"][t0:t0 + 128, :], writes=[z_], q="sp")
            if ti == 0:
                c.memset("pool", zp_[0:1, :], 0.0, [zp_])
                c.dma(zp_[1:128, :], sc["rwz"][0:127, :], writes=[zp_], q="act")
                c.memset("pool", cod_[:, :, 0:1], 0.0, [cod_])
                c.dma(cod_[:, :, 1:129], rwcsrc[:, :, 0:128], writes=[cod_], q="pool")
            else:
                c.dma(zp_[:], sc["rwz"][t0 - 1:t0 + 127, :], writes=[zp_], q="act")
                c.dma(cod_[:], rwcsrc[:, :, t0 - 1:t0 + 128], writes=[cod_], q="pool")
            c.tt("dve", zm_[:], zp_[:], z_[:], ALU.subtract, [zp_, z_], [zm_])
            c.tt("pool", zm_[:], zm_[:], mur[:], ALU.mult, [zm_, mur], [zm_])
            c.tt("dve", zm_[:], zm_[:], z_[:], ALU.add, [zm_, z_], [zm_])
            r_, k_, v_ = zm_[:, 0:512], zm_[:, 512:1024], zm_[:, 1024:1536]
            c.tt("pool", dcd[:], cod_[:, :, 0:128], cod_[:, :, 1:129], ALU.subtract, [cod_], [dcd])
            for ch in range(2):
                c.stt(cm[:, ch, :], dcd[:, ch, :], mucol[:, ch:ch + 1], cod_[:, ch, 1:129], ALU.mult, ALU.add, [dcd, mucol, cod_], [cm])
            c.act(lw[0:64, :], cm[0:64, 0, :], AF.Tanh, [cm], [lw])
            c.cp("pool", lw[64:128, :], cm[64:128, 0, :], [cm], [lw])
            c.act(sgd[:], cm[:, 1, :], AF.Sigmoid, [cm], [sgd])
            c.mm(B0[:], lw[0:64, :], WUP[0:64, :], True, True, [lw, WUP], [B0])
            c.mm(B1[:], lw[64:128, :], WUP[64:128, :], True, True, [lw, WUP], [B1])
            c.mm(B2[:], sgd[:], GUP[:], True, True, [sgd, GUP], [B2])
            c.tt("dve", tmpw[:], B0[:], w0r[:], ALU.add, [B0, w0r], [tmpw])
            c.act(sigw[:], tmpw[:], AF.Sigmoid, [tmpw], [sigw])
            c.tt("dve", tmpw[:], B1[:], a0r[:], ALU.add, [B1, a0r, sigw], [tmpw])
            c.act(av[:], tmpw[:], AF.Sigmoid, [tmpw], [av])
            c.cp("act", gv[:], B2[:], [B2], [gv])
            c.tt("pool", kkraw[:], k_, kkr[:], ALU.mult, [zm_, kkr], [kkraw])
            c.tt("pool", sqk[:], kkraw[:], kkraw[:], ALU.mult, [kkraw], [sqk])
            c.red(s8[0][:], v3(sqk[:]), ALU.add, [sqk], [s8[0]])
            c.act(s8[0][:], s8[0][:], AF.Sqrt, [s8[0]], [s8[0]])
            c.ts("dve", s8[0][:], s8[0][:], 1e-12, ALU.max, [s8[0]], [s8[0]])
            c.recip(s8[0][:], s8[0][:], [s8[0]], [s8[0]])
            c.tt("dve", v3(kk[:]), v3(kkraw[:]), bc3(s8[0][:]), ALU.mult, [kkraw, s8[0]], [kk])
            c.stt(kmod[:], av[:], -1.0, kar[:], ALU.add, ALU.mult, [av, kar], [kmod])
            c.stt(kmod[:], kmod[:], 1.0, k_, ALU.add, ALU.mult, [kmod, zm_], [kmod])
            c.tt("pool", bv[:], kk[:], av[:], ALU.mult, [kk, av], [bv])
            c.mm(B0[:], rwm[:, 0:128], sigw[:], True, True, [rwm, sigw], [B0])
            c.mm(B1[:], rwm[:, 128:256], sigw[:], True, True, [rwm, sigw], [B1])
            c.mm(B2[:], rwm[:, 256:384], sigw[:], True, True, [rwm, sigw], [B2])
            c.act(ee[:, 0, :], B1[:], AF.Exp, [B1], [ee], scale=-LAM)
            c.act(ee[:, 1, :], B0[:], AF.Exp, [B0], [ee], scale=-LAM)
            c.act(ee[:, 2, :], B0[:], AF.Exp, [B0], [ee], scale=LAM)
            c.act(ee[:, 3, :], B2[:], AF.Exp, [B2], [ee], scale=-LAM)
            c.stt(TM[:, 0, :], kk[:], -1.0, ee[:, 0, :], ALU.mult, ALU.mult, [kk, ee], [TM])
            c.tt("pool", TM[:, 1, :], r_, ee[:, 1, :], ALU.mult, [zm_, ee], [TM])
            c.tt("dve", TM[:, 2, :], bv[:], ee[:, 2, :], ALU.mult, [bv, ee], [TM])
            c.tt("pool", TM[:, 3, :], kmod[:], ee[:, 2, :], ALU.mult, [kmod, ee], [TM])
            c.tt("dve", BH[:], bv[:], ee[:, 3, :], ALU.mult, [bv, ee], [BH])
            c.tt("pool", KH[:], kmod[:], ee[:, 3, :], ALU.mult, [kmod, ee], [KH])
            for p in range(4):
                c.mm(B3[:, 2 * p:2 * p + 2], sigw[:, p * 128:(p + 1) * 128], ones[:, 0:2], True, True, [sigw, ones], [B3])
            c.act(PCc[:], B3[:, 0:8].rearrange("p (a b) -> p a b", b=2)[:, :, 0], AF.Exp, [B3], [PCc], scale=-LAM)
            STc, STn = STs[ti % 2], STs[(ti + 1) % 2]
            for p in range(4):
                pc = slice(p * 128, (p + 1) * 128)
                ft = FT[p]
                for q in range(4):
                    c.tr(B3[:, q * 128:(q + 1) * 128], TM[:, q, pc], ident[:], [TM, ident], [B3])
                c.cp("act", ft[:].rearrange("p a b -> p (a b)"), B3[:], [B3], [ft])
                def head_gen(hl, p=p, ft=ft):
                    h = 2 * p + hl
                    P = slice(hl * 64, (hl + 1) * 64)
                    hc = slice(h * 64, (h + 1) * 64)
                    vh = zm_[:, 1024 + h * 64:1024 + (h + 1) * 64]
                    bG = (B4, B0)[hl]
                    bD = (B5, B2)[hl]
                    gm = GM[hl]
                    Lr, Wr, Wbr = LcH[hl], WcH[hl], WbH[hl]
                    c.mm(bG[:, 0:256], ft[P, 2, :], ft[P, 0:2, :], True, True, [ft], [bG])
                    c.mm(bG[:, 256:512], ft[P, 3, :], ft[P, 0:2, :], True, True, [ft], [bG])
                    c.tt("dve", gm[:], bG[:], maskg[:], ALU.mult, [bG, maskg], [gm])
                    yield
                    LabT, MrbT, LakT, MrkT = gm[:, 0:128], gm[:, 128:256], gm[:, 256:384], gm[:, 384:512]
                    lc, wc, wb = Lr[0], Wr[0], Wbr[0]
                    c.tr(bD[:, 384:512], LabT, ident[:], [gm, ident], [bD])
                    c.mm(bD[:, 0:64], LakT, vh, True, True, [gm, zm_], [bD])
                    c.cp("act", lc[:, 0, :], bD[:, 384:512], [bD], [lc])
                    c.cp("pool", lc[:, 1, :], LabT, [gm], [lc])
                    c.cp("act", wc[:, 64:128], bD[:, 0:64], [bD], [wc])
                    c.cp("pool", wc[:, 0:64], TM[:, 0, hc], [TM], [wc])
                    c.cp("pool", wb[:], wc[:], [wc], [wb])
                    yield
                    for j in range(7):
                        lcn, wcn, wbn = Lr[(j + 1) % 4], Wr[(j + 1) % 4], Wbr[(j + 1) % 4]
                        c.mm(bD[:, 0:128], lc[:, 1, :], wb[:], True, True, [lc, wb], [bD])
                        if j < 6:
                            c.mm(bD[:, 128:256], lc[:, 1, :], lc[:, 0, :], True, True, [lc], [bD])
                            c.mm(bD[:, 256:384], lc[:, 0, :], lc[:, 1, :], True, True, [lc], [bD])
                        if j < 6:
                            c.tt("dve", wbn[:], bD[:, 0:128], wc[:], ALU.add, [bD, wc], [wbn])
                        c.tt("dve", wcn[:], bD[:, 0:128], wc[:], ALU.add, [bD, wc], [wcn])
                        if j < 6:
                            c.cp("act", lcn[:].rearrange("p a b -> p (a b)"), bD[:, 128:384], [bD], [lcn])
                        lc, wc, wb = lcn, wcn, wbn
                        yield
                    c.mm(bG[:, 0:128], MrbT, wc[:], True, False, [gm, wc], [bG])
                    c.mm(bG[:, 64:128], MrkT, vh, False, True, [gm, zm_], [bG])
                    gcol = slice(128 + hl * 64, 128 + (hl + 1) * 64)
                    c.mm(bG[P, gcol], wc[:, 0:64], BH[:, hc], True, True, [wc, BH], [bG])
                    c.mm(bG[P, 256:320], BH[:, hc], wc[:, 64:128], True, False, [BH, wc], [bG])
                    c.mm(bG[P, 256:320], KH[:, hc], vh, False, True, [KH, zm_], [bG])
                    c.tt("dve", Rp[:, hc], TM[:, 1, hc], bG[:, 0:64], ALU.add, [TM, bG], [Rp])
                    c.stt(Gbd[p][P, hl * 64:(hl + 1) * 64], ident[P, hl * 64:(hl + 1) * 64], PCc[P, p:p + 1], bG[P, gcol],
                          ALU.mult, ALU.add, [ident, PCc, bG], [Gbd[p]])
                    c.cp("act", Yl[:, hc], bG[:, 64:128], [bG], [Yl])
                    c.cp("act", Hs[P, p, :], bG[P, 256:320], [bG], [Hs])

                gens = [head_gen(0), head_gen(1)]
                while gens:
                    for g in list(gens):
                        try:
                            next(g)
                        except StopIteration:
                            gens.remove(g)
                rt = RT[p % 2]
                c.tr(pRT[:, 384:512], Rp[:, pc], ident[:], [Rp, ident], [pRT])
                c.cp("act", rt[:], pRT[:, 384:512], [pRT], [rt])
                for hl in range(2):
                    h = 2 * p + hl
                    P = slice(hl * 64, (hl + 1) * 64)
                    yb = B7 if hl == 0 else B1
                    c.mm(yb[:, h * 64:(h + 1) * 64], rt[P, :], STc[P, p, :], True, True, [rt, STc], [yb])
                c.mm(pS[:, 320:384], Gbd[p][:], STc[:, p, :], True, True, [Gbd[p], STc], [pS])
                c.tt("dve", STn[:, p, :], pS[:, 320:384], Hs[:, p, :], ALU.add, [pS, Hs], [STn])
            for hl in range(2):
                yb = B7 if hl == 0 else B1
                c.tt("dve", v3(yv[:])[:, hl::2, :], v3(yb[:])[:, hl::2, :], v3(Yl[:])[:, hl::2, :], ALU.add, [yb, Yl], [yv])
            c.red(s8[1][:], v3(yv[:]), ALU.add, [yv], [s8[1]])
            c.ts("dve", s8[1][:], s8[1][:], 1.0 / 64, ALU.mult, [s8[1]], [s8[1]])
            c.tt("dve", v3(yc[:]), v3(yv[:]), bc3(s8[1][:]), ALU.subtract, [yv, s8[1]], [yc])
            c.tt("pool", sq2[:], yc[:], yc[:], ALU.mult, [yc], [sq2])
            c.red(s8[2][:], v3(sq2[:]), ALU.add, [sq2], [s8[2]])
            c.act(s8[2][:], s8[2][:], AF.Sqrt, [s8[2], epsg], [s8[2]], bias=epsg[:, 0:1], scale=1.0 / 64)
            c.recip(s8[2][:], s8[2][:], [s8[2]], [s8[2]])
            c.tt("dve", v3(yc[:]), v3(yc[:]), bc3(s8[2][:]), ALU.mult, [yc, s8[2]], [yc])
            c.tt("pool", yc[:], yc[:], lngr[:], ALU.mult, [yc, lngr], [yc])
            c.tt("pool", yc[:], yc[:], lnbr[:], ALU.add, [yc, lnbr], [yc])
            c.tt("pool", sq2[:], r_, kmod[:], ALU.mult, [zm_, kmod], [sq2])
            c.tt("pool", sq2[:], sq2[:], rkr[:], ALU.mult, [sq2, rkr], [sq2])
            c.red(s8[3][:], v3(sq2[:]), ALU.add, [sq2], [s8[3]])
            c.tt("dve", v3(bon[:]), v3(v_), bc3(s8[3][:]), ALU.mult, [zm_, s8[3]], [bon])
            c.tt("dve", yo[:], yc[:], bon[:], ALU.add, [yc, bon], [yo])
            c.tt("dve", yo[:], yo[:], gv[:], ALU.mult, [yo, gv], [yo])
            for q in range(4):
                c.tr(B4[:, q * 128:(q + 1) * 128], yo[:, q * 128:(q + 1) * 128], ident[:], [yo, ident], [B4])
            c.cp("act", obT[i][:].rearrange("p a b -> p (a b)"), B4[:], [B4], [obT[i]])
            c.dma(sc["brT"][2].rearrange("(h v) t -> v h t", v=128)[:, :, t0:t0 + 128], obT[i][:], reads=[obT[i]])
        m.sched.emit()


def build(S, debug=False, nlayers=L, ph="ABCDEFGH"):
    m = Model(S, debug=debug)
    xa = m.scratch("xa", [S, D], F32)
    xb = m.scratch("xb", [S, D], F32)
    xc = m.scratch("xc", [S, D], F32)
    phase_prep(m)
    xin = m.din["x"]
    for l in range(nlayers):
        if "hg" not in m.scr:
            m.scratch("hg", [S, 2048], F32)
            m.scratch("dqT", [512, S], BF16)
            m.scratch("dkT", [512, S], BF16)
            m.scratch("dvv", [S, 512], BF16)
            m.scratch("rwz", [S, 1536], F32)
            m.scratch("rwcT", [256, S], F32)
            m.scratch("gateT", [3072, S], BF16)
        if "A" in ph:
            phase_A(m, l, xin)
        if "brT" not in m.scr:
            m.scratch("brT", [3, 512, S], BF16)
        if "B" in ph:
            phase_B(m, l)
        if "C" in ph:
            phase_C(m, l)
        if "D" in ph:
            phase_D(m, l)
        if "E" in ph:
            phase_E(m, l, xin, xa)
        if "F" in ph:
            phase_F(m, l, xa, xb)
        if "G" in ph:
            phase_G(m, l, xb, xc)
        xin = xc
    if "H" in ph:
        phase_H(m, xin)
    return m


_CACHE = {}


def kernel(**inputs):
    S = inputs["x"].shape[1]
    B = inputs["x"].shape[0]
    if S not in _CACHE:
        _CACHE[S] = build(S)
    m = _CACHE[S]
    consts = host_consts()
    base = {n: np.ascontiguousarray(np.asarray(inputs[n], np.float32)) for n, _ in PARAMS}
    for n, v in consts.items():
        base["c_" + n] = v
    in_maps = []
    for b in range(B):
        d = dict(base)
        d["x"] = np.ascontiguousarray(np.asarray(inputs["x"][b], np.float32))
        d["mem"] = np.ascontiguousarray(np.asarray(inputs["mem"][b], np.float32))
        in_maps.append(d)
    res = run_bass_kernel_spmd(m.nc, in_maps, core_ids=list(range(B)))
    return np.stack([np.asarray(r["out"], np.float32) for r in res.results], axis=0)
```
